# Optimizing a Trainium2 kernel written in Bass

```python
import jax, jax.numpy as jnp
from jax import lax
import numpy as np


D_MODEL = 1024
BATCH = 8
SEQ = 2048
DEPTH = 1

GRID_W = 64
CTX_LEN = 256

MLA_HEADS = 8
MLA_Q_RANK = 256
MLA_KV_RANK = 256
MLA_NOPE = 64
MLA_ROPE = 32
MLA_V = 64
MLA_SCALE = (MLA_NOPE + MLA_ROPE) ** -0.5
Q_BLOCK = 128

RET_HEADS = 4
RET_DK = 128
RET_DV = 128
RET_CHUNK = 128

D_FF = 2816
CONV_WIDTH = 3

ROPE_BASE = 10000.0
EPS = 1e-6

MLA_WIDTH = MLA_HEADS * MLA_V
RET_WIDTH = RET_HEADS * RET_DV
MIX_WIDTH = MLA_WIDTH + RET_WIDTH
IN_SIZES = (MLA_Q_RANK, MLA_KV_RANK, MLA_ROPE, RET_HEADS * RET_DK, RET_HEADS * RET_DK, RET_WIDTH, RET_WIDTH)
IN_SPLITS = tuple(int(v) for v in np.cumsum(IN_SIZES)[:-1])
IN_COLS = int(sum(IN_SIZES))

kernel_name = 'hybrid_mla_retention_dit_block'


def rmsnorm(x, g):
    xf = x.astype(jnp.float32)
    y = xf * lax.rsqrt(jnp.mean(xf * xf, axis=-1, keepdims=True) + EPS)
    return (y * g.astype(jnp.float32)).astype(x.dtype)


def modulate(h, shift, scale):
    return h * (1.0 + scale) + shift


def rope_angles(pos, dim):
    inv = ROPE_BASE ** (-jnp.arange(0, dim, 2, dtype=jnp.float32) / dim)
    return pos.astype(jnp.float32)[:, None] * inv[None, :]


def apply_rope(x, ang):
    x1, x2 = jnp.split(x, 2, axis=-1)
    cos = jnp.cos(ang)[None, :, None, :].astype(x.dtype)
    sin = jnp.sin(ang)[None, :, None, :].astype(x.dtype)
    return jnp.concatenate([x1 * cos - x2 * sin, x1 * sin + x2 * cos], axis=-1)


def axial_rope(x, pos):
    pos_row, pos_col = pos
    half = x.shape[-1] // 2
    xr = apply_rope(x[..., :half], rope_angles(pos_row, half))
    xc = apply_rope(x[..., half:], rope_angles(pos_col, half))
    return jnp.concatenate([xr, xc], axis=-1)


def split_proj(p):
    return jnp.split(p, IN_SPLITS, axis=-1)


def mla_queries(c_q, g_q, w_uq, pos):
    B, S, _ = c_q.shape
    q = (rmsnorm(c_q, g_q) @ w_uq).reshape(B, S, MLA_HEADS, MLA_NOPE + MLA_ROPE)
    q_nope, q_pe = q[..., :MLA_NOPE], q[..., MLA_NOPE:]
    if pos is not None:
        q_pe = axial_rope(q_pe, pos)
    return jnp.concatenate([q_nope, q_pe], axis=-1)


def mla_keys(c_kv, k_pe, g_kv, w_ukv, pos):
    B, S, _ = c_kv.shape
    kv = (rmsnorm(c_kv, g_kv) @ w_ukv).reshape(B, S, MLA_HEADS, MLA_NOPE + MLA_V)
    k_nope, v = kv[..., :MLA_NOPE], kv[..., MLA_NOPE:]
    k_pe = k_pe[:, :, None, :]
    if pos is not None:
        k_pe = axial_rope(k_pe, pos)
    k_pe = jnp.broadcast_to(k_pe, (B, S, MLA_HEADS, MLA_ROPE))
    return jnp.concatenate([k_nope, k_pe], axis=-1), v


def softmax_attention(q, k, v):
    s = jnp.einsum('bqhd,bkhd->bhqk', q, k).astype(jnp.float32) * MLA_SCALE
    p = jax.nn.softmax(s, axis=-1).astype(v.dtype)
    return jnp.einsum('bhqk,bkhd->bqhd', p, v)


def latent_attention(q, k_lat, v_lat, k_ctx, v_ctx):
    k = jnp.concatenate([k_lat, k_ctx], axis=1)
    v = jnp.concatenate([v_lat, v_ctx], axis=1)
    B, S, H, dq = q.shape
    qb = q.reshape(B, S // Q_BLOCK, Q_BLOCK, H, dq).swapaxes(0, 1)
    o = lax.map(lambda qi: softmax_attention(qi, k, v), qb)
    return o.swapaxes(0, 1).reshape(B, S, H * MLA_V)


def retention_inputs(rq, rk, rv, pos):
    B, S, _ = rq.shape
    q = rq.reshape(B, S, RET_HEADS, RET_DK)
    k = rk.reshape(B, S, RET_HEADS, RET_DK) * (RET_DK ** -0.5)
    v = rv.reshape(B, S, RET_HEADS, RET_DV)
    if pos is not None:
        ang = rope_angles(pos, RET_DK)
        q = apply_rope(q, ang)
        k = apply_rope(k, ang)
    return q, k, v


def retention_final_state(k, v, log_gamma, reverse):
    L = k.shape[1]
    j = jnp.arange(L, dtype=jnp.float32)
    expo = j if reverse else (L - 1.0 - j)
    w = jnp.exp(log_gamma[:, None] * expo[None, :])
    return jnp.einsum('bjhd,hj,bjhe->bhde', k.astype(jnp.float32), w, v.astype(jnp.float32))


def retention_chunkwise(q, k, v, log_gamma, init_state, strict):
    B, S, H, dk = q.shape
    dv = v.shape[-1]
    C = RET_CHUNK
    n = S // C
    idx = jnp.arange(C, dtype=jnp.float32)
    diff = idx[:, None] - idx[None, :]
    mask = (diff > 0) if strict else (diff >= 0)
    expo = jnp.where(mask, diff, 0.0)
    lg = log_gamma.astype(jnp.float32)
    decay_in = jnp.where(mask[None], jnp.exp(lg[:, None, None] * expo[None]), 0.0)
    xi = jnp.exp(lg[:, None] * (idx + 1.0)[None, :])[None, :, :, None]
    zeta = jnp.exp(lg[:, None] * (C - 1.0 - idx)[None, :])[None, :, :, None]
    g_chunk = jnp.exp(lg * C)[None, :, None, None]

    def to_chunks(a):
        return a.astype(jnp.float32).reshape(B, n, C, H, a.shape[-1]).transpose(1, 0, 3, 2, 4)

    def step(state, inp):
        qi, ki, vi = inp
        att = jnp.einsum('bhqd,bhkd->bhqk', qi, ki) * decay_in[None]
        inner = jnp.einsum('bhqk,bhke->bhqe', att, vi)
        cross = jnp.einsum('bhqd,bhde->bhqe', qi, state) * xi
        new_state = state * g_chunk + jnp.einsum('bhkd,bhke->bhde', ki * zeta, vi)
        return new_state, inner + cross

    _, out = lax.scan(step, init_state.astype(jnp.float32), (to_chunks(q), to_chunks(k), to_chunks(v)))
    return out.transpose(1, 0, 3, 2, 4).reshape(B, S, H, dv)


def head_groupnorm(y, g):
    B, S, H, dv = y.shape
    mu = jnp.mean(y, axis=-1, keepdims=True)
    var = jnp.mean(jnp.square(y - mu), axis=-1, keepdims=True)
    yn = ((y - mu) * lax.rsqrt(var + EPS)).reshape(B, S, H * dv)
    return yn * g.astype(jnp.float32)


def retention_bidir(q, k, v, gate, log_gamma, state_f, state_b, g_ret):
    flip = lambda a: jnp.flip(a, axis=1)
    y_f = retention_chunkwise(q, k, v, log_gamma[0], state_f, False)
    y_b = flip(retention_chunkwise(flip(q), flip(k), flip(v), log_gamma[1], state_b, True))
    y = head_groupnorm(y_f + y_b, g_ret)
    return (jax.nn.silu(gate.astype(jnp.float32)) * y).astype(v.dtype)


def conv_ffn(h, w_up, conv_w, conv_b, w_down):
    u = h @ w_up
    u = lax.conv_general_dilated(u, conv_w[:, None, :].astype(u.dtype), window_strides=(1,),
                                 padding=((CONV_WIDTH // 2, CONV_WIDTH // 2),),
                                 dimension_numbers=('NWC', 'WIO', 'NWC'),
                                 feature_group_count=u.shape[-1]) + conv_b
    a, g = jnp.split(u, 2, axis=-1)
    return (jax.nn.silu(g) * a) @ w_down


def setup_inputs(seed: int = 0) -> dict:
    key = jax.random.key(seed)
    ks = jax.random.split(key, 21)
    nrm = lambda k, shape, scale: jax.random.normal(k, shape, jnp.float32) * scale
    h = jnp.arange(RET_HEADS, dtype=jnp.float32)
    decay_init = jnp.log(-jnp.log(1.0 - 2.0 ** (-5.0 - h)))
    return {
        'x': nrm(ks[0], (BATCH, SEQ, D_MODEL), 1.0),
        'c': nrm(ks[1], (BATCH, D_MODEL), 1.0),
        'ctx': nrm(ks[2], (BATCH, CTX_LEN, D_MODEL), 1.0),
        'c_ctx': nrm(ks[3], (D_MODEL,), 1.0),
        'w_ada': nrm(ks[4], (DEPTH, D_MODEL, 6 * D_MODEL), 0.5 * D_MODEL ** -0.5),
        'b_ada': nrm(ks[5], (DEPTH, 6 * D_MODEL), 0.02),
        'g_norm1': 1.0 + nrm(ks[6], (DEPTH, D_MODEL), 0.02),
        'w_in': nrm(ks[7], (DEPTH, D_MODEL, IN_COLS), D_MODEL ** -0.5),
        'g_q': 1.0 + nrm(ks[8], (DEPTH, MLA_Q_RANK), 0.02),
        'w_uq': nrm(ks[9], (DEPTH, MLA_Q_RANK, MLA_HEADS * (MLA_NOPE + MLA_ROPE)), MLA_Q_RANK ** -0.5),
        'g_kv': 1.0 + nrm(ks[10], (DEPTH, MLA_KV_RANK), 0.02),
        'w_ukv': nrm(ks[11], (DEPTH, MLA_KV_RANK, MLA_HEADS * (MLA_NOPE + MLA_V)), MLA_KV_RANK ** -0.5),
        'ret_decay': decay_init[None, None, :] + nrm(ks[12], (DEPTH, 2, RET_HEADS), 0.05),
        'g_ret': 1.0 + nrm(ks[13], (DEPTH, RET_WIDTH), 0.02),
        'w_out': nrm(ks[14], (DEPTH, MIX_WIDTH, D_MODEL), MIX_WIDTH ** -0.5),
        'g_norm2': 1.0 + nrm(ks[15], (DEPTH, D_MODEL), 0.02),
        'w_up': nrm(ks[16], (DEPTH, D_MODEL, 2 * D_FF), D_MODEL ** -0.5),
        'conv_w': nrm(ks[17], (DEPTH, CONV_WIDTH, 2 * D_FF), CONV_WIDTH ** -0.5),
        'conv_b': nrm(ks[18], (DEPTH, 2 * D_FF), 0.02),
        'w_down': nrm(ks[19], (DEPTH, D_FF, D_MODEL), D_FF ** -0.5),
        'g_final': 1.0 + nrm(ks[20], (D_MODEL,), 0.02),
    }


def reference(x, c, ctx, c_ctx, w_ada, b_ada, g_norm1, w_in, g_q, w_uq, g_kv, w_ukv,
              ret_decay, g_ret, w_out, g_norm2, w_up, conv_w, conv_b, w_down, g_final):
    B, S, _ = x.shape
    rows = S // GRID_W
    pos_row = jnp.repeat(jnp.arange(rows, dtype=jnp.int32), GRID_W)
    pos_col = jnp.tile(jnp.arange(GRID_W, dtype=jnp.int32), rows)
    pos_grid = (pos_row, pos_col)
    pos_seq = jnp.arange(S, dtype=jnp.int32)

    for l in range(DEPTH):
        last = l == DEPTH - 1
        mod = jax.nn.silu(c) @ w_ada[l] + b_ada[l]
        sh1, sc1, gt1, sh2, sc2, gt2 = jnp.split(mod[:, None, :], 6, axis=-1)
        mod_c = jax.nn.silu(c_ctx) @ w_ada[l] + b_ada[l]
        shc1, scc1, gtc1, shc2, scc2, gtc2 = jnp.split(mod_c, 6)
        log_gamma = -jnp.exp(ret_decay[l].astype(jnp.float32))

        h = modulate(rmsnorm(x, g_norm1[l]), sh1, sc1)
        hc = modulate(rmsnorm(ctx, g_norm1[l]), shc1, scc1)
        cq, ckv, kpe, rq, rk, rv, rg = split_proj(h @ w_in[l])
        cq_c, ckv_c, kpe_c, rq_c, rk_c, rv_c, rg_c = split_proj(hc @ w_in[l])

        k_c, v_c = mla_keys(ckv_c, kpe_c, g_kv[l], w_ukv[l], None)
        qr_c, kr_c, vr_c = retention_inputs(rq_c, rk_c, rv_c, None)
        st_f = retention_final_state(kr_c, vr_c, log_gamma[0], False)
        st_b = retention_final_state(kr_c, vr_c, log_gamma[1], True)

        q = mla_queries(cq, g_q[l], w_uq[l], pos_grid)
        k, v = mla_keys(ckv, kpe, g_kv[l], w_ukv[l], pos_grid)
        o_mla = latent_attention(q, k, v, k_c, v_c)
        qr, kr, vr = retention_inputs(rq, rk, rv, pos_seq)
        o_ret = retention_bidir(qr, kr, vr, rg, log_gamma, st_f, st_b, g_ret[l])
        o = jnp.concatenate([o_mla, o_ret.astype(o_mla.dtype)], axis=-1) @ w_out[l]
        x_new = x + gt1 * o
        x_new = x_new + gt2 * conv_ffn(modulate(rmsnorm(x_new, g_norm2[l]), sh2, sc2),
                                       w_up[l], conv_w[l], conv_b[l], w_down[l])

        if not last:
            q_c = mla_queries(cq_c, g_q[l], w_uq[l], None)
            o_mla_c = softmax_attention(q_c, k_c, v_c).reshape(B, ctx.shape[1], MLA_WIDTH)
            zero_state = jnp.zeros_like(st_f)
            o_ret_c = retention_bidir(qr_c, kr_c, vr_c, rg_c, log_gamma, zero_state, zero_state, g_ret[l])
            o_c = jnp.concatenate([o_mla_c, o_ret_c.astype(o_mla_c.dtype)], axis=-1) @ w_out[l]
            ctx = ctx + gtc1 * o_c
            ctx = ctx + gtc2 * conv_ffn(modulate(rmsnorm(ctx, g_norm2[l]), shc2, scc2),
                                        w_up[l], conv_w[l], conv_b[l], w_down[l])
        x = x_new

    return rmsnorm(x, g_final)
```

```python
import math
import os
import numpy as np
import ml_dtypes
import concourse.bass as bass
import concourse.mybir as mybir
from concourse.bass_utils import run_bass_kernel_spmd

F32 = mybir.dt.float32
BF = mybir.dt.bfloat16
AF = mybir.ActivationFunctionType
ALU = mybir.AluOpType

S = 2048
D = 1024
L = 256
NT = 16
NTC = 18
SC = S + L
DFF = 2816
NCH = 22
EPS = 1e-6
MLA_SCALE = 96 ** -0.5
LN_S = math.log(128 ** -0.5)
GROUPS = [(0, 8), (8, 15), (15, 22)]
GMAX = 8


class Tile:
    __slots__ = ("name", "w", "r", "excl")

    def __init__(self, name, excl=False):
        self.name = name
        self.w = None
        self.r = {}
        self.excl = excl


class _Rec:
    def __getattr__(self, name):
        def call(*a, **k):
            self.__dict__["call"] = (name, a, k)
            return self
        return call


class Sched:
    ENG = ("pe", "act", "dve", "pool", "sp")

    def __init__(self, nc, ndma=48):
        self.nc = nc
        self.ops = {e: [] for e in self.ENG}
        self.cnt = {e: 0 for e in self.ENG}
        self.seen = {e: {} for e in self.ENG}
        self.sems = {}
        self.ndma = ndma
        self.dcnt = [0] * ndma
        self.rr = {"sp": 0, "pool": 0}
        self.half = ndma // 2
        self.stack = None

    def open(self, stack):
        for e in self.ENG:
            self.sems[e] = stack.enter_context(self.nc.semaphore("s_" + e))
        for j in range(self.ndma):
            self.sems[("d", j)] = stack.enter_context(self.nc.semaphore("s_d%d" % j))

    def _deps(self, eng, reads, writes):
        deps = []
        writes = list(writes) + [t for t in reads if t.excl and t not in writes]
        for t in reads:
            if t.w is not None:
                deps.append(t.w)
        for t in writes:
            if t.w is not None:
                deps.append(t.w)
            deps.extend(t.r.items())
        waits = []
        seen = self.seen[eng]
        for k, v in deps:
            if eng == "pe" and k == "pe":
                continue
            if seen.get(k, 0) >= v:
                continue
            seen[k] = v
            waits.append((k, v))
        m = {}
        for k, v in waits:
            m[k] = max(m.get(k, 0), v)
        return list(m.items())

    def _mark(self, ev, reads, writes):
        k, v = ev
        writes = list(writes) + [t for t in reads if t.excl and t not in writes]
        for t in reads:
            if t.r.get(k, 0) < v:
                t.r[k] = v
        for t in writes:
            t.w = ev
            t.r = {}

    def op(self, eng, fn, reads=(), writes=()):
        rec = _Rec()
        fn(rec)
        name, a, k = rec.call
        fn = lambda e, name=name, a=a, k=k: getattr(e, name)(*a, **k)
        waits = self._deps(eng, reads, writes)
        self.cnt[eng] += 1
        ev = (eng, self.cnt[eng])
        self.ops[eng].append((waits, fn, (eng, 1)))
        self._mark(ev, reads, writes)

    def dma(self, q, pairs, reads=(), writes=(), **kw):
        j = self.rr[q] + (0 if q == "sp" else self.half)
        self.rr[q] = (self.rr[q] + 1) % self.half
        key = ("d", j)
        waits = self._deps(q, reads, writes)
        seen = self.seen[q]
        if self.dcnt[j] > 0 and seen.get(key, 0) < self.dcnt[j]:
            seen[key] = self.dcnt[j]
            waits.append((key, self.dcnt[j]))
        for i, (o, i_) in enumerate(pairs):
            def fn(e, o=o, i_=i_):
                return e.dma_start(out=o, in_=i_, **kw)
            self.ops[q].append((waits if i == 0 else [], fn, (key, 16)))
            self.dcnt[j] += 16
        ev = (key, self.dcnt[j])
        self._mark(ev, reads, writes)

    def wait_tiles(self, eng, tiles):
        waits = self._deps(eng, tiles, tiles)
        if waits:
            self.cnt[eng] += 1
            self.ops[eng].append((waits, lambda e: e.nop(), (eng, 1)))

    def drain_dmas(self):
        waits = []
        seen = self.seen["sp"]
        for j in range(self.ndma):
            key = ("d", j)
            if self.dcnt[j] > 0 and seen.get(key, 0) < self.dcnt[j]:
                seen[key] = self.dcnt[j]
                waits.append((key, self.dcnt[j]))
        if waits:
            self.cnt["sp"] += 1
            self.ops["sp"].append((waits, lambda e: e.nop(), ("sp", 1)))

    def flush(self):
        self.drain_dmas()
        nc = self.nc
        if os.environ.get("KSBUF"):
            print("SBUF remaining at flush:", nc.sbuf_bytes_remaining)
        sems = self.sems
        ops = self.ops

        def replay(name):
            def run(e):
                for waits, fn, inc in ops[name]:
                    for k, v in waits:
                        e.wait_ge(sems[k], v)
                    ins = fn(e)
                    ins.then_inc(sems[inc[0]], inc[1])
            return run

        with nc.Block() as block:
            block.tensor(replay("pe"))
            block.scalar(replay("act"))
            block.vector(replay("dve"))
            block.gpsimd(replay("pool"))
            block.sync(replay("sp"))
        self.ops = {e: [] for e in self.ENG}


def _consts():
    c = {}
    c["ident"] = np.eye(128, dtype=np.float32).astype(ml_dtypes.bfloat16)
    pos = np.arange(S, dtype=np.float64)
    inv = 10000.0 ** (-np.arange(0, 128, 2, dtype=np.float64) / 128.0)
    ang = pos[:, None] * inv[None, :]
    rt = np.stack([np.cos(ang), np.sin(ang)], axis=1)
    c["rt"] = np.ascontiguousarray(rt.reshape(NT, 128, 2, 64).transpose(1, 0, 2, 3)).reshape(128, NT * 128).astype(np.float32)
    inv8 = 10000.0 ** (-np.arange(0, 16, 2, dtype=np.float64) / 16.0)
    prow = (np.arange(S) // 64).astype(np.float64)
    pcol = (np.arange(S) % 64).astype(np.float64)
    ct = np.zeros((32, S)); st = np.zeros((32, S))
    for r in range(32):
        p = prow if r < 16 else pcol
        a = p * inv8[r % 8]
        ct[r] = np.cos(a)
        st[r] = -np.sin(a) if (r % 16) < 8 else np.sin(a)
    c["mt"] = np.stack([ct, st], axis=1).astype(np.float32)
    i = np.arange(128, dtype=np.float64)
    e1 = np.maximum(i[None, :] - i[:, None], 0.0)
    e2 = np.maximum(i[:, None] - i[None, :], 0.0)
    c3 = np.tile((i + 1.0)[None, :], (128, 1))
    c4 = np.tile((128.0 - i)[None, :], (128, 1))
    cv = np.zeros((128, 8))
    cv[:, 0] = 127.0 - i
    cv[:, 1] = i
    cv[:, 2] = 255.0 - i
    cv[:, 3] = 127.0 - i
    cv[:, 4] = i
    cv[:, 5] = 128.0 + i
    c["cst"] = np.concatenate([e1, e2, c3, c4, cv], axis=1).astype(np.float32)
    return c


def build(debug=(), stop=None):
    from contextlib import ExitStack
    nc = bass.Bass("TRN2", target_bir_lowering=False)
    dbg_out = {}

    def dram_in(name, shape, dt=F32):
        return nc.dram_tensor(name, list(shape), dt, kind="ExternalInput")

    x_d = dram_in("x", [S, D]).ap()
    c_d = dram_in("c", [1, D])
    ctx_d = dram_in("ctx", [L, D]).ap()
    cctx_d = dram_in("c_ctx", [1, D])
    wada_d = dram_in("w_ada", [D, 6 * D]).ap()
    bada_d = dram_in("b_ada", [1, 6 * D])
    g1_d = dram_in("g_norm1", [1, D])
    win_d = dram_in("w_in", [D, 2592]).ap()
    gq_d = dram_in("g_q", [1, 256])
    wuq_d = dram_in("w_uq", [256, 768]).ap()
    gkv_d = dram_in("g_kv", [1, 256])
    wukv_d = dram_in("w_ukv", [256, 1024]).ap()
    rdec_d = dram_in("ret_decay", [1, 8])
    gret_d = dram_in("g_ret", [1, 512])
    wout_d = dram_in("w_out", [D, D]).ap()
    g2_d = dram_in("g_norm2", [1, D])
    wup_d = dram_in("w_up", [D, 2 * DFF]).ap()
    cw_d = dram_in("conv_w", [3, 2 * DFF])
    cb_d = dram_in("conv_b", [1, 2 * DFF])
    wdn_d = dram_in("w_down", [DFF, D]).ap()
    gf_d = dram_in("g_final", [1, D])
    ident_d = dram_in("k_ident", [128, 128], BF).ap()
    rt_d = dram_in("k_rt", [128, NT * 128]).ap()
    mt_d = dram_in("k_mt", [32, 2, S]).ap()
    cst_d = dram_in("k_cst", [128, 520]).ap()
    out_d = nc.dram_tensor("out", [S, D], F32, kind="ExternalOutput").ap()
    modp_d = nc.dram_tensor("modp_scratch", [1, 4 * D], F32, kind="Internal")

    def bc(t, off, n, parts=128):
        return bass.AP(t, off, [[0, parts], [1, n]])

    es = ExitStack()
    with es:
        sch = Sched(nc)
        sch.open(es)
        op = sch.op

        def sb(name, shape, dt, stack=es):
            return stack.enter_context(nc.sbuf_tensor(name, list(shape), dt))

        def dbg(name, ap, tiles, shape, dt=F32):
            if name not in debug:
                return
            dd = nc.dram_tensor("dbg_" + name, list(shape), dt, kind="ExternalOutput").ap()
            dbg_out[name] = dd
            t = Tile("dbg_" + name)
            sch.dma("sp", [(dd, ap)], reads=tiles, writes=[t])
            sch.wait_tiles("sp", [t])

        ident = sb("ident", [128, 128], BF)
        cst = sb("cst", [128, 520], F32)
        OT = sb("OT", [128, 8, S], BF)
        ARENA = sb("ARENA", [128, 16384], F32)
        t_ident = Tile("ident"); t_cst = Tile("cst")
        sch.dma("sp", [(ident[:], ident_d[:, :])], writes=[t_ident])
        sch.dma("sp", [(cst[:], cst_d[:, :])], writes=[t_cst])
        E1 = cst[:, 0:128]; E2 = cst[:, 128:256]; C3 = cst[:, 256:384]; C4 = cst[:, 384:512]
        CV = cst[:, 512:520]
        hT = ARENA[:, 0:9216].bitcast(BF).rearrange("p (k t) -> p k t", k=8)
        SPARE = ARENA[:, 9216:16384]
        XN = ARENA[:].rearrange("p (t d) -> p t d", t=NT)
        t_hT = Tile("hT")
        t_XN = [Tile("XN%d" % t) for t in range(NT)]
        t_OT = [Tile("OT%d" % j) for j in range(8)]
        t_modp = Tile("modp")

        with ExitStack() as ph:
            PS = ph.enter_context(nc.psum_tensor("psA", [128, 4096], F32))
            t_ps = [Tile("psA%d" % b, True) for b in range(8)]
            cc = sb("cc", [128, 8, 2], F32, ph)
            scs = sb("scs", [128, 8, 2], F32, ph)
            CR = sb("CR", [128, 16, 128], BF, ph)
            WA = [sb("WA%d" % i, [128, 8, 512], BF, ph) for i in range(4)]
            t_WA = [Tile("WA%d" % i) for i in range(4)]
            BB = [sb("BB%d" % i, [128, 512], F32, ph) for i in range(2)]
            t_BB = [Tile("BB%d" % i) for i in range(2)]
            MOD1 = sb("MOD1", [128, 2048], F32, ph)
            MODC = sb("MODC", [128, 2048], F32, ph)
            MT = [sb("MT%d" % i, [128, 512], F32, ph) for i in range(2)]
            t_MT = [Tile("MT%d" % i) for i in range(2)]
            G1B = sb("G1B", [128, 1024], F32, ph)
            S1 = sb("S1", [128, 1024], F32, ph)
            S1C = sb("S1C", [128, 1024], F32, ph)
            XT = [sb("XT%d" % i, [128, 1024], F32, ph) for i in range(4)]
            t_XT = [Tile("XT%d" % i) for i in range(4)]
            TMP = [sb("TMP%d" % i, [128, 1024], F32, ph) for i in range(2)]
            t_TMP = [Tile("TMP%d" % i) for i in range(2)]
            HB = [sb("HB%d" % i, [128, 1024], BF, ph) for i in range(2)]
            t_HB = [Tile("HB%d" % i) for i in range(2)]
            junk = sb("junk", [128, 1024], BF, ph)
            t_junk = Tile("junk")
            ss = sb("ss", [128, NTC], F32, ph)
            rs = sb("rs", [128, NTC], F32, ph)
            t_cc = Tile("cc"); t_scs = Tile("scs"); t_CR = Tile("CR")
            t_MOD1 = Tile("MOD1"); t_MODC = Tile("MODC"); t_G1B = Tile("G1B")
            t_S1 = Tile("S1"); t_S1C = Tile("S1C")
            t_ss = [Tile("ss%d" % t) for t in range(NTC)]
            t_rs = [Tile("rs%d" % t) for t in range(NTC)]
            t_ssall = Tile("ssall")

            sch.dma("sp", [(cc[:, :, 0], c_d.ap().rearrange("o (k p) -> p (o k)", p=128)),
                           (cc[:, :, 1], cctx_d.ap().rearrange("o (k p) -> p (o k)", p=128))],
                    writes=[t_cc], allow_slow_non_contiguous=True)
            sch.dma("sp", [(G1B[:], bc(g1_d, 0, 1024))], writes=[t_G1B])
            op("act", lambda e: e.activation(out=scs[:], in_=cc[:], func=AF.Silu), [t_cc], [t_scs])
            op("dve", lambda e: e.tensor_copy(out=CR[:], in_=scs[:].rearrange("p k v -> p (k v)").unsqueeze(2).to_broadcast([128, 16, 128])),
               [t_scs], [t_CR])
            op("dve", lambda e: e.memset(ss[:], 0.0), [], [t_ssall])
            CRv = CR[:].rearrange("p (k v) r -> p k v r", v=2)
            pbc = [0]

            def ada_block(j):
                w = j % 4
                sch.dma("pool", [(WA[w][:], wada_d[:, j * 512:(j + 1) * 512].rearrange("(k p) n -> p k n", p=128))],
                        writes=[t_WA[w]])
                sch.dma("sp", [(BB[j % 2][:], bc(bada_d, j * 512, 512))], writes=[t_BB[j % 2]])
                for v in ((0, 1) if j < 4 else (0,)):
                    b = 2 + (pbc[0] % 6); pbc[0] += 1
                    for k in range(8):
                        op("pe", lambda e, k=k, v=v, b=b, w=w: e.matmul(PS[:, b * 512:(b + 1) * 512], CRv[:, k, v, :], WA[w][:, k, :],
                                                                        start=(k == 0), stop=(k == 7)),
                           [t_CR, t_WA[w]], [t_ps[b]])
                    if j < 4:
                        dst = (MOD1 if v == 0 else MODC)[:, j * 512:(j + 1) * 512]
                        td = t_MOD1 if v == 0 else t_MODC
                        op("dve", lambda e, b=b, dst=dst, j=j: e.tensor_tensor(out=dst, in0=PS[:, b * 512:(b + 1) * 512], in1=BB[j % 2][:], op=ALU.add),
                           [t_ps[b], t_BB[j % 2]], [td])
                    else:
                        m = j % 2
                        op("dve", lambda e, b=b, m=m, j=j: e.tensor_tensor(out=MT[m][:], in0=PS[:, b * 512:(b + 1) * 512], in1=BB[j % 2][:], op=ALU.add),
                           [t_ps[b], t_BB[j % 2]], [t_MT[m]])
                        sch.dma("sp", [(modp_d.ap()[0:1, (j - 4) * 512:(j - 3) * 512], MT[m][0:1, :])],
                                reads=[t_MT[m]], writes=[t_modp])

            for j in range(4):
                ada_block(j)
            ada_rest = list(range(4, 12))
            op("dve", lambda e: e.scalar_tensor_tensor(out=S1[:], in0=MOD1[:, 1024:2048], scalar=1.0, in1=G1B[:], op0=ALU.add, op1=ALU.mult),
               [t_MOD1, t_G1B], [t_S1])
            op("dve", lambda e: e.scalar_tensor_tensor(out=S1C[:], in0=MODC[:, 1024:2048], scalar=1.0, in1=G1B[:], op0=ALU.add, op1=ALU.mult),
               [t_MODC, t_G1B], [t_S1C])

            def norm_a(t, src_ap, src_tiles, scale_ap, t_scale, shift_ap, t_shift):
                i2 = t % 2
                op("act", lambda e: e.activation(out=junk[:], in_=src_ap, func=AF.Square, accum_out=ss[:, t:t + 1]),
                   src_tiles + [t_ssall], [t_junk, t_ss[t]])
                op("act", lambda e: e.activation(out=rs[:, t:t + 1], in_=ss[:, t:t + 1], func=AF.Sqrt, scale=1.0 / D, bias=EPS),
                   [t_ss[t]], [t_rs[t]])
                op("dve", lambda e: e.reciprocal(out=rs[:, t:t + 1], in_=rs[:, t:t + 1]), [t_rs[t]], [t_rs[t]])
                op("dve", lambda e: e.scalar_tensor_tensor(out=TMP[i2][:], in0=src_ap, scalar=rs[:, t:t + 1], in1=scale_ap,
                                                           op0=ALU.mult, op1=ALU.mult),
                   src_tiles + [t_rs[t], t_scale], [t_TMP[i2]])
                op("dve", lambda e: e.tensor_tensor(out=HB[i2][:], in0=TMP[i2][:], in1=shift_ap, op=ALU.add),
                   [t_TMP[i2], t_shift], [t_HB[i2]])

            def norm_b(t, dstT, t_dst, pbank):
                i2 = t % 2
                pv = PS[:, pbank * 512:(pbank + 1) * 512].bitcast(BF)
                for k in range(8):
                    op("pe", lambda e, k=k: e.transpose(pv[:, k * 128:(k + 1) * 128], HB[i2][:, k * 128:(k + 1) * 128], ident[:]),
                       [t_HB[i2], t_ident], [t_ps[pbank]])
                op("act", lambda e: e.activation(out=dstT[:, :, t * 128:(t + 1) * 128], in_=pv.rearrange("p (k t) -> p k t", k=8), func=AF.Copy),
                   [t_ps[pbank]], [t_dst])

            for t in range(NTC + 1):
                if t < NTC:
                    i3 = t % 4
                    src = x_d[t * 128:(t + 1) * 128, :] if t < NT else ctx_d[(t - NT) * 128:(t - NT + 1) * 128, :]
                    sch.dma("sp", [(XT[i3][:], src)], writes=[t_XT[i3]])
                    if t < NT:
                        norm_a(t, XT[i3][:], [t_XT[i3]], S1[:], t_S1, MOD1[:, 0:1024], t_MOD1)
                    else:
                        norm_a(t, XT[i3][:], [t_XT[i3]], S1C[:], t_S1C, MODC[:, 0:1024], t_MODC)
                if t >= 1:
                    norm_b(t - 1, hT, t_hT, (t - 1) % 2)
                if ada_rest and t % 2 == 1:
                    ada_block(ada_rest.pop(0))
            while ada_rest:
                ada_block(ada_rest.pop(0))
            dbg("hT", hT, [t_hT], [128, 8, SC], BF)
            sch.flush()
            if stop == "A":
                return nc, dbg_out

        with ExitStack() as ph:
            PS = ph.enter_context(nc.psum_tensor("psB", [128, 4096], F32))
            t_ps = [Tile("psB%d" % b, True) for b in range(8)]
            RT = SPARE[:, 0:2048].rearrange("p (t c f) -> p t c f", t=NT, c=2)
            VR = SPARE[:, 2048:2048 + 4608].bitcast(BF).rearrange("p (t n) -> p t n", t=NTC)
            t_RT = Tile("RT"); t_VR = Tile("VR")
            WB = [sb("WB%d" % i, [128, 8, 512], BF, ph) for i in range(2)]
            t_WB = [Tile("WB%d" % i) for i in range(2)]
            QT = sb("QT", [128, 4, S], BF, ph); t_QT = Tile("QT")
            KR = sb("KR", [128, NT, 512], BF, ph); t_KR = Tile("KR")
            KC = sb("KC", [128, 2, 2, 512], BF, ph); t_KC = Tile("KC")
            SBall = OT[:, 0:4, :].rearrange("p j (c n) -> p (j c) n", n=512)
            t_SBall = [Tile("SBall%d" % c) for c in range(NT)]
            RD = sb("RD", [128, 8], F32, ph); LG = sb("LG", [128, 8], F32, ph); GC = sb("GC", [128, 8], F32, ph)
            GCF = sb("GCF", [128, 2, 512], F32, ph)
            DTm = sb("DTm", [128, 512], BF, ph)
            XFB = sb("XFB", [128, 2, 512], BF, ph)
            ZZ = sb("ZZ", [128, 2, 4], F32, ph)
            ZZF = sb("ZZF", [128, 2, 512], F32, ph)
            WC = sb("WC", [128, 2, 2, 4], F32, ph)
            tmpd = sb("tmpd", [128, 128], F32, ph)
            GRB = sb("GRB", [128, 512], F32, ph)
            t_dec = Tile("dec"); t_tmpd = Tile("tmpd"); t_GRB = Tile("GRB")
            ROT = [sb("ROT%d" % i, [128, 2, 256], F32, ph) for i in range(2)]
            t_ROT = [Tile("ROT%d" % i) for i in range(2)]
            QR = [sb("QR%d" % i, [128, 512], BF, ph) for i in range(3)]
            t_QR = [Tile("QR%d" % i) for i in range(3)]
            SX = [sb("SX%d" % i, [128, 512], F32, ph) for i in range(3)]
            t_SX = [Tile("SX%d" % i) for i in range(3)]
            SXb = [0, 1]
            SXf = [2, 0]
            SFb = [sb("SFb%d" % i, [128, 512], BF, ph) for i in range(2)]
            t_SFb = [Tile("SFb%d" % i) for i in range(2)]
            NB = 3
            AD = [sb("AD%d" % i, [128, 512], BF, ph) for i in range(NB)]
            QF = [sb("QF%d" % i, [128, 512], BF, ph) for i in range(NB)]
            QB = [sb("QB%d" % i, [128, 512], BF, ph) for i in range(NB)]
            KFc = [sb("KFc%d" % i, [128, 512], BF, ph) for i in range(NB)]
            GS = [sb("GS%d" % i, [128, 512], BF, ph) for i in range(NB)]
            YN = [sb("YN%d" % i, [128, 512], F32, ph) for i in range(NB)]
            KTc = [sb("KTc%d" % i, [128, 512], BF, ph) for i in range(NB)]
            ORk = [sb("ORk%d" % i, [128, 512], BF, ph) for i in range(NB)]
            BST = [sb("BST%d" % i, [128, 4, 6], F32, ph) for i in range(NB)]
            MV = [sb("MV%d" % i, [128, 4, 2], F32, ph) for i in range(NB)]
            SD = [sb("SD%d" % i, [128, 4], F32, ph) for i in range(NB)]
            def tl(n):
                return [Tile("%s%d" % (n, i)) for i in range(NB)]
            t_AD, t_QF, t_QB, t_KFc, t_KBc, t_GS, t_YN, t_KTc, t_ORk, t_BST, t_MV, t_SD = [tl(n) for n in
                ("AD", "QF", "QB", "KFc", "KBc", "GS", "YN", "KTc", "ORk", "BST", "MV", "SD")]
            KBc = KFc; t_KBc = t_KFc

            sch.dma("sp", [(RT, rt_d[:, :].rearrange("p (t c f) -> p t c f", t=NT, c=2))], writes=[t_RT])
            sch.dma("sp", [(RD[:], bc(rdec_d, 0, 8))], writes=[t_dec])
            sch.dma("sp", [(GRB[:], bc(gret_d, 0, 512))], writes=[t_GRB])
            def decay_tables():
                op("act", lambda e: e.activation(out=LG[:], in_=RD[:], func=AF.Exp), [t_dec], [t_dec])
                op("dve", lambda e: e.tensor_scalar(out=LG[:], in0=LG[:], scalar1=-1.0, scalar2=None, op0=ALU.mult), [t_dec], [t_dec])
                op("act", lambda e: e.activation(out=GC[:], in_=LG[:], func=AF.Exp, scale=128.0), [t_dec], [t_dec])
                op("dve", lambda e: e.tensor_copy(out=GCF[:].rearrange("p d (h n) -> p (d h) n", h=4),
                                                  in_=GC[:].unsqueeze(2).to_broadcast([128, 8, 128])), [t_dec], [t_dec])
                for h in range(4):
                    op("dve", lambda e, h=h: e.tensor_scalar(out=tmpd[:], in0=E1, scalar1=LG[:, h:h + 1], scalar2=None, op0=ALU.mult),
                       [t_cst, t_dec], [t_tmpd])
                    op("dve", lambda e, h=h: e.scalar_tensor_tensor(out=tmpd[:], in0=E2, scalar=LG[:, 4 + h:5 + h], in1=tmpd[:], op0=ALU.mult, op1=ALU.add),
                       [t_cst, t_dec, t_tmpd], [t_tmpd])
                    op("act", lambda e, h=h: e.activation(out=DTm[:, h * 128:(h + 1) * 128], in_=tmpd[:], func=AF.Exp, bias=LN_S),
                       [t_tmpd], [t_dec])
                    op("act", lambda e, h=h: e.activation(out=XFB[:, 0, h * 128:(h + 1) * 128], in_=C3, func=AF.Exp, scale=LG[:, h:h + 1]),
                       [t_cst, t_dec], [t_dec])
                    op("act", lambda e, h=h: e.activation(out=XFB[:, 1, h * 128:(h + 1) * 128], in_=C4, func=AF.Exp, scale=LG[:, 4 + h:5 + h]),
                       [t_cst, t_dec], [t_dec])
                    for d in range(2):
                        op("act", lambda e, h=h, d=d: e.activation(out=ZZ[:, d, h:h + 1], in_=CV[:, d:d + 1], func=AF.Exp,
                                                                   scale=LG[:, 4 * d + h:4 * d + h + 1], bias=LN_S), [t_cst, t_dec], [t_dec])
                        for t in range(2):
                            op("act", lambda e, h=h, d=d, t=t: e.activation(out=WC[:, d, t, h:h + 1], in_=CV[:, 2 + 2 * d + t:3 + 2 * d + t], func=AF.Exp,
                                                                            scale=LG[:, 4 * d + h:4 * d + h + 1], bias=LN_S), [t_cst, t_dec], [t_dec])
                op("dve", lambda e: e.tensor_copy(out=ZZF[:].rearrange("p d (h n) -> p (d h) n", h=4),
                                                  in_=ZZ[:].rearrange("p d h -> p (d h)").unsqueeze(2).to_broadcast([128, 8, 128])), [t_dec], [t_dec])


            def inproj(cols, wbi, tiles, post, lag=2, extra=None):
                sch.dma("pool", [(WB[wbi][:], win_d[:, cols:cols + 512].rearrange("(k p) n -> p k n", p=128))], writes=[t_WB[wbi]])
                pend = []
                for n, t in enumerate(tiles):
                    b = n % 4
                    for k in range(8):
                        op("pe", lambda e, k=k, b=b, t=t: e.matmul(PS[:, b * 512:(b + 1) * 512], hT[:, k, t * 128:(t + 1) * 128], WB[wbi][:, k, :],
                                                                   start=(k == 0), stop=(k == 7)),
                           [t_hT, t_WB[wbi]], [t_ps[b]])
                    pend.append(post(t, b, n))
                    if extra is not None:
                        extra(n)
                    if len(pend) > lag:
                        f = pend.pop(0)
                        if f is not None:
                            f()
                for f in pend:
                    if f is not None:
                        f()

            decay_tables()
            inproj(1568, 0, range(NTC),
                   lambda t, b, n: op("act", lambda e: e.activation(out=VR[:, t, :], in_=PS[:, b * 512:(b + 1) * 512], func=AF.Copy),
                                      [t_ps[b]], [t_VR]))
            if stop == "B1":
                sch.flush()
                return nc, dbg_out

            def rope_post(dst, t_dst, dstT, t_dstT):
                def post(t, b, n):
                    do_T = dstT is not None
                    i2 = n % 2
                    i4 = n % 3
                    pv = PS[:, b * 512:(b + 1) * 512].rearrange("p (h c f) -> p h c f", h=4, c=2)
                    x1 = pv[:, :, 0, :]; x2 = pv[:, :, 1, :]
                    cs = RT[:, t, 0, :].unsqueeze(1).to_broadcast([128, 4, 64])
                    sn = RT[:, t, 1, :].unsqueeze(1).to_broadcast([128, 4, 64])
                    ra = ROT[i2][:, 0, :].rearrange("p (h f) -> p h f", h=4)
                    rb = ROT[i2][:, 1, :].rearrange("p (h f) -> p h f", h=4)
                    o = dst(t, i4).rearrange("p (h c f) -> p h c f", h=4, c=2)
                    tds = t_dst(t, i4)
                    op("dve", lambda e: e.tensor_tensor(out=ra, in0=x1, in1=cs, op=ALU.mult), [t_ps[b], t_RT], [t_ROT[i2]])
                    op("dve", lambda e: e.tensor_tensor(out=rb, in0=x2, in1=sn, op=ALU.mult), [t_ps[b], t_RT], [t_ROT[i2]])
                    op("pool", lambda e: e.tensor_tensor(out=o[:, :, 0, :], in0=ra, in1=rb, op=ALU.subtract), [t_ROT[i2]], [tds])
                    i2b = i2
                    op("dve", lambda e: e.tensor_tensor(out=ra, in0=x1, in1=sn, op=ALU.mult), [t_ps[b], t_RT], [t_ROT[i2]])
                    op("dve", lambda e: e.tensor_tensor(out=rb, in0=x2, in1=cs, op=ALU.mult), [t_ps[b], t_RT], [t_ROT[i2]])
                    op("pool", lambda e: e.tensor_tensor(out=o[:, :, 1, :], in0=ra, in1=rb, op=ALU.add), [t_ROT[i2]], [tds])
                    if not do_T:
                        return None

                    def part2():
                        pb = 4 + (n % 2)
                        pvb = PS[:, pb * 512:pb * 512 + 256].bitcast(BF)
                        src = dst(t, i4)
                        for h in range(4):
                            op("pe", lambda e, h=h: e.transpose(pvb[:, h * 128:(h + 1) * 128], src[:, h * 128:(h + 1) * 128], ident[:]),
                               [tds, t_ident], [t_ps[pb]])
                        op("act", lambda e: e.activation(out=dstT[:, :, t * 128:(t + 1) * 128], in_=pvb.rearrange("p (h t) -> p h t", h=4), func=AF.Copy),
                           [t_ps[pb]], [t_dstT])
                    return part2
                return post

            if stop == "B3":
                sch.flush()
                return nc, dbg_out

            kpost = rope_post(lambda t, i2: KR[:, t, :], lambda t, i2: t_KR, None, None)

            def kpost_all(t, b, n):
                if t < NT:
                    return kpost(t, b, n)
                else:
                    tc_ = t - NT
                    for d in range(2):
                        op("dve", lambda e, d=d: e.tensor_tensor(out=KC[:, d, tc_, :].rearrange("p (h n) -> p h n", h=4),
                                                                 in0=PS[:, b * 512:(b + 1) * 512].rearrange("p (h n) -> p h n", h=4),
                                                                 in1=WC[:, d, tc_, :].unsqueeze(2).to_broadcast([128, 4, 128]), op=ALU.mult),
                           [t_ps[b], t_dec], [t_KC])
            inproj(1056, 1, range(NTC), kpost_all)
            dbg("VR", VR, [t_VR], [128, NTC, 512], BF)
            dbg("QT", QT[:], [t_QT], [128, 4, S], BF)
            dbg("KR", KR[:], [t_KR], [128, NT, 512], BF)
            dbg("DTm", DTm[:], [t_dec], [128, 512], BF)
            if stop == "B4":
                sch.flush()
                return nc, dbg_out

            sch.dma("pool", [(WB[1][:], win_d[:, 2080:2592].rearrange("(k p) n -> p k n", p=128))], writes=[t_WB[1]])

            def hs(h):
                return slice(h * 128, (h + 1) * 128)

            import os
            KV = os.environ.get("KVAR", "")
            for d, (S32, t_S32) in enumerate(((SX[SXf[0]], t_SX[SXf[0]]), (SX[SXb[0]], t_SX[SXb[0]]))):
                if "noinit" in KV:
                    break
                if "init1" in KV and d == 1:
                    break
                for h in range(4):
                    for t in range(2):
                        op("pe", lambda e, d=d, h=h, t=t: e.matmul(PS[:, 6 * 512 + h * 128:6 * 512 + (h + 1) * 128], KC[:, d, t, hs(h)], VR[:, NT + t, hs(h)],
                                                                   start=(t == 0), stop=(t == 1)), [t_KC, t_VR], [t_ps[6]])
                if "nocopy" in KV:
                    continue
                op("dve", lambda e, S32=S32: e.tensor_copy(out=S32[:], in_=PS[:, 6 * 512:7 * 512]), [t_ps[6]], [t_S32])
                if "noact" in KV:
                    continue
                if d == 0:
                    op("act", lambda e: e.activation(out=SFb[0][:], in_=PS[:, 6 * 512:7 * 512], func=AF.Copy), [t_ps[6]], [t_SFb[0]])
                else:
                    op("act", lambda e: e.activation(out=SBall[:, NT - 1, :], in_=PS[:, 6 * 512:7 * 512], func=AF.Copy), [t_ps[6]], [t_SBall[NT - 1]])
            if stop == "B5a":
                sch.flush()
                return nc, dbg_out
            def bwd_step(c):
                i2 = c % NB
                op("pool", lambda e: e.tensor_tensor(out=KBc[i2][:], in0=KR[:, c, :], in1=ZZF[:, 1, :], op=ALU.mult),
                   [t_KR, t_dec], [t_KBc[i2]])
                pbk = 6 + (c % 2)
                for h in range(4):
                    op("pe", lambda e, h=h: e.matmul(PS[:, pbk * 512 + h * 128:pbk * 512 + (h + 1) * 128], KBc[i2][:, hs(h)], VR[:, c, hs(h)],
                                                     start=True, stop=True), [t_KBc[i2], t_VR], [t_ps[pbk]])
                k_ = NT - 1 - c
                src_ = SXb[k_ % 2]; dst_ = SXb[(k_ + 1) % 2]
                op("dve", lambda e: e.tensor_tensor(out=SX[dst_][:], in0=SX[src_][:], in1=GCF[:, 1, :], op=ALU.mult), [t_SX[src_], t_dec], [t_SX[dst_]])
                op("dve", lambda e: e.tensor_tensor(out=SX[dst_][:], in0=SX[dst_][:], in1=PS[:, pbk * 512:(pbk + 1) * 512], op=ALU.add),
                   [t_SX[dst_], t_ps[pbk]], [t_SX[dst_]])
                op("act", lambda e: e.activation(out=SBall[:, c - 1, :], in_=SX[dst_][:], func=AF.Copy), [t_SX[dst_]], [t_SBall[c - 1]])

            bsteps = list(range(NT - 1, 0, -1))
            inproj(544, 0, range(NT), rope_post(lambda t, i2: QR[i2][:], lambda t, i2: t_QR[i2], QT, t_QT))
            while bsteps:
                bwd_step(bsteps.pop(0))
            dbg("SBall", SBall[:], t_SBall, [128, NT, 512], BF)
            if stop == "B5b":
                sch.flush()
                return nc, dbg_out

            def stage1(c):
                i2 = c % NB
                cs_ = slice(c * 128, (c + 1) * 128)
                pa = c % 2
                pk_ = 6 + (c % 2)
                pkb = PS[:, pk_ * 512:pk_ * 512 + 256].bitcast(BF)
                for h in range(4):
                    op("pe", lambda e, h=h: e.transpose(pkb[:, h * 128:(h + 1) * 128], KR[:, c, h * 128:(h + 1) * 128], ident[:]), [t_KR, t_ident], [t_ps[pk_]])
                op("act", lambda e: e.activation(out=KTc[i2][:], in_=pkb, func=AF.Copy), [t_ps[pk_]], [t_KTc[i2]])
                for h in range(4):
                    op("pe", lambda e, h=h: e.matmul(PS[:, pa * 512 + h * 128:pa * 512 + (h + 1) * 128], KTc[i2][:, h * 128:(h + 1) * 128], QT[:, h, cs_], start=True, stop=True),
                       [t_KTc[i2], t_QT], [t_ps[pa]])
                op("dve", lambda e: e.tensor_tensor(out=AD[i2][:], in0=PS[:, pa * 512:(pa + 1) * 512], in1=DTm[:], op=ALU.mult),
                   [t_ps[pa], t_dec], [t_AD[i2]])
                op("pool", lambda e: e.tensor_tensor(out=QF[i2][:].rearrange("p (h n) -> p h n", h=4), in0=QT[:, :, cs_],
                                                     in1=XFB[:, 0, :].rearrange("p (h n) -> p h n", h=4), op=ALU.mult), [t_QT, t_dec], [t_QF[i2]])
                op("pool", lambda e: e.tensor_tensor(out=QB[i2][:].rearrange("p (h n) -> p h n", h=4), in0=QT[:, :, cs_],
                                                     in1=XFB[:, 1, :].rearrange("p (h n) -> p h n", h=4), op=ALU.mult), [t_QT, t_dec], [t_QB[i2]])
                op("pool", lambda e: e.tensor_tensor(out=KFc[i2][:], in0=KR[:, c, :], in1=ZZF[:, 0, :], op=ALU.mult),
                   [t_KR, t_dec], [t_KFc[i2]])
                pg = 2 + (c % 2)
                for k in range(8):
                    op("pe", lambda e, k=k: e.matmul(PS[:, pg * 512:(pg + 1) * 512], hT[:, k, cs_], WB[1][:, k, :], start=(k == 0), stop=(k == 7)),
                       [t_hT, t_WB[1]], [t_ps[pg]])
                op("act", lambda e: e.activation(out=GS[i2][:], in_=PS[:, pg * 512:(pg + 1) * 512], func=AF.Silu), [t_ps[pg]], [t_GS[i2]])

            def stage2(c):
                i2 = c % NB
                cs_ = slice(c * 128, (c + 1) * 128)
                py = 4 + (c % 2)
                for h in range(4):
                    o = PS[:, py * 512 + h * 128:py * 512 + (h + 1) * 128]
                    op("pe", lambda e, h=h, o=o: e.matmul(o, AD[i2][:, hs(h)], VR[:, c, hs(h)], start=True, stop=False), [t_AD[i2], t_VR], [t_ps[py]])
                    op("pe", lambda e, h=h, o=o: e.matmul(o, QF[i2][:, hs(h)], SFb[c % 2][:, hs(h)], start=False, stop=False), [t_QF[i2], t_SFb[c % 2]], [t_ps[py]])
                    op("pe", lambda e, h=h, o=o: e.matmul(o, QB[i2][:, hs(h)], SBall[:, c, hs(h)], start=False, stop=True), [t_QB[i2], t_SBall[c]], [t_ps[py]])
                if c < NT - 1:
                    pu = 6 + (c % 2)
                    for h in range(4):
                        op("pe", lambda e, h=h: e.matmul(PS[:, pu * 512 + h * 128:pu * 512 + (h + 1) * 128], KFc[i2][:, hs(h)], VR[:, c, hs(h)], start=True, stop=True),
                           [t_KFc[i2], t_VR], [t_ps[pu]])
                    src_ = SXf[c % 2]; dst_ = SXf[(c + 1) % 2]
                    op("dve", lambda e: e.tensor_tensor(out=SX[dst_][:], in0=SX[src_][:], in1=GCF[:, 0, :], op=ALU.mult), [t_SX[src_], t_dec], [t_SX[dst_]])
                    op("dve", lambda e: e.tensor_tensor(out=SX[dst_][:], in0=SX[dst_][:], in1=PS[:, pu * 512:(pu + 1) * 512], op=ALU.add), [t_SX[dst_], t_ps[pu]], [t_SX[dst_]])
                    op("act", lambda e: e.activation(out=SFb[(c + 1) % 2][:], in_=SX[dst_][:], func=AF.Copy), [t_SX[dst_]], [t_SFb[(c + 1) % 2]])
                for h in range(4):
                    op("dve", lambda e, h=h: e.bn_stats(out=BST[i2][:, h, :], in_=PS[:, py * 512 + h * 128:py * 512 + (h + 1) * 128]), [t_ps[py]], [t_BST[i2]])
                for h in range(4):
                    op("dve", lambda e, h=h: e.bn_aggr(out=MV[i2][:, h, :], in_=BST[i2][:, h, :]), [t_BST[i2]], [t_MV[i2]])
                op("act", lambda e: e.activation(out=SD[i2][:], in_=MV[i2][:, :, 1], func=AF.Sqrt, bias=EPS), [t_MV[i2]], [t_SD[i2]])
                op("dve", lambda e: e.reciprocal(out=SD[i2][:], in_=SD[i2][:]), [t_SD[i2]], [t_SD[i2]])
                pv = PS[:, py * 512:(py + 1) * 512].rearrange("p (h n) -> p h n", h=4)
                op("dve", lambda e: e.tensor_tensor(out=YN[i2][:].rearrange("p (h n) -> p h n", h=4), in0=pv,
                                                    in1=MV[i2][:, :, 0:1].to_broadcast([128, 4, 128]), op=ALU.subtract), [t_ps[py], t_MV[i2]], [t_YN[i2]])
                op("dve", lambda e: e.tensor_tensor(out=YN[i2][:].rearrange("p (h n) -> p h n", h=4), in0=YN[i2][:].rearrange("p (h n) -> p h n", h=4),
                                                    in1=SD[i2][:].unsqueeze(2).to_broadcast([128, 4, 128]), op=ALU.mult), [t_YN[i2], t_SD[i2]], [t_YN[i2]])

            def stage3(c):
                i2 = c % NB
                cs_ = slice(c * 128, (c + 1) * 128)
                op("pool", lambda e: e.tensor_tensor(out=YN[i2][:], in0=YN[i2][:], in1=GRB[:], op=ALU.mult), [t_YN[i2], t_GRB], [t_YN[i2]])
                op("pool", lambda e: e.tensor_tensor(out=ORk[i2][:], in0=YN[i2][:], in1=GS[i2][:], op=ALU.mult), [t_YN[i2], t_GS[i2]], [t_ORk[i2]])
                pt_ = 2 + (c % 2)
                pvb = PS[:, pt_ * 512:pt_ * 512 + 256].bitcast(BF)
                for h in range(4):
                    op("pe", lambda e, h=h: e.transpose(pvb[:, h * 128:(h + 1) * 128], ORk[i2][:, hs(h)], ident[:]), [t_ORk[i2], t_ident], [t_ps[pt_]])
                op("act", lambda e: e.activation(out=OT[:, 4:8, cs_], in_=pvb.rearrange("p (h t) -> p h t", h=4), func=AF.Copy),
                   [t_ps[pt_]], t_OT[4:8])

            for c in range(NT + 2):
                if c < NT:
                    stage1(c)
                if 1 <= c <= NT:
                    stage2(c - 1)
                if c >= 2:
                    stage3(c - 2)
            dbg("ORT", OT[:, 4:8, :], t_OT[4:8], [128, 4, S], BF)
            sch.flush()
            if stop == "B":
                return nc, dbg_out

        with ExitStack() as ph:
            PS = ph.enter_context(nc.psum_tensor("psC", [128, 4096], F32))
            t_ps = [Tile("psC%d" % b, True) for b in range(8)]
            VM = SPARE[:, 0:4608].bitcast(BF).rearrange("p (t h e) -> p t h e", t=NTC, h=8)
            t_VM = Tile("VM")
            WM = sb("WM", [128, 8, 512], BF, ph); t_WM = Tile("WM")
            WKP = sb("WKP", [128, 8, 2, 96], BF, ph); t_WKP = Tile("WKP")
            CQT = sb("CQT", [128, 2, S], BF, ph); t_CQT = Tile("CQT")
            CKT = sb("CKT", [128, 2, SC], BF, ph); t_CKT = Tile("CKT")
            WUQ = sb("WUQ", [128, 2, 8, 2, 96], BF, ph); t_WUQ = Tile("WUQ")
            WUKV = sb("WUKV", [128, 2, 8, 128], BF, ph); t_WUKV = Tile("WUKV")
            KPT = sb("KPT", [96, SC], BF, ph); t_KPT = Tile("KPT")
            MTab = sb("MTab", [96, 2, S], F32, ph); t_MTab = Tile("MTab")
            GQK = sb("GQK", [128, 4], F32, ph); t_GQK = Tile("GQK")
            QTm = [sb("QTm%d" % i, [96, S], BF, ph) for i in range(2)]
            KTm = [sb("KTm%d" % i, [96, SC], BF, ph) for i in range(2)]
            VA = [sb("VA%d" % i, [128, NTC, 128], BF, ph) for i in range(2)]
            t_QTm = [Tile("QTm%d" % i) for i in range(2)]
            t_KTm = [Tile("KTm%d" % i) for i in range(2)]
            t_VA = [Tile("VA%d" % i) for i in range(2)]
            PTb = [sb("PTb%d" % i, [128, 512], BF, ph) for i in range(4)]
            t_PTb = [Tile("PTb%d" % i) for i in range(4)]
            RDn = [sb("RDn%d" % i, [128, 512], F32, ph) for i in range(2)]
            t_RDn = [Tile("RDn%d" % i) for i in range(2)]
            T1 = [sb("T1_%d" % i, [96, 512], F32, ph) for i in range(2)]
            T2 = [sb("T2_%d" % i, [96, 512], F32, ph) for i in range(2)]
            t_T1 = [Tile("T1_%d" % i) for i in range(2)]
            t_T2 = [Tile("T2_%d" % i) for i in range(2)]
            CN = [sb("CN%d" % i, [128, 512], BF, ph) for i in range(4)]
            t_CN = [Tile("CN%d" % i) for i in range(4)]
            junk2 = sb("junk2", [128, 256], BF, ph); t_junk2 = Tile("junk2")
            ss2 = sb("ss2", [128, NTC, 2], F32, ph)
            t_ss2 = [Tile("ss2_%d" % t) for t in range(NTC)]
            t_ss2all = Tile("ss2all")

            sch.dma("pool", [(WM[:], win_d[:, 0:512].rearrange("(k p) n -> p k n", p=128))], writes=[t_WM])
            op("dve", lambda e: e.memset(WKP[:], 0.0), [], [t_WKP])
            op("dve", lambda e: e.memset(ss2[:], 0.0), [], [t_ss2all])
            sch.dma("pool", [(WKP[:, :, 0, 64:96], win_d[:, 512:544].rearrange("(k p) n -> p k n", p=128))], writes=[t_WKP])
            for a, b_ in ((64, 72), (72, 64), (80, 88), (88, 80)):
                op("dve", lambda e, a=a, b_=b_: e.tensor_copy(out=WKP[:, :, 1, a:a + 8], in_=WKP[:, :, 0, b_:b_ + 8]), [t_WKP], [t_WKP])
            sch.dma("pool", [(WUQ[:, r, :, 0, :], wuq_d[r * 128:(r + 1) * 128, :].rearrange("p (h e) -> p h e", h=8)) for r in range(2)], writes=[t_WUQ])
            op("pool", lambda e: e.tensor_copy(out=WUQ[:, :, :, 1, 0:64], in_=WUQ[:, :, :, 0, 0:64]), [t_WUQ], [t_WUQ])
            for a, b_ in ((64, 72), (72, 64), (80, 88), (88, 80)):
                op("pool", lambda e, a=a, b_=b_: e.tensor_copy(out=WUQ[:, :, :, 1, a:a + 8], in_=WUQ[:, :, :, 0, b_:b_ + 8]), [t_WUQ], [t_WUQ])
            sch.dma("pool", [(WUKV[:], wukv_d[:, :].rearrange("(r p) (h e) -> p r h e", p=128, h=8))], writes=[t_WUKV])
            sch.dma("sp", [(MTab[64:96, :, :], mt_d[:, :, :])], writes=[t_MTab])
            sch.dma("sp", [(GQK[:, 0:2], gq_d.ap().rearrange("o (r p) -> p (o r)", p=128)),
                           (GQK[:, 2:4], gkv_d.ap().rearrange("o (r p) -> p (o r)", p=128))], writes=[t_GQK], allow_slow_non_contiguous=True)
            for i in range(2):
                op("pool", lambda e, i=i: e.memset(VA[i][:, :, (64 - 64 * i):(128 - 64 * i)], 1.0), [], [t_VA[i]])

            if stop == "C0":
                sch.flush()
                return nc, dbg_out
            def c1_a(t):
                b = t % 3
                i4 = t % 4
                ts_ = slice(t * 128, (t + 1) * 128)
                pb_ = PS[:, b * 512:(b + 1) * 512]
                for k in range(8):
                    op("pe", lambda e, k=k: e.matmul(pb_, hT[:, k, ts_], WM[:, k, :], start=(k == 0), stop=(k == 7)), [t_hT, t_WM], [t_ps[b]])
                for g in range(2):
                    op("act", lambda e, g=g: e.activation(out=junk2[:], in_=pb_[:, g * 256:(g + 1) * 256], func=AF.Square, accum_out=ss2[:, t, g:g + 1]),
                       [t_ps[b], t_ss2all], [t_junk2, t_ss2[t]])
                op("act", lambda e: e.activation(out=ss2[:, t, :], in_=ss2[:, t, :], func=AF.Sqrt, scale=1.0 / 256, bias=EPS), [t_ss2[t]], [t_ss2[t]])
                op("dve", lambda e: e.reciprocal(out=ss2[:, t, :], in_=ss2[:, t, :]), [t_ss2[t]], [t_ss2[t]])
                op("dve", lambda e: e.tensor_tensor(out=CN[i4][:].rearrange("p (g n) -> p g n", g=2), in0=pb_.rearrange("p (g n) -> p g n", g=2),
                                                    in1=ss2[:, t, :].unsqueeze(2).to_broadcast([128, 2, 256]), op=ALU.mult),
                   [t_ps[b], t_ss2[t]], [t_CN[i4]])

            def c1_b(t):
                i4 = t % 4
                ts_ = slice(t * 128, (t + 1) * 128)
                pt_ = 3 + (t % 2)
                pvb = PS[:, pt_ * 512:pt_ * 512 + 256].bitcast(BF)
                for j in range(4):
                    op("pe", lambda e, j=j: e.transpose(pvb[:, j * 128:(j + 1) * 128], CN[i4][:, j * 128:(j + 1) * 128], ident[:]),
                       [t_CN[i4], t_ident], [t_ps[pt_]])
                pv3 = pvb.rearrange("p (j t) -> p j t", j=4)
                if t < NT:
                    op("dve", lambda e: e.tensor_tensor(out=CQT[:, :, ts_], in0=pv3[:, 0:2, :], in1=GQK[:, 0:2].unsqueeze(2).to_broadcast([128, 2, 128]), op=ALU.mult),
                       [t_ps[pt_], t_GQK], [t_CQT])
                op("dve", lambda e: e.tensor_tensor(out=CKT[:, :, ts_], in0=pv3[:, 2:4, :], in1=GQK[:, 2:4].unsqueeze(2).to_broadcast([128, 2, 128]), op=ALU.mult),
                   [t_ps[pt_], t_GQK], [t_CKT])

            for t in range(NTC + 2):
                if t < NTC:
                    c1_a(t)
                if t >= 2:
                    c1_b(t - 2)

            if stop == "C1":
                sch.flush()
                return nc, dbg_out
            def rope_rows(psA, psB, tA, tB, cols, dst, t_dst, n):
                i2 = n % 2
                op("dve", lambda e: e.tensor_tensor(out=T1[i2][64:96, :], in0=psA[64:96, :], in1=MTab[64:96, 0, cols], op=ALU.mult), [tA, t_MTab], [t_T1[i2]])
                op("dve", lambda e: e.tensor_tensor(out=T2[i2][64:96, :], in0=psB[64:96, :], in1=MTab[64:96, 1, cols], op=ALU.mult), [tB, t_MTab], [t_T2[i2]])
                op("pool", lambda e: e.tensor_tensor(out=dst, in0=T1[i2][64:96, :], in1=T2[i2][64:96, :], op=ALU.add), [t_T1[i2], t_T2[i2]], [t_dst])

            for blk in range(5):
                cols = slice(blk * 512, blk * 512 + (512 if blk < 4 else 256))
                ncol = 512 if blk < 4 else 256
                ba = 5 + (blk % 2) * 0
                pA = PS[0:96, 5 * 512:5 * 512 + ncol]; pB = PS[0:96, 6 * 512:6 * 512 + ncol]
                for k in range(8):
                    op("pe", lambda e, k=k, pA=pA, cols=cols: e.matmul(pA, WKP[:, k, 0, :], hT[:, k, cols], start=(k == 0), stop=(k == 7)), [t_WKP, t_hT], [t_ps[5]])
                if blk < 4:
                    for k in range(8):
                        op("pe", lambda e, k=k, pB=pB, cols=cols: e.matmul(pB, WKP[:, k, 1, :], hT[:, k, cols], start=(k == 0), stop=(k == 7)), [t_WKP, t_hT], [t_ps[6]])
                    rope_rows(pA, pB, t_ps[5], t_ps[6], cols, KPT[64:96, cols], t_KPT, blk)
                else:
                    op("act", lambda e, pA=pA, cols=cols: e.activation(out=KPT[64:96, cols], in_=pA[64:96, :], func=AF.Copy), [t_ps[5]], [t_KPT])
            if stop == "C1k":
                sch.flush()
                return nc, dbg_out
            NPRE = 9
            for t in range(NPRE):
                sch.dma("sp", [(XN[:, t, :], x_d[t * 128:(t + 1) * 128, :])], writes=[t_hT, t_XN[t]])
            for t in range(NTC):
                b = t % 3
                ts_ = slice(t * 128, (t + 1) * 128)
                pb_ = PS[:, b * 512:(b + 1) * 512]
                for r in range(2):
                    op("pe", lambda e, r=r, pb_=pb_, ts_=ts_: e.matmul(pb_.rearrange("p (h e) -> p h e", h=8), CKT[:, r, ts_], WUKV[:, r, :, 64:128], start=(r == 0), stop=(r == 1)),
                       [t_CKT, t_WUKV], [t_ps[b]])
                op("act", lambda e, pb_=pb_, t=t: e.activation(out=VM[:, t, :, :], in_=pb_.rearrange("p (h e) -> p h e", h=8), func=AF.Copy), [t_ps[b]], [t_VM])
            dbg("CQT", CQT[:], [t_CQT], [128, 2, S], BF)
            dbg("CKT", CKT[:], [t_CKT], [128, 2, SC], BF)
            dbg("KPT", KPT[64:96, :], [t_KPT], [32, SC], BF)
            dbg("VM", VM, [t_VM], [128, NTC, 8, 64], BF)

            if stop == "C1v":
                sch.flush()
                return nc, dbg_out
            def proj_units(h):
                i2 = h % 2
                units = []

                def q_unit(blk):
                    cols = slice(blk * 512, (blk + 1) * 512)
                    pA = PS[0:96, 5 * 512:6 * 512]; pB = PS[0:96, 6 * 512:7 * 512]
                    for v, (pp, tp) in enumerate(((pA, t_ps[5]), (pB, t_ps[6]))):
                        for r in range(2):
                            op("pe", lambda e, v=v, r=r, pp=pp: e.matmul(pp, WUQ[:, r, h, v, :], CQT[:, r, cols], start=(r == 0), stop=(r == 1)),
                               [t_WUQ, t_CQT], [tp])
                    op("dve", lambda e: e.tensor_copy(out=QTm[i2][0:64, cols], in_=pA[0:64, :]), [t_ps[5]], [t_QTm[i2]])
                    rope_rows(pA, pB, t_ps[5], t_ps[6], cols, QTm[i2][64:96, cols], t_QTm[i2], blk)

                def k_unit(blk):
                    ncol = 512 if blk < 4 else 256
                    cols = slice(blk * 512, blk * 512 + ncol)
                    pk = PS[0:64, 7 * 512:7 * 512 + ncol]
                    for r in range(2):
                        op("pe", lambda e, r=r: e.matmul(pk, WUKV[:, r, h, 0:64], CKT[:, r, cols], start=(r == 0), stop=(r == 1)), [t_WUKV, t_CKT], [t_ps[7]])
                    op("dve", lambda e: e.tensor_copy(out=KTm[i2][0:64, cols], in_=pk), [t_ps[7]], [t_KTm[i2]])

                def v_unit():
                    op("pool", lambda e: e.tensor_copy(out=KTm[i2][64:96, :], in_=KPT[64:96, :]), [t_KPT], [t_KTm[i2]])
                    op("pool", lambda e: e.tensor_copy(out=VA[i2][:, :, 64 * i2:64 * i2 + 64], in_=VM[:, :, h, :]), [t_VM], [t_VA[i2]])

                units.append(v_unit)
                for blk in range(5):
                    units.append(lambda blk=blk: k_unit(blk))
                for blk in range(4):
                    units.append(lambda blk=blk: q_unit(blk))
                return units

            def proj(h):
                for u in proj_units(h):
                    u()

            items = [(h, qb, kt) for h in range(8) for qb in range(4) for kt in range(NTC)]
            LAG = 2
            NPT = 4

            def emit_S(n):
                h, qb, kt = items[n]
                i2 = h % 2
                b = n % 3
                pS_ = PS[:, b * 512:(b + 1) * 512]
                op("pe", lambda e: e.matmul(pS_, KTm[i2][0:96, kt * 128:(kt + 1) * 128], QTm[i2][0:96, qb * 512:(qb + 1) * 512], start=True, stop=True),
                   [t_KTm[i2], t_QTm[i2]], [t_ps[b]])
                op("act", lambda e: e.activation(out=PTb[n % NPT][:], in_=pS_, func=AF.Exp, scale=MLA_SCALE), [t_ps[b]], [t_PTb[n % NPT]])

            def emit_PV(n):
                h, qb, kt = items[n]
                i2 = h % 2
                po = 3 + (qb % 2)
                pO = PS[:, po * 512:(po + 1) * 512]
                qs = slice(qb * 512, (qb + 1) * 512)
                op("pe", lambda e: e.matmul(pO, VA[i2][:, kt, :], PTb[n % NPT][:], start=(kt == 0), stop=(kt == NTC - 1)), [t_VA[i2], t_PTb[n % NPT]], [t_ps[po]])
                if kt != NTC - 1:
                    return
                r2 = qb % 2
                ro = (h % 2) * 64
                dn = 64 - ro
                op("dve", lambda e: e.tensor_copy(out=RDn[r2][dn:dn + 64, :], in_=pO[dn:dn + 64, :]), [t_ps[po]], [t_RDn[r2]])
                op("dve", lambda e: e.reciprocal(out=RDn[r2][dn:dn + 64, :], in_=RDn[r2][dn:dn + 64, :]), [t_RDn[r2]], [t_RDn[r2]])
                if ro == 64:
                    op("dve", lambda e: e.tensor_copy(out=RDn[r2][64:128, :], in_=RDn[r2][0:64, :]), [t_RDn[r2]], [t_RDn[r2]])
                    dn = 64
                op("dve", lambda e: e.tensor_tensor(out=OT[ro:ro + 64, h // 2, qs], in0=pO[ro:ro + 64, :], in1=RDn[r2][dn:dn + 64, :], op=ALU.mult),
                   [t_ps[po], t_RDn[r2]], [t_OT[h // 2]])

            proj(0)
            if "QTm0" in debug:
                dbg("QTm0", QTm[0][:], [t_QTm[0]], [96, S], BF)
                dbg("KTm0", KTm[0][:], [t_KTm[0]], [96, SC], BF)
            if stop == "C2":
                sch.flush()
                return nc, dbg_out
            pending = []
            for n in range(len(items) + LAG):
                if n < len(items):
                    emit_S(n)
                if n - LAG >= 0:
                    emit_PV(n - LAG)
                    h, qb, kt = items[n - LAG]
                    if qb == 0 and kt == 0 and h + 1 < 8:
                        assert not pending
                        pending = proj_units(h + 1)
                    if pending and (n % 6 == 0):
                        pending.pop(0)()
            dbg("OMT", OT[:, 0:4, :], t_OT[0:4], [128, 4, S], BF)
            sch.flush()
            if stop == "C":
                return nc, dbg_out

        with ExitStack() as ph:
            PS = ph.enter_context(nc.psum_tensor("psD", [128, 4096], F32))
            t_ps = [Tile("psD%d" % b, True) for b in range(8)]
            WO = sb("WO", [128, 8, D], BF, ph); t_WO = Tile("WO")
            GT1 = sb("GT1", [128, D], F32, ph); t_GT1 = Tile("GT1")
            XT = [sb("XTd%d" % i, [128, D], F32, ph) for i in range(3)]
            t_XT = [Tile("XTd%d" % i) for i in range(3)]
            TMP = [sb("TMPd%d" % i, [128, D], F32, ph) for i in range(2)]
            t_TMP = [Tile("TMPd%d" % i) for i in range(2)]
            t_WOh = [Tile("WO0"), Tile("WO1")]
            for hf in range(2):
                sch.dma("pool", [(WO[:, :, hf * 512:(hf + 1) * 512], wout_d[:, hf * 512:(hf + 1) * 512].rearrange("(j p) n -> p j n", p=128))], writes=[t_WOh[hf]])
            sch.dma("sp", [(GT1[:], bc(modp_d, 0, D))], reads=[t_modp], writes=[t_GT1])
            for t in range(NT):
                ts_ = slice(t * 128, (t + 1) * 128)
                i3 = t % 3; i2 = t % 2
                pre = t < 9
                if not pre:
                    sch.dma("sp", [(XT[i3][:], x_d[ts_, :])], writes=[t_XT[i3]])
                b0 = (t % 4) * 2
                for hf in range(2):
                    for j in range(8):
                        op("pe", lambda e, hf=hf, j=j, b0=b0, ts_=ts_: e.matmul(PS[:, (b0 + hf) * 512:(b0 + hf + 1) * 512], OT[:, j, ts_], WO[:, j, hf * 512:(hf + 1) * 512],
                                                                                start=(j == 0), stop=(j == 7)), [t_OT[j], t_WOh[hf]], [t_ps[b0 + hf]])
                op("dve", lambda e, b0=b0, i2=i2: e.tensor_tensor(out=TMP[i2][:], in0=PS[:, b0 * 512:(b0 + 2) * 512], in1=GT1[:], op=ALU.mult),
                   [t_ps[b0], t_ps[b0 + 1], t_GT1], [t_TMP[i2]])
                if pre:
                    op("pool", lambda e, t=t, i2=i2: e.tensor_tensor(out=XN[:, t, :], in0=TMP[i2][:], in1=XN[:, t, :], op=ALU.add),
                       [t_TMP[i2], t_XN[t]], [t_XN[t]])
                else:
                    op("pool", lambda e, t=t, i2=i2, i3=i3: e.tensor_tensor(out=XN[:, t, :], in0=TMP[i2][:], in1=XT[i3][:], op=ALU.add),
                       [t_TMP[i2], t_XT[i3]] + list(t_OT) + [t_hT], [t_XN[t]])
            dbg("XN", XN, t_XN, [128, NT, D], F32)
            sch.flush()
            if stop == "D":
                return nc, dbg_out

        H2T = OT
        t_H2T = Tile("H2T")
        with ExitStack() as ph:
            PS = ph.enter_context(nc.psum_tensor("psE", [128, 4096], F32))
            t_ps = [Tile("psE%d" % b, True) for b in range(8)]
            CW = sb("CW", [128, 44, 3], F32, ph); CB = sb("CB", [128, 44], F32, ph)
            t_CW = Tile("CW")
            GT2 = sb("GT2", [128, D], F32, ph); t_GT2 = Tile("GT2")
            GFB = sb("GFB", [128, D], F32, ph); t_GFB = Tile("GFB")
            def late_loads():
                sch.dma("sp", [(GT2[:], bc(modp_d, 3 * D, D))], reads=[t_modp], writes=[t_GT2])
                sch.dma("sp", [(GFB[:], bc(gf_d, 0, D))], writes=[t_GFB])
                sch.dma("sp", [(CW[:, :, t], cw_d.ap()[t:t + 1, :].rearrange("o (c p) -> p (o c)", p=128)) for t in range(3)] +
                        [(CB[:], cb_d.ap().rearrange("o (c p) -> p (o c)", p=128))], writes=[t_CW], allow_slow_non_contiguous=True)
            ss = sb("ssE", [128, 2 * NT], F32, ph)
            rs = sb("rsE", [128, 2 * NT], F32, ph)
            t_ss = [Tile("ssE%d" % t) for t in range(2 * NT)]
            t_rs = [Tile("rsE%d" % t) for t in range(2 * NT)]
            t_ssall = Tile("ssEall")
            op("dve", lambda e: e.memset(ss[:], 0.0), [], [t_ssall])
            t_junk = Tile("junkE")
            TMP = [sb("TMPe%d" % i, [128, D], F32, ph) for i in range(2)]
            t_TMP = [Tile("TMPe%d" % i) for i in range(2)]
            WU = [sb("WU%d" % i, [128, 8, 2, 128], BF, ph) for i in range(3)]
            t_WU = [Tile("WU%d" % i) for i in range(3)]
            WD = [sb("WD%d" % i, [128, GMAX, D], BF, ph) for i in range(1)]
            t_WD = [Tile("WD%d" % i) for i in range(1)]

            def load_wu(j):
                w = j % 3
                sch.dma("pool", [(WU[w][:, :, 0, :], wup_d[:, j * 128:(j + 1) * 128].rearrange("(k p) n -> p k n", p=128)),
                                 (WU[w][:, :, 1, :], wup_d[:, DFF + j * 128:DFF + (j + 1) * 128].rearrange("(k p) n -> p k n", p=128))], writes=[t_WU[w]])

            def load_wd(gi):
                j0, j1 = GROUPS[gi]
                sch.dma("pool", [(WD[0][:, 0:j1 - j0, :], wdn_d[j0 * 128:j1 * 128, :].rearrange("(j p) n -> p j n", p=128))], writes=[t_WD[0]])

            load_wu(0); load_wu(1); load_wd(0)
            with ExitStack() as ph0:
                junk = sb("junkE", [128, D], BF, ph0)
                S2 = sb("S2", [128, D], F32, ph0); SH2 = sb("SH2", [128, D], F32, ph0); G2B = sb("G2B", [128, D], F32, ph0)
                HB = [sb("HBe%d" % i, [128, D], BF, ph0) for i in range(2)]
                t_HB = [Tile("HBe%d" % i) for i in range(2)]
                t_S2 = Tile("S2"); t_SH2 = Tile("SH2"); t_G2B = Tile("G2B")
                sch.dma("sp", [(SH2[:], bc(modp_d, D, D))], reads=[t_modp], writes=[t_SH2])
                sch.dma("sp", [(S2[:], bc(modp_d, 2 * D, D))], reads=[t_modp], writes=[t_S2])
                sch.dma("sp", [(G2B[:], bc(g2_d, 0, D))], writes=[t_G2B])
                late_loads()
                op("dve", lambda e: e.scalar_tensor_tensor(out=S2[:], in0=S2[:], scalar=1.0, in1=G2B[:], op0=ALU.add, op1=ALU.mult), [t_S2, t_G2B], [t_S2])
                def e0_a(t):
                    i2 = t % 2
                    src = XN[:, t, :]
                    op("act", lambda e: e.activation(out=junk[:], in_=src, func=AF.Square, accum_out=ss[:, t:t + 1]), [t_XN[t], t_ssall], [t_junk, t_ss[t]])
                    op("act", lambda e: e.activation(out=rs[:, t:t + 1], in_=ss[:, t:t + 1], func=AF.Sqrt, scale=1.0 / D, bias=EPS), [t_ss[t]], [t_rs[t]])
                    op("dve", lambda e: e.reciprocal(out=rs[:, t:t + 1], in_=rs[:, t:t + 1]), [t_rs[t]], [t_rs[t]])
                    op("dve", lambda e: e.scalar_tensor_tensor(out=TMP[i2][:], in0=src, scalar=rs[:, t:t + 1], in1=S2[:], op0=ALU.mult, op1=ALU.mult),
                       [t_XN[t], t_rs[t], t_S2], [t_TMP[i2]])
                    op("dve", lambda e: e.tensor_tensor(out=HB[i2][:], in0=TMP[i2][:], in1=SH2[:], op=ALU.add), [t_TMP[i2], t_SH2], [t_HB[i2]])

                def e0_b(t):
                    i2 = t % 2
                    pbk = t % 2
                    pv = PS[:, pbk * 512:(pbk + 1) * 512].bitcast(BF)
                    for k in range(8):
                        op("pe", lambda e, k=k: e.transpose(pv[:, k * 128:(k + 1) * 128], HB[i2][:, k * 128:(k + 1) * 128], ident[:]), [t_HB[i2], t_ident], [t_ps[pbk]])
                    op("act", lambda e: e.activation(out=H2T[:, :, t * 128:(t + 1) * 128], in_=pv.rearrange("p (k t) -> p k t", k=8), func=AF.Copy),
                       [t_ps[pbk]], [t_H2T])

                for t in range(NT + 1):
                    if t < NT:
                        e0_a(t)
                    if t >= 1:
                        e0_b(t - 1)
                dbg("H2T", H2T[:], [t_H2T], [128, 8, S], BF)
                sch.flush()
            with ExitStack() as ph1:
                ACTT = sb("ACTT", [128, GMAX, S], BF, ph1)
                t_ACTT = [Tile("ACTT%d" % i) for i in range(GMAX)]
                AaL = [sb("Aa%d" % i, [128, S], F32, ph1) for i in range(2)]
                GgL = [sb("Gg%d" % i, [128, S], F32, ph1) for i in range(2)]
                t_AaL = [(Tile("Aa%da" % i), Tile("Aa%db" % i)) for i in range(2)]; t_GgL = [(Tile("Gg%da" % i), Tile("Gg%db" % i)) for i in range(2)]
                t_out = Tile("out")
                npair = [0]

                for gi, (j0, j1) in enumerate(GROUPS):
                    ng = j1 - j0
                    wd = 0
                    if gi > 0:
                        load_wd(gi)
                    for j in range(j0, j1):
                        jj = j - j0
                        w = j % 3
                        if j + 2 < NCH:
                            load_wu(j + 2)
                        Aa = AaL[j % 2]; Gg = GgL[j % 2]; t_Aa = t_AaL[j % 2]; t_Gg = t_GgL[j % 2]
                        for half, (ACC, t_ACC, pb0) in enumerate(((Aa, t_Aa, 0), (Gg, t_Gg, 4))):
                            c = half * NCH + j
                            pall = PS[:, pb0 * 512:(pb0 + 4) * 512]
                            for blk in range(4):
                                for k in range(8):
                                    op("pe", lambda e, k=k, blk=blk, half=half, pb0=pb0, w=w: e.matmul(PS[:, (pb0 + blk) * 512:(pb0 + blk + 1) * 512], WU[w][:, k, half, :],
                                                                                                      H2T[:, k, blk * 512:(blk + 1) * 512], start=(k == 0), stop=(k == 7)),
                                       [t_WU[w], t_H2T], [t_ps[pb0 + blk]])
                            H = S // 2
                            tA = t_ps[pb0:pb0 + 2]; tB = t_ps[pb0 + 2:pb0 + 4]
                            tacc = t_ACC
                            op("act", lambda e, ACC=ACC, pall=pall, c=c: e.activation(out=ACC[:, 0:H], in_=pall[:, 0:H], func=AF.Identity, scale=CW[:, c, 1:2], bias=CB[:, c:c + 1]),
                               tA + [t_CW], [tacc[0]])
                            op("act", lambda e, ACC=ACC, pall=pall, c=c: e.activation(out=ACC[:, H:S], in_=pall[:, H:S], func=AF.Identity, scale=CW[:, c, 1:2], bias=CB[:, c:c + 1]),
                               tB + [t_CW], [tacc[1]])
                            op("dve", lambda e, ACC=ACC, pall=pall, c=c: e.scalar_tensor_tensor(out=ACC[:, 1:H], in0=pall[:, 0:H - 1], scalar=CW[:, c, 0:1], in1=ACC[:, 1:H],
                                                                                                op0=ALU.mult, op1=ALU.add), tA + [t_CW, tacc[0]], [tacc[0]])
                            op("dve", lambda e, ACC=ACC, pall=pall, c=c: e.scalar_tensor_tensor(out=ACC[:, H:S], in0=pall[:, H - 1:S - 1], scalar=CW[:, c, 0:1], in1=ACC[:, H:S],
                                                                                                op0=ALU.mult, op1=ALU.add), tA[1:2] + tB + [t_CW, tacc[1]], [tacc[1]])
                            op("dve", lambda e, ACC=ACC, pall=pall, c=c: e.scalar_tensor_tensor(out=ACC[:, 0:H], in0=pall[:, 1:H + 1], scalar=CW[:, c, 2:3], in1=ACC[:, 0:H],
                                                                                                op0=ALU.mult, op1=ALU.add), tA + tB[0:1] + [t_CW, tacc[0]], [tacc[0]])
                            op("dve", lambda e, ACC=ACC, pall=pall, c=c: e.scalar_tensor_tensor(out=ACC[:, H:S - 1], in0=pall[:, H + 1:S], scalar=CW[:, c, 2:3], in1=ACC[:, H:S - 1],
                                                                                                op0=ALU.mult, op1=ALU.add), tB + [t_CW, tacc[1]], [tacc[1]])
                        op("act", lambda e: e.activation(out=Gg[:], in_=Gg[:], func=AF.Silu), list(t_Gg), list(t_Gg))
                        op("pool", lambda e, jj=jj: e.tensor_tensor(out=ACTT[:, jj, :], in0=Aa[:], in1=Gg[:], op=ALU.mult), list(t_Aa) + list(t_Gg), [t_ACTT[jj]])
                    last = gi == len(GROUPS) - 1
                    for t in range(NT):
                        ts_ = slice(t * 128, (t + 1) * 128)
                        b0 = (t % 4) * 2
                        i2 = t % 2
                        for hf in range(2):
                            for jj in range(ng):
                                op("pe", lambda e, hf=hf, jj=jj, b0=b0, ts_=ts_, wd=wd, ng=ng: e.matmul(PS[:, (b0 + hf) * 512:(b0 + hf + 1) * 512], ACTT[:, jj, ts_],
                                                                                                        WD[wd][:, jj, hf * 512:(hf + 1) * 512], start=(jj == 0), stop=(jj == ng - 1)),
                                   [t_ACTT[jj], t_WD[wd]], [t_ps[b0 + hf]])
                        op("dve", lambda e, b0=b0, i2=i2: e.tensor_tensor(out=TMP[i2][:], in0=PS[:, b0 * 512:(b0 + 2) * 512], in1=GT2[:], op=ALU.mult),
                           [t_ps[b0], t_ps[b0 + 1], t_GT2], [t_TMP[i2]])
                        op("pool", lambda e, t=t, i2=i2: e.tensor_tensor(out=XN[:, t, :], in0=XN[:, t, :], in1=TMP[i2][:], op=ALU.add), [t_TMP[i2], t_XN[t]], [t_XN[t]])
                        if last:
                            u = NT + t
                            src = XN[:, t, :]
                            jv = TMP[i2][:, 0:D // 2].bitcast(BF)
                            op("act", lambda e, src=src, u=u, jv=jv: e.activation(out=jv, in_=src, func=AF.Square, accum_out=ss[:, u:u + 1]), [t_XN[t], t_ssall], [t_TMP[i2], t_ss[u]])
                            op("act", lambda e, u=u: e.activation(out=rs[:, u:u + 1], in_=ss[:, u:u + 1], func=AF.Sqrt, scale=1.0 / D, bias=EPS), [t_ss[u]], [t_rs[u]])
                            op("dve", lambda e, u=u: e.reciprocal(out=rs[:, u:u + 1], in_=rs[:, u:u + 1]), [t_rs[u]], [t_rs[u]])
                            op("dve", lambda e, src=src, u=u: e.scalar_tensor_tensor(out=src, in0=src, scalar=rs[:, u:u + 1], in1=GFB[:], op0=ALU.mult, op1=ALU.mult),
                               [t_XN[t], t_rs[u], t_GFB], [t_XN[t]])
                            sch.dma("sp", [(out_d[ts_, :], src)], reads=[t_XN[t]], writes=[t_out])
                sch.wait_tiles("sp", [t_out])
                sch.flush()
    return nc, dbg_out


_CACHE = {}


def _prep_inputs(inputs):
    c = _consts()
    f = lambda a: np.ascontiguousarray(np.asarray(a, dtype=np.float32))
    shared = {
        "c_ctx": f(inputs["c_ctx"]).reshape(1, D),
        "w_ada": f(inputs["w_ada"])[0],
        "b_ada": f(inputs["b_ada"]).reshape(1, 6 * D),
        "g_norm1": f(inputs["g_norm1"]).reshape(1, D),
        "w_in": f(inputs["w_in"])[0],
        "g_q": f(inputs["g_q"]).reshape(1, 256),
        "w_uq": f(inputs["w_uq"])[0],
        "g_kv": f(inputs["g_kv"]).reshape(1, 256),
        "w_ukv": f(inputs["w_ukv"])[0],
        "ret_decay": f(inputs["ret_decay"]).reshape(1, 8),
        "g_ret": f(inputs["g_ret"]).reshape(1, 512),
        "w_out": f(inputs["w_out"])[0],
        "g_norm2": f(inputs["g_norm2"]).reshape(1, D),
        "w_up": f(inputs["w_up"])[0],
        "conv_w": f(inputs["conv_w"])[0],
        "conv_b": f(inputs["conv_b"]).reshape(1, 2 * DFF),
        "w_down": f(inputs["w_down"])[0],
        "g_final": f(inputs["g_final"]).reshape(1, D),
        "k_ident": c["ident"], "k_rt": c["rt"], "k_mt": c["mt"], "k_cst": c["cst"],
    }
    x = f(inputs["x"]); cc = f(inputs["c"]); ctx = f(inputs["ctx"])
    maps = []
    for b in range(8):
        m = dict(shared)
        m["x"] = x[b]
        m["c"] = cc[b].reshape(1, D)
        m["ctx"] = ctx[b]
        maps.append(m)
    return maps


def kernel(**inputs):
    if "nc" not in _CACHE:
        _CACHE["nc"] = build()[0]
    nc = _CACHE["nc"]
    maps = _prep_inputs(inputs)
    res = run_bass_kernel_spmd(nc, maps, core_ids=list(range(8)))
    out = np.stack([np.asarray(r["out"], dtype=np.float32) for r in res.results], axis=0)
    return out
```

```python
import math
import os
import numpy as np
import ml_dtypes
import concourse.bass as bass
import concourse.mybir as mybir
from concourse.bass_utils import run_bass_kernel_spmd

F32 = mybir.dt.float32
BF = mybir.dt.bfloat16
AF = mybir.ActivationFunctionType
ALU = mybir.AluOpType

S = 2048
D = 1024
L = 256
NT = 16
NTC = 18
SC = S + L
DFF = 2816
NCH = 22
EPS = 1e-6
MLA_SCALE = 96 ** -0.5
LN_S = math.log(128 ** -0.5)
GROUPS = [(0, 8), (8, 15), (15, 22)]
GMAX = 8


class Tile:
    __slots__ = ("name", "w", "r", "excl")

    def __init__(self, name, excl=False):
        self.name = name
        self.w = None
        self.r = {}
        self.excl = excl


class _Rec:
    def __getattr__(self, name):
        def call(*a, **k):
            self.__dict__["call"] = (name, a, k)
            return self
        return call


class Sched:
    ENG = ("pe", "act", "dve", "pool", "sp")

    def __init__(self, nc, ndma=48):
        self.nc = nc
        self.ops = {e: [] for e in self.ENG}
        self.cnt = {e: 0 for e in self.ENG}
        self.seen = {e: {} for e in self.ENG}
        self.sems = {}
        self.ndma = ndma
        self.dcnt = [0] * ndma
        self.rr = {"sp": 0, "pool": 0}
        self.half = ndma // 2
        self.stack = None

    def open(self, stack):
        for e in self.ENG:
            self.sems[e] = stack.enter_context(self.nc.semaphore("s_" + e))
        for j in range(self.ndma):
            self.sems[("d", j)] = stack.enter_context(self.nc.semaphore("s_d%d" % j))

    def _deps(self, eng, reads, writes):
        deps = []
        writes = list(writes) + [t for t in reads if t.excl and t not in writes]
        for t in reads:
            if t.w is not None:
                deps.append(t.w)
        for t in writes:
            if t.w is not None:
                deps.append(t.w)
            deps.extend(t.r.items())
        waits = []
        seen = self.seen[eng]
        for k, v in deps:
            if eng == "pe" and k == "pe":
                continue
            if seen.get(k, 0) >= v:
                continue
            seen[k] = v
            waits.append((k, v))
        m = {}
        for k, v in waits:
            m[k] = max(m.get(k, 0), v)
        return list(m.items())

    def _mark(self, ev, reads, writes):
        k, v = ev
        writes = list(writes) + [t for t in reads if t.excl and t not in writes]
        for t in reads:
            if t.r.get(k, 0) < v:
                t.r[k] = v
        for t in writes:
            t.w = ev
            t.r = {}

    def op(self, eng, fn, reads=(), writes=()):
        rec = _Rec()
        fn(rec)
        name, a, k = rec.call
        fn = lambda e, name=name, a=a, k=k: getattr(e, name)(*a, **k)
        waits = self._deps(eng, reads, writes)
        self.cnt[eng] += 1
        ev = (eng, self.cnt[eng])
        self.ops[eng].append((waits, fn, (eng, 1)))
        self._mark(ev, reads, writes)

    def dma(self, q, pairs, reads=(), writes=(), **kw):
        j = self.rr[q] + (0 if q == "sp" else self.half)
        self.rr[q] = (self.rr[q] + 1) % self.half
        key = ("d", j)
        waits = self._deps(q, reads, writes)
        seen = self.seen[q]
        if self.dcnt[j] > 0 and seen.get(key, 0) < self.dcnt[j]:
            seen[key] = self.dcnt[j]
            waits.append((key, self.dcnt[j]))
        for i, (o, i_) in enumerate(pairs):
            def fn(e, o=o, i_=i_):
                return e.dma_start(out=o, in_=i_, **kw)
            self.ops[q].append((waits if i == 0 else [], fn, (key, 16)))
            self.dcnt[j] += 16
        ev = (key, self.dcnt[j])
        self._mark(ev, reads, writes)

    def wait_tiles(self, eng, tiles):
        waits = self._deps(eng, tiles, tiles)
        if waits:
            self.cnt[eng] += 1
            self.ops[eng].append((waits, lambda e: e.nop(), (eng, 1)))

    def drain_dmas(self):
        waits = []
        seen = self.seen["sp"]
        for j in range(self.ndma):
            key = ("d", j)
            if self.dcnt[j] > 0 and seen.get(key, 0) < self.dcnt[j]:
                seen[key] = self.dcnt[j]
                waits.append((key, self.dcnt[j]))
        if waits:
            self.cnt["sp"] += 1
            self.ops["sp"].append((waits, lambda e: e.nop(), ("sp", 1)))

    def flush(self):
        self.drain_dmas()
        nc = self.nc
        if os.environ.get("KSBUF"):
            print("SBUF remaining at flush:", nc.sbuf_bytes_remaining)
        sems = self.sems
        ops = self.ops

        def replay(name):
            def run(e):
                for waits, fn, inc in ops[name]:
                    for k, v in waits:
                        e.wait_ge(sems[k], v)
                    ins = fn(e)
                    ins.then_inc(sems[inc[0]], inc[1])
            return run

        with nc.Block() as block:
            block.tensor(replay("pe"))
            block.scalar(replay("act"))
            block.vector(replay("dve"))
            block.gpsimd(replay("pool"))
            block.sync(replay("sp"))
        self.ops = {e: [] for e in self.ENG}


def _consts():
    c = {}
    c["ident"] = np.eye(128, dtype=np.float32).astype(ml_dtypes.bfloat16)
    pos = np.arange(S, dtype=np.float64)
    inv = 10000.0 ** (-np.arange(0, 128, 2, dtype=np.float64) / 128.0)
    ang = pos[:, None] * inv[None, :]
    rt = np.stack([np.cos(ang), np.sin(ang)], axis=1)
    c["rt"] = np.ascontiguousarray(rt.reshape(NT, 128, 2, 64).transpose(1, 0, 2, 3)).reshape(128, NT * 128).astype(np.float32)
    inv8 = 10000.0 ** (-np.arange(0, 16, 2, dtype=np.float64) / 16.0)
    prow = (np.arange(S) // 64).astype(np.float64)
    pcol = (np.arange(S) % 64).astype(np.float64)
    ct = np.zeros((32, S)); st = np.zeros((32, S))
    for r in range(32):
        p = prow if r < 16 else pcol
        a = p * inv8[r % 8]
        ct[r] = np.cos(a)
        st[r] = -np.sin(a) if (r % 16) < 8 else np.sin(a)
    c["mt"] = np.stack([ct, st], axis=1).astype(np.float32)
    i = np.arange(128, dtype=np.float64)
    e1 = np.maximum(i[None, :] - i[:, None], 0.0)
    e2 = np.maximum(i[:, None] - i[None, :], 0.0)
    c3 = np.tile((i + 1.0)[None, :], (128, 1))
    c4 = np.tile((128.0 - i)[None, :], (128, 1))
    cv = np.zeros((128, 8))
    cv[:, 0] = 127.0 - i
    cv[:, 1] = i
    cv[:, 2] = 255.0 - i
    cv[:, 3] = 127.0 - i
    cv[:, 4] = i
    cv[:, 5] = 128.0 + i
    c["cst"] = np.concatenate([e1, e2, c3, c4, cv], axis=1).astype(np.float32)
    return c


def build(debug=(), stop=None):
    from contextlib import ExitStack
    nc = bass.Bass("TRN2", target_bir_lowering=False)
    dbg_out = {}

    def dram_in(name, shape, dt=F32):
        return nc.dram_tensor(name, list(shape), dt, kind="ExternalInput")

    x_d = dram_in("x", [S, D]).ap()
    c_d = dram_in("c", [1, D])
    ctx_d = dram_in("ctx", [L, D]).ap()
    cctx_d = dram_in("c_ctx", [1, D])
    wada_d = dram_in("w_ada", [D, 6 * D]).ap()
    bada_d = dram_in("b_ada", [1, 6 * D])
    g1_d = dram_in("g_norm1", [1, D])
    win_d = dram_in("w_in", [D, 2592]).ap()
    gq_d = dram_in("g_q", [1, 256])
    wuq_d = dram_in("w_uq", [256, 768]).ap()
    gkv_d = dram_in("g_kv", [1, 256])
    wukv_d = dram_in("w_ukv", [256, 1024]).ap()
    rdec_d = dram_in("ret_decay", [1, 8])
    gret_d = dram_in("g_ret", [1, 512])
    wout_d = dram_in("w_out", [D, D]).ap()
    g2_d = dram_in("g_norm2", [1, D])
    wup_d = dram_in("w_up", [D, 2 * DFF]).ap()
    cw_d = dram_in("conv_w", [3, 2 * DFF])
    cb_d = dram_in("conv_b", [1, 2 * DFF])
    wdn_d = dram_in("w_down", [DFF, D]).ap()
    gf_d = dram_in("g_final", [1, D])
    ident_d = dram_in("k_ident", [128, 128], BF).ap()
    rt_d = dram_in("k_rt", [128, NT * 128]).ap()
    mt_d = dram_in("k_mt", [32, 2, S]).ap()
    cst_d = dram_in("k_cst", [128, 520]).ap()
    out_d = nc.dram_tensor("out", [S, D], F32, kind="ExternalOutput").ap()
    modp_d = nc.dram_tensor("modp_scratch", [1, 4 * D], F32, kind="Internal")

    def bc(t, off, n, parts=128):
        return bass.AP(t, off, [[0, parts], [1, n]])

    es = ExitStack()
    with es:
        sch = Sched(nc)
        sch.open(es)
        op = sch.op

        def sb(name, shape, dt, stack=es):
            return stack.enter_context(nc.sbuf_tensor(name, list(shape), dt))

        def dbg(name, ap, tiles, shape, dt=F32):
            if name not in debug:
                return
            dd = nc.dram_tensor("dbg_" + name, list(shape), dt, kind="ExternalOutput").ap()
            dbg_out[name] = dd
            t = Tile("dbg_" + name)
            sch.dma("sp", [(dd, ap)], reads=tiles, writes=[t])
            sch.wait_tiles("sp", [t])

        ident = sb("ident", [128, 128], BF)
        cst = sb("cst", [128, 520], F32)
        OT = sb("OT", [128, 8, S], BF)
        ARENA = sb("ARENA", [128, 16384], F32)
        t_ident = Tile("ident"); t_cst = Tile("cst")
        sch.dma("sp", [(ident[:], ident_d[:, :])], writes=[t_ident])
        sch.dma("sp", [(cst[:], cst_d[:, :])], writes=[t_cst])
        E1 = cst[:, 0:128]; E2 = cst[:, 128:256]; C3 = cst[:, 256:384]; C4 = cst[:, 384:512]
        CV = cst[:, 512:520]
        hT = ARENA[:, 0:9216].bitcast(BF).rearrange("p (k t) -> p k t", k=8)
        SPARE = ARENA[:, 9216:16384]
        XN = ARENA[:].rearrange("p (t d) -> p t d", t=NT)
        t_hT = Tile("hT")
        t_XN = [Tile("XN%d" % t) for t in range(NT)]
        t_OT = [Tile("OT%d" % j) for j in range(8)]
        t_modp = Tile("modp")

        with ExitStack() as ph:
            PS = ph.enter_context(nc.psum_tensor("psA", [128, 4096], F32))
            t_ps = [Tile("psA%d" % b, True) for b in range(8)]
            cc = sb("cc", [128, 8, 2], F32, ph)
            scs = sb("scs", [128, 8, 2], F32, ph)
            CR = sb("CR", [128, 16, 128], BF, ph)
            WA = [sb("WA%d" % i, [128, 8, 512], BF, ph) for i in range(4)]
            t_WA = [Tile("WA%d" % i) for i in range(4)]
            BB = [sb("BB%d" % i, [128, 512], F32, ph) for i in range(2)]
            t_BB = [Tile("BB%d" % i) for i in range(2)]
            MOD1 = sb("MOD1", [128, 2048], F32, ph)
            MODC = sb("MODC", [128, 2048], F32, ph)
            MT = [sb("MT%d" % i, [128, 512], F32, ph) for i in range(4)]
            t_MT = [Tile("MT%d" % i) for i in range(4)]
            G1B = sb("G1B", [128, 1024], F32, ph)
            S1 = sb("S1", [128, 1024], F32, ph)
            S1C = sb("S1C", [128, 1024], F32, ph)
            XT = [sb("XT%d" % i, [128, 1024], F32, ph) for i in range(4)]
            t_XT = [Tile("XT%d" % i) for i in range(4)]
            TMP = [sb("TMP%d" % i, [128, 1024], F32, ph) for i in range(2)]
            t_TMP = [Tile("TMP%d" % i) for i in range(2)]
            HB = [sb("HB%d" % i, [128, 1024], BF, ph) for i in range(2)]
            t_HB = [Tile("HB%d" % i) for i in range(2)]
            junk = sb("junk", [128, 1024], BF, ph)
            t_junk = Tile("junk")
            ss = sb("ss", [128, NTC], F32, ph)
            rs = sb("rs", [128, NTC], F32, ph)
            t_cc = Tile("cc"); t_scs = Tile("scs"); t_CR = Tile("CR")
            t_MOD1 = Tile("MOD1"); t_MODC = Tile("MODC"); t_G1B = Tile("G1B")
            t_S1 = Tile("S1"); t_S1C = Tile("S1C")
            t_ss = [Tile("ss%d" % t) for t in range(NTC)]
            t_rs = [Tile("rs%d" % t) for t in range(NTC)]
            t_ssall = Tile("ssall")

            sch.dma("sp", [(cc[:, :, 0], c_d.ap().rearrange("o (k p) -> p (o k)", p=128)),
                           (cc[:, :, 1], cctx_d.ap().rearrange("o (k p) -> p (o k)", p=128))],
                    writes=[t_cc], allow_slow_non_contiguous=True)
            sch.dma("sp", [(G1B[:], bc(g1_d, 0, 1024))], writes=[t_G1B])
            op("act", lambda e: e.activation(out=scs[:], in_=cc[:], func=AF.Silu), [t_cc], [t_scs])
            op("dve", lambda e: e.tensor_copy(out=CR[:], in_=scs[:].rearrange("p k v -> p (k v)").unsqueeze(2).to_broadcast([128, 16, 128])),
               [t_scs], [t_CR])
            op("dve", lambda e: e.memset(ss[:], 0.0), [], [t_ssall])
            CRv = CR[:].rearrange("p (k v) r -> p k v r", v=2)
            pbc = [0]
            modp_pend = []

            def ada_block(j):
                w = j % 4
                sch.dma("pool", [(WA[w][:], wada_d[:, j * 512:(j + 1) * 512].rearrange("(k p) n -> p k n", p=128))],
                        writes=[t_WA[w]])
                sch.dma("pool", [(BB[j % 2][:], bc(bada_d, j * 512, 512))], writes=[t_BB[j % 2]])
                while len(modp_pend) > 1:
                    modp_pend.pop(0)()
                for v in ((0, 1) if j < 4 else (0,)):
                    b = 2 + (pbc[0] % 6); pbc[0] += 1
                    for k in range(8):
                        op("pe", lambda e, k=k, v=v, b=b, w=w: e.matmul(PS[:, b * 512:(b + 1) * 512], CRv[:, k, v, :], WA[w][:, k, :],
                                                                        start=(k == 0), stop=(k == 7)),
                           [t_CR, t_WA[w]], [t_ps[b]])
                    if j < 4:
                        dst = (MOD1 if v == 0 else MODC)[:, j * 512:(j + 1) * 512]
                        td = t_MOD1 if v == 0 else t_MODC
                        op("dve", lambda e, b=b, dst=dst, j=j: e.tensor_tensor(out=dst, in0=PS[:, b * 512:(b + 1) * 512], in1=BB[j % 2][:], op=ALU.add),
                           [t_ps[b], t_BB[j % 2]], [td])
                    else:
                        m = j % 4
                        op("dve", lambda e, b=b, m=m, j=j: e.tensor_tensor(out=MT[m][:], in0=PS[:, b * 512:(b + 1) * 512], in1=BB[j % 2][:], op=ALU.add),
                           [t_ps[b], t_BB[j % 2]], [t_MT[m]])
                        modp_pend.append(lambda j=j, m=m: sch.dma("pool", [(modp_d.ap()[0:1, (j - 4) * 512:(j - 3) * 512], MT[m][0:1, :])],
                                                                  reads=[t_MT[m]], writes=[t_modp]))

            for j in range(4):
                ada_block(j)
            ada_rest = list(range(4, 12))
            op("dve", lambda e: e.scalar_tensor_tensor(out=S1[:], in0=MOD1[:, 1024:2048], scalar=1.0, in1=G1B[:], op0=ALU.add, op1=ALU.mult),
               [t_MOD1, t_G1B], [t_S1])
            op("dve", lambda e: e.scalar_tensor_tensor(out=S1C[:], in0=MODC[:, 1024:2048], scalar=1.0, in1=G1B[:], op0=ALU.add, op1=ALU.mult),
               [t_MODC, t_G1B], [t_S1C])

            def norm_a(t, src_ap, src_tiles, scale_ap, t_scale, shift_ap, t_shift):
                i2 = t % 2
                op("act", lambda e: e.activation(out=junk[:], in_=src_ap, func=AF.Square, accum_out=ss[:, t:t + 1]),
                   src_tiles + [t_ssall], [t_junk, t_ss[t]])
                op("act", lambda e: e.activation(out=rs[:, t:t + 1], in_=ss[:, t:t + 1], func=AF.Sqrt, scale=1.0 / D, bias=EPS),
                   [t_ss[t]], [t_rs[t]])
                op("dve", lambda e: e.reciprocal(out=rs[:, t:t + 1], in_=rs[:, t:t + 1]), [t_rs[t]], [t_rs[t]])
                op("dve", lambda e: e.scalar_tensor_tensor(out=TMP[i2][:], in0=src_ap, scalar=rs[:, t:t + 1], in1=scale_ap,
                                                           op0=ALU.mult, op1=ALU.mult),
                   src_tiles + [t_rs[t], t_scale], [t_TMP[i2]])
                op("dve", lambda e: e.tensor_tensor(out=HB[i2][:], in0=TMP[i2][:], in1=shift_ap, op=ALU.add),
                   [t_TMP[i2], t_shift], [t_HB[i2]])

            def norm_b(t, dstT, t_dst, pbank):
                i2 = t % 2
                pv = PS[:, pbank * 512:(pbank + 1) * 512].bitcast(BF)
                for k in range(8):
                    op("pe", lambda e, k=k: e.transpose(pv[:, k * 128:(k + 1) * 128], HB[i2][:, k * 128:(k + 1) * 128], ident[:]),
                       [t_HB[i2], t_ident], [t_ps[pbank]])
                op("act", lambda e: e.activation(out=dstT[:, :, t * 128:(t + 1) * 128], in_=pv.rearrange("p (k t) -> p k t", k=8), func=AF.Copy),
                   [t_ps[pbank]], [t_dst])

            for t in range(NTC + 1):
                if t < NTC:
                    i3 = t % 4
                    src = x_d[t * 128:(t + 1) * 128, :] if t < NT else ctx_d[(t - NT) * 128:(t - NT + 1) * 128, :]
                    sch.dma("sp", [(XT[i3][:], src)], writes=[t_XT[i3]])
                    if t < NT:
                        norm_a(t, XT[i3][:], [t_XT[i3]], S1[:], t_S1, MOD1[:, 0:1024], t_MOD1)
                    else:
                        norm_a(t, XT[i3][:], [t_XT[i3]], S1C[:], t_S1C, MODC[:, 0:1024], t_MODC)
                if t >= 1:
                    norm_b(t - 1, hT, t_hT, (t - 1) % 2)
                if ada_rest and t % 2 == 1:
                    ada_block(ada_rest.pop(0))
            while ada_rest:
                ada_block(ada_rest.pop(0))
            while modp_pend:
                modp_pend.pop(0)()
            dbg("hT", hT, [t_hT], [128, 8, SC], BF)
            sch.flush()
            if stop == "A":
                return nc, dbg_out

        with ExitStack() as ph:
            PS = ph.enter_context(nc.psum_tensor("psB", [128, 4096], F32))
            t_ps = [Tile("psB%d" % b, True) for b in range(8)]
            RT = SPARE[:, 0:2048].rearrange("p (t c f) -> p t c f", t=NT, c=2)
            VR = SPARE[:, 2048:2048 + 4608].bitcast(BF).rearrange("p (t n) -> p t n", t=NTC)
            t_RT = Tile("RT"); t_VR = Tile("VR")
            WB = [sb("WB%d" % i, [128, 8, 512], BF, ph) for i in range(2)]
            t_WB = [Tile("WB%d" % i) for i in range(2)]
            QT = sb("QT", [128, 4, S], BF, ph); t_QT = Tile("QT")
            KR = sb("KR", [128, NT, 512], BF, ph); t_KR = Tile("KR")
            KC = sb("KC", [128, 2, 2, 512], BF, ph); t_KC = Tile("KC")
            SBall = OT[:, 0:4, :].rearrange("p j (c n) -> p (j c) n", n=512)
            t_SBall = [Tile("SBall%d" % c) for c in range(NT)]
            RD = sb("RD", [128, 8], F32, ph); LG = sb("LG", [128, 8], F32, ph); GC = sb("GC", [128, 8], F32, ph)
            GCF = sb("GCF", [128, 2, 512], F32, ph)
            DTm = sb("DTm", [128, 512], BF, ph)
            XFB = sb("XFB", [128, 2, 512], BF, ph)
            ZZ = sb("ZZ", [128, 2, 4], F32, ph)
            ZZF = sb("ZZF", [128, 2, 512], F32, ph)
            WC = sb("WC", [128, 2, 2, 4], F32, ph)
            tmpd = sb("tmpd", [128, 128], F32, ph)
            GRB = sb("GRB", [128, 512], F32, ph)
            t_dec = Tile("dec"); t_tmpd = Tile("tmpd"); t_GRB = Tile("GRB")
            ROT = [sb("ROT%d" % i, [128, 2, 256], F32, ph) for i in range(2)]
            t_ROT = [Tile("ROT%d" % i) for i in range(2)]
            QR = [sb("QR%d" % i, [128, 512], BF, ph) for i in range(3)]
            t_QR = [Tile("QR%d" % i) for i in range(3)]
            SX = [sb("SX%d" % i, [128, 512], F32, ph) for i in range(3)]
            t_SX = [Tile("SX%d" % i) for i in range(3)]
            SXb = [0, 1]
            SXf = [2, 0]
            SFb = [sb("SFb%d" % i, [128, 512], BF, ph) for i in range(2)]
            t_SFb = [Tile("SFb%d" % i) for i in range(2)]
            NB = 3
            AD = [sb("AD%d" % i, [128, 512], BF, ph) for i in range(NB)]
            QF = [sb("QF%d" % i, [128, 512], BF, ph) for i in range(NB)]
            QB = [sb("QB%d" % i, [128, 512], BF, ph) for i in range(NB)]
            KFc = [sb("KFc%d" % i, [128, 512], BF, ph) for i in range(NB)]
            GS = [sb("GS%d" % i, [128, 512], BF, ph) for i in range(NB)]
            YN = [sb("YN%d" % i, [128, 512], F32, ph) for i in range(NB)]
            KTc = [sb("KTc%d" % i, [128, 512], BF, ph) for i in range(NB)]
            ORk = [sb("ORk%d" % i, [128, 512], BF, ph) for i in range(NB)]
            BST = [sb("BST%d" % i, [128, 4, 6], F32, ph) for i in range(NB)]
            MV = [sb("MV%d" % i, [128, 4, 2], F32, ph) for i in range(NB)]
            SD = [sb("SD%d" % i, [128, 4], F32, ph) for i in range(NB)]
            def tl(n):
                return [Tile("%s%d" % (n, i)) for i in range(NB)]
            t_AD, t_QF, t_QB, t_KFc, t_KBc, t_GS, t_YN, t_KTc, t_ORk, t_BST, t_MV, t_SD = [tl(n) for n in
                ("AD", "QF", "QB", "KFc", "KBc", "GS", "YN", "KTc", "ORk", "BST", "MV", "SD")]
            KBc = KFc; t_KBc = t_KFc

            sch.dma("sp", [(RT, rt_d[:, :].rearrange("p (t c f) -> p t c f", t=NT, c=2))], writes=[t_RT])
            sch.dma("sp", [(RD[:], bc(rdec_d, 0, 8))], writes=[t_dec])
            sch.dma("sp", [(GRB[:], bc(gret_d, 0, 512))], writes=[t_GRB])
            def decay_tables():
                op("act", lambda e: e.activation(out=LG[:], in_=RD[:], func=AF.Exp), [t_dec], [t_dec])
                op("dve", lambda e: e.tensor_scalar(out=LG[:], in0=LG[:], scalar1=-1.0, scalar2=None, op0=ALU.mult), [t_dec], [t_dec])
                op("act", lambda e: e.activation(out=GC[:], in_=LG[:], func=AF.Exp, scale=128.0), [t_dec], [t_dec])
                op("dve", lambda e: e.tensor_copy(out=GCF[:].rearrange("p d (h n) -> p (d h) n", h=4),
                                                  in_=GC[:].unsqueeze(2).to_broadcast([128, 8, 128])), [t_dec], [t_dec])
                for h in range(4):
                    op("dve", lambda e, h=h: e.tensor_scalar(out=tmpd[:], in0=E1, scalar1=LG[:, h:h + 1], scalar2=None, op0=ALU.mult),
                       [t_cst, t_dec], [t_tmpd])
                    op("dve", lambda e, h=h: e.scalar_tensor_tensor(out=tmpd[:], in0=E2, scalar=LG[:, 4 + h:5 + h], in1=tmpd[:], op0=ALU.mult, op1=ALU.add),
                       [t_cst, t_dec, t_tmpd], [t_tmpd])
                    op("act", lambda e, h=h: e.activation(out=DTm[:, h * 128:(h + 1) * 128], in_=tmpd[:], func=AF.Exp, bias=LN_S),
                       [t_tmpd], [t_dec])
                    op("act", lambda e, h=h: e.activation(out=XFB[:, 0, h * 128:(h + 1) * 128], in_=C3, func=AF.Exp, scale=LG[:, h:h + 1]),
                       [t_cst, t_dec], [t_dec])
                    op("act", lambda e, h=h: e.activation(out=XFB[:, 1, h * 128:(h + 1) * 128], in_=C4, func=AF.Exp, scale=LG[:, 4 + h:5 + h]),
                       [t_cst, t_dec], [t_dec])
                    for d in range(2):
                        op("act", lambda e, h=h, d=d: e.activation(out=ZZ[:, d, h:h + 1], in_=CV[:, d:d + 1], func=AF.Exp,
                                                                   scale=LG[:, 4 * d + h:4 * d + h + 1], bias=LN_S), [t_cst, t_dec], [t_dec])
                        for t in range(2):
                            op("act", lambda e, h=h, d=d, t=t: e.activation(out=WC[:, d, t, h:h + 1], in_=CV[:, 2 + 2 * d + t:3 + 2 * d + t], func=AF.Exp,
                                                                            scale=LG[:, 4 * d + h:4 * d + h + 1], bias=LN_S), [t_cst, t_dec], [t_dec])
                op("dve", lambda e: e.tensor_copy(out=ZZF[:].rearrange("p d (h n) -> p (d h) n", h=4),
                                                  in_=ZZ[:].rearrange("p d h -> p (d h)").unsqueeze(2).to_broadcast([128, 8, 128])), [t_dec], [t_dec])


            def load_wb(cols, wbi):
                sch.dma("pool", [(WB[wbi][:], win_d[:, cols:cols + 512].rearrange("(k p) n -> p k n", p=128))], writes=[t_WB[wbi]])

            def inproj(cols, wbi, tiles, post, lag=2, extra=None, prefetch=None):
                if prefetch is not None:
                    load_wb(*prefetch)
                pend = []
                for n, t in enumerate(tiles):
                    b = n % 4
                    for k in range(8):
                        op("pe", lambda e, k=k, b=b, t=t: e.matmul(PS[:, b * 512:(b + 1) * 512], hT[:, k, t * 128:(t + 1) * 128], WB[wbi][:, k, :],
                                                                   start=(k == 0), stop=(k == 7)),
                           [t_hT, t_WB[wbi]], [t_ps[b]])
                    pend.append(post(t, b, n))
                    if extra is not None:
                        extra(n)
                    if len(pend) > lag:
                        f = pend.pop(0)
                        if f is not None:
                            f()
                for f in pend:
                    if f is not None:
                        f()

            load_wb(1568, 0)
            load_wb(1056, 1)
            decay_tables()
            inproj(1568, 0, range(NTC),
                   lambda t, b, n: op("act", lambda e: e.activation(out=VR[:, t, :], in_=PS[:, b * 512:(b + 1) * 512], func=AF.Copy),
                                      [t_ps[b]], [t_VR]))
            if stop == "B1":
                sch.flush()
                return nc, dbg_out

            def rope_post(dst, t_dst, dstT, t_dstT):
                def post(t, b, n):
                    do_T = dstT is not None
                    i2 = n % 2
                    i4 = n % 3
                    pv = PS[:, b * 512:(b + 1) * 512].rearrange("p (h c f) -> p h c f", h=4, c=2)
                    x1 = pv[:, :, 0, :]; x2 = pv[:, :, 1, :]
                    cs = RT[:, t, 0, :].unsqueeze(1).to_broadcast([128, 4, 64])
                    sn = RT[:, t, 1, :].unsqueeze(1).to_broadcast([128, 4, 64])
                    ra = ROT[i2][:, 0, :].rearrange("p (h f) -> p h f", h=4)
                    rb = ROT[i2][:, 1, :].rearrange("p (h f) -> p h f", h=4)
                    o = dst(t, i4).rearrange("p (h c f) -> p h c f", h=4, c=2)
                    tds = t_dst(t, i4)
                    op("dve", lambda e: e.tensor_tensor(out=ra, in0=x1, in1=cs, op=ALU.mult), [t_ps[b], t_RT], [t_ROT[i2]])
                    op("dve", lambda e: e.tensor_tensor(out=rb, in0=x2, in1=sn, op=ALU.mult), [t_ps[b], t_RT], [t_ROT[i2]])
                    op("pool", lambda e: e.tensor_tensor(out=o[:, :, 0, :], in0=ra, in1=rb, op=ALU.subtract), [t_ROT[i2]], [tds])
                    i2b = i2
                    op("dve", lambda e: e.tensor_tensor(out=ra, in0=x1, in1=sn, op=ALU.mult), [t_ps[b], t_RT], [t_ROT[i2]])
                    op("dve", lambda e: e.tensor_tensor(out=rb, in0=x2, in1=cs, op=ALU.mult), [t_ps[b], t_RT], [t_ROT[i2]])
                    op("pool", lambda e: e.tensor_tensor(out=o[:, :, 1, :], in0=ra, in1=rb, op=ALU.add), [t_ROT[i2]], [tds])
                    if not do_T:
                        return None

                    def part2():
                        pb = 4 + (n % 2)
                        pvb = PS[:, pb * 512:pb * 512 + 256].bitcast(BF)
                        src = dst(t, i4)
                        for h in range(4):
                            op("pe", lambda e, h=h: e.transpose(pvb[:, h * 128:(h + 1) * 128], src[:, h * 128:(h + 1) * 128], ident[:]),
                               [tds, t_ident], [t_ps[pb]])
                        op("act", lambda e: e.activation(out=dstT[:, :, t * 128:(t + 1) * 128], in_=pvb.rearrange("p (h t) -> p h t", h=4), func=AF.Copy),
                           [t_ps[pb]], [t_dstT])
                    return part2
                return post

            if stop == "B3":
                sch.flush()
                return nc, dbg_out

            kpost = rope_post(lambda t, i2: KR[:, t, :], lambda t, i2: t_KR, None, None)

            def kpost_all(t, b, n):
                if t < NT:
                    return kpost(t, b, n)
                else:
                    tc_ = t - NT
                    for d in range(2):
                        op("dve", lambda e, d=d: e.tensor_tensor(out=KC[:, d, tc_, :].rearrange("p (h n) -> p h n", h=4),
                                                                 in0=PS[:, b * 512:(b + 1) * 512].rearrange("p (h n) -> p h n", h=4),
                                                                 in1=WC[:, d, tc_, :].unsqueeze(2).to_broadcast([128, 4, 128]), op=ALU.mult),
                           [t_ps[b], t_dec], [t_KC])
            inproj(1056, 1, range(NTC), kpost_all, prefetch=(544, 0))
            dbg("VR", VR, [t_VR], [128, NTC, 512], BF)
            dbg("QT", QT[:], [t_QT], [128, 4, S], BF)
            dbg("KR", KR[:], [t_KR], [128, NT, 512], BF)
            dbg("DTm", DTm[:], [t_dec], [128, 512], BF)
            if stop == "B4":
                sch.flush()
                return nc, dbg_out


            def hs(h):
                return slice(h * 128, (h + 1) * 128)

            import os
            KV = os.environ.get("KVAR", "")
            for d, (S32, t_S32) in enumerate(((SX[SXf[0]], t_SX[SXf[0]]), (SX[SXb[0]], t_SX[SXb[0]]))):
                if "noinit" in KV:
                    break
                if "init1" in KV and d == 1:
                    break
                for h in range(4):
                    for t in range(2):
                        op("pe", lambda e, d=d, h=h, t=t: e.matmul(PS[:, 6 * 512 + h * 128:6 * 512 + (h + 1) * 128], KC[:, d, t, hs(h)], VR[:, NT + t, hs(h)],
                                                                   start=(t == 0), stop=(t == 1)), [t_KC, t_VR], [t_ps[6]])
                if "nocopy" in KV:
                    continue
                op("dve", lambda e, S32=S32: e.tensor_copy(out=S32[:], in_=PS[:, 6 * 512:7 * 512]), [t_ps[6]], [t_S32])
                if "noact" in KV:
                    continue
                if d == 0:
                    op("act", lambda e: e.activation(out=SFb[0][:], in_=PS[:, 6 * 512:7 * 512], func=AF.Copy), [t_ps[6]], [t_SFb[0]])
                else:
                    op("act", lambda e: e.activation(out=SBall[:, NT - 1, :], in_=PS[:, 6 * 512:7 * 512], func=AF.Copy), [t_ps[6]], [t_SBall[NT - 1]])
            if stop == "B5a":
                sch.flush()
                return nc, dbg_out
            def bwd_step(c):
                i2 = c % NB
                op("pool", lambda e: e.tensor_tensor(out=KBc[i2][:], in0=KR[:, c, :], in1=ZZF[:, 1, :], op=ALU.mult),
                   [t_KR, t_dec], [t_KBc[i2]])
                pbk = 6 + (c % 2)
                for h in range(4):
                    op("pe", lambda e, h=h: e.matmul(PS[:, pbk * 512 + h * 128:pbk * 512 + (h + 1) * 128], KBc[i2][:, hs(h)], VR[:, c, hs(h)],
                                                     start=True, stop=True), [t_KBc[i2], t_VR], [t_ps[pbk]])
                k_ = NT - 1 - c
                src_ = SXb[k_ % 2]; dst_ = SXb[(k_ + 1) % 2]
                op("dve", lambda e: e.tensor_tensor(out=SX[dst_][:], in0=SX[src_][:], in1=GCF[:, 1, :], op=ALU.mult), [t_SX[src_], t_dec], [t_SX[dst_]])
                op("dve", lambda e: e.tensor_tensor(out=SX[dst_][:], in0=SX[dst_][:], in1=PS[:, pbk * 512:(pbk + 1) * 512], op=ALU.add),
                   [t_SX[dst_], t_ps[pbk]], [t_SX[dst_]])
                op("act", lambda e: e.activation(out=SBall[:, c - 1, :], in_=SX[dst_][:], func=AF.Copy), [t_SX[dst_]], [t_SBall[c - 1]])

            bsteps = list(range(NT - 1, 0, -1))
            inproj(544, 0, range(NT), rope_post(lambda t, i2: QR[i2][:], lambda t, i2: t_QR[i2], QT, t_QT), prefetch=(2080, 1))
            while bsteps:
                bwd_step(bsteps.pop(0))
            dbg("SBall", SBall[:], t_SBall, [128, NT, 512], BF)
            if stop == "B5b":
                sch.flush()
                return nc, dbg_out

            def stage1(c):
                i2 = c % NB
                cs_ = slice(c * 128, (c + 1) * 128)
                pa = c % 2
                pk_ = 6 + (c % 2)
                pkb = PS[:, pk_ * 512:pk_ * 512 + 256].bitcast(BF)
                for h in range(4):
                    op("pe", lambda e, h=h: e.transpose(pkb[:, h * 128:(h + 1) * 128], KR[:, c, h * 128:(h + 1) * 128], ident[:]), [t_KR, t_ident], [t_ps[pk_]])
                op("act", lambda e: e.activation(out=KTc[i2][:], in_=pkb, func=AF.Copy), [t_ps[pk_]], [t_KTc[i2]])
                for h in range(4):
                    op("pe", lambda e, h=h: e.matmul(PS[:, pa * 512 + h * 128:pa * 512 + (h + 1) * 128], KTc[i2][:, h * 128:(h + 1) * 128], QT[:, h, cs_], start=True, stop=True),
                       [t_KTc[i2], t_QT], [t_ps[pa]])
                op("dve", lambda e: e.tensor_tensor(out=AD[i2][:], in0=PS[:, pa * 512:(pa + 1) * 512], in1=DTm[:], op=ALU.mult),
                   [t_ps[pa], t_dec], [t_AD[i2]])
                op("pool", lambda e: e.tensor_tensor(out=QF[i2][:].rearrange("p (h n) -> p h n", h=4), in0=QT[:, :, cs_],
                                                     in1=XFB[:, 0, :].rearrange("p (h n) -> p h n", h=4), op=ALU.mult), [t_QT, t_dec], [t_QF[i2]])
                op("pool", lambda e: e.tensor_tensor(out=QB[i2][:].rearrange("p (h n) -> p h n", h=4), in0=QT[:, :, cs_],
                                                     in1=XFB[:, 1, :].rearrange("p (h n) -> p h n", h=4), op=ALU.mult), [t_QT, t_dec], [t_QB[i2]])
                op("pool", lambda e: e.tensor_tensor(out=KFc[i2][:], in0=KR[:, c, :], in1=ZZF[:, 0, :], op=ALU.mult),
                   [t_KR, t_dec], [t_KFc[i2]])
                pg = 2 + (c % 2)
                for k in range(8):
                    op("pe", lambda e, k=k: e.matmul(PS[:, pg * 512:(pg + 1) * 512], hT[:, k, cs_], WB[1][:, k, :], start=(k == 0), stop=(k == 7)),
                       [t_hT, t_WB[1]], [t_ps[pg]])
                op("act", lambda e: e.activation(out=GS[i2][:], in_=PS[:, pg * 512:(pg + 1) * 512], func=AF.Silu), [t_ps[pg]], [t_GS[i2]])

            def stage2(c):
                i2 = c % NB
                cs_ = slice(c * 128, (c + 1) * 128)
                py = 4 + (c % 2)
                for h in range(4):
                    o = PS[:, py * 512 + h * 128:py * 512 + (h + 1) * 128]
                    op("pe", lambda e, h=h, o=o: e.matmul(o, AD[i2][:, hs(h)], VR[:, c, hs(h)], start=True, stop=False), [t_AD[i2], t_VR], [t_ps[py]])
                    op("pe", lambda e, h=h, o=o: e.matmul(o, QF[i2][:, hs(h)], SFb[c % 2][:, hs(h)], start=False, stop=False), [t_QF[i2], t_SFb[c % 2]], [t_ps[py]])
                    op("pe", lambda e, h=h, o=o: e.matmul(o, QB[i2][:, hs(h)], SBall[:, c, hs(h)], start=False, stop=True), [t_QB[i2], t_SBall[c]], [t_ps[py]])
                if c < NT - 1:
                    pu = 6 + (c % 2)
                    for h in range(4):
                        op("pe", lambda e, h=h: e.matmul(PS[:, pu * 512 + h * 128:pu * 512 + (h + 1) * 128], KFc[i2][:, hs(h)], VR[:, c, hs(h)], start=True, stop=True),
                           [t_KFc[i2], t_VR], [t_ps[pu]])
                    src_ = SXf[c % 2]; dst_ = SXf[(c + 1) % 2]
                    op("dve", lambda e: e.tensor_tensor(out=SX[dst_][:], in0=SX[src_][:], in1=GCF[:, 0, :], op=ALU.mult), [t_SX[src_], t_dec], [t_SX[dst_]])
                    op("dve", lambda e: e.tensor_tensor(out=SX[dst_][:], in0=SX[dst_][:], in1=PS[:, pu * 512:(pu + 1) * 512], op=ALU.add), [t_SX[dst_], t_ps[pu]], [t_SX[dst_]])
                    op("act", lambda e: e.activation(out=SFb[(c + 1) % 2][:], in_=SX[dst_][:], func=AF.Copy), [t_SX[dst_]], [t_SFb[(c + 1) % 2]])
                for h in range(4):
                    op("dve", lambda e, h=h: e.bn_stats(out=BST[i2][:, h, :], in_=PS[:, py * 512 + h * 128:py * 512 + (h + 1) * 128]), [t_ps[py]], [t_BST[i2]])
                for h in range(4):
                    op("dve", lambda e, h=h: e.bn_aggr(out=MV[i2][:, h, :], in_=BST[i2][:, h, :]), [t_BST[i2]], [t_MV[i2]])
                op("act", lambda e: e.activation(out=SD[i2][:], in_=MV[i2][:, :, 1], func=AF.Sqrt, bias=EPS), [t_MV[i2]], [t_SD[i2]])
                op("dve", lambda e: e.reciprocal(out=SD[i2][:], in_=SD[i2][:]), [t_SD[i2]], [t_SD[i2]])
                pv = PS[:, py * 512:(py + 1) * 512].rearrange("p (h n) -> p h n", h=4)
                op("dve", lambda e: e.tensor_tensor(out=YN[i2][:].rearrange("p (h n) -> p h n", h=4), in0=pv,
                                                    in1=MV[i2][:, :, 0:1].to_broadcast([128, 4, 128]), op=ALU.subtract), [t_ps[py], t_MV[i2]], [t_YN[i2]])
                op("dve", lambda e: e.tensor_tensor(out=YN[i2][:].rearrange("p (h n) -> p h n", h=4), in0=YN[i2][:].rearrange("p (h n) -> p h n", h=4),
                                                    in1=SD[i2][:].unsqueeze(2).to_broadcast([128, 4, 128]), op=ALU.mult), [t_YN[i2], t_SD[i2]], [t_YN[i2]])

            def stage3(c):
                i2 = c % NB
                cs_ = slice(c * 128, (c + 1) * 128)
                op("pool", lambda e: e.tensor_tensor(out=YN[i2][:], in0=YN[i2][:], in1=GRB[:], op=ALU.mult), [t_YN[i2], t_GRB], [t_YN[i2]])
                op("pool", lambda e: e.tensor_tensor(out=ORk[i2][:], in0=YN[i2][:], in1=GS[i2][:], op=ALU.mult), [t_YN[i2], t_GS[i2]], [t_ORk[i2]])
                pt_ = 2 + (c % 2)
                pvb = PS[:, pt_ * 512:pt_ * 512 + 256].bitcast(BF)
                for h in range(4):
                    op("pe", lambda e, h=h: e.transpose(pvb[:, h * 128:(h + 1) * 128], ORk[i2][:, hs(h)], ident[:]), [t_ORk[i2], t_ident], [t_ps[pt_]])
                op("act", lambda e: e.activation(out=OT[:, 4:8, cs_], in_=pvb.rearrange("p (h t) -> p h t", h=4), func=AF.Copy),
                   [t_ps[pt_]], t_OT[4:8])

            for c in range(NT + 2):
                if c < NT:
                    stage1(c)
                if 1 <= c <= NT:
                    stage2(c - 1)
                if c >= 2:
                    stage3(c - 2)
            dbg("ORT", OT[:, 4:8, :], t_OT[4:8], [128, 4, S], BF)
            sch.flush()
            if stop == "B":
                return nc, dbg_out

        with ExitStack() as ph:
            PS = ph.enter_context(nc.psum_tensor("psC", [128, 4096], F32))
            t_ps = [Tile("psC%d" % b, True) for b in range(8)]
            VM = SPARE[:, 0:4608].bitcast(BF).rearrange("p (t h e) -> p t h e", t=NTC, h=8)
            t_VM = Tile("VM")
            WM = sb("WM", [128, 8, 512], BF, ph); t_WM = Tile("WM")
            WKP = sb("WKP", [128, 8, 2, 96], BF, ph); t_WKP = Tile("WKP")
            CQT = sb("CQT", [128, 2, S], BF, ph); t_CQT = Tile("CQT")
            CKT = sb("CKT", [128, 2, SC], BF, ph); t_CKT = Tile("CKT")
            WUQ = sb("WUQ", [128, 2, 8, 2, 96], BF, ph); t_WUQ = Tile("WUQ")
            WUKV = sb("WUKV", [128, 2, 8, 128], BF, ph); t_WUKV = Tile("WUKV")
            KPT = sb("KPT", [96, SC], BF, ph); t_KPT = Tile("KPT")
            MTab = sb("MTab", [96, 2, S], F32, ph); t_MTab = Tile("MTab")
            GQK = sb("GQK", [128, 4], F32, ph); t_GQK = Tile("GQK")
            QTm = [sb("QTm%d" % i, [96, S], BF, ph) for i in range(2)]
            KTm = [sb("KTm%d" % i, [96, SC], BF, ph) for i in range(2)]
            VA = [sb("VA%d" % i, [128, NTC, 128], BF, ph) for i in range(2)]
            t_QTm = [Tile("QTm%d" % i) for i in range(2)]
            t_KTm = [Tile("KTm%d" % i) for i in range(2)]
            t_VA = [Tile("VA%d" % i) for i in range(2)]
            PTb = [sb("PTb%d" % i, [128, 512], BF, ph) for i in range(4)]
            t_PTb = [Tile("PTb%d" % i) for i in range(4)]
            RDn = [sb("RDn%d" % i, [128, 512], F32, ph) for i in range(2)]
            t_RDn = [Tile("RDn%d" % i) for i in range(2)]
            T1 = [sb("T1_%d" % i, [96, 512], F32, ph) for i in range(2)]
            T2 = [sb("T2_%d" % i, [96, 512], F32, ph) for i in range(2)]
            t_T1 = [Tile("T1_%d" % i) for i in range(2)]
            t_T2 = [Tile("T2_%d" % i) for i in range(2)]
            CN = [sb("CN%d" % i, [128, 512], BF, ph) for i in range(4)]
            t_CN = [Tile("CN%d" % i) for i in range(4)]
            junk2 = sb("junk2", [128, 256], BF, ph); t_junk2 = Tile("junk2")
            ss2 = sb("ss2", [128, NTC, 2], F32, ph)
            t_ss2 = [Tile("ss2_%d" % t) for t in range(NTC)]
            t_ss2all = Tile("ss2all")

            sch.dma("pool", [(WM[:], win_d[:, 0:512].rearrange("(k p) n -> p k n", p=128))], writes=[t_WM])
            op("dve", lambda e: e.memset(WKP[:], 0.0), [], [t_WKP])
            op("dve", lambda e: e.memset(ss2[:], 0.0), [], [t_ss2all])
            sch.dma("pool", [(WKP[:, :, 0, 64:96], win_d[:, 512:544].rearrange("(k p) n -> p k n", p=128))], writes=[t_WKP])
            for a, b_ in ((64, 72), (72, 64), (80, 88), (88, 80)):
                op("dve", lambda e, a=a, b_=b_: e.tensor_copy(out=WKP[:, :, 1, a:a + 8], in_=WKP[:, :, 0, b_:b_ + 8]), [t_WKP], [t_WKP])
            sch.dma("pool", [(WUQ[:, r, :, 0, :], wuq_d[r * 128:(r + 1) * 128, :].rearrange("p (h e) -> p h e", h=8)) for r in range(2)], writes=[t_WUQ])
            op("pool", lambda e: e.tensor_copy(out=WUQ[:, :, :, 1, 0:64], in_=WUQ[:, :, :, 0, 0:64]), [t_WUQ], [t_WUQ])
            for a, b_ in ((64, 72), (72, 64), (80, 88), (88, 80)):
                op("pool", lambda e, a=a, b_=b_: e.tensor_copy(out=WUQ[:, :, :, 1, a:a + 8], in_=WUQ[:, :, :, 0, b_:b_ + 8]), [t_WUQ], [t_WUQ])
            sch.dma("pool", [(WUKV[:], wukv_d[:, :].rearrange("(r p) (h e) -> p r h e", p=128, h=8))], writes=[t_WUKV])
            sch.dma("sp", [(MTab[64:96, :, :], mt_d[:, :, :])], writes=[t_MTab])
            sch.dma("sp", [(GQK[:, 0:2], gq_d.ap().rearrange("o (r p) -> p (o r)", p=128)),
                           (GQK[:, 2:4], gkv_d.ap().rearrange("o (r p) -> p (o r)", p=128))], writes=[t_GQK], allow_slow_non_contiguous=True)
            for i in range(2):
                op("pool", lambda e, i=i: e.memset(VA[i][:, :, (64 - 64 * i):(128 - 64 * i)], 1.0), [], [t_VA[i]])

            if stop == "C0":
                sch.flush()
                return nc, dbg_out
            def c1_a(t):
                b = t % 3
                i4 = t % 4
                ts_ = slice(t * 128, (t + 1) * 128)
                pb_ = PS[:, b * 512:(b + 1) * 512]
                for k in range(8):
                    op("pe", lambda e, k=k: e.matmul(pb_, hT[:, k, ts_], WM[:, k, :], start=(k == 0), stop=(k == 7)), [t_hT, t_WM], [t_ps[b]])
                for g in range(2):
                    op("act", lambda e, g=g: e.activation(out=junk2[:], in_=pb_[:, g * 256:(g + 1) * 256], func=AF.Square, accum_out=ss2[:, t, g:g + 1]),
                       [t_ps[b], t_ss2all], [t_junk2, t_ss2[t]])
                op("act", lambda e: e.activation(out=ss2[:, t, :], in_=ss2[:, t, :], func=AF.Sqrt, scale=1.0 / 256, bias=EPS), [t_ss2[t]], [t_ss2[t]])
                op("dve", lambda e: e.reciprocal(out=ss2[:, t, :], in_=ss2[:, t, :]), [t_ss2[t]], [t_ss2[t]])
                op("dve", lambda e: e.tensor_tensor(out=CN[i4][:].rearrange("p (g n) -> p g n", g=2), in0=pb_.rearrange("p (g n) -> p g n", g=2),
                                                    in1=ss2[:, t, :].unsqueeze(2).to_broadcast([128, 2, 256]), op=ALU.mult),
                   [t_ps[b], t_ss2[t]], [t_CN[i4]])

            def c1_b(t):
                i4 = t % 4
                ts_ = slice(t * 128, (t + 1) * 128)
                pt_ = 3 + (t % 2)
                pvb = PS[:, pt_ * 512:pt_ * 512 + 256].bitcast(BF)
                for j in range(4):
                    op("pe", lambda e, j=j: e.transpose(pvb[:, j * 128:(j + 1) * 128], CN[i4][:, j * 128:(j + 1) * 128], ident[:]),
                       [t_CN[i4], t_ident], [t_ps[pt_]])
                pv3 = pvb.rearrange("p (j t) -> p j t", j=4)
                if t < NT:
                    op("dve", lambda e: e.tensor_tensor(out=CQT[:, :, ts_], in0=pv3[:, 0:2, :], in1=GQK[:, 0:2].unsqueeze(2).to_broadcast([128, 2, 128]), op=ALU.mult),
                       [t_ps[pt_], t_GQK], [t_CQT])
                op("dve", lambda e: e.tensor_tensor(out=CKT[:, :, ts_], in0=pv3[:, 2:4, :], in1=GQK[:, 2:4].unsqueeze(2).to_broadcast([128, 2, 128]), op=ALU.mult),
                   [t_ps[pt_], t_GQK], [t_CKT])

            for t in range(NTC + 2):
                if t < NTC:
                    c1_a(t)
                if t >= 2:
                    c1_b(t - 2)

            if stop == "C1":
                sch.flush()
                return nc, dbg_out
            def rope_rows(psA, psB, tA, tB, cols, dst, t_dst, n):
                i2 = n % 2
                op("dve", lambda e: e.tensor_tensor(out=T1[i2][64:96, :], in0=psA[64:96, :], in1=MTab[64:96, 0, cols], op=ALU.mult), [tA, t_MTab], [t_T1[i2]])
                op("dve", lambda e: e.tensor_tensor(out=T2[i2][64:96, :], in0=psB[64:96, :], in1=MTab[64:96, 1, cols], op=ALU.mult), [tB, t_MTab], [t_T2[i2]])
                op("pool", lambda e: e.tensor_tensor(out=dst, in0=T1[i2][64:96, :], in1=T2[i2][64:96, :], op=ALU.add), [t_T1[i2], t_T2[i2]], [t_dst])

            for blk in range(5):
                cols = slice(blk * 512, blk * 512 + (512 if blk < 4 else 256))
                ncol = 512 if blk < 4 else 256
                ba = 5 + (blk % 2) * 0
                pA = PS[0:96, 5 * 512:5 * 512 + ncol]; pB = PS[0:96, 6 * 512:6 * 512 + ncol]
                for k in range(8):
                    op("pe", lambda e, k=k, pA=pA, cols=cols: e.matmul(pA, WKP[:, k, 0, :], hT[:, k, cols], start=(k == 0), stop=(k == 7)), [t_WKP, t_hT], [t_ps[5]])
                if blk < 4:
                    for k in range(8):
                        op("pe", lambda e, k=k, pB=pB, cols=cols: e.matmul(pB, WKP[:, k, 1, :], hT[:, k, cols], start=(k == 0), stop=(k == 7)), [t_WKP, t_hT], [t_ps[6]])
                    rope_rows(pA, pB, t_ps[5], t_ps[6], cols, KPT[64:96, cols], t_KPT, blk)
                else:
                    op("act", lambda e, pA=pA, cols=cols: e.activation(out=KPT[64:96, cols], in_=pA[64:96, :], func=AF.Copy), [t_ps[5]], [t_KPT])
            if stop == "C1k":
                sch.flush()
                return nc, dbg_out
            NPRE = 9
            for t in range(NPRE):
                sch.dma("sp", [(XN[:, t, :], x_d[t * 128:(t + 1) * 128, :])], writes=[t_hT, t_XN[t]])
            for t in range(NTC):
                b = t % 3
                ts_ = slice(t * 128, (t + 1) * 128)
                pb_ = PS[:, b * 512:(b + 1) * 512]
                for r in range(2):
                    op("pe", lambda e, r=r, pb_=pb_, ts_=ts_: e.matmul(pb_.rearrange("p (h e) -> p h e", h=8), CKT[:, r, ts_], WUKV[:, r, :, 64:128], start=(r == 0), stop=(r == 1)),
                       [t_CKT, t_WUKV], [t_ps[b]])
                op("act", lambda e, pb_=pb_, t=t: e.activation(out=VM[:, t, :, :], in_=pb_.rearrange("p (h e) -> p h e", h=8), func=AF.Copy), [t_ps[b]], [t_VM])
            dbg("CQT", CQT[:], [t_CQT], [128, 2, S], BF)
            dbg("CKT", CKT[:], [t_CKT], [128, 2, SC], BF)
            dbg("KPT", KPT[64:96, :], [t_KPT], [32, SC], BF)
            dbg("VM", VM, [t_VM], [128, NTC, 8, 64], BF)

            if stop == "C1v":
                sch.flush()
                return nc, dbg_out
            def proj_units(h):
                i2 = h % 2
                units = []

                def q_unit(blk):
                    cols = slice(blk * 512, (blk + 1) * 512)
                    pA = PS[0:96, 5 * 512:6 * 512]; pB = PS[0:96, 6 * 512:7 * 512]
                    for v, (pp, tp) in enumerate(((pA, t_ps[5]), (pB, t_ps[6]))):
                        for r in range(2):
                            op("pe", lambda e, v=v, r=r, pp=pp: e.matmul(pp, WUQ[:, r, h, v, :], CQT[:, r, cols], start=(r == 0), stop=(r == 1)),
                               [t_WUQ, t_CQT], [tp])
                    op("dve", lambda e: e.tensor_copy(out=QTm[i2][0:64, cols], in_=pA[0:64, :]), [t_ps[5]], [t_QTm[i2]])
                    rope_rows(pA, pB, t_ps[5], t_ps[6], cols, QTm[i2][64:96, cols], t_QTm[i2], blk)

                def k_unit(blk):
                    ncol = 512 if blk < 4 else 256
                    cols = slice(blk * 512, blk * 512 + ncol)
                    pk = PS[0:64, 7 * 512:7 * 512 + ncol]
                    for r in range(2):
                        op("pe", lambda e, r=r: e.matmul(pk, WUKV[:, r, h, 0:64], CKT[:, r, cols], start=(r == 0), stop=(r == 1)), [t_WUKV, t_CKT], [t_ps[7]])
                    op("dve", lambda e: e.tensor_copy(out=KTm[i2][0:64, cols], in_=pk), [t_ps[7]], [t_KTm[i2]])

                def v_unit():
                    op("pool", lambda e: e.tensor_copy(out=KTm[i2][64:96, :], in_=KPT[64:96, :]), [t_KPT], [t_KTm[i2]])
                    op("pool", lambda e: e.tensor_copy(out=VA[i2][:, :, 64 * i2:64 * i2 + 64], in_=VM[:, :, h, :]), [t_VM], [t_VA[i2]])

                units.append(v_unit)
                for blk in range(5):
                    units.append(lambda blk=blk: k_unit(blk))
                for blk in range(4):
                    units.append(lambda blk=blk: q_unit(blk))
                return units

            def proj(h):
                for u in proj_units(h):
                    u()

            items = [(h, qb, kt) for h in range(8) for qb in range(4) for kt in range(NTC)]
            LAG = 2
            NPT = 4

            def emit_S(n):
                h, qb, kt = items[n]
                i2 = h % 2
                b = n % 3
                pS_ = PS[:, b * 512:(b + 1) * 512]
                op("pe", lambda e: e.matmul(pS_, KTm[i2][0:96, kt * 128:(kt + 1) * 128], QTm[i2][0:96, qb * 512:(qb + 1) * 512], start=True, stop=True),
                   [t_KTm[i2], t_QTm[i2]], [t_ps[b]])
                op("act", lambda e: e.activation(out=PTb[n % NPT][:], in_=pS_, func=AF.Exp, scale=MLA_SCALE), [t_ps[b]], [t_PTb[n % NPT]])

            def emit_PV(n):
                h, qb, kt = items[n]
                i2 = h % 2
                po = 3 + (qb % 2)
                pO = PS[:, po * 512:(po + 1) * 512]
                qs = slice(qb * 512, (qb + 1) * 512)
                op("pe", lambda e: e.matmul(pO, VA[i2][:, kt, :], PTb[n % NPT][:], start=(kt == 0), stop=(kt == NTC - 1)), [t_VA[i2], t_PTb[n % NPT]], [t_ps[po]])
                if kt != NTC - 1:
                    return
                r2 = qb % 2
                ro = (h % 2) * 64
                dn = 64 - ro
                op("dve", lambda e: e.tensor_copy(out=RDn[r2][dn:dn + 64, :], in_=pO[dn:dn + 64, :]), [t_ps[po]], [t_RDn[r2]])
                op("dve", lambda e: e.reciprocal(out=RDn[r2][dn:dn + 64, :], in_=RDn[r2][dn:dn + 64, :]), [t_RDn[r2]], [t_RDn[r2]])
                if ro == 64:
                    op("dve", lambda e: e.tensor_copy(out=RDn[r2][64:128, :], in_=RDn[r2][0:64, :]), [t_RDn[r2]], [t_RDn[r2]])
                    dn = 64
                op("dve", lambda e: e.tensor_tensor(out=OT[ro:ro + 64, h // 2, qs], in0=pO[ro:ro + 64, :], in1=RDn[r2][dn:dn + 64, :], op=ALU.mult),
                   [t_ps[po], t_RDn[r2]], [t_OT[h // 2]])

            proj(0)
            if "QTm0" in debug:
                dbg("QTm0", QTm[0][:], [t_QTm[0]], [96, S], BF)
                dbg("KTm0", KTm[0][:], [t_KTm[0]], [96, SC], BF)
            if stop == "C2":
                sch.flush()
                return nc, dbg_out
            pending = []
            for n in range(len(items) + LAG):
                if n < len(items):
                    emit_S(n)
                if n - LAG >= 0:
                    emit_PV(n - LAG)
                    h, qb, kt = items[n - LAG]
                    if qb == 0 and kt == 0 and h + 1 < 8:
                        assert not pending
                        pending = proj_units(h + 1)
                    if pending and (n % 6 == 0):
                        pending.pop(0)()
            dbg("OMT", OT[:, 0:4, :], t_OT[0:4], [128, 4, S], BF)
            sch.flush()
            if stop == "C":
                return nc, dbg_out

        with ExitStack() as ph:
            PS = ph.enter_context(nc.psum_tensor("psD", [128, 4096], F32))
            t_ps = [Tile("psD%d" % b, True) for b in range(8)]
            WO = sb("WO", [128, 8, D], BF, ph); t_WO = Tile("WO")
            GT1 = sb("GT1", [128, D], F32, ph); t_GT1 = Tile("GT1")
            XT = [sb("XTd%d" % i, [128, D], F32, ph) for i in range(3)]
            t_XT = [Tile("XTd%d" % i) for i in range(3)]
            TMP = [sb("TMPd%d" % i, [128, D], F32, ph) for i in range(2)]
            t_TMP = [Tile("TMPd%d" % i) for i in range(2)]
            t_WOh = [Tile("WO0"), Tile("WO1")]
            for hf in range(2):
                sch.dma("pool", [(WO[:, :, hf * 512:(hf + 1) * 512], wout_d[:, hf * 512:(hf + 1) * 512].rearrange("(j p) n -> p j n", p=128))], writes=[t_WOh[hf]])
            sch.dma("sp", [(GT1[:], bc(modp_d, 0, D))], reads=[t_modp], writes=[t_GT1])
            for t in range(NT):
                ts_ = slice(t * 128, (t + 1) * 128)
                i3 = t % 3; i2 = t % 2
                pre = t < 9
                if not pre:
                    sch.dma("sp", [(XT[i3][:], x_d[ts_, :])], writes=[t_XT[i3]])
                b0 = (t % 4) * 2
                for hf in range(2):
                    for j in range(8):
                        op("pe", lambda e, hf=hf, j=j, b0=b0, ts_=ts_: e.matmul(PS[:, (b0 + hf) * 512:(b0 + hf + 1) * 512], OT[:, j, ts_], WO[:, j, hf * 512:(hf + 1) * 512],
                                                                                start=(j == 0), stop=(j == 7)), [t_OT[j], t_WOh[hf]], [t_ps[b0 + hf]])
                op("dve", lambda e, b0=b0, i2=i2: e.tensor_tensor(out=TMP[i2][:], in0=PS[:, b0 * 512:(b0 + 2) * 512], in1=GT1[:], op=ALU.mult),
                   [t_ps[b0], t_ps[b0 + 1], t_GT1], [t_TMP[i2]])
                if pre:
                    op("pool", lambda e, t=t, i2=i2: e.tensor_tensor(out=XN[:, t, :], in0=TMP[i2][:], in1=XN[:, t, :], op=ALU.add),
                       [t_TMP[i2], t_XN[t]], [t_XN[t]])
                else:
                    op("pool", lambda e, t=t, i2=i2, i3=i3: e.tensor_tensor(out=XN[:, t, :], in0=TMP[i2][:], in1=XT[i3][:], op=ALU.add),
                       [t_TMP[i2], t_XT[i3]] + list(t_OT) + [t_hT], [t_XN[t]])
            dbg("XN", XN, t_XN, [128, NT, D], F32)
            sch.flush()
            if stop == "D":
                return nc, dbg_out

        H2T = OT
        t_H2T = Tile("H2T")
        with ExitStack() as ph:
            PS = ph.enter_context(nc.psum_tensor("psE", [128, 4096], F32))
            t_ps = [Tile("psE%d" % b, True) for b in range(8)]
            CW = sb("CW", [128, 44, 3], F32, ph); CB = sb("CB", [128, 44], F32, ph)
            t_CW = Tile("CW")
            GT2 = sb("GT2", [128, D], F32, ph); t_GT2 = Tile("GT2")
            GFB = sb("GFB", [128, D], F32, ph); t_GFB = Tile("GFB")
            def late_loads():
                sch.dma("sp", [(GT2[:], bc(modp_d, 3 * D, D))], reads=[t_modp], writes=[t_GT2])
                sch.dma("sp", [(GFB[:], bc(gf_d, 0, D))], writes=[t_GFB])
                sch.dma("sp", [(CW[:, :, t], cw_d.ap()[t:t + 1, :].rearrange("o (c p) -> p (o c)", p=128)) for t in range(3)] +
                        [(CB[:], cb_d.ap().rearrange("o (c p) -> p (o c)", p=128))], writes=[t_CW], allow_slow_non_contiguous=True)
            ss = sb("ssE", [128, 2 * NT], F32, ph)
            rs = sb("rsE", [128, 2 * NT], F32, ph)
            t_ss = [Tile("ssE%d" % t) for t in range(2 * NT)]
            t_rs = [Tile("rsE%d" % t) for t in range(2 * NT)]
            t_ssall = Tile("ssEall")
            op("dve", lambda e: e.memset(ss[:], 0.0), [], [t_ssall])
            t_junk = Tile("junkE")
            TMP = [sb("TMPe%d" % i, [128, D], F32, ph) for i in range(2)]
            t_TMP = [Tile("TMPe%d" % i) for i in range(2)]
            WU = [sb("WU%d" % i, [128, 8, 2, 128], BF, ph) for i in range(3)]
            t_WU = [Tile("WU%d" % i) for i in range(3)]
            WD = [sb("WD%d" % i, [128, GMAX, D], BF, ph) for i in range(1)]
            t_WD = [Tile("WD%d" % i) for i in range(1)]

            def load_wu(j):
                w = j % 3
                sch.dma("pool", [(WU[w][:, :, 0, :], wup_d[:, j * 128:(j + 1) * 128].rearrange("(k p) n -> p k n", p=128)),
                                 (WU[w][:, :, 1, :], wup_d[:, DFF + j * 128:DFF + (j + 1) * 128].rearrange("(k p) n -> p k n", p=128))], writes=[t_WU[w]])

            def load_wd(gi):
                j0, j1 = GROUPS[gi]
                sch.dma("pool", [(WD[0][:, 0:j1 - j0, :], wdn_d[j0 * 128:j1 * 128, :].rearrange("(j p) n -> p j n", p=128))], writes=[t_WD[0]])

            load_wu(0); load_wu(1); load_wd(0)
            with ExitStack() as ph0:
                junk = sb("junkE", [128, D], BF, ph0)
                S2 = sb("S2", [128, D], F32, ph0); SH2 = sb("SH2", [128, D], F32, ph0); G2B = sb("G2B", [128, D], F32, ph0)
                HB = [sb("HBe%d" % i, [128, D], BF, ph0) for i in range(2)]
                t_HB = [Tile("HBe%d" % i) for i in range(2)]
                t_S2 = Tile("S2"); t_SH2 = Tile("SH2"); t_G2B = Tile("G2B")
                sch.dma("sp", [(SH2[:], bc(modp_d, D, D))], reads=[t_modp], writes=[t_SH2])
                sch.dma("sp", [(S2[:], bc(modp_d, 2 * D, D))], reads=[t_modp], writes=[t_S2])
                sch.dma("sp", [(G2B[:], bc(g2_d, 0, D))], writes=[t_G2B])
                late_loads()
                op("dve", lambda e: e.scalar_tensor_tensor(out=S2[:], in0=S2[:], scalar=1.0, in1=G2B[:], op0=ALU.add, op1=ALU.mult), [t_S2, t_G2B], [t_S2])
                def e0_a(t):
                    i2 = t % 2
                    src = XN[:, t, :]
                    op("act", lambda e: e.activation(out=junk[:], in_=src, func=AF.Square, accum_out=ss[:, t:t + 1]), [t_XN[t], t_ssall], [t_junk, t_ss[t]])
                    op("act", lambda e: e.activation(out=rs[:, t:t + 1], in_=ss[:, t:t + 1], func=AF.Sqrt, scale=1.0 / D, bias=EPS), [t_ss[t]], [t_rs[t]])
                    op("dve", lambda e: e.reciprocal(out=rs[:, t:t + 1], in_=rs[:, t:t + 1]), [t_rs[t]], [t_rs[t]])
                    op("dve", lambda e: e.scalar_tensor_tensor(out=TMP[i2][:], in0=src, scalar=rs[:, t:t + 1], in1=S2[:], op0=ALU.mult, op1=ALU.mult),
                       [t_XN[t], t_rs[t], t_S2], [t_TMP[i2]])
                    op("dve", lambda e: e.tensor_tensor(out=HB[i2][:], in0=TMP[i2][:], in1=SH2[:], op=ALU.add), [t_TMP[i2], t_SH2], [t_HB[i2]])

                def e0_b(t):
                    i2 = t % 2
                    pbk = t % 2
                    pv = PS[:, pbk * 512:(pbk + 1) * 512].bitcast(BF)
                    for k in range(8):
                        op("pe", lambda e, k=k: e.transpose(pv[:, k * 128:(k + 1) * 128], HB[i2][:, k * 128:(k + 1) * 128], ident[:]), [t_HB[i2], t_ident], [t_ps[pbk]])
                    op("act", lambda e: e.activation(out=H2T[:, :, t * 128:(t + 1) * 128], in_=pv.rearrange("p (k t) -> p k t", k=8), func=AF.Copy),
                       [t_ps[pbk]], [t_H2T])

                for t in range(NT + 1):
                    if t < NT:
                        e0_a(t)
                    if t >= 1:
                        e0_b(t - 1)
                dbg("H2T", H2T[:], [t_H2T], [128, 8, S], BF)
                sch.flush()
            with ExitStack() as ph1:
                ACTT = sb("ACTT", [128, GMAX, S], BF, ph1)
                t_ACTT = [Tile("ACTT%d" % i) for i in range(GMAX)]
                AaL = [sb("Aa%d" % i, [128, S], F32, ph1) for i in range(2)]
                GgL = [sb("Gg%d" % i, [128, S], F32, ph1) for i in range(2)]
                t_AaL = [(Tile("Aa%da" % i), Tile("Aa%db" % i)) for i in range(2)]; t_GgL = [(Tile("Gg%da" % i), Tile("Gg%db" % i)) for i in range(2)]
                t_out = Tile("out")
                npair = [0]

                for gi, (j0, j1) in enumerate(GROUPS):
                    ng = j1 - j0
                    wd = 0
                    if gi > 0:
                        load_wd(gi)
                    for j in range(j0, j1):
                        jj = j - j0
                        w = j % 3
                        if j + 2 < NCH:
                            load_wu(j + 2)
                        Aa = AaL[j % 2]; Gg = GgL[j % 2]; t_Aa = t_AaL[j % 2]; t_Gg = t_GgL[j % 2]
                        for half, (ACC, t_ACC, pb0) in enumerate(((Aa, t_Aa, 0), (Gg, t_Gg, 4))):
                            c = half * NCH + j
                            pall = PS[:, pb0 * 512:(pb0 + 4) * 512]
                            for blk in range(4):
                                for k in range(8):
                                    op("pe", lambda e, k=k, blk=blk, half=half, pb0=pb0, w=w: e.matmul(PS[:, (pb0 + blk) * 512:(pb0 + blk + 1) * 512], WU[w][:, k, half, :],
                                                                                                      H2T[:, k, blk * 512:(blk + 1) * 512], start=(k == 0), stop=(k == 7)),
                                       [t_WU[w], t_H2T], [t_ps[pb0 + blk]])
                            H = S // 2
                            tA = t_ps[pb0:pb0 + 2]; tB = t_ps[pb0 + 2:pb0 + 4]
                            tacc = t_ACC
                            op("act", lambda e, ACC=ACC, pall=pall, c=c: e.activation(out=ACC[:, 0:H], in_=pall[:, 0:H], func=AF.Identity, scale=CW[:, c, 1:2], bias=CB[:, c:c + 1]),
                               tA + [t_CW], [tacc[0]])
                            op("act", lambda e, ACC=ACC, pall=pall, c=c: e.activation(out=ACC[:, H:S], in_=pall[:, H:S], func=AF.Identity, scale=CW[:, c, 1:2], bias=CB[:, c:c + 1]),
                               tB + [t_CW], [tacc[1]])
                            op("dve", lambda e, ACC=ACC, pall=pall, c=c: e.scalar_tensor_tensor(out=ACC[:, 1:H], in0=pall[:, 0:H - 1], scalar=CW[:, c, 0:1], in1=ACC[:, 1:H],
                                                                                                op0=ALU.mult, op1=ALU.add), tA + [t_CW, tacc[0]], [tacc[0]])
                            op("dve", lambda e, ACC=ACC, pall=pall, c=c: e.scalar_tensor_tensor(out=ACC[:, H:S], in0=pall[:, H - 1:S - 1], scalar=CW[:, c, 0:1], in1=ACC[:, H:S],
                                                                                                op0=ALU.mult, op1=ALU.add), tA[1:2] + tB + [t_CW, tacc[1]], [tacc[1]])
                            op("dve", lambda e, ACC=ACC, pall=pall, c=c: e.scalar_tensor_tensor(out=ACC[:, 0:H], in0=pall[:, 1:H + 1], scalar=CW[:, c, 2:3], in1=ACC[:, 0:H],
                                                                                                op0=ALU.mult, op1=ALU.add), tA + tB[0:1] + [t_CW, tacc[0]], [tacc[0]])
                            op("dve", lambda e, ACC=ACC, pall=pall, c=c: e.scalar_tensor_tensor(out=ACC[:, H:S - 1], in0=pall[:, H + 1:S], scalar=CW[:, c, 2:3], in1=ACC[:, H:S - 1],
                                                                                                op0=ALU.mult, op1=ALU.add), tB + [t_CW, tacc[1]], [tacc[1]])
                        op("act", lambda e: e.activation(out=Gg[:], in_=Gg[:], func=AF.Silu), list(t_Gg), list(t_Gg))
                        op("pool", lambda e, jj=jj: e.tensor_tensor(out=ACTT[:, jj, :], in0=Aa[:], in1=Gg[:], op=ALU.mult), list(t_Aa) + list(t_Gg), [t_ACTT[jj]])
                    last = gi == len(GROUPS) - 1
                    for t in range(NT):
                        ts_ = slice(t * 128, (t + 1) * 128)
                        b0 = (t % 4) * 2
                        i2 = t % 2
                        for hf in range(2):
                            for jj in range(ng):
                                op("pe", lambda e, hf=hf, jj=jj, b0=b0, ts_=ts_, wd=wd, ng=ng: e.matmul(PS[:, (b0 + hf) * 512:(b0 + hf + 1) * 512], ACTT[:, jj, ts_],
                                                                                                        WD[wd][:, jj, hf * 512:(hf + 1) * 512], start=(jj == 0), stop=(jj == ng - 1)),
                                   [t_ACTT[jj], t_WD[wd]], [t_ps[b0 + hf]])
                        op("dve", lambda e, b0=b0, i2=i2: e.tensor_tensor(out=TMP[i2][:], in0=PS[:, b0 * 512:(b0 + 2) * 512], in1=GT2[:], op=ALU.mult),
                           [t_ps[b0], t_ps[b0 + 1], t_GT2], [t_TMP[i2]])
                        op("pool", lambda e, t=t, i2=i2: e.tensor_tensor(out=XN[:, t, :], in0=XN[:, t, :], in1=TMP[i2][:], op=ALU.add), [t_TMP[i2], t_XN[t]], [t_XN[t]])
                        if last:
                            u = NT + t
                            src = XN[:, t, :]
                            jv = TMP[i2][:, 0:D // 2].bitcast(BF)
                            op("act", lambda e, src=src, u=u, jv=jv: e.activation(out=jv, in_=src, func=AF.Square, accum_out=ss[:, u:u + 1]), [t_XN[t], t_ssall], [t_TMP[i2], t_ss[u]])
                            op("act", lambda e, u=u: e.activation(out=rs[:, u:u + 1], in_=ss[:, u:u + 1], func=AF.Sqrt, scale=1.0 / D, bias=EPS), [t_ss[u]], [t_rs[u]])
                            op("dve", lambda e, u=u: e.reciprocal(out=rs[:, u:u + 1], in_=rs[:, u:u + 1]), [t_rs[u]], [t_rs[u]])
                            op("dve", lambda e, src=src, u=u: e.scalar_tensor_tensor(out=src, in0=src, scalar=rs[:, u:u + 1], in1=GFB[:], op0=ALU.mult, op1=ALU.mult),
                               [t_XN[t], t_rs[u], t_GFB], [t_XN[t]])
                            sch.dma("sp", [(out_d[ts_, :], src)], reads=[t_XN[t]], writes=[t_out])
                sch.wait_tiles("sp", [t_out])
                sch.flush()
    return nc, dbg_out


_CACHE = {}


def _prep_inputs(inputs):
    c = _consts()
    f = lambda a: np.ascontiguousarray(np.asarray(a, dtype=np.float32))
    shared = {
        "c_ctx": f(inputs["c_ctx"]).reshape(1, D),
        "w_ada": f(inputs["w_ada"])[0],
        "b_ada": f(inputs["b_ada"]).reshape(1, 6 * D),
        "g_norm1": f(inputs["g_norm1"]).reshape(1, D),
        "w_in": f(inputs["w_in"])[0],
        "g_q": f(inputs["g_q"]).reshape(1, 256),
        "w_uq": f(inputs["w_uq"])[0],
        "g_kv": f(inputs["g_kv"]).reshape(1, 256),
        "w_ukv": f(inputs["w_ukv"])[0],
        "ret_decay": f(inputs["ret_decay"]).reshape(1, 8),
        "g_ret": f(inputs["g_ret"]).reshape(1, 512),
        "w_out": f(inputs["w_out"])[0],
        "g_norm2": f(inputs["g_norm2"]).reshape(1, D),
        "w_up": f(inputs["w_up"])[0],
        "conv_w": f(inputs["conv_w"])[0],
        "conv_b": f(inputs["conv_b"]).reshape(1, 2 * DFF),
        "w_down": f(inputs["w_down"])[0],
        "g_final": f(inputs["g_final"]).reshape(1, D),
        "k_ident": c["ident"], "k_rt": c["rt"], "k_mt": c["mt"], "k_cst": c["cst"],
    }
    x = f(inputs["x"]); cc = f(inputs["c"]); ctx = f(inputs["ctx"])
    maps = []
    for b in range(8):
        m = dict(shared)
        m["x"] = x[b]
        m["c"] = cc[b].reshape(1, D)
        m["ctx"] = ctx[b]
        maps.append(m)
    return maps


def kernel(**inputs):
    if "nc" not in _CACHE:
        _CACHE["nc"] = build()[0]
    nc = _CACHE["nc"]
    maps = _prep_inputs(inputs)
    res = run_bass_kernel_spmd(nc, maps, core_ids=list(range(8)))
    out = np.stack([np.asarray(r["out"], dtype=np.float32) for r in res.results], axis=0)
    return out
```

```python
import math
import os
import numpy as np
import ml_dtypes
import concourse.bass as bass
import concourse.mybir as mybir
from concourse.bass_utils import run_bass_kernel_spmd

F32 = mybir.dt.float32
BF = mybir.dt.bfloat16
AF = mybir.ActivationFunctionType
ALU = mybir.AluOpType

S = 2048
D = 1024
L = 256
NT = 16
NTC = 18
SC = S + L
DFF = 2816
NCH = 22
EPS = 1e-6
MLA_SCALE = 96 ** -0.5
LN_S = math.log(128 ** -0.5)
GROUPS = [(0, 8), (8, 15), (15, 22)]
GMAX = 8


class Tile:
    __slots__ = ("name", "w", "r", "excl")

    def __init__(self, name, excl=False):
        self.name = name
        self.w = None
        self.r = {}
        self.excl = excl


class _Rec:
    def __getattr__(self, name):
        def call(*a, **k):
            self.__dict__["call"] = (name, a, k)
            return self
        return call


class Sched:
    ENG = ("pe", "act", "dve", "pool", "sp")

    def __init__(self, nc, ndma=48):
        self.nc = nc
        self.ops = {e: [] for e in self.ENG}
        self.cnt = {e: 0 for e in self.ENG}
        self.seen = {e: {} for e in self.ENG}
        self.sems = {}
        self.ndma = ndma
        self.dcnt = [0] * ndma
        self.rr = {"sp": 0, "pool": 0}
        self.half = ndma // 2
        self.stack = None

    def open(self, stack):
        for e in self.ENG:
            self.sems[e] = stack.enter_context(self.nc.semaphore("s_" + e))
        for j in range(self.ndma):
            self.sems[("d", j)] = stack.enter_context(self.nc.semaphore("s_d%d" % j))

    def _deps(self, eng, reads, writes):
        deps = []
        writes = list(writes) + [t for t in reads if t.excl and t not in writes]
        for t in reads:
            if t.w is not None:
                deps.append(t.w)
        for t in writes:
            if t.w is not None:
                deps.append(t.w)
            deps.extend(t.r.items())
        waits = []
        seen = self.seen[eng]
        for k, v in deps:
            if eng == "pe" and k == "pe":
                continue
            if seen.get(k, 0) >= v:
                continue
            seen[k] = v
            waits.append((k, v))
        m = {}
        for k, v in waits:
            m[k] = max(m.get(k, 0), v)
        return list(m.items())

    def _mark(self, ev, reads, writes):
        k, v = ev
        writes = list(writes) + [t for t in reads if t.excl and t not in writes]
        for t in reads:
            if t.r.get(k, 0) < v:
                t.r[k] = v
        for t in writes:
            t.w = ev
            t.r = {}

    def op(self, eng, fn, reads=(), writes=()):
        rec = _Rec()
        fn(rec)
        name, a, k = rec.call
        fn = lambda e, name=name, a=a, k=k: getattr(e, name)(*a, **k)
        waits = self._deps(eng, reads, writes)
        self.cnt[eng] += 1
        ev = (eng, self.cnt[eng])
        self.ops[eng].append((waits, fn, (eng, 1)))
        self._mark(ev, reads, writes)

    def dma(self, q, pairs, reads=(), writes=(), **kw):
        j = self.rr[q] + (0 if q == "sp" else self.half)
        self.rr[q] = (self.rr[q] + 1) % self.half
        key = ("d", j)
        waits = self._deps(q, reads, writes)
        seen = self.seen[q]
        if self.dcnt[j] > 0 and seen.get(key, 0) < self.dcnt[j]:
            seen[key] = self.dcnt[j]
            waits.append((key, self.dcnt[j]))
        for i, (o, i_) in enumerate(pairs):
            def fn(e, o=o, i_=i_):
                return e.dma_start(out=o, in_=i_, **kw)
            self.ops[q].append((waits if i == 0 else [], fn, (key, 16)))
            self.dcnt[j] += 16
        ev = (key, self.dcnt[j])
        self._mark(ev, reads, writes)

    def wait_tiles(self, eng, tiles):
        waits = self._deps(eng, tiles, tiles)
        if waits:
            self.cnt[eng] += 1
            self.ops[eng].append((waits, lambda e: e.nop(), (eng, 1)))

    def drain_dmas(self):
        waits = []
        seen = self.seen["sp"]
        for j in range(self.ndma):
            key = ("d", j)
            if self.dcnt[j] > 0 and seen.get(key, 0) < self.dcnt[j]:
                seen[key] = self.dcnt[j]
                waits.append((key, self.dcnt[j]))
        if waits:
            self.cnt["sp"] += 1
            self.ops["sp"].append((waits, lambda e: e.nop(), ("sp", 1)))

    def flush(self):
        self.drain_dmas()
        nc = self.nc
        if os.environ.get("KSBUF"):
            print("SBUF remaining at flush:", nc.sbuf_bytes_remaining)
        sems = self.sems
        ops = self.ops

        def replay(name):
            def run(e):
                for waits, fn, inc in ops[name]:
                    for k, v in waits:
                        e.wait_ge(sems[k], v)
                    ins = fn(e)
                    ins.then_inc(sems[inc[0]], inc[1])
            return run

        with nc.Block() as block:
            block.tensor(replay("pe"))
            block.scalar(replay("act"))
            block.vector(replay("dve"))
            block.gpsimd(replay("pool"))
            block.sync(replay("sp"))
        self.ops = {e: [] for e in self.ENG}


def _consts():
    c = {}
    c["ident"] = np.eye(128, dtype=np.float32).astype(ml_dtypes.bfloat16)
    pos = np.arange(S, dtype=np.float64)
    inv = 10000.0 ** (-np.arange(0, 128, 2, dtype=np.float64) / 128.0)
    ang = pos[:, None] * inv[None, :]
    rt = np.stack([np.cos(ang), np.sin(ang)], axis=1)
    c["rt"] = np.ascontiguousarray(rt.reshape(NT, 128, 2, 64).transpose(1, 0, 2, 3)).reshape(128, NT * 128).astype(np.float32)
    inv8 = 10000.0 ** (-np.arange(0, 16, 2, dtype=np.float64) / 16.0)
    prow = (np.arange(S) // 64).astype(np.float64)
    pcol = (np.arange(S) % 64).astype(np.float64)
    ct = np.zeros((32, S)); st = np.zeros((32, S))
    for r in range(32):
        p = prow if r < 16 else pcol
        a = p * inv8[r % 8]
        ct[r] = np.cos(a)
        st[r] = -np.sin(a) if (r % 16) < 8 else np.sin(a)
    c["mt"] = np.stack([ct, st], axis=1).astype(np.float32)
    i = np.arange(128, dtype=np.float64)
    e1 = np.maximum(i[None, :] - i[:, None], 0.0)
    e2 = np.maximum(i[:, None] - i[None, :], 0.0)
    c3 = np.tile((i + 1.0)[None, :], (128, 1))
    c4 = np.tile((128.0 - i)[None, :], (128, 1))
    cv = np.zeros((128, 8))
    cv[:, 0] = 127.0 - i
    cv[:, 1] = i
    cv[:, 2] = 255.0 - i
    cv[:, 3] = 127.0 - i
    cv[:, 4] = i
    cv[:, 5] = 128.0 + i
    c["cst"] = np.concatenate([e1, e2, c3, c4, cv], axis=1).astype(np.float32)
    return c


def build(debug=(), stop=None):
    from contextlib import ExitStack
    nc = bass.Bass("TRN2", target_bir_lowering=False)
    dbg_out = {}

    def dram_in(name, shape, dt=F32):
        return nc.dram_tensor(name, list(shape), dt, kind="ExternalInput")

    x_d = dram_in("x", [S, D]).ap()
    c_d = dram_in("c", [1, D])
    ctx_d = dram_in("ctx", [L, D]).ap()
    cctx_d = dram_in("c_ctx", [1, D])
    wada_d = dram_in("w_ada", [D, 6 * D]).ap()
    bada_d = dram_in("b_ada", [1, 6 * D])
    g1_d = dram_in("g_norm1", [1, D])
    win_d = dram_in("w_in", [D, 2592]).ap()
    gq_d = dram_in("g_q", [1, 256])
    wuq_d = dram_in("w_uq", [256, 768]).ap()
    gkv_d = dram_in("g_kv", [1, 256])
    wukv_d = dram_in("w_ukv", [256, 1024]).ap()
    rdec_d = dram_in("ret_decay", [1, 8])
    gret_d = dram_in("g_ret", [1, 512])
    wout_d = dram_in("w_out", [D, D]).ap()
    g2_d = dram_in("g_norm2", [1, D])
    wup_d = dram_in("w_up", [D, 2 * DFF]).ap()
    cw_d = dram_in("conv_w", [3, 2 * DFF])
    cb_d = dram_in("conv_b", [1, 2 * DFF])
    wdn_d = dram_in("w_down", [DFF, D]).ap()
    gf_d = dram_in("g_final", [1, D])
    ident_d = dram_in("k_ident", [128, 128], BF).ap()
    rt_d = dram_in("k_rt", [128, NT * 128]).ap()
    mt_d = dram_in("k_mt", [32, 2, S]).ap()
    cst_d = dram_in("k_cst", [128, 520]).ap()
    out_d = nc.dram_tensor("out", [S, D], F32, kind="ExternalOutput").ap()
    modp_d = nc.dram_tensor("modp_scratch", [1, 4 * D], F32, kind="Internal")

    def bc(t, off, n, parts=128):
        return bass.AP(t, off, [[0, parts], [1, n]])

    es = ExitStack()
    with es:
        sch = Sched(nc)
        sch.open(es)
        op = sch.op

        def sb(name, shape, dt, stack=es):
            return stack.enter_context(nc.sbuf_tensor(name, list(shape), dt))

        def dbg(name, ap, tiles, shape, dt=F32):
            if name not in debug:
                return
            dd = nc.dram_tensor("dbg_" + name, list(shape), dt, kind="ExternalOutput").ap()
            dbg_out[name] = dd
            t = Tile("dbg_" + name)
            sch.dma("sp", [(dd, ap)], reads=tiles, writes=[t])
            sch.wait_tiles("sp", [t])

        ident = sb("ident", [128, 128], BF)
        cst = sb("cst", [128, 520], F32)
        OT = sb("OT", [128, 8, S], BF)
        ARENA = sb("ARENA", [128, 16384], F32)
        t_ident = Tile("ident"); t_cst = Tile("cst")
        sch.dma("sp", [(ident[:], ident_d[:, :])], writes=[t_ident])
        sch.dma("sp", [(cst[:], cst_d[:, :])], writes=[t_cst])
        E1 = cst[:, 0:128]; E2 = cst[:, 128:256]; C3 = cst[:, 256:384]; C4 = cst[:, 384:512]
        CV = cst[:, 512:520]
        hT = ARENA[:, 0:9216].bitcast(BF).rearrange("p (k t) -> p k t", k=8)
        SPARE = ARENA[:, 9216:16384]
        XN = ARENA[:].rearrange("p (t d) -> p t d", t=NT)
        t_hT = Tile("hT")
        t_XN = [Tile("XN%d" % t) for t in range(NT)]
        t_OT = [Tile("OT%d" % j) for j in range(8)]
        t_modp = Tile("modp")

        with ExitStack() as ph:
            PS = ph.enter_context(nc.psum_tensor("psA", [128, 4096], F32))
            t_ps = [Tile("psA%d" % b, True) for b in range(8)]
            cc = sb("cc", [128, 8, 2], F32, ph)
            scs = sb("scs", [128, 8, 2], F32, ph)
            CR = sb("CR", [128, 16, 128], BF, ph)
            WA = [sb("WA%d" % i, [128, 8, 512], BF, ph) for i in range(4)]
            t_WA = [Tile("WA%d" % i) for i in range(4)]
            BB = [sb("BB%d" % i, [128, 512], F32, ph) for i in range(2)]
            t_BB = [Tile("BB%d" % i) for i in range(2)]
            MOD1 = sb("MOD1", [128, 2048], F32, ph)
            MODC = sb("MODC", [128, 2048], F32, ph)
            MT = [sb("MT%d" % i, [128, 512], F32, ph) for i in range(4)]
            t_MT = [Tile("MT%d" % i) for i in range(4)]
            G1B = sb("G1B", [128, 1024], F32, ph)
            S1 = sb("S1", [128, 1024], F32, ph)
            S1C = sb("S1C", [128, 1024], F32, ph)
            XT = [sb("XT%d" % i, [128, 1024], F32, ph) for i in range(4)]
            t_XT = [Tile("XT%d" % i) for i in range(4)]
            TMP = [sb("TMP%d" % i, [128, 1024], F32, ph) for i in range(2)]
            t_TMP = [Tile("TMP%d" % i) for i in range(2)]
            HB = [sb("HB%d" % i, [128, 1024], BF, ph) for i in range(2)]
            t_HB = [Tile("HB%d" % i) for i in range(2)]
            junk = sb("junk", [128, 1024], BF, ph)
            t_junk = Tile("junk")
            ss = sb("ss", [128, NTC], F32, ph)
            rs = sb("rs", [128, NTC], F32, ph)
            t_cc = Tile("cc"); t_scs = Tile("scs"); t_CR = Tile("CR")
            t_MOD1 = Tile("MOD1"); t_MODC = Tile("MODC"); t_G1B = Tile("G1B")
            t_S1 = Tile("S1"); t_S1C = Tile("S1C")
            t_ss = [Tile("ss%d" % t) for t in range(NTC)]
            t_rs = [Tile("rs%d" % t) for t in range(NTC)]
            t_ssall = Tile("ssall")

            sch.dma("sp", [(cc[:, :, 0], c_d.ap().rearrange("o (k p) -> p (o k)", p=128)),
                           (cc[:, :, 1], cctx_d.ap().rearrange("o (k p) -> p (o k)", p=128))],
                    writes=[t_cc], allow_slow_non_contiguous=True)
            sch.dma("sp", [(G1B[:], bc(g1_d, 0, 1024))], writes=[t_G1B])
            op("act", lambda e: e.activation(out=scs[:], in_=cc[:], func=AF.Silu), [t_cc], [t_scs])
            op("dve", lambda e: e.tensor_copy(out=CR[:], in_=scs[:].rearrange("p k v -> p (k v)").unsqueeze(2).to_broadcast([128, 16, 128])),
               [t_scs], [t_CR])
            op("dve", lambda e: e.memset(ss[:], 0.0), [], [t_ssall])
            CRv = CR[:].rearrange("p (k v) r -> p k v r", v=2)
            pbc = [0]
            modp_pend = []

            def ada_block(j):
                w = j % 4
                sch.dma("pool", [(WA[w][:], wada_d[:, j * 512:(j + 1) * 512].rearrange("(k p) n -> p k n", p=128))],
                        writes=[t_WA[w]])
                sch.dma("pool", [(BB[j % 2][:], bc(bada_d, j * 512, 512))], writes=[t_BB[j % 2]])
                while len(modp_pend) > 1:
                    modp_pend.pop(0)()
                for v in ((0, 1) if j < 4 else (0,)):
                    b = 2 + (pbc[0] % 6); pbc[0] += 1
                    for k in range(8):
                        op("pe", lambda e, k=k, v=v, b=b, w=w: e.matmul(PS[:, b * 512:(b + 1) * 512], CRv[:, k, v, :], WA[w][:, k, :],
                                                                        start=(k == 0), stop=(k == 7)),
                           [t_CR, t_WA[w]], [t_ps[b]])
                    if j < 4:
                        dst = (MOD1 if v == 0 else MODC)[:, j * 512:(j + 1) * 512]
                        td = t_MOD1 if v == 0 else t_MODC
                        op("dve", lambda e, b=b, dst=dst, j=j: e.tensor_tensor(out=dst, in0=PS[:, b * 512:(b + 1) * 512], in1=BB[j % 2][:], op=ALU.add),
                           [t_ps[b], t_BB[j % 2]], [td])
                    else:
                        m = j % 4
                        op("dve", lambda e, b=b, m=m, j=j: e.tensor_tensor(out=MT[m][:], in0=PS[:, b * 512:(b + 1) * 512], in1=BB[j % 2][:], op=ALU.add),
                           [t_ps[b], t_BB[j % 2]], [t_MT[m]])
                        modp_pend.append(lambda j=j, m=m: sch.dma("pool", [(modp_d.ap()[0:1, (j - 4) * 512:(j - 3) * 512], MT[m][0:1, :])],
                                                                  reads=[t_MT[m]], writes=[t_modp]))

            for j in range(4):
                ada_block(j)
            ada_rest = list(range(4, 12))
            op("dve", lambda e: e.scalar_tensor_tensor(out=S1[:], in0=MOD1[:, 1024:2048], scalar=1.0, in1=G1B[:], op0=ALU.add, op1=ALU.mult),
               [t_MOD1, t_G1B], [t_S1])
            op("dve", lambda e: e.scalar_tensor_tensor(out=S1C[:], in0=MODC[:, 1024:2048], scalar=1.0, in1=G1B[:], op0=ALU.add, op1=ALU.mult),
               [t_MODC, t_G1B], [t_S1C])

            def norm_a(t, src_ap, src_tiles, scale_ap, t_scale, shift_ap, t_shift):
                i2 = t % 2
                op("act", lambda e: e.activation(out=junk[:], in_=src_ap, func=AF.Square, accum_out=ss[:, t:t + 1]),
                   src_tiles + [t_ssall], [t_junk, t_ss[t]])
                op("act", lambda e: e.activation(out=rs[:, t:t + 1], in_=ss[:, t:t + 1], func=AF.Sqrt, scale=1.0 / D, bias=EPS),
                   [t_ss[t]], [t_rs[t]])
                op("dve", lambda e: e.reciprocal(out=rs[:, t:t + 1], in_=rs[:, t:t + 1]), [t_rs[t]], [t_rs[t]])
                op("dve", lambda e: e.scalar_tensor_tensor(out=TMP[i2][:], in0=src_ap, scalar=rs[:, t:t + 1], in1=scale_ap,
                                                           op0=ALU.mult, op1=ALU.mult),
                   src_tiles + [t_rs[t], t_scale], [t_TMP[i2]])
                op("dve", lambda e: e.tensor_tensor(out=HB[i2][:], in0=TMP[i2][:], in1=shift_ap, op=ALU.add),
                   [t_TMP[i2], t_shift], [t_HB[i2]])

            def norm_b(t, dstT, t_dst, pbank):
                i2 = t % 2
                pv = PS[:, pbank * 512:(pbank + 1) * 512].bitcast(BF)
                for k in range(8):
                    op("pe", lambda e, k=k: e.transpose(pv[:, k * 128:(k + 1) * 128], HB[i2][:, k * 128:(k + 1) * 128], ident[:]),
                       [t_HB[i2], t_ident], [t_ps[pbank]])
                op("act", lambda e: e.activation(out=dstT[:, :, t * 128:(t + 1) * 128], in_=pv.rearrange("p (k t) -> p k t", k=8), func=AF.Copy),
                   [t_ps[pbank]], [t_dst])

            for t in range(NTC + 1):
                if t < NTC:
                    i3 = t % 4
                    src = x_d[t * 128:(t + 1) * 128, :] if t < NT else ctx_d[(t - NT) * 128:(t - NT + 1) * 128, :]
                    sch.dma("sp", [(XT[i3][:], src)], writes=[t_XT[i3]])
                    if t < NT:
                        norm_a(t, XT[i3][:], [t_XT[i3]], S1[:], t_S1, MOD1[:, 0:1024], t_MOD1)
                    else:
                        norm_a(t, XT[i3][:], [t_XT[i3]], S1C[:], t_S1C, MODC[:, 0:1024], t_MODC)
                if t >= 1:
                    norm_b(t - 1, hT, t_hT, (t - 1) % 2)
                if ada_rest and t % 2 == 1:
                    ada_block(ada_rest.pop(0))
            while ada_rest:
                ada_block(ada_rest.pop(0))
            while modp_pend:
                modp_pend.pop(0)()
            dbg("hT", hT, [t_hT], [128, 8, SC], BF)
            sch.flush()
            if stop == "A":
                return nc, dbg_out

        with ExitStack() as ph:
            PS = ph.enter_context(nc.psum_tensor("psB", [128, 4096], F32))
            t_ps = [Tile("psB%d" % b, True) for b in range(8)]
            RT = SPARE[:, 0:2048].rearrange("p (t c f) -> p t c f", t=NT, c=2)
            VR = SPARE[:, 2048:2048 + 4608].bitcast(BF).rearrange("p (t n) -> p t n", t=NTC)
            t_RT = Tile("RT"); t_VR = Tile("VR")
            WB = [sb("WB%d" % i, [128, 8, 512], BF, ph) for i in range(2)]
            t_WB = [Tile("WB%d" % i) for i in range(2)]
            QT = sb("QT", [128, 4, S], BF, ph); t_QT = Tile("QT")
            KR = sb("KR", [128, NT, 512], BF, ph); t_KR = Tile("KR")
            KC = sb("KC", [128, 2, 2, 512], BF, ph); t_KC = Tile("KC")
            SBall = OT[:, 0:4, :].rearrange("p j (c n) -> p (j c) n", n=512)
            t_SBall = [Tile("SBall%d" % c) for c in range(NT)]
            RD = sb("RD", [128, 8], F32, ph); LG = sb("LG", [128, 8], F32, ph); GC = sb("GC", [128, 8], F32, ph)
            GCF = sb("GCF", [128, 2, 512], F32, ph)
            DTm = sb("DTm", [128, 512], BF, ph)
            XFB = sb("XFB", [128, 2, 512], BF, ph)
            ZZ = sb("ZZ", [128, 2, 4], F32, ph)
            ZZF = sb("ZZF", [128, 2, 512], F32, ph)
            WC = sb("WC", [128, 2, 2, 4], F32, ph)
            tmpd = sb("tmpd", [128, 128], F32, ph)
            GRB = sb("GRB", [128, 512], F32, ph)
            t_dec = Tile("dec"); t_tmpd = Tile("tmpd"); t_GRB = Tile("GRB")
            ROT = [sb("ROT%d" % i, [128, 2, 256], F32, ph) for i in range(2)]
            t_ROT = [Tile("ROT%d" % i) for i in range(2)]
            QR = [sb("QR%d" % i, [128, 512], BF, ph) for i in range(3)]
            t_QR = [Tile("QR%d" % i) for i in range(3)]
            SX = [sb("SX%d" % i, [128, 512], F32, ph) for i in range(3)]
            t_SX = [Tile("SX%d" % i) for i in range(3)]
            SXb = [0, 1]
            SXf = [2, 0]
            SFb = [sb("SFb%d" % i, [128, 512], BF, ph) for i in range(2)]
            t_SFb = [Tile("SFb%d" % i) for i in range(2)]
            NB = 3
            AD = [sb("AD%d" % i, [128, 512], BF, ph) for i in range(NB)]
            QF = [sb("QF%d" % i, [128, 512], BF, ph) for i in range(NB)]
            QB = [sb("QB%d" % i, [128, 512], BF, ph) for i in range(NB)]
            KFc = [sb("KFc%d" % i, [128, 512], BF, ph) for i in range(NB)]
            GS = [sb("GS%d" % i, [128, 512], BF, ph) for i in range(NB)]
            YN = [sb("YN%d" % i, [128, 512], F32, ph) for i in range(NB)]
            KTc = [sb("KTc%d" % i, [128, 512], BF, ph) for i in range(NB)]
            ORk = [sb("ORk%d" % i, [128, 512], BF, ph) for i in range(NB)]
            BST = [sb("BST%d" % i, [128, 4, 6], F32, ph) for i in range(NB)]
            MV = [sb("MV%d" % i, [128, 4, 2], F32, ph) for i in range(NB)]
            SD = [sb("SD%d" % i, [128, 4], F32, ph) for i in range(NB)]
            def tl(n):
                return [Tile("%s%d" % (n, i)) for i in range(NB)]
            t_AD, t_QF, t_QB, t_KFc, t_KBc, t_GS, t_YN, t_KTc, t_ORk, t_BST, t_MV, t_SD = [tl(n) for n in
                ("AD", "QF", "QB", "KFc", "KBc", "GS", "YN", "KTc", "ORk", "BST", "MV", "SD")]
            KBc = KFc; t_KBc = t_KFc

            sch.dma("sp", [(RT, rt_d[:, :].rearrange("p (t c f) -> p t c f", t=NT, c=2))], writes=[t_RT])
            sch.dma("sp", [(RD[:], bc(rdec_d, 0, 8))], writes=[t_dec])
            sch.dma("sp", [(GRB[:], bc(gret_d, 0, 512))], writes=[t_GRB])
            def decay_tables():
                op("act", lambda e: e.activation(out=LG[:], in_=RD[:], func=AF.Exp), [t_dec], [t_dec])
                op("dve", lambda e: e.tensor_scalar(out=LG[:], in0=LG[:], scalar1=-1.0, scalar2=None, op0=ALU.mult), [t_dec], [t_dec])
                op("act", lambda e: e.activation(out=GC[:], in_=LG[:], func=AF.Exp, scale=128.0), [t_dec], [t_dec])
                op("dve", lambda e: e.tensor_copy(out=GCF[:].rearrange("p d (h n) -> p (d h) n", h=4),
                                                  in_=GC[:].unsqueeze(2).to_broadcast([128, 8, 128])), [t_dec], [t_dec])
                for h in range(4):
                    op("dve", lambda e, h=h: e.tensor_scalar(out=tmpd[:], in0=E1, scalar1=LG[:, h:h + 1], scalar2=None, op0=ALU.mult),
                       [t_cst, t_dec], [t_tmpd])
                    op("dve", lambda e, h=h: e.scalar_tensor_tensor(out=tmpd[:], in0=E2, scalar=LG[:, 4 + h:5 + h], in1=tmpd[:], op0=ALU.mult, op1=ALU.add),
                       [t_cst, t_dec, t_tmpd], [t_tmpd])
                    op("act", lambda e, h=h: e.activation(out=DTm[:, h * 128:(h + 1) * 128], in_=tmpd[:], func=AF.Exp, bias=LN_S),
                       [t_tmpd], [t_dec])
                    op("act", lambda e, h=h: e.activation(out=XFB[:, 0, h * 128:(h + 1) * 128], in_=C3, func=AF.Exp, scale=LG[:, h:h + 1]),
                       [t_cst, t_dec], [t_dec])
                    op("act", lambda e, h=h: e.activation(out=XFB[:, 1, h * 128:(h + 1) * 128], in_=C4, func=AF.Exp, scale=LG[:, 4 + h:5 + h]),
                       [t_cst, t_dec], [t_dec])
                    for d in range(2):
                        op("act", lambda e, h=h, d=d: e.activation(out=ZZ[:, d, h:h + 1], in_=CV[:, d:d + 1], func=AF.Exp,
                                                                   scale=LG[:, 4 * d + h:4 * d + h + 1], bias=LN_S), [t_cst, t_dec], [t_dec])
                        for t in range(2):
                            op("act", lambda e, h=h, d=d, t=t: e.activation(out=WC[:, d, t, h:h + 1], in_=CV[:, 2 + 2 * d + t:3 + 2 * d + t], func=AF.Exp,
                                                                            scale=LG[:, 4 * d + h:4 * d + h + 1], bias=LN_S), [t_cst, t_dec], [t_dec])
                op("dve", lambda e: e.tensor_copy(out=ZZF[:].rearrange("p d (h n) -> p (d h) n", h=4),
                                                  in_=ZZ[:].rearrange("p d h -> p (d h)").unsqueeze(2).to_broadcast([128, 8, 128])), [t_dec], [t_dec])


            def load_wb(cols, wbi):
                sch.dma("pool", [(WB[wbi][:], win_d[:, cols:cols + 512].rearrange("(k p) n -> p k n", p=128))], writes=[t_WB[wbi]])

            def inproj(cols, wbi, tiles, post, lag=2, extra=None, prefetch=None):
                if prefetch is not None:
                    load_wb(*prefetch)
                pend = []
                for n, t in enumerate(tiles):
                    b = n % 4
                    for k in range(8):
                        op("pe", lambda e, k=k, b=b, t=t: e.matmul(PS[:, b * 512:(b + 1) * 512], hT[:, k, t * 128:(t + 1) * 128], WB[wbi][:, k, :],
                                                                   start=(k == 0), stop=(k == 7)),
                           [t_hT, t_WB[wbi]], [t_ps[b]])
                    pend.append(post(t, b, n))
                    if extra is not None:
                        extra(n)
                    if len(pend) > lag:
                        f = pend.pop(0)
                        if f is not None:
                            f()
                for f in pend:
                    if f is not None:
                        f()

            load_wb(1568, 0)
            load_wb(1056, 1)
            decay_tables()
            inproj(1568, 0, range(NTC),
                   lambda t, b, n: op("act", lambda e: e.activation(out=VR[:, t, :], in_=PS[:, b * 512:(b + 1) * 512], func=AF.Copy),
                                      [t_ps[b]], [t_VR]))
            if stop == "B1":
                sch.flush()
                return nc, dbg_out

            def rope_post(dst, t_dst, dstT, t_dstT):
                def post(t, b, n):
                    do_T = dstT is not None
                    i2 = n % 2
                    i4 = n % 3
                    pv = PS[:, b * 512:(b + 1) * 512].rearrange("p (h c f) -> p h c f", h=4, c=2)
                    x1 = pv[:, :, 0, :]; x2 = pv[:, :, 1, :]
                    cs = RT[:, t, 0, :].unsqueeze(1).to_broadcast([128, 4, 64])
                    sn = RT[:, t, 1, :].unsqueeze(1).to_broadcast([128, 4, 64])
                    ra = ROT[i2][:, 0, :].rearrange("p (h f) -> p h f", h=4)
                    rb = ROT[i2][:, 1, :].rearrange("p (h f) -> p h f", h=4)
                    o = dst(t, i4).rearrange("p (h c f) -> p h c f", h=4, c=2)
                    tds = t_dst(t, i4)
                    op("dve", lambda e: e.tensor_tensor(out=ra, in0=x1, in1=cs, op=ALU.mult), [t_ps[b], t_RT], [t_ROT[i2]])
                    op("dve", lambda e: e.tensor_tensor(out=rb, in0=x2, in1=sn, op=ALU.mult), [t_ps[b], t_RT], [t_ROT[i2]])
                    op("pool", lambda e: e.tensor_tensor(out=o[:, :, 0, :], in0=ra, in1=rb, op=ALU.subtract), [t_ROT[i2]], [tds])
                    i2b = i2
                    op("dve", lambda e: e.tensor_tensor(out=ra, in0=x1, in1=sn, op=ALU.mult), [t_ps[b], t_RT], [t_ROT[i2]])
                    op("dve", lambda e: e.tensor_tensor(out=rb, in0=x2, in1=cs, op=ALU.mult), [t_ps[b], t_RT], [t_ROT[i2]])
                    op("pool", lambda e: e.tensor_tensor(out=o[:, :, 1, :], in0=ra, in1=rb, op=ALU.add), [t_ROT[i2]], [tds])
                    if not do_T:
                        return None

                    def part2():
                        pb = 4 + (n % 2)
                        pvb = PS[:, pb * 512:pb * 512 + 256].bitcast(BF)
                        src = dst(t, i4)
                        for h in range(4):
                            op("pe", lambda e, h=h: e.transpose(pvb[:, h * 128:(h + 1) * 128], src[:, h * 128:(h + 1) * 128], ident[:]),
                               [tds, t_ident], [t_ps[pb]])
                        op("act", lambda e: e.activation(out=dstT[:, :, t * 128:(t + 1) * 128], in_=pvb.rearrange("p (h t) -> p h t", h=4), func=AF.Copy),
                           [t_ps[pb]], [t_dstT])
                    return part2
                return post

            if stop == "B3":
                sch.flush()
                return nc, dbg_out

            kpost = rope_post(lambda t, i2: KR[:, t, :], lambda t, i2: t_KR, None, None)

            def kpost_all(t, b, n):
                if t < NT:
                    return kpost(t, b, n)
                else:
                    tc_ = t - NT
                    for d in range(2):
                        op("dve", lambda e, d=d: e.tensor_tensor(out=KC[:, d, tc_, :].rearrange("p (h n) -> p h n", h=4),
                                                                 in0=PS[:, b * 512:(b + 1) * 512].rearrange("p (h n) -> p h n", h=4),
                                                                 in1=WC[:, d, tc_, :].unsqueeze(2).to_broadcast([128, 4, 128]), op=ALU.mult),
                           [t_ps[b], t_dec], [t_KC])
            inproj(1056, 1, range(NTC), kpost_all, prefetch=(544, 0))
            dbg("VR", VR, [t_VR], [128, NTC, 512], BF)
            dbg("QT", QT[:], [t_QT], [128, 4, S], BF)
            dbg("KR", KR[:], [t_KR], [128, NT, 512], BF)
            dbg("DTm", DTm[:], [t_dec], [128, 512], BF)
            if stop == "B4":
                sch.flush()
                return nc, dbg_out


            def hs(h):
                return slice(h * 128, (h + 1) * 128)

            import os
            KV = os.environ.get("KVAR", "")
            for d, (S32, t_S32) in enumerate(((SX[SXf[0]], t_SX[SXf[0]]), (SX[SXb[0]], t_SX[SXb[0]]))):
                if "noinit" in KV:
                    break
                if "init1" in KV and d == 1:
                    break
                for h in range(4):
                    for t in range(2):
                        op("pe", lambda e, d=d, h=h, t=t: e.matmul(PS[:, 6 * 512 + h * 128:6 * 512 + (h + 1) * 128], KC[:, d, t, hs(h)], VR[:, NT + t, hs(h)],
                                                                   start=(t == 0), stop=(t == 1)), [t_KC, t_VR], [t_ps[6]])
                if "nocopy" in KV:
                    continue
                op("dve", lambda e, S32=S32: e.tensor_copy(out=S32[:], in_=PS[:, 6 * 512:7 * 512]), [t_ps[6]], [t_S32])
                if "noact" in KV:
                    continue
                if d == 0:
                    op("act", lambda e: e.activation(out=SFb[0][:], in_=PS[:, 6 * 512:7 * 512], func=AF.Copy), [t_ps[6]], [t_SFb[0]])
                else:
                    op("act", lambda e: e.activation(out=SBall[:, NT - 1, :], in_=PS[:, 6 * 512:7 * 512], func=AF.Copy), [t_ps[6]], [t_SBall[NT - 1]])
            if stop == "B5a":
                sch.flush()
                return nc, dbg_out
            def bwd_step(c):
                i2 = c % NB
                op("pool", lambda e: e.tensor_tensor(out=KBc[i2][:], in0=KR[:, c, :], in1=ZZF[:, 1, :], op=ALU.mult),
                   [t_KR, t_dec], [t_KBc[i2]])
                pbk = 6 + (c % 2)
                for h in range(4):
                    op("pe", lambda e, h=h: e.matmul(PS[:, pbk * 512 + h * 128:pbk * 512 + (h + 1) * 128], KBc[i2][:, hs(h)], VR[:, c, hs(h)],
                                                     start=True, stop=True), [t_KBc[i2], t_VR], [t_ps[pbk]])
                k_ = NT - 1 - c
                src_ = SXb[k_ % 2]; dst_ = SXb[(k_ + 1) % 2]
                op("dve", lambda e: e.tensor_tensor(out=SX[dst_][:], in0=SX[src_][:], in1=GCF[:, 1, :], op=ALU.mult), [t_SX[src_], t_dec], [t_SX[dst_]])
                op("dve", lambda e: e.tensor_tensor(out=SX[dst_][:], in0=SX[dst_][:], in1=PS[:, pbk * 512:(pbk + 1) * 512], op=ALU.add),
                   [t_SX[dst_], t_ps[pbk]], [t_SX[dst_]])
                op("act", lambda e: e.activation(out=SBall[:, c - 1, :], in_=SX[dst_][:], func=AF.Copy), [t_SX[dst_]], [t_SBall[c - 1]])

            bsteps = list(range(NT - 1, 0, -1))
            inproj(544, 0, range(NT), rope_post(lambda t, i2: QR[i2][:], lambda t, i2: t_QR[i2], QT, t_QT), prefetch=(2080, 1))
            while bsteps:
                bwd_step(bsteps.pop(0))
            dbg("SBall", SBall[:], t_SBall, [128, NT, 512], BF)
            if stop == "B5b":
                sch.flush()
                return nc, dbg_out

            def stage1(c):
                i2 = c % NB
                cs_ = slice(c * 128, (c + 1) * 128)
                pa = c % 2
                pk_ = 6 + (c % 2)
                pkb = PS[:, pk_ * 512:pk_ * 512 + 256].bitcast(BF)
                for h in range(4):
                    op("pe", lambda e, h=h: e.transpose(pkb[:, h * 128:(h + 1) * 128], KR[:, c, h * 128:(h + 1) * 128], ident[:]), [t_KR, t_ident], [t_ps[pk_]])
                op("act", lambda e: e.activation(out=KTc[i2][:], in_=pkb, func=AF.Copy), [t_ps[pk_]], [t_KTc[i2]])
                for h in range(4):
                    op("pe", lambda e, h=h: e.matmul(PS[:, pa * 512 + h * 128:pa * 512 + (h + 1) * 128], KTc[i2][:, h * 128:(h + 1) * 128], QT[:, h, cs_], start=True, stop=True),
                       [t_KTc[i2], t_QT], [t_ps[pa]])
                op("dve", lambda e: e.tensor_tensor(out=AD[i2][:], in0=PS[:, pa * 512:(pa + 1) * 512], in1=DTm[:], op=ALU.mult),
                   [t_ps[pa], t_dec], [t_AD[i2]])
                op("pool", lambda e: e.tensor_tensor(out=QF[i2][:].rearrange("p (h n) -> p h n", h=4), in0=QT[:, :, cs_],
                                                     in1=XFB[:, 0, :].rearrange("p (h n) -> p h n", h=4), op=ALU.mult), [t_QT, t_dec], [t_QF[i2]])
                op("pool", lambda e: e.tensor_tensor(out=QB[i2][:].rearrange("p (h n) -> p h n", h=4), in0=QT[:, :, cs_],
                                                     in1=XFB[:, 1, :].rearrange("p (h n) -> p h n", h=4), op=ALU.mult), [t_QT, t_dec], [t_QB[i2]])
                op("pool", lambda e: e.tensor_tensor(out=KFc[i2][:], in0=KR[:, c, :], in1=ZZF[:, 0, :], op=ALU.mult),
                   [t_KR, t_dec], [t_KFc[i2]])
                pg = 2 + (c % 2)
                for k in range(8):
                    op("pe", lambda e, k=k: e.matmul(PS[:, pg * 512:(pg + 1) * 512], hT[:, k, cs_], WB[1][:, k, :], start=(k == 0), stop=(k == 7)),
                       [t_hT, t_WB[1]], [t_ps[pg]])
                op("act", lambda e: e.activation(out=GS[i2][:], in_=PS[:, pg * 512:(pg + 1) * 512], func=AF.Silu), [t_ps[pg]], [t_GS[i2]])

            def stage2(c):
                i2 = c % NB
                cs_ = slice(c * 128, (c + 1) * 128)
                py = 4 + (c % 2)
                for h in range(4):
                    o = PS[:, py * 512 + h * 128:py * 512 + (h + 1) * 128]
                    op("pe", lambda e, h=h, o=o: e.matmul(o, AD[i2][:, hs(h)], VR[:, c, hs(h)], start=True, stop=False), [t_AD[i2], t_VR], [t_ps[py]])
                    op("pe", lambda e, h=h, o=o: e.matmul(o, QF[i2][:, hs(h)], SFb[c % 2][:, hs(h)], start=False, stop=False), [t_QF[i2], t_SFb[c % 2]], [t_ps[py]])
                    op("pe", lambda e, h=h, o=o: e.matmul(o, QB[i2][:, hs(h)], SBall[:, c, hs(h)], start=False, stop=True), [t_QB[i2], t_SBall[c]], [t_ps[py]])
                if c < NT - 1:
                    pu = 6 + (c % 2)
                    for h in range(4):
                        op("pe", lambda e, h=h: e.matmul(PS[:, pu * 512 + h * 128:pu * 512 + (h + 1) * 128], KFc[i2][:, hs(h)], VR[:, c, hs(h)], start=True, stop=True),
                           [t_KFc[i2], t_VR], [t_ps[pu]])
                    src_ = SXf[c % 2]; dst_ = SXf[(c + 1) % 2]
                    op("dve", lambda e: e.tensor_tensor(out=SX[dst_][:], in0=SX[src_][:], in1=GCF[:, 0, :], op=ALU.mult), [t_SX[src_], t_dec], [t_SX[dst_]])
                    op("dve", lambda e: e.tensor_tensor(out=SX[dst_][:], in0=SX[dst_][:], in1=PS[:, pu * 512:(pu + 1) * 512], op=ALU.add), [t_SX[dst_], t_ps[pu]], [t_SX[dst_]])
                    op("act", lambda e: e.activation(out=SFb[(c + 1) % 2][:], in_=SX[dst_][:], func=AF.Copy), [t_SX[dst_]], [t_SFb[(c + 1) % 2]])
                for h in range(4):
                    op("dve", lambda e, h=h: e.bn_stats(out=BST[i2][:, h, :], in_=PS[:, py * 512 + h * 128:py * 512 + (h + 1) * 128]), [t_ps[py]], [t_BST[i2]])
                for h in range(4):
                    op("dve", lambda e, h=h: e.bn_aggr(out=MV[i2][:, h, :], in_=BST[i2][:, h, :]), [t_BST[i2]], [t_MV[i2]])
                op("act", lambda e: e.activation(out=SD[i2][:], in_=MV[i2][:, :, 1], func=AF.Sqrt, bias=EPS), [t_MV[i2]], [t_SD[i2]])
                op("dve", lambda e: e.reciprocal(out=SD[i2][:], in_=SD[i2][:]), [t_SD[i2]], [t_SD[i2]])
                pv = PS[:, py * 512:(py + 1) * 512].rearrange("p (h n) -> p h n", h=4)
                op("dve", lambda e: e.tensor_tensor(out=YN[i2][:].rearrange("p (h n) -> p h n", h=4), in0=pv,
                                                    in1=MV[i2][:, :, 0:1].to_broadcast([128, 4, 128]), op=ALU.subtract), [t_ps[py], t_MV[i2]], [t_YN[i2]])
                op("dve", lambda e: e.tensor_tensor(out=YN[i2][:].rearrange("p (h n) -> p h n", h=4), in0=YN[i2][:].rearrange("p (h n) -> p h n", h=4),
                                                    in1=SD[i2][:].unsqueeze(2).to_broadcast([128, 4, 128]), op=ALU.mult), [t_YN[i2], t_SD[i2]], [t_YN[i2]])

            def stage3(c):
                i2 = c % NB
                cs_ = slice(c * 128, (c + 1) * 128)
                op("pool", lambda e: e.tensor_tensor(out=YN[i2][:], in0=YN[i2][:], in1=GRB[:], op=ALU.mult), [t_YN[i2], t_GRB], [t_YN[i2]])
                op("pool", lambda e: e.tensor_tensor(out=ORk[i2][:], in0=YN[i2][:], in1=GS[i2][:], op=ALU.mult), [t_YN[i2], t_GS[i2]], [t_ORk[i2]])
                pt_ = 2 + (c % 2)
                pvb = PS[:, pt_ * 512:pt_ * 512 + 256].bitcast(BF)
                for h in range(4):
                    op("pe", lambda e, h=h: e.transpose(pvb[:, h * 128:(h + 1) * 128], ORk[i2][:, hs(h)], ident[:]), [t_ORk[i2], t_ident], [t_ps[pt_]])
                op("act", lambda e: e.activation(out=OT[:, 4:8, cs_], in_=pvb.rearrange("p (h t) -> p h t", h=4), func=AF.Copy),
                   [t_ps[pt_]], t_OT[4:8])

            for c in range(NT + 2):
                if c < NT:
                    stage1(c)
                if 1 <= c <= NT:
                    stage2(c - 1)
                if c >= 2:
                    stage3(c - 2)
            dbg("ORT", OT[:, 4:8, :], t_OT[4:8], [128, 4, S], BF)
            sch.flush()
            if stop == "B":
                return nc, dbg_out

        with ExitStack() as ph:
            PS = ph.enter_context(nc.psum_tensor("psC", [128, 4096], F32))
            t_ps = [Tile("psC%d" % b, True) for b in range(8)]
            VM = SPARE[:, 0:4608].bitcast(BF).rearrange("p (t h e) -> p t h e", t=NTC, h=8)
            t_VM = Tile("VM")
            WM = sb("WM", [128, 8, 512], BF, ph); t_WM = Tile("WM")
            WKP = sb("WKP", [128, 8, 2, 96], BF, ph); t_WKP = Tile("WKP")
            CQT = sb("CQT", [128, 2, S], BF, ph); t_CQT = Tile("CQT")
            CKT = sb("CKT", [128, 2, SC], BF, ph); t_CKT = Tile("CKT")
            WUQ = sb("WUQ", [128, 2, 8, 2, 96], BF, ph); t_WUQ = Tile("WUQ")
            WUKV = sb("WUKV", [128, 2, 8, 128], BF, ph); t_WUKV = Tile("WUKV")
            KPT = sb("KPT", [96, SC], BF, ph); t_KPT = Tile("KPT")
            MTab = sb("MTab", [96, 2, S], F32, ph); t_MTab = Tile("MTab")
            GQK = sb("GQK", [128, 4], F32, ph); t_GQK = Tile("GQK")
            QTm = [sb("QTm%d" % i, [96, S], BF, ph) for i in range(2)]
            KTm = [sb("KTm%d" % i, [96, SC], BF, ph) for i in range(2)]
            VA = [sb("VA%d" % i, [128, NTC, 128], BF, ph) for i in range(2)]
            t_QTm = [Tile("QTm%d" % i) for i in range(2)]
            t_KTm = [Tile("KTm%d" % i) for i in range(2)]
            t_VA = [Tile("VA%d" % i) for i in range(2)]
            PTb = [sb("PTb%d" % i, [128, 512], BF, ph) for i in range(5)]
            t_PTb = [Tile("PTb%d" % i) for i in range(5)]
            RDn = [sb("RDn%d" % i, [128, 512], F32, ph) for i in range(2)]
            t_RDn = [Tile("RDn%d" % i) for i in range(2)]
            T1 = [sb("T1_%d" % i, [96, 512], F32, ph) for i in range(2)]
            T2 = [sb("T2_%d" % i, [96, 512], F32, ph) for i in range(2)]
            t_T1 = [Tile("T1_%d" % i) for i in range(2)]
            t_T2 = [Tile("T2_%d" % i) for i in range(2)]
            CN = [sb("CN%d" % i, [128, 512], BF, ph) for i in range(4)]
            t_CN = [Tile("CN%d" % i) for i in range(4)]
            junk2 = sb("junk2", [128, 256], BF, ph); t_junk2 = Tile("junk2")
            ss2 = sb("ss2", [128, NTC, 2], F32, ph)
            t_ss2 = [Tile("ss2_%d" % t) for t in range(NTC)]
            t_ss2all = Tile("ss2all")

            sch.dma("pool", [(WM[:], win_d[:, 0:512].rearrange("(k p) n -> p k n", p=128))], writes=[t_WM])
            op("dve", lambda e: e.memset(WKP[:], 0.0), [], [t_WKP])
            op("dve", lambda e: e.memset(ss2[:], 0.0), [], [t_ss2all])
            sch.dma("pool", [(WKP[:, :, 0, 64:96], win_d[:, 512:544].rearrange("(k p) n -> p k n", p=128))], writes=[t_WKP])
            for a, b_ in ((64, 72), (72, 64), (80, 88), (88, 80)):
                op("dve", lambda e, a=a, b_=b_: e.tensor_copy(out=WKP[:, :, 1, a:a + 8], in_=WKP[:, :, 0, b_:b_ + 8]), [t_WKP], [t_WKP])
            sch.dma("pool", [(WUQ[:, r, :, 0, :], wuq_d[r * 128:(r + 1) * 128, :].rearrange("p (h e) -> p h e", h=8)) for r in range(2)], writes=[t_WUQ])
            op("pool", lambda e: e.tensor_copy(out=WUQ[:, :, :, 1, 0:64], in_=WUQ[:, :, :, 0, 0:64]), [t_WUQ], [t_WUQ])
            for a, b_ in ((64, 72), (72, 64), (80, 88), (88, 80)):
                op("pool", lambda e, a=a, b_=b_: e.tensor_copy(out=WUQ[:, :, :, 1, a:a + 8], in_=WUQ[:, :, :, 0, b_:b_ + 8]), [t_WUQ], [t_WUQ])
            sch.dma("pool", [(WUKV[:], wukv_d[:, :].rearrange("(r p) (h e) -> p r h e", p=128, h=8))], writes=[t_WUKV])
            sch.dma("sp", [(MTab[64:96, :, :], mt_d[:, :, :])], writes=[t_MTab])
            sch.dma("sp", [(GQK[:, 0:2], gq_d.ap().rearrange("o (r p) -> p (o r)", p=128)),
                           (GQK[:, 2:4], gkv_d.ap().rearrange("o (r p) -> p (o r)", p=128))], writes=[t_GQK], allow_slow_non_contiguous=True)
            for i in range(2):
                op("pool", lambda e, i=i: e.memset(VA[i][:, :, (64 - 64 * i):(128 - 64 * i)], 1.0), [], [t_VA[i]])

            if stop == "C0":
                sch.flush()
                return nc, dbg_out
            def c1_a(t):
                b = t % 3
                i4 = t % 4
                ts_ = slice(t * 128, (t + 1) * 128)
                pb_ = PS[:, b * 512:(b + 1) * 512]
                for k in range(8):
                    op("pe", lambda e, k=k: e.matmul(pb_, hT[:, k, ts_], WM[:, k, :], start=(k == 0), stop=(k == 7)), [t_hT, t_WM], [t_ps[b]])
                for g in range(2):
                    op("act", lambda e, g=g: e.activation(out=junk2[:], in_=pb_[:, g * 256:(g + 1) * 256], func=AF.Square, accum_out=ss2[:, t, g:g + 1]),
                       [t_ps[b], t_ss2all], [t_junk2, t_ss2[t]])
                op("act", lambda e: e.activation(out=ss2[:, t, :], in_=ss2[:, t, :], func=AF.Sqrt, scale=1.0 / 256, bias=EPS), [t_ss2[t]], [t_ss2[t]])
                op("dve", lambda e: e.reciprocal(out=ss2[:, t, :], in_=ss2[:, t, :]), [t_ss2[t]], [t_ss2[t]])
                op("dve", lambda e: e.tensor_tensor(out=CN[i4][:].rearrange("p (g n) -> p g n", g=2), in0=pb_.rearrange("p (g n) -> p g n", g=2),
                                                    in1=ss2[:, t, :].unsqueeze(2).to_broadcast([128, 2, 256]), op=ALU.mult),
                   [t_ps[b], t_ss2[t]], [t_CN[i4]])

            def c1_b(t):
                i4 = t % 4
                ts_ = slice(t * 128, (t + 1) * 128)
                pt_ = 3 + (t % 2)
                pvb = PS[:, pt_ * 512:pt_ * 512 + 256].bitcast(BF)
                for j in range(4):
                    op("pe", lambda e, j=j: e.transpose(pvb[:, j * 128:(j + 1) * 128], CN[i4][:, j * 128:(j + 1) * 128], ident[:]),
                       [t_CN[i4], t_ident], [t_ps[pt_]])
                pv3 = pvb.rearrange("p (j t) -> p j t", j=4)
                if t < NT:
                    op("dve", lambda e: e.tensor_tensor(out=CQT[:, :, ts_], in0=pv3[:, 0:2, :], in1=GQK[:, 0:2].unsqueeze(2).to_broadcast([128, 2, 128]), op=ALU.mult),
                       [t_ps[pt_], t_GQK], [t_CQT])
                op("dve", lambda e: e.tensor_tensor(out=CKT[:, :, ts_], in0=pv3[:, 2:4, :], in1=GQK[:, 2:4].unsqueeze(2).to_broadcast([128, 2, 128]), op=ALU.mult),
                   [t_ps[pt_], t_GQK], [t_CKT])

            for t in range(NTC + 2):
                if t < NTC:
                    c1_a(t)
                if t >= 2:
                    c1_b(t - 2)

            if stop == "C1":
                sch.flush()
                return nc, dbg_out
            def rope_rows(psA, psB, tA, tB, cols, dst, t_dst, n):
                i2 = n % 2
                op("dve", lambda e: e.tensor_tensor(out=T1[i2][64:96, :], in0=psA[64:96, :], in1=MTab[64:96, 0, cols], op=ALU.mult), [tA, t_MTab], [t_T1[i2]])
                op("dve", lambda e: e.tensor_tensor(out=T2[i2][64:96, :], in0=psB[64:96, :], in1=MTab[64:96, 1, cols], op=ALU.mult), [tB, t_MTab], [t_T2[i2]])
                op("pool", lambda e: e.tensor_tensor(out=dst, in0=T1[i2][64:96, :], in1=T2[i2][64:96, :], op=ALU.add), [t_T1[i2], t_T2[i2]], [t_dst])

            for blk in range(5):
                cols = slice(blk * 512, blk * 512 + (512 if blk < 4 else 256))
                ncol = 512 if blk < 4 else 256
                ba = 5 + (blk % 2) * 0
                pA = PS[0:96, 5 * 512:5 * 512 + ncol]; pB = PS[0:96, 6 * 512:6 * 512 + ncol]
                for k in range(8):
                    op("pe", lambda e, k=k, pA=pA, cols=cols: e.matmul(pA, WKP[:, k, 0, :], hT[:, k, cols], start=(k == 0), stop=(k == 7)), [t_WKP, t_hT], [t_ps[5]])
                if blk < 4:
                    for k in range(8):
                        op("pe", lambda e, k=k, pB=pB, cols=cols: e.matmul(pB, WKP[:, k, 1, :], hT[:, k, cols], start=(k == 0), stop=(k == 7)), [t_WKP, t_hT], [t_ps[6]])
                    rope_rows(pA, pB, t_ps[5], t_ps[6], cols, KPT[64:96, cols], t_KPT, blk)
                else:
                    op("act", lambda e, pA=pA, cols=cols: e.activation(out=KPT[64:96, cols], in_=pA[64:96, :], func=AF.Copy), [t_ps[5]], [t_KPT])
            if stop == "C1k":
                sch.flush()
                return nc, dbg_out
            NPRE = 9
            for t in range(NPRE):
                sch.dma("sp", [(XN[:, t, :], x_d[t * 128:(t + 1) * 128, :])], writes=[t_hT, t_XN[t]])
            for t in range(NTC):
                b = t % 3
                ts_ = slice(t * 128, (t + 1) * 128)
                pb_ = PS[:, b * 512:(b + 1) * 512]
                for r in range(2):
                    op("pe", lambda e, r=r, pb_=pb_, ts_=ts_: e.matmul(pb_.rearrange("p (h e) -> p h e", h=8), CKT[:, r, ts_], WUKV[:, r, :, 64:128], start=(r == 0), stop=(r == 1)),
                       [t_CKT, t_WUKV], [t_ps[b]])
                op("act", lambda e, pb_=pb_, t=t: e.activation(out=VM[:, t, :, :], in_=pb_.rearrange("p (h e) -> p h e", h=8), func=AF.Copy), [t_ps[b]], [t_VM])
            dbg("CQT", CQT[:], [t_CQT], [128, 2, S], BF)
            dbg("CKT", CKT[:], [t_CKT], [128, 2, SC], BF)
            dbg("KPT", KPT[64:96, :], [t_KPT], [32, SC], BF)
            dbg("VM", VM, [t_VM], [128, NTC, 8, 64], BF)

            if stop == "C1v":
                sch.flush()
                return nc, dbg_out
            def proj_units(h):
                i2 = h % 2
                units = []

                def q_unit(blk):
                    cols = slice(blk * 512, (blk + 1) * 512)
                    pA = PS[0:96, 5 * 512:6 * 512]; pB = PS[0:96, 6 * 512:7 * 512]
                    for v, (pp, tp) in enumerate(((pA, t_ps[5]), (pB, t_ps[6]))):
                        for r in range(2):
                            op("pe", lambda e, v=v, r=r, pp=pp: e.matmul(pp, WUQ[:, r, h, v, :], CQT[:, r, cols], start=(r == 0), stop=(r == 1)),
                               [t_WUQ, t_CQT], [tp])
                    op("dve", lambda e: e.tensor_copy(out=QTm[i2][0:64, cols], in_=pA[0:64, :]), [t_ps[5]], [t_QTm[i2]])
                    rope_rows(pA, pB, t_ps[5], t_ps[6], cols, QTm[i2][64:96, cols], t_QTm[i2], blk)

                def k_unit(blk):
                    ncol = 512 if blk < 4 else 256
                    cols = slice(blk * 512, blk * 512 + ncol)
                    pk = PS[0:64, 6 * 512:6 * 512 + ncol]
                    for r in range(2):
                        op("pe", lambda e, r=r: e.matmul(pk, WUKV[:, r, h, 0:64], CKT[:, r, cols], start=(r == 0), stop=(r == 1)), [t_WUKV, t_CKT], [t_ps[6]])
                    op("dve", lambda e: e.tensor_copy(out=KTm[i2][0:64, cols], in_=pk), [t_ps[6]], [t_KTm[i2]])

                def v_unit():
                    op("pool", lambda e: e.tensor_copy(out=KTm[i2][64:96, :], in_=KPT[64:96, :]), [t_KPT], [t_KTm[i2]])
                    op("pool", lambda e: e.tensor_copy(out=VA[i2][:, :, 64 * i2:64 * i2 + 64], in_=VM[:, :, h, :]), [t_VM], [t_VA[i2]])

                units.append(v_unit)
                for blk in range(5):
                    units.append(lambda blk=blk: k_unit(blk))
                for blk in range(4):
                    units.append(lambda blk=blk: q_unit(blk))
                return units

            def proj(h):
                for u in proj_units(h):
                    u()

            items = [(h, qb, kt) for h in range(8) for qb in range(4) for kt in range(NTC)]
            LAG = 3
            NPT = 5
            SBANK = (0, 1, 2, 7)

            def emit_S(n):
                h, qb, kt = items[n]
                i2 = h % 2
                b = SBANK[n % 4]
                pS_ = PS[:, b * 512:(b + 1) * 512]
                op("pe", lambda e: e.matmul(pS_, KTm[i2][0:96, kt * 128:(kt + 1) * 128], QTm[i2][0:96, qb * 512:(qb + 1) * 512], start=True, stop=True),
                   [t_KTm[i2], t_QTm[i2]], [t_ps[b]])
                op("act", lambda e: e.activation(out=PTb[n % NPT][:], in_=pS_, func=AF.Exp, scale=MLA_SCALE), [t_ps[b]], [t_PTb[n % NPT]])

            def emit_PV(n):
                h, qb, kt = items[n]
                i2 = h % 2
                po = 3 + (qb % 2)
                pO = PS[:, po * 512:(po + 1) * 512]
                qs = slice(qb * 512, (qb + 1) * 512)
                op("pe", lambda e: e.matmul(pO, VA[i2][:, kt, :], PTb[n % NPT][:], start=(kt == 0), stop=(kt == NTC - 1)), [t_VA[i2], t_PTb[n % NPT]], [t_ps[po]])
                if kt != NTC - 1:
                    return
                r2 = qb % 2
                ro = (h % 2) * 64
                dn = 64 - ro
                op("dve", lambda e: e.tensor_copy(out=RDn[r2][dn:dn + 64, :], in_=pO[dn:dn + 64, :]), [t_ps[po]], [t_RDn[r2]])
                op("dve", lambda e: e.reciprocal(out=RDn[r2][dn:dn + 64, :], in_=RDn[r2][dn:dn + 64, :]), [t_RDn[r2]], [t_RDn[r2]])
                if ro == 64:
                    op("dve", lambda e: e.tensor_copy(out=RDn[r2][64:128, :], in_=RDn[r2][0:64, :]), [t_RDn[r2]], [t_RDn[r2]])
                    dn = 64
                op("dve", lambda e: e.tensor_tensor(out=OT[ro:ro + 64, h // 2, qs], in0=pO[ro:ro + 64, :], in1=RDn[r2][dn:dn + 64, :], op=ALU.mult),
                   [t_ps[po], t_RDn[r2]], [t_OT[h // 2]])

            proj(0)
            if "QTm0" in debug:
                dbg("QTm0", QTm[0][:], [t_QTm[0]], [96, S], BF)
                dbg("KTm0", KTm[0][:], [t_KTm[0]], [96, SC], BF)
            if stop == "C2":
                sch.flush()
                return nc, dbg_out
            pending = []
            for n in range(len(items) + LAG):
                if n < len(items):
                    emit_S(n)
                if n - LAG >= 0:
                    emit_PV(n - LAG)
                    h, qb, kt = items[n - LAG]
                    if qb == 0 and kt == 0 and h + 1 < 8:
                        assert not pending
                        pending = proj_units(h + 1)
                    if pending and (n % 6 == 0):
                        pending.pop(0)()
            dbg("OMT", OT[:, 0:4, :], t_OT[0:4], [128, 4, S], BF)
            sch.flush()
            if stop == "C":
                return nc, dbg_out

        with ExitStack() as ph:
            PS = ph.enter_context(nc.psum_tensor("psD", [128, 4096], F32))
            t_ps = [Tile("psD%d" % b, True) for b in range(8)]
            WO = sb("WO", [128, 8, D], BF, ph); t_WO = Tile("WO")
            GT1 = sb("GT1", [128, D], F32, ph); t_GT1 = Tile("GT1")
            XT = [sb("XTd%d" % i, [128, D], F32, ph) for i in range(3)]
            t_XT = [Tile("XTd%d" % i) for i in range(3)]
            TMP = [sb("TMPd%d" % i, [128, D], F32, ph) for i in range(2)]
            t_TMP = [Tile("TMPd%d" % i) for i in range(2)]
            t_WOh = [Tile("WO0"), Tile("WO1")]
            for hf in range(2):
                sch.dma("pool", [(WO[:, :, hf * 512:(hf + 1) * 512], wout_d[:, hf * 512:(hf + 1) * 512].rearrange("(j p) n -> p j n", p=128))], writes=[t_WOh[hf]])
            sch.dma("sp", [(GT1[:], bc(modp_d, 0, D))], reads=[t_modp], writes=[t_GT1])
            for t in range(NT):
                ts_ = slice(t * 128, (t + 1) * 128)
                i3 = t % 3; i2 = t % 2
                pre = t < 9
                if not pre:
                    sch.dma("sp", [(XT[i3][:], x_d[ts_, :])], writes=[t_XT[i3]])
                b0 = (t % 4) * 2
                for hf in range(2):
                    for j in range(8):
                        op("pe", lambda e, hf=hf, j=j, b0=b0, ts_=ts_: e.matmul(PS[:, (b0 + hf) * 512:(b0 + hf + 1) * 512], OT[:, j, ts_], WO[:, j, hf * 512:(hf + 1) * 512],
                                                                                start=(j == 0), stop=(j == 7)), [t_OT[j], t_WOh[hf]], [t_ps[b0 + hf]])
                op("dve", lambda e, b0=b0, i2=i2: e.tensor_tensor(out=TMP[i2][:], in0=PS[:, b0 * 512:(b0 + 2) * 512], in1=GT1[:], op=ALU.mult),
                   [t_ps[b0], t_ps[b0 + 1], t_GT1], [t_TMP[i2]])
                if pre:
                    op("pool", lambda e, t=t, i2=i2: e.tensor_tensor(out=XN[:, t, :], in0=TMP[i2][:], in1=XN[:, t, :], op=ALU.add),
                       [t_TMP[i2], t_XN[t]], [t_XN[t]])
                else:
                    op("pool", lambda e, t=t, i2=i2, i3=i3: e.tensor_tensor(out=XN[:, t, :], in0=TMP[i2][:], in1=XT[i3][:], op=ALU.add),
                       [t_TMP[i2], t_XT[i3]] + list(t_OT) + [t_hT], [t_XN[t]])
            dbg("XN", XN, t_XN, [128, NT, D], F32)
            sch.flush()
            if stop == "D":
                return nc, dbg_out

        H2T = OT
        t_H2T = Tile("H2T")
        with ExitStack() as ph:
            PS = ph.enter_context(nc.psum_tensor("psE", [128, 4096], F32))
            t_ps = [Tile("psE%d" % b, True) for b in range(8)]
            CW = sb("CW", [128, 44, 3], F32, ph); CB = sb("CB", [128, 44], F32, ph)
            t_CW = Tile("CW")
            GT2 = sb("GT2", [128, D], F32, ph); t_GT2 = Tile("GT2")
            GFB = sb("GFB", [128, D], F32, ph); t_GFB = Tile("GFB")
            def late_loads():
                sch.dma("sp", [(GT2[:], bc(modp_d, 3 * D, D))], reads=[t_modp], writes=[t_GT2])
                sch.dma("sp", [(GFB[:], bc(gf_d, 0, D))], writes=[t_GFB])
                sch.dma("sp", [(CW[:, :, t], cw_d.ap()[t:t + 1, :].rearrange("o (c p) -> p (o c)", p=128)) for t in range(3)] +
                        [(CB[:], cb_d.ap().rearrange("o (c p) -> p (o c)", p=128))], writes=[t_CW], allow_slow_non_contiguous=True)
            ss = sb("ssE", [128, 2 * NT], F32, ph)
            rs = sb("rsE", [128, 2 * NT], F32, ph)
            t_ss = [Tile("ssE%d" % t) for t in range(2 * NT)]
            t_rs = [Tile("rsE%d" % t) for t in range(2 * NT)]
            t_ssall = Tile("ssEall")
            op("dve", lambda e: e.memset(ss[:], 0.0), [], [t_ssall])
            t_junk = Tile("junkE")
            TMP = [sb("TMPe%d" % i, [128, D], F32, ph) for i in range(2)]
            t_TMP = [Tile("TMPe%d" % i) for i in range(2)]
            WU = [sb("WU%d" % i, [128, 8, 2, 128], BF, ph) for i in range(3)]
            t_WU = [Tile("WU%d" % i) for i in range(3)]
            WD = [sb("WD%d" % i, [128, GMAX, D], BF, ph) for i in range(1)]
            t_WD = [Tile("WD%d" % i) for i in range(1)]

            def load_wu(j):
                w = j % 3
                sch.dma("pool", [(WU[w][:, :, 0, :], wup_d[:, j * 128:(j + 1) * 128].rearrange("(k p) n -> p k n", p=128)),
                                 (WU[w][:, :, 1, :], wup_d[:, DFF + j * 128:DFF + (j + 1) * 128].rearrange("(k p) n -> p k n", p=128))], writes=[t_WU[w]])

            def load_wd(gi):
                j0, j1 = GROUPS[gi]
                sch.dma("pool", [(WD[0][:, 0:j1 - j0, :], wdn_d[j0 * 128:j1 * 128, :].rearrange("(j p) n -> p j n", p=128))], writes=[t_WD[0]])

            load_wu(0); load_wu(1); load_wd(0)
            with ExitStack() as ph0:
                junk = sb("junkE", [128, D], BF, ph0)
                S2 = sb("S2", [128, D], F32, ph0); SH2 = sb("SH2", [128, D], F32, ph0); G2B = sb("G2B", [128, D], F32, ph0)
                HB = [sb("HBe%d" % i, [128, D], BF, ph0) for i in range(2)]
                t_HB = [Tile("HBe%d" % i) for i in range(2)]
                t_S2 = Tile("S2"); t_SH2 = Tile("SH2"); t_G2B = Tile("G2B")
                sch.dma("sp", [(SH2[:], bc(modp_d, D, D))], reads=[t_modp], writes=[t_SH2])
                sch.dma("sp", [(S2[:], bc(modp_d, 2 * D, D))], reads=[t_modp], writes=[t_S2])
                sch.dma("sp", [(G2B[:], bc(g2_d, 0, D))], writes=[t_G2B])
                late_loads()
                op("dve", lambda e: e.scalar_tensor_tensor(out=S2[:], in0=S2[:], scalar=1.0, in1=G2B[:], op0=ALU.add, op1=ALU.mult), [t_S2, t_G2B], [t_S2])
                def e0_a(t):
                    i2 = t % 2
                    src = XN[:, t, :]
                    op("act", lambda e: e.activation(out=junk[:], in_=src, func=AF.Square, accum_out=ss[:, t:t + 1]), [t_XN[t], t_ssall], [t_junk, t_ss[t]])
                    op("act", lambda e: e.activation(out=rs[:, t:t + 1], in_=ss[:, t:t + 1], func=AF.Sqrt, scale=1.0 / D, bias=EPS), [t_ss[t]], [t_rs[t]])
                    op("dve", lambda e: e.reciprocal(out=rs[:, t:t + 1], in_=rs[:, t:t + 1]), [t_rs[t]], [t_rs[t]])
                    op("dve", lambda e: e.scalar_tensor_tensor(out=TMP[i2][:], in0=src, scalar=rs[:, t:t + 1], in1=S2[:], op0=ALU.mult, op1=ALU.mult),
                       [t_XN[t], t_rs[t], t_S2], [t_TMP[i2]])
                    op("dve", lambda e: e.tensor_tensor(out=HB[i2][:], in0=TMP[i2][:], in1=SH2[:], op=ALU.add), [t_TMP[i2], t_SH2], [t_HB[i2]])

                def e0_b(t):
                    i2 = t % 2
                    pbk = t % 2
                    pv = PS[:, pbk * 512:(pbk + 1) * 512].bitcast(BF)
                    for k in range(8):
                        op("pe", lambda e, k=k: e.transpose(pv[:, k * 128:(k + 1) * 128], HB[i2][:, k * 128:(k + 1) * 128], ident[:]), [t_HB[i2], t_ident], [t_ps[pbk]])
                    op("act", lambda e: e.activation(out=H2T[:, :, t * 128:(t + 1) * 128], in_=pv.rearrange("p (k t) -> p k t", k=8), func=AF.Copy),
                       [t_ps[pbk]], [t_H2T])

                for t in range(NT + 1):
                    if t < NT:
                        e0_a(t)
                    if t >= 1:
                        e0_b(t - 1)
                dbg("H2T", H2T[:], [t_H2T], [128, 8, S], BF)
                sch.flush()
            with ExitStack() as ph1:
                ACTT = sb("ACTT", [128, GMAX, S], BF, ph1)
                t_ACTT = [Tile("ACTT%d" % i) for i in range(GMAX)]
                AaL = [sb("Aa%d" % i, [128, S], F32, ph1) for i in range(2)]
                GgL = [sb("Gg%d" % i, [128, S], F32, ph1) for i in range(2)]
                t_AaL = [(Tile("Aa%da" % i), Tile("Aa%db" % i)) for i in range(2)]; t_GgL = [(Tile("Gg%da" % i), Tile("Gg%db" % i)) for i in range(2)]
                t_out = Tile("out")
                npair = [0]

                for gi, (j0, j1) in enumerate(GROUPS):
                    ng = j1 - j0
                    wd = 0
                    if gi > 0:
                        load_wd(gi)
                    for j in range(j0, j1):
                        jj = j - j0
                        w = j % 3
                        if j + 2 < NCH:
                            load_wu(j + 2)
                        Aa = AaL[j % 2]; Gg = GgL[j % 2]; t_Aa = t_AaL[j % 2]; t_Gg = t_GgL[j % 2]
                        for half, (ACC, t_ACC, pb0) in enumerate(((Aa, t_Aa, 0), (Gg, t_Gg, 4))):
                            c = half * NCH + j
                            pall = PS[:, pb0 * 512:(pb0 + 4) * 512]
                            for blk in range(4):
                                for k in range(8):
                                    op("pe", lambda e, k=k, blk=blk, half=half, pb0=pb0, w=w: e.matmul(PS[:, (pb0 + blk) * 512:(pb0 + blk + 1) * 512], WU[w][:, k, half, :],
                                                                                                      H2T[:, k, blk * 512:(blk + 1) * 512], start=(k == 0), stop=(k == 7)),
                                       [t_WU[w], t_H2T], [t_ps[pb0 + blk]])
                            H = S // 2
                            tA = t_ps[pb0:pb0 + 2]; tB = t_ps[pb0 + 2:pb0 + 4]
                            tacc = t_ACC
                            op("act", lambda e, ACC=ACC, pall=pall, c=c: e.activation(out=ACC[:, 0:H], in_=pall[:, 0:H], func=AF.Identity, scale=CW[:, c, 1:2], bias=CB[:, c:c + 1]),
                               tA + [t_CW], [tacc[0]])
                            op("act", lambda e, ACC=ACC, pall=pall, c=c: e.activation(out=ACC[:, H:S], in_=pall[:, H:S], func=AF.Identity, scale=CW[:, c, 1:2], bias=CB[:, c:c + 1]),
                               tB + [t_CW], [tacc[1]])
                            op("dve", lambda e, ACC=ACC, pall=pall, c=c: e.scalar_tensor_tensor(out=ACC[:, 1:H], in0=pall[:, 0:H - 1], scalar=CW[:, c, 0:1], in1=ACC[:, 1:H],
                                                                                                op0=ALU.mult, op1=ALU.add), tA + [t_CW, tacc[0]], [tacc[0]])
                            op("dve", lambda e, ACC=ACC, pall=pall, c=c: e.scalar_tensor_tensor(out=ACC[:, H:S], in0=pall[:, H - 1:S - 1], scalar=CW[:, c, 0:1], in1=ACC[:, H:S],
                                                                                                op0=ALU.mult, op1=ALU.add), tA[1:2] + tB + [t_CW, tacc[1]], [tacc[1]])
                            op("dve", lambda e, ACC=ACC, pall=pall, c=c: e.scalar_tensor_tensor(out=ACC[:, 0:H], in0=pall[:, 1:H + 1], scalar=CW[:, c, 2:3], in1=ACC[:, 0:H],
                                                                                                op0=ALU.mult, op1=ALU.add), tA + tB[0:1] + [t_CW, tacc[0]], [tacc[0]])
                            op("dve", lambda e, ACC=ACC, pall=pall, c=c: e.scalar_tensor_tensor(out=ACC[:, H:S - 1], in0=pall[:, H + 1:S], scalar=CW[:, c, 2:3], in1=ACC[:, H:S - 1],
                                                                                                op0=ALU.mult, op1=ALU.add), tB + [t_CW, tacc[1]], [tacc[1]])
                        op("act", lambda e: e.activation(out=Gg[:], in_=Gg[:], func=AF.Silu), list(t_Gg), list(t_Gg))
                        op("pool", lambda e, jj=jj: e.tensor_tensor(out=ACTT[:, jj, :], in0=Aa[:], in1=Gg[:], op=ALU.mult), list(t_Aa) + list(t_Gg), [t_ACTT[jj]])
                    last = gi == len(GROUPS) - 1
                    for t in range(NT):
                        ts_ = slice(t * 128, (t + 1) * 128)
                        b0 = (t % 4) * 2
                        i2 = t % 2
                        for hf in range(2):
                            for jj in range(ng):
                                op("pe", lambda e, hf=hf, jj=jj, b0=b0, ts_=ts_, wd=wd, ng=ng: e.matmul(PS[:, (b0 + hf) * 512:(b0 + hf + 1) * 512], ACTT[:, jj, ts_],
                                                                                                        WD[wd][:, jj, hf * 512:(hf + 1) * 512], start=(jj == 0), stop=(jj == ng - 1)),
                                   [t_ACTT[jj], t_WD[wd]], [t_ps[b0 + hf]])
                        op("dve", lambda e, b0=b0, i2=i2: e.tensor_tensor(out=TMP[i2][:], in0=PS[:, b0 * 512:(b0 + 2) * 512], in1=GT2[:], op=ALU.mult),
                           [t_ps[b0], t_ps[b0 + 1], t_GT2], [t_TMP[i2]])
                        op("pool", lambda e, t=t, i2=i2: e.tensor_tensor(out=XN[:, t, :], in0=XN[:, t, :], in1=TMP[i2][:], op=ALU.add), [t_TMP[i2], t_XN[t]], [t_XN[t]])
                        if last:
                            u = NT + t
                            src = XN[:, t, :]
                            jv = TMP[i2][:, 0:D // 2].bitcast(BF)
                            op("act", lambda e, src=src, u=u, jv=jv: e.activation(out=jv, in_=src, func=AF.Square, accum_out=ss[:, u:u + 1]), [t_XN[t], t_ssall], [t_TMP[i2], t_ss[u]])
                            op("act", lambda e, u=u: e.activation(out=rs[:, u:u + 1], in_=ss[:, u:u + 1], func=AF.Sqrt, scale=1.0 / D, bias=EPS), [t_ss[u]], [t_rs[u]])
                            op("dve", lambda e, u=u: e.reciprocal(out=rs[:, u:u + 1], in_=rs[:, u:u + 1]), [t_rs[u]], [t_rs[u]])
                            op("dve", lambda e, src=src, u=u: e.scalar_tensor_tensor(out=src, in0=src, scalar=rs[:, u:u + 1], in1=GFB[:], op0=ALU.mult, op1=ALU.mult),
                               [t_XN[t], t_rs[u], t_GFB], [t_XN[t]])
                            sch.dma("sp", [(out_d[ts_, :], src)], reads=[t_XN[t]], writes=[t_out])
                sch.wait_tiles("sp", [t_out])
                sch.flush()
    return nc, dbg_out


_CACHE = {}


def _prep_inputs(inputs):
    c = _consts()
    f = lambda a: np.ascontiguousarray(np.asarray(a, dtype=np.float32))
    shared = {
        "c_ctx": f(inputs["c_ctx"]).reshape(1, D),
        "w_ada": f(inputs["w_ada"])[0],
        "b_ada": f(inputs["b_ada"]).reshape(1, 6 * D),
        "g_norm1": f(inputs["g_norm1"]).reshape(1, D),
        "w_in": f(inputs["w_in"])[0],
        "g_q": f(inputs["g_q"]).reshape(1, 256),
        "w_uq": f(inputs["w_uq"])[0],
        "g_kv": f(inputs["g_kv"]).reshape(1, 256),
        "w_ukv": f(inputs["w_ukv"])[0],
        "ret_decay": f(inputs["ret_decay"]).reshape(1, 8),
        "g_ret": f(inputs["g_ret"]).reshape(1, 512),
        "w_out": f(inputs["w_out"])[0],
        "g_norm2": f(inputs["g_norm2"]).reshape(1, D),
        "w_up": f(inputs["w_up"])[0],
        "conv_w": f(inputs["conv_w"])[0],
        "conv_b": f(inputs["conv_b"]).reshape(1, 2 * DFF),
        "w_down": f(inputs["w_down"])[0],
        "g_final": f(inputs["g_final"]).reshape(1, D),
        "k_ident": c["ident"], "k_rt": c["rt"], "k_mt": c["mt"], "k_cst": c["cst"],
    }
    x = f(inputs["x"]); cc = f(inputs["c"]); ctx = f(inputs["ctx"])
    maps = []
    for b in range(8):
        m = dict(shared)
        m["x"] = x[b]
        m["c"] = cc[b].reshape(1, D)
        m["ctx"] = ctx[b]
        maps.append(m)
    return maps


def kernel(**inputs):
    if "nc" not in _CACHE:
        _CACHE["nc"] = build()[0]
    nc = _CACHE["nc"]
    maps = _prep_inputs(inputs)
    res = run_bass_kernel_spmd(nc, maps, core_ids=list(range(8)))
    out = np.stack([np.asarray(r["out"], dtype=np.float32) for r in res.results], axis=0)
    return out
```

```python
import math
import os
import numpy as np
import ml_dtypes
import concourse.bass as bass
import concourse.mybir as mybir
from concourse.bass_utils import run_bass_kernel_spmd

F32 = mybir.dt.float32
BF = mybir.dt.bfloat16
AF = mybir.ActivationFunctionType
ALU = mybir.AluOpType

S = 2048
D = 1024
L = 256
NT = 16
NTC = 18
SC = S + L
DFF = 2816
NCH = 22
EPS = 1e-6
MLA_SCALE = 96 ** -0.5
LN_S = math.log(128 ** -0.5)
GROUPS = [(0, 8), (8, 15), (15, 22)]
GMAX = 8


class Tile:
    __slots__ = ("name", "w", "r", "excl")

    def __init__(self, name, excl=False):
        self.name = name
        self.w = None
        self.r = {}
        self.excl = excl


class _Rec:
    def __getattr__(self, name):
        def call(*a, **k):
            self.__dict__["call"] = (name, a, k)
            return self
        return call


class Sched:
    ENG = ("pe", "act", "dve", "pool", "sp")

    def __init__(self, nc, ndma=48):
        self.nc = nc
        self.ops = {e: [] for e in self.ENG}
        self.cnt = {e: 0 for e in self.ENG}
        self.seen = {e: {} for e in self.ENG}
        self.sems = {}
        self.ndma = ndma
        self.dcnt = [0] * ndma
        self.rr = {"sp": 0, "pool": 0}
        self.half = ndma // 2
        self.stack = None

    def open(self, stack):
        for e in self.ENG:
            self.sems[e] = stack.enter_context(self.nc.semaphore("s_" + e))
        for j in range(self.ndma):
            self.sems[("d", j)] = stack.enter_context(self.nc.semaphore("s_d%d" % j))

    def _deps(self, eng, reads, writes):
        deps = []
        writes = list(writes) + [t for t in reads if t.excl and t not in writes]
        for t in reads:
            if t.w is not None:
                deps.append(t.w)
        for t in writes:
            if t.w is not None:
                deps.append(t.w)
            deps.extend(t.r.items())
        waits = []
        seen = self.seen[eng]
        for k, v in deps:
            if eng == "pe" and k == "pe":
                continue
            if seen.get(k, 0) >= v:
                continue
            seen[k] = v
            waits.append((k, v))
        m = {}
        for k, v in waits:
            m[k] = max(m.get(k, 0), v)
        return list(m.items())

    def _mark(self, ev, reads, writes):
        k, v = ev
        writes = list(writes) + [t for t in reads if t.excl and t not in writes]
        for t in reads:
            if t.r.get(k, 0) < v:
                t.r[k] = v
        for t in writes:
            t.w = ev
            t.r = {}

    def op(self, eng, fn, reads=(), writes=()):
        rec = _Rec()
        fn(rec)
        name, a, k = rec.call
        fn = lambda e, name=name, a=a, k=k: getattr(e, name)(*a, **k)
        waits = self._deps(eng, reads, writes)
        self.cnt[eng] += 1
        ev = (eng, self.cnt[eng])
        self.ops[eng].append((waits, fn, (eng, 1)))
        self._mark(ev, reads, writes)

    def dma(self, q, pairs, reads=(), writes=(), **kw):
        j = self.rr[q] + (0 if q == "sp" else self.half)
        self.rr[q] = (self.rr[q] + 1) % self.half
        key = ("d", j)
        waits = self._deps(q, reads, writes)
        seen = self.seen[q]
        if self.dcnt[j] > 0 and seen.get(key, 0) < self.dcnt[j]:
            seen[key] = self.dcnt[j]
            waits.append((key, self.dcnt[j]))
        for i, (o, i_) in enumerate(pairs):
            def fn(e, o=o, i_=i_):
                return e.dma_start(out=o, in_=i_, **kw)
            self.ops[q].append((waits if i == 0 else [], fn, (key, 16)))
            self.dcnt[j] += 16
        ev = (key, self.dcnt[j])
        self._mark(ev, reads, writes)

    def wait_tiles(self, eng, tiles):
        waits = self._deps(eng, tiles, tiles)
        if waits:
            self.cnt[eng] += 1
            self.ops[eng].append((waits, lambda e: e.nop(), (eng, 1)))

    def drain_dmas(self):
        waits = []
        seen = self.seen["sp"]
        for j in range(self.ndma):
            key = ("d", j)
            if self.dcnt[j] > 0 and seen.get(key, 0) < self.dcnt[j]:
                seen[key] = self.dcnt[j]
                waits.append((key, self.dcnt[j]))
        if waits:
            self.cnt["sp"] += 1
            self.ops["sp"].append((waits, lambda e: e.nop(), ("sp", 1)))

    def flush(self):
        self.drain_dmas()
        nc = self.nc
        if os.environ.get("KSBUF"):
            print("SBUF remaining at flush:", nc.sbuf_bytes_remaining)
        sems = self.sems
        ops = self.ops

        def replay(name):
            def run(e):
                for waits, fn, inc in ops[name]:
                    for k, v in waits[:-1]:
                        e.wait_ge(sems[k], v)
                    ins = fn(e)
                    if waits:
                        ins._wait_ge(sems[waits[-1][0]], waits[-1][1])
                    ins.then_inc(sems[inc[0]], inc[1])
            return run

        with nc.Block() as block:
            block.tensor(replay("pe"))
            block.scalar(replay("act"))
            block.vector(replay("dve"))
            block.gpsimd(replay("pool"))
            block.sync(replay("sp"))
        self.ops = {e: [] for e in self.ENG}


def _consts():
    c = {}
    c["ident"] = np.eye(128, dtype=np.float32).astype(ml_dtypes.bfloat16)
    pos = np.arange(S, dtype=np.float64)
    inv = 10000.0 ** (-np.arange(0, 128, 2, dtype=np.float64) / 128.0)
    ang = pos[:, None] * inv[None, :]
    rt = np.stack([np.cos(ang), np.sin(ang)], axis=1)
    c["rt"] = np.ascontiguousarray(rt.reshape(NT, 128, 2, 64).transpose(1, 0, 2, 3)).reshape(128, NT * 128).astype(np.float32)
    inv8 = 10000.0 ** (-np.arange(0, 16, 2, dtype=np.float64) / 16.0)
    prow = (np.arange(S) // 64).astype(np.float64)
    pcol = (np.arange(S) % 64).astype(np.float64)
    ct = np.zeros((32, S)); st = np.zeros((32, S))
    for r in range(32):
        p = prow if r < 16 else pcol
        a = p * inv8[r % 8]
        ct[r] = np.cos(a)
        st[r] = -np.sin(a) if (r % 16) < 8 else np.sin(a)
    c["mt"] = np.stack([ct, st], axis=1).astype(np.float32)
    i = np.arange(128, dtype=np.float64)
    e1 = np.maximum(i[None, :] - i[:, None], 0.0)
    e2 = np.maximum(i[:, None] - i[None, :], 0.0)
    c3 = np.tile((i + 1.0)[None, :], (128, 1))
    c4 = np.tile((128.0 - i)[None, :], (128, 1))
    cv = np.zeros((128, 8))
    cv[:, 0] = 127.0 - i
    cv[:, 1] = i
    cv[:, 2] = 255.0 - i
    cv[:, 3] = 127.0 - i
    cv[:, 4] = i
    cv[:, 5] = 128.0 + i
    c["cst"] = np.concatenate([e1, e2, c3, c4, cv], axis=1).astype(np.float32)
    return c


def build(debug=(), stop=None):
    from contextlib import ExitStack
    nc = bass.Bass("TRN2", target_bir_lowering=False)
    dbg_out = {}

    def dram_in(name, shape, dt=F32):
        return nc.dram_tensor(name, list(shape), dt, kind="ExternalInput")

    x_d = dram_in("x", [S, D]).ap()
    c_d = dram_in("c", [1, D])
    ctx_d = dram_in("ctx", [L, D]).ap()
    cctx_d = dram_in("c_ctx", [1, D])
    wada_d = dram_in("w_ada", [D, 6 * D]).ap()
    bada_d = dram_in("b_ada", [1, 6 * D])
    g1_d = dram_in("g_norm1", [1, D])
    win_d = dram_in("w_in", [D, 2592]).ap()
    gq_d = dram_in("g_q", [1, 256])
    wuq_d = dram_in("w_uq", [256, 768]).ap()
    gkv_d = dram_in("g_kv", [1, 256])
    wukv_d = dram_in("w_ukv", [256, 1024]).ap()
    rdec_d = dram_in("ret_decay", [1, 8])
    gret_d = dram_in("g_ret", [1, 512])
    wout_d = dram_in("w_out", [D, D]).ap()
    g2_d = dram_in("g_norm2", [1, D])
    wup_d = dram_in("w_up", [D, 2 * DFF]).ap()
    cw_d = dram_in("conv_w", [3, 2 * DFF])
    cb_d = dram_in("conv_b", [1, 2 * DFF])
    wdn_d = dram_in("w_down", [DFF, D]).ap()
    gf_d = dram_in("g_final", [1, D])
    ident_d = dram_in("k_ident", [128, 128], BF).ap()
    rt_d = dram_in("k_rt", [128, NT * 128]).ap()
    mt_d = dram_in("k_mt", [32, 2, S]).ap()
    cst_d = dram_in("k_cst", [128, 520]).ap()
    out_d = nc.dram_tensor("out", [S, D], F32, kind="ExternalOutput").ap()
    modp_d = nc.dram_tensor("modp_scratch", [1, 4 * D], F32, kind="Internal")

    def bc(t, off, n, parts=128):
        return bass.AP(t, off, [[0, parts], [1, n]])

    es = ExitStack()
    with es:
        sch = Sched(nc)
        sch.open(es)
        op = sch.op

        def sb(name, shape, dt, stack=es):
            return stack.enter_context(nc.sbuf_tensor(name, list(shape), dt))

        def dbg(name, ap, tiles, shape, dt=F32):
            if name not in debug:
                return
            dd = nc.dram_tensor("dbg_" + name, list(shape), dt, kind="ExternalOutput").ap()
            dbg_out[name] = dd
            t = Tile("dbg_" + name)
            sch.dma("sp", [(dd, ap)], reads=tiles, writes=[t])
            sch.wait_tiles("sp", [t])

        ident = sb("ident", [128, 128], BF)
        cst = sb("cst", [128, 520], F32)
        OT = sb("OT", [128, 8, S], BF)
        ARENA = sb("ARENA", [128, 16384], F32)
        t_ident = Tile("ident"); t_cst = Tile("cst")
        sch.dma("sp", [(ident[:], ident_d[:, :])], writes=[t_ident])
        sch.dma("sp", [(cst[:], cst_d[:, :])], writes=[t_cst])
        E1 = cst[:, 0:128]; E2 = cst[:, 128:256]; C3 = cst[:, 256:384]; C4 = cst[:, 384:512]
        CV = cst[:, 512:520]
        hT = ARENA[:, 0:9216].bitcast(BF).rearrange("p (k t) -> p k t", k=8)
        SPARE = ARENA[:, 9216:16384]
        XN = ARENA[:].rearrange("p (t d) -> p t d", t=NT)
        t_hT = Tile("hT")
        t_XN = [Tile("XN%d" % t) for t in range(NT)]
        t_OT = [Tile("OT%d" % j) for j in range(8)]
        t_modp = Tile("modp")

        with ExitStack() as ph:
            PS = ph.enter_context(nc.psum_tensor("psA", [128, 4096], F32))
            t_ps = [Tile("psA%d" % b, True) for b in range(8)]
            cc = sb("cc", [128, 8, 2], F32, ph)
            scs = sb("scs", [128, 8, 2], F32, ph)
            CR = sb("CR", [128, 16, 128], BF, ph)
            WA = [sb("WA%d" % i, [128, 8, 512], BF, ph) for i in range(4)]
            t_WA = [Tile("WA%d" % i) for i in range(4)]
            BB = [sb("BB%d" % i, [128, 512], F32, ph) for i in range(2)]
            t_BB = [Tile("BB%d" % i) for i in range(2)]
            MOD1 = sb("MOD1", [128, 2048], F32, ph)
            MODC = sb("MODC", [128, 2048], F32, ph)
            MT = [sb("MT%d" % i, [128, 512], F32, ph) for i in range(4)]
            t_MT = [Tile("MT%d" % i) for i in range(4)]
            G1B = sb("G1B", [128, 1024], F32, ph)
            S1 = sb("S1", [128, 1024], F32, ph)
            S1C = sb("S1C", [128, 1024], F32, ph)
            XT = [sb("XT%d" % i, [128, 1024], F32, ph) for i in range(4)]
            t_XT = [Tile("XT%d" % i) for i in range(4)]
            TMP = [sb("TMP%d" % i, [128, 1024], F32, ph) for i in range(2)]
            t_TMP = [Tile("TMP%d" % i) for i in range(2)]
            HB = [sb("HB%d" % i, [128, 1024], BF, ph) for i in range(2)]
            t_HB = [Tile("HB%d" % i) for i in range(2)]
            junk = sb("junk", [128, 1024], BF, ph)
            t_junk = Tile("junk")
            ss = sb("ss", [128, NTC], F32, ph)
            rs = sb("rs", [128, NTC], F32, ph)
            t_cc = Tile("cc"); t_scs = Tile("scs"); t_CR = Tile("CR")
            t_MOD1 = Tile("MOD1"); t_MODC = Tile("MODC"); t_G1B = Tile("G1B")
            t_S1 = Tile("S1"); t_S1C = Tile("S1C")
            t_ss = [Tile("ss%d" % t) for t in range(NTC)]
            t_rs = [Tile("rs%d" % t) for t in range(NTC)]
            t_ssall = Tile("ssall")

            sch.dma("sp", [(cc[:, :, 0], c_d.ap().rearrange("o (k p) -> p (o k)", p=128)),
                           (cc[:, :, 1], cctx_d.ap().rearrange("o (k p) -> p (o k)", p=128))],
                    writes=[t_cc], allow_slow_non_contiguous=True)
            sch.dma("sp", [(G1B[:], bc(g1_d, 0, 1024))], writes=[t_G1B])
            op("act", lambda e: e.activation(out=scs[:], in_=cc[:], func=AF.Silu), [t_cc], [t_scs])
            op("dve", lambda e: e.tensor_copy(out=CR[:], in_=scs[:].rearrange("p k v -> p (k v)").unsqueeze(2).to_broadcast([128, 16, 128])),
               [t_scs], [t_CR])
            op("dve", lambda e: e.memset(ss[:], 0.0), [], [t_ssall])
            CRv = CR[:].rearrange("p (k v) r -> p k v r", v=2)
            pbc = [0]
            modp_pend = []

            def ada_block(j):
                w = j % 4
                sch.dma("pool", [(WA[w][:], wada_d[:, j * 512:(j + 1) * 512].rearrange("(k p) n -> p k n", p=128))],
                        writes=[t_WA[w]])
                sch.dma("pool", [(BB[j % 2][:], bc(bada_d, j * 512, 512))], writes=[t_BB[j % 2]])
                while len(modp_pend) > 1:
                    modp_pend.pop(0)()
                for v in ((0, 1) if j < 4 else (0,)):
                    b = 2 + (pbc[0] % 6); pbc[0] += 1
                    for k in range(8):
                        op("pe", lambda e, k=k, v=v, b=b, w=w: e.matmul(PS[:, b * 512:(b + 1) * 512], CRv[:, k, v, :], WA[w][:, k, :],
                                                                        start=(k == 0), stop=(k == 7)),
                           [t_CR, t_WA[w]], [t_ps[b]])
                    if j < 4:
                        dst = (MOD1 if v == 0 else MODC)[:, j * 512:(j + 1) * 512]
                        td = t_MOD1 if v == 0 else t_MODC
                        op("dve", lambda e, b=b, dst=dst, j=j: e.tensor_tensor(out=dst, in0=PS[:, b * 512:(b + 1) * 512], in1=BB[j % 2][:], op=ALU.add),
                           [t_ps[b], t_BB[j % 2]], [td])
                    else:
                        m = j % 4
                        op("dve", lambda e, b=b, m=m, j=j: e.tensor_tensor(out=MT[m][:], in0=PS[:, b * 512:(b + 1) * 512], in1=BB[j % 2][:], op=ALU.add),
                           [t_ps[b], t_BB[j % 2]], [t_MT[m]])
                        modp_pend.append(lambda j=j, m=m: sch.dma("pool", [(modp_d.ap()[0:1, (j - 4) * 512:(j - 3) * 512], MT[m][0:1, :])],
                                                                  reads=[t_MT[m]], writes=[t_modp]))

            for j in range(4):
                ada_block(j)
            ada_rest = list(range(4, 12))
            op("dve", lambda e: e.scalar_tensor_tensor(out=S1[:], in0=MOD1[:, 1024:2048], scalar=1.0, in1=G1B[:], op0=ALU.add, op1=ALU.mult),
               [t_MOD1, t_G1B], [t_S1])
            op("dve", lambda e: e.scalar_tensor_tensor(out=S1C[:], in0=MODC[:, 1024:2048], scalar=1.0, in1=G1B[:], op0=ALU.add, op1=ALU.mult),
               [t_MODC, t_G1B], [t_S1C])

            def norm_a(t, src_ap, src_tiles, scale_ap, t_scale, shift_ap, t_shift):
                i2 = t % 2
                op("act", lambda e: e.activation(out=junk[:], in_=src_ap, func=AF.Square, accum_out=ss[:, t:t + 1]),
                   src_tiles + [t_ssall], [t_junk, t_ss[t]])
                op("act", lambda e: e.activation(out=rs[:, t:t + 1], in_=ss[:, t:t + 1], func=AF.Sqrt, scale=1.0 / D, bias=EPS),
                   [t_ss[t]], [t_rs[t]])
                op("dve", lambda e: e.reciprocal(out=rs[:, t:t + 1], in_=rs[:, t:t + 1]), [t_rs[t]], [t_rs[t]])
                op("dve", lambda e: e.scalar_tensor_tensor(out=TMP[i2][:], in0=src_ap, scalar=rs[:, t:t + 1], in1=scale_ap,
                                                           op0=ALU.mult, op1=ALU.mult),
                   src_tiles + [t_rs[t], t_scale], [t_TMP[i2]])
                op("dve", lambda e: e.tensor_tensor(out=HB[i2][:], in0=TMP[i2][:], in1=shift_ap, op=ALU.add),
                   [t_TMP[i2], t_shift], [t_HB[i2]])

            def norm_b(t, dstT, t_dst, pbank):
                i2 = t % 2
                pv = PS[:, pbank * 512:(pbank + 1) * 512].bitcast(BF)
                for k in range(8):
                    op("pe", lambda e, k=k: e.transpose(pv[:, k * 128:(k + 1) * 128], HB[i2][:, k * 128:(k + 1) * 128], ident[:]),
                       [t_HB[i2], t_ident], [t_ps[pbank]])
                op("act", lambda e: e.activation(out=dstT[:, :, t * 128:(t + 1) * 128], in_=pv.rearrange("p (k t) -> p k t", k=8), func=AF.Copy),
                   [t_ps[pbank]], [t_dst])

            for t in range(NTC + 1):
                if t < NTC:
                    i3 = t % 4
                    src = x_d[t * 128:(t + 1) * 128, :] if t < NT else ctx_d[(t - NT) * 128:(t - NT + 1) * 128, :]
                    sch.dma("sp", [(XT[i3][:], src)], writes=[t_XT[i3]])
                    if t < NT:
                        norm_a(t, XT[i3][:], [t_XT[i3]], S1[:], t_S1, MOD1[:, 0:1024], t_MOD1)
                    else:
                        norm_a(t, XT[i3][:], [t_XT[i3]], S1C[:], t_S1C, MODC[:, 0:1024], t_MODC)
                if t >= 1:
                    norm_b(t - 1, hT, t_hT, (t - 1) % 2)
                if ada_rest and t % 2 == 1:
                    ada_block(ada_rest.pop(0))
            while ada_rest:
                ada_block(ada_rest.pop(0))
            while modp_pend:
                modp_pend.pop(0)()
            dbg("hT", hT, [t_hT], [128, 8, SC], BF)
            sch.flush()
            if stop == "A":
                return nc, dbg_out

        with ExitStack() as ph:
            PS = ph.enter_context(nc.psum_tensor("psB", [128, 4096], F32))
            t_ps = [Tile("psB%d" % b, True) for b in range(8)]
            RT = SPARE[:, 0:2048].rearrange("p (t c f) -> p t c f", t=NT, c=2)
            VR = SPARE[:, 2048:2048 + 4608].bitcast(BF).rearrange("p (t n) -> p t n", t=NTC)
            t_RT = Tile("RT"); t_VR = Tile("VR")
            WB = [sb("WB%d" % i, [128, 8, 512], BF, ph) for i in range(2)]
            t_WB = [Tile("WB%d" % i) for i in range(2)]
            QT = sb("QT", [128, 4, S], BF, ph); t_QT = Tile("QT")
            KR = sb("KR", [128, NT, 512], BF, ph); t_KR = Tile("KR")
            KC = sb("KC", [128, 2, 2, 512], BF, ph); t_KC = Tile("KC")
            SBall = OT[:, 0:4, :].rearrange("p j (c n) -> p (j c) n", n=512)
            t_SBall = [Tile("SBall%d" % c) for c in range(NT)]
            RD = sb("RD", [128, 8], F32, ph); LG = sb("LG", [128, 8], F32, ph); GC = sb("GC", [128, 8], F32, ph)
            GCF = sb("GCF", [128, 2, 512], F32, ph)
            DTm = sb("DTm", [128, 512], BF, ph)
            XFB = sb("XFB", [128, 2, 512], BF, ph)
            ZZ = sb("ZZ", [128, 2, 4], F32, ph)
            ZZF = sb("ZZF", [128, 2, 512], F32, ph)
            WC = sb("WC", [128, 2, 2, 4], F32, ph)
            tmpd = sb("tmpd", [128, 128], F32, ph)
            GRB = sb("GRB", [128, 512], F32, ph)
            t_dec = Tile("dec"); t_tmpd = Tile("tmpd"); t_GRB = Tile("GRB")
            ROT = [sb("ROT%d" % i, [128, 2, 256], F32, ph) for i in range(2)]
            t_ROT = [Tile("ROT%d" % i) for i in range(2)]
            QR = [sb("QR%d" % i, [128, 512], BF, ph) for i in range(3)]
            t_QR = [Tile("QR%d" % i) for i in range(3)]
            SX = [sb("SX%d" % i, [128, 512], F32, ph) for i in range(3)]
            t_SX = [Tile("SX%d" % i) for i in range(3)]
            SXb = [0, 1]
            SXf = [2, 0]
            SFb = [sb("SFb%d" % i, [128, 512], BF, ph) for i in range(2)]
            t_SFb = [Tile("SFb%d" % i) for i in range(2)]
            NB = 3
            AD = [sb("AD%d" % i, [128, 512], BF, ph) for i in range(NB)]
            QF = [sb("QF%d" % i, [128, 512], BF, ph) for i in range(NB)]
            QB = [sb("QB%d" % i, [128, 512], BF, ph) for i in range(NB)]
            KFc = [sb("KFc%d" % i, [128, 512], BF, ph) for i in range(NB)]
            GS = [sb("GS%d" % i, [128, 512], BF, ph) for i in range(NB)]
            YN = [sb("YN%d" % i, [128, 512], F32, ph) for i in range(NB)]
            KTc = [sb("KTc%d" % i, [128, 512], BF, ph) for i in range(NB)]
            ORk = [sb("ORk%d" % i, [128, 512], BF, ph) for i in range(NB)]
            BST = [sb("BST%d" % i, [128, 4, 6], F32, ph) for i in range(NB)]
            MV = [sb("MV%d" % i, [128, 4, 2], F32, ph) for i in range(NB)]
            SD = [sb("SD%d" % i, [128, 4], F32, ph) for i in range(NB)]
            def tl(n):
                return [Tile("%s%d" % (n, i)) for i in range(NB)]
            t_AD, t_QF, t_QB, t_KFc, t_KBc, t_GS, t_YN, t_KTc, t_ORk, t_BST, t_MV, t_SD = [tl(n) for n in
                ("AD", "QF", "QB", "KFc", "KBc", "GS", "YN", "KTc", "ORk", "BST", "MV", "SD")]
            KBc = KFc; t_KBc = t_KFc

            sch.dma("sp", [(RT, rt_d[:, :].rearrange("p (t c f) -> p t c f", t=NT, c=2))], writes=[t_RT])
            sch.dma("sp", [(RD[:], bc(rdec_d, 0, 8))], writes=[t_dec])
            sch.dma("sp", [(GRB[:], bc(gret_d, 0, 512))], writes=[t_GRB])
            def decay_tables():
                op("act", lambda e: e.activation(out=LG[:], in_=RD[:], func=AF.Exp), [t_dec], [t_dec])
                op("dve", lambda e: e.tensor_scalar(out=LG[:], in0=LG[:], scalar1=-1.0, scalar2=None, op0=ALU.mult), [t_dec], [t_dec])
                op("act", lambda e: e.activation(out=GC[:], in_=LG[:], func=AF.Exp, scale=128.0), [t_dec], [t_dec])
                op("dve", lambda e: e.tensor_copy(out=GCF[:].rearrange("p d (h n) -> p (d h) n", h=4),
                                                  in_=GC[:].unsqueeze(2).to_broadcast([128, 8, 128])), [t_dec], [t_dec])
                for h in range(4):
                    op("dve", lambda e, h=h: e.tensor_scalar(out=tmpd[:], in0=E1, scalar1=LG[:, h:h + 1], scalar2=None, op0=ALU.mult),
                       [t_cst, t_dec], [t_tmpd])
                    op("dve", lambda e, h=h: e.scalar_tensor_tensor(out=tmpd[:], in0=E2, scalar=LG[:, 4 + h:5 + h], in1=tmpd[:], op0=ALU.mult, op1=ALU.add),
                       [t_cst, t_dec, t_tmpd], [t_tmpd])
                    op("act", lambda e, h=h: e.activation(out=DTm[:, h * 128:(h + 1) * 128], in_=tmpd[:], func=AF.Exp, bias=LN_S),
                       [t_tmpd], [t_dec])
                    op("act", lambda e, h=h: e.activation(out=XFB[:, 0, h * 128:(h + 1) * 128], in_=C3, func=AF.Exp, scale=LG[:, h:h + 1]),
                       [t_cst, t_dec], [t_dec])
                    op("act", lambda e, h=h: e.activation(out=XFB[:, 1, h * 128:(h + 1) * 128], in_=C4, func=AF.Exp, scale=LG[:, 4 + h:5 + h]),
                       [t_cst, t_dec], [t_dec])
                    for d in range(2):
                        op("act", lambda e, h=h, d=d: e.activation(out=ZZ[:, d, h:h + 1], in_=CV[:, d:d + 1], func=AF.Exp,
                                                                   scale=LG[:, 4 * d + h:4 * d + h + 1], bias=LN_S), [t_cst, t_dec], [t_dec])
                        for t in range(2):
                            op("act", lambda e, h=h, d=d, t=t: e.activation(out=WC[:, d, t, h:h + 1], in_=CV[:, 2 + 2 * d + t:3 + 2 * d + t], func=AF.Exp,
                                                                            scale=LG[:, 4 * d + h:4 * d + h + 1], bias=LN_S), [t_cst, t_dec], [t_dec])
                op("dve", lambda e: e.tensor_copy(out=ZZF[:].rearrange("p d (h n) -> p (d h) n", h=4),
                                                  in_=ZZ[:].rearrange("p d h -> p (d h)").unsqueeze(2).to_broadcast([128, 8, 128])), [t_dec], [t_dec])


            def load_wb(cols, wbi):
                sch.dma("pool", [(WB[wbi][:], win_d[:, cols:cols + 512].rearrange("(k p) n -> p k n", p=128))], writes=[t_WB[wbi]])

            def inproj(cols, wbi, tiles, post, lag=2, extra=None, prefetch=None):
                if prefetch is not None:
                    load_wb(*prefetch)
                pend = []
                for n, t in enumerate(tiles):
                    b = n % 4
                    for k in range(8):
                        op("pe", lambda e, k=k, b=b, t=t: e.matmul(PS[:, b * 512:(b + 1) * 512], hT[:, k, t * 128:(t + 1) * 128], WB[wbi][:, k, :],
                                                                   start=(k == 0), stop=(k == 7)),
                           [t_hT, t_WB[wbi]], [t_ps[b]])
                    pend.append(post(t, b, n))
                    if extra is not None:
                        extra(n)
                    if len(pend) > lag:
                        f = pend.pop(0)
                        if f is not None:
                            f()
                for f in pend:
                    if f is not None:
                        f()

            load_wb(1568, 0)
            load_wb(1056, 1)
            decay_tables()
            inproj(1568, 0, range(NTC),
                   lambda t, b, n: op("act", lambda e: e.activation(out=VR[:, t, :], in_=PS[:, b * 512:(b + 1) * 512], func=AF.Copy),
                                      [t_ps[b]], [t_VR]))
            if stop == "B1":
                sch.flush()
                return nc, dbg_out

            def rope_post(dst, t_dst, dstT, t_dstT):
                def post(t, b, n):
                    do_T = dstT is not None
                    i2 = n % 2
                    i4 = n % 3
                    pv = PS[:, b * 512:(b + 1) * 512].rearrange("p (h c f) -> p h c f", h=4, c=2)
                    x1 = pv[:, :, 0, :]; x2 = pv[:, :, 1, :]
                    cs = RT[:, t, 0, :].unsqueeze(1).to_broadcast([128, 4, 64])
                    sn = RT[:, t, 1, :].unsqueeze(1).to_broadcast([128, 4, 64])
                    ra = ROT[i2][:, 0, :].rearrange("p (h f) -> p h f", h=4)
                    rb = ROT[i2][:, 1, :].rearrange("p (h f) -> p h f", h=4)
                    o = dst(t, i4).rearrange("p (h c f) -> p h c f", h=4, c=2)
                    tds = t_dst(t, i4)
                    op("dve", lambda e: e.tensor_tensor(out=ra, in0=x1, in1=cs, op=ALU.mult), [t_ps[b], t_RT], [t_ROT[i2]])
                    op("dve", lambda e: e.tensor_tensor(out=rb, in0=x2, in1=sn, op=ALU.mult), [t_ps[b], t_RT], [t_ROT[i2]])
                    op("pool", lambda e: e.tensor_tensor(out=o[:, :, 0, :], in0=ra, in1=rb, op=ALU.subtract), [t_ROT[i2]], [tds])
                    i2b = i2
                    op("dve", lambda e: e.tensor_tensor(out=ra, in0=x1, in1=sn, op=ALU.mult), [t_ps[b], t_RT], [t_ROT[i2]])
                    op("dve", lambda e: e.tensor_tensor(out=rb, in0=x2, in1=cs, op=ALU.mult), [t_ps[b], t_RT], [t_ROT[i2]])
                    op("pool", lambda e: e.tensor_tensor(out=o[:, :, 1, :], in0=ra, in1=rb, op=ALU.add), [t_ROT[i2]], [tds])
                    if not do_T:
                        return None

                    def part2():
                        pb = 4 + (n % 2)
                        pvb = PS[:, pb * 512:pb * 512 + 256].bitcast(BF)
                        src = dst(t, i4)
                        for h in range(4):
                            op("pe", lambda e, h=h: e.transpose(pvb[:, h * 128:(h + 1) * 128], src[:, h * 128:(h + 1) * 128], ident[:]),
                               [tds, t_ident], [t_ps[pb]])
                        op("act", lambda e: e.activation(out=dstT[:, :, t * 128:(t + 1) * 128], in_=pvb.rearrange("p (h t) -> p h t", h=4), func=AF.Copy),
                           [t_ps[pb]], [t_dstT])
                    return part2
                return post

            if stop == "B3":
                sch.flush()
                return nc, dbg_out

            kpost = rope_post(lambda t, i2: KR[:, t, :], lambda t, i2: t_KR, None, None)

            def kpost_all(t, b, n):
                if t < NT:
                    return kpost(t, b, n)
                else:
                    tc_ = t - NT
                    for d in range(2):
                        op("dve", lambda e, d=d: e.tensor_tensor(out=KC[:, d, tc_, :].rearrange("p (h n) -> p h n", h=4),
                                                                 in0=PS[:, b * 512:(b + 1) * 512].rearrange("p (h n) -> p h n", h=4),
                                                                 in1=WC[:, d, tc_, :].unsqueeze(2).to_broadcast([128, 4, 128]), op=ALU.mult),
                           [t_ps[b], t_dec], [t_KC])
            inproj(1056, 1, range(NTC), kpost_all, prefetch=(544, 0))
            dbg("VR", VR, [t_VR], [128, NTC, 512], BF)
            dbg("QT", QT[:], [t_QT], [128, 4, S], BF)
            dbg("KR", KR[:], [t_KR], [128, NT, 512], BF)
            dbg("DTm", DTm[:], [t_dec], [128, 512], BF)
            if stop == "B4":
                sch.flush()
                return nc, dbg_out


            def hs(h):
                return slice(h * 128, (h + 1) * 128)

            import os
            KV = os.environ.get("KVAR", "")
            for d, (S32, t_S32) in enumerate(((SX[SXf[0]], t_SX[SXf[0]]), (SX[SXb[0]], t_SX[SXb[0]]))):
                if "noinit" in KV:
                    break
                if "init1" in KV and d == 1:
                    break
                for h in range(4):
                    for t in range(2):
                        op("pe", lambda e, d=d, h=h, t=t: e.matmul(PS[:, 6 * 512 + h * 128:6 * 512 + (h + 1) * 128], KC[:, d, t, hs(h)], VR[:, NT + t, hs(h)],
                                                                   start=(t == 0), stop=(t == 1)), [t_KC, t_VR], [t_ps[6]])
                if "nocopy" in KV:
                    continue
                op("dve", lambda e, S32=S32: e.tensor_copy(out=S32[:], in_=PS[:, 6 * 512:7 * 512]), [t_ps[6]], [t_S32])
                if "noact" in KV:
                    continue
                if d == 0:
                    op("act", lambda e: e.activation(out=SFb[0][:], in_=PS[:, 6 * 512:7 * 512], func=AF.Copy), [t_ps[6]], [t_SFb[0]])
                else:
                    op("act", lambda e: e.activation(out=SBall[:, NT - 1, :], in_=PS[:, 6 * 512:7 * 512], func=AF.Copy), [t_ps[6]], [t_SBall[NT - 1]])
            if stop == "B5a":
                sch.flush()
                return nc, dbg_out
            def bwd_step(c):
                i2 = c % NB
                op("pool", lambda e: e.tensor_tensor(out=KBc[i2][:], in0=KR[:, c, :], in1=ZZF[:, 1, :], op=ALU.mult),
                   [t_KR, t_dec], [t_KBc[i2]])
                pbk = 6 + (c % 2)
                for h in range(4):
                    op("pe", lambda e, h=h: e.matmul(PS[:, pbk * 512 + h * 128:pbk * 512 + (h + 1) * 128], KBc[i2][:, hs(h)], VR[:, c, hs(h)],
                                                     start=True, stop=True), [t_KBc[i2], t_VR], [t_ps[pbk]])
                k_ = NT - 1 - c
                src_ = SXb[k_ % 2]; dst_ = SXb[(k_ + 1) % 2]
                op("dve", lambda e: e.tensor_tensor(out=SX[dst_][:], in0=SX[src_][:], in1=GCF[:, 1, :], op=ALU.mult), [t_SX[src_], t_dec], [t_SX[dst_]])
                op("dve", lambda e: e.tensor_tensor(out=SX[dst_][:], in0=SX[dst_][:], in1=PS[:, pbk * 512:(pbk + 1) * 512], op=ALU.add),
                   [t_SX[dst_], t_ps[pbk]], [t_SX[dst_]])
                op("act", lambda e: e.activation(out=SBall[:, c - 1, :], in_=SX[dst_][:], func=AF.Copy), [t_SX[dst_]], [t_SBall[c - 1]])

            bsteps = list(range(NT - 1, 0, -1))
            inproj(544, 0, range(NT), rope_post(lambda t, i2: QR[i2][:], lambda t, i2: t_QR[i2], QT, t_QT), prefetch=(2080, 1))
            while bsteps:
                bwd_step(bsteps.pop(0))
            dbg("SBall", SBall[:], t_SBall, [128, NT, 512], BF)
            if stop == "B5b":
                sch.flush()
                return nc, dbg_out

            def stage1(c):
                i2 = c % NB
                cs_ = slice(c * 128, (c + 1) * 128)
                pa = c % 2
                pk_ = 6 + (c % 2)
                pkb = PS[:, pk_ * 512:pk_ * 512 + 256].bitcast(BF)
                for h in range(4):
                    op("pe", lambda e, h=h: e.transpose(pkb[:, h * 128:(h + 1) * 128], KR[:, c, h * 128:(h + 1) * 128], ident[:]), [t_KR, t_ident], [t_ps[pk_]])
                op("act", lambda e: e.activation(out=KTc[i2][:], in_=pkb, func=AF.Copy), [t_ps[pk_]], [t_KTc[i2]])
                for h in range(4):
                    op("pe", lambda e, h=h: e.matmul(PS[:, pa * 512 + h * 128:pa * 512 + (h + 1) * 128], KTc[i2][:, h * 128:(h + 1) * 128], QT[:, h, cs_], start=True, stop=True),
                       [t_KTc[i2], t_QT], [t_ps[pa]])
                op("dve", lambda e: e.tensor_tensor(out=AD[i2][:], in0=PS[:, pa * 512:(pa + 1) * 512], in1=DTm[:], op=ALU.mult),
                   [t_ps[pa], t_dec], [t_AD[i2]])
                op("pool", lambda e: e.tensor_tensor(out=QF[i2][:].rearrange("p (h n) -> p h n", h=4), in0=QT[:, :, cs_],
                                                     in1=XFB[:, 0, :].rearrange("p (h n) -> p h n", h=4), op=ALU.mult), [t_QT, t_dec], [t_QF[i2]])
                op("pool", lambda e: e.tensor_tensor(out=QB[i2][:].rearrange("p (h n) -> p h n", h=4), in0=QT[:, :, cs_],
                                                     in1=XFB[:, 1, :].rearrange("p (h n) -> p h n", h=4), op=ALU.mult), [t_QT, t_dec], [t_QB[i2]])
                op("pool", lambda e: e.tensor_tensor(out=KFc[i2][:], in0=KR[:, c, :], in1=ZZF[:, 0, :], op=ALU.mult),
                   [t_KR, t_dec], [t_KFc[i2]])
                pg = 2 + (c % 2)
                for k in range(8):
                    op("pe", lambda e, k=k: e.matmul(PS[:, pg * 512:(pg + 1) * 512], hT[:, k, cs_], WB[1][:, k, :], start=(k == 0), stop=(k == 7)),
                       [t_hT, t_WB[1]], [t_ps[pg]])
                op("act", lambda e: e.activation(out=GS[i2][:], in_=PS[:, pg * 512:(pg + 1) * 512], func=AF.Silu), [t_ps[pg]], [t_GS[i2]])

            def stage2(c):
                i2 = c % NB
                cs_ = slice(c * 128, (c + 1) * 128)
                py = 4 + (c % 2)
                for h in range(4):
                    o = PS[:, py * 512 + h * 128:py * 512 + (h + 1) * 128]
                    op("pe", lambda e, h=h, o=o: e.matmul(o, AD[i2][:, hs(h)], VR[:, c, hs(h)], start=True, stop=False), [t_AD[i2], t_VR], [t_ps[py]])
                    op("pe", lambda e, h=h, o=o: e.matmul(o, QF[i2][:, hs(h)], SFb[c % 2][:, hs(h)], start=False, stop=False), [t_QF[i2], t_SFb[c % 2]], [t_ps[py]])
                    op("pe", lambda e, h=h, o=o: e.matmul(o, QB[i2][:, hs(h)], SBall[:, c, hs(h)], start=False, stop=True), [t_QB[i2], t_SBall[c]], [t_ps[py]])
                if c < NT - 1:
                    pu = 6 + (c % 2)
                    for h in range(4):
                        op("pe", lambda e, h=h: e.matmul(PS[:, pu * 512 + h * 128:pu * 512 + (h + 1) * 128], KFc[i2][:, hs(h)], VR[:, c, hs(h)], start=True, stop=True),
                           [t_KFc[i2], t_VR], [t_ps[pu]])
                    src_ = SXf[c % 2]; dst_ = SXf[(c + 1) % 2]
                    op("dve", lambda e: e.tensor_tensor(out=SX[dst_][:], in0=SX[src_][:], in1=GCF[:, 0, :], op=ALU.mult), [t_SX[src_], t_dec], [t_SX[dst_]])
                    op("dve", lambda e: e.tensor_tensor(out=SX[dst_][:], in0=SX[dst_][:], in1=PS[:, pu * 512:(pu + 1) * 512], op=ALU.add), [t_SX[dst_], t_ps[pu]], [t_SX[dst_]])
                    op("act", lambda e: e.activation(out=SFb[(c + 1) % 2][:], in_=SX[dst_][:], func=AF.Copy), [t_SX[dst_]], [t_SFb[(c + 1) % 2]])
                for h in range(4):
                    op("dve", lambda e, h=h: e.bn_stats(out=BST[i2][:, h, :], in_=PS[:, py * 512 + h * 128:py * 512 + (h + 1) * 128]), [t_ps[py]], [t_BST[i2]])
                for h in range(4):
                    op("dve", lambda e, h=h: e.bn_aggr(out=MV[i2][:, h, :], in_=BST[i2][:, h, :]), [t_BST[i2]], [t_MV[i2]])
                op("act", lambda e: e.activation(out=SD[i2][:], in_=MV[i2][:, :, 1], func=AF.Sqrt, bias=EPS), [t_MV[i2]], [t_SD[i2]])
                op("dve", lambda e: e.reciprocal(out=SD[i2][:], in_=SD[i2][:]), [t_SD[i2]], [t_SD[i2]])
                pv = PS[:, py * 512:(py + 1) * 512].rearrange("p (h n) -> p h n", h=4)
                op("dve", lambda e: e.tensor_tensor(out=YN[i2][:].rearrange("p (h n) -> p h n", h=4), in0=pv,
                                                    in1=MV[i2][:, :, 0:1].to_broadcast([128, 4, 128]), op=ALU.subtract), [t_ps[py], t_MV[i2]], [t_YN[i2]])
                op("dve", lambda e: e.tensor_tensor(out=YN[i2][:].rearrange("p (h n) -> p h n", h=4), in0=YN[i2][:].rearrange("p (h n) -> p h n", h=4),
                                                    in1=SD[i2][:].unsqueeze(2).to_broadcast([128, 4, 128]), op=ALU.mult), [t_YN[i2], t_SD[i2]], [t_YN[i2]])

            def stage3(c):
                i2 = c % NB
                cs_ = slice(c * 128, (c + 1) * 128)
                op("pool", lambda e: e.tensor_tensor(out=YN[i2][:], in0=YN[i2][:], in1=GRB[:], op=ALU.mult), [t_YN[i2], t_GRB], [t_YN[i2]])
                op("pool", lambda e: e.tensor_tensor(out=ORk[i2][:], in0=YN[i2][:], in1=GS[i2][:], op=ALU.mult), [t_YN[i2], t_GS[i2]], [t_ORk[i2]])
                pt_ = 2 + (c % 2)
                pvb = PS[:, pt_ * 512:pt_ * 512 + 256].bitcast(BF)
                for h in range(4):
                    op("pe", lambda e, h=h: e.transpose(pvb[:, h * 128:(h + 1) * 128], ORk[i2][:, hs(h)], ident[:]), [t_ORk[i2], t_ident], [t_ps[pt_]])
                op("act", lambda e: e.activation(out=OT[:, 4:8, cs_], in_=pvb.rearrange("p (h t) -> p h t", h=4), func=AF.Copy),
                   [t_ps[pt_]], t_OT[4:8])

            for c in range(NT + 2):
                if c < NT:
                    stage1(c)
                if 1 <= c <= NT:
                    stage2(c - 1)
                if c >= 2:
                    stage3(c - 2)
            dbg("ORT", OT[:, 4:8, :], t_OT[4:8], [128, 4, S], BF)
            sch.flush()
            if stop == "B":
                return nc, dbg_out

        with ExitStack() as ph:
            PS = ph.enter_context(nc.psum_tensor("psC", [128, 4096], F32))
            t_ps = [Tile("psC%d" % b, True) for b in range(8)]
            VM = SPARE[:, 0:4608].bitcast(BF).rearrange("p (t h e) -> p t h e", t=NTC, h=8)
            t_VM = Tile("VM")
            WM = sb("WM", [128, 8, 512], BF, ph); t_WM = Tile("WM")
            WKP = sb("WKP", [128, 8, 2, 96], BF, ph); t_WKP = Tile("WKP")
            CQT = sb("CQT", [128, 2, S], BF, ph); t_CQT = Tile("CQT")
            CKT = sb("CKT", [128, 2, SC], BF, ph); t_CKT = Tile("CKT")
            WUQ = sb("WUQ", [128, 2, 8, 2, 96], BF, ph); t_WUQ = Tile("WUQ")
            WUKV = sb("WUKV", [128, 2, 8, 128], BF, ph); t_WUKV = Tile("WUKV")
            KPT = sb("KPT", [96, SC], BF, ph); t_KPT = Tile("KPT")
            MTab = sb("MTab", [96, 2, S], F32, ph); t_MTab = Tile("MTab")
            GQK = sb("GQK", [128, 4], F32, ph); t_GQK = Tile("GQK")
            QTm = [sb("QTm%d" % i, [96, S], BF, ph) for i in range(2)]
            KTm = [sb("KTm%d" % i, [96, SC], BF, ph) for i in range(2)]
            VA = [sb("VA%d" % i, [128, NTC, 128], BF, ph) for i in range(2)]
            t_QTm = [Tile("QTm%d" % i) for i in range(2)]
            t_KTm = [Tile("KTm%d" % i) for i in range(2)]
            t_VA = [Tile("VA%d" % i) for i in range(2)]
            PTb = [sb("PTb%d" % i, [128, 512], BF, ph) for i in range(5)]
            t_PTb = [Tile("PTb%d" % i) for i in range(5)]
            RDn = [sb("RDn%d" % i, [128, 512], F32, ph) for i in range(2)]
            t_RDn = [Tile("RDn%d" % i) for i in range(2)]
            T1 = [sb("T1_%d" % i, [96, 512], F32, ph) for i in range(2)]
            T2 = [sb("T2_%d" % i, [96, 512], F32, ph) for i in range(2)]
            t_T1 = [Tile("T1_%d" % i) for i in range(2)]
            t_T2 = [Tile("T2_%d" % i) for i in range(2)]
            CN = [sb("CN%d" % i, [128, 512], BF, ph) for i in range(4)]
            t_CN = [Tile("CN%d" % i) for i in range(4)]
            junk2 = sb("junk2", [128, 256], BF, ph); t_junk2 = Tile("junk2")
            ss2 = sb("ss2", [128, NTC, 2], F32, ph)
            t_ss2 = [Tile("ss2_%d" % t) for t in range(NTC)]
            t_ss2all = Tile("ss2all")

            sch.dma("pool", [(WM[:], win_d[:, 0:512].rearrange("(k p) n -> p k n", p=128))], writes=[t_WM])
            op("dve", lambda e: e.memset(WKP[:], 0.0), [], [t_WKP])
            op("dve", lambda e: e.memset(ss2[:], 0.0), [], [t_ss2all])
            sch.dma("pool", [(WKP[:, :, 0, 64:96], win_d[:, 512:544].rearrange("(k p) n -> p k n", p=128))], writes=[t_WKP])
            for a, b_ in ((64, 72), (72, 64), (80, 88), (88, 80)):
                op("dve", lambda e, a=a, b_=b_: e.tensor_copy(out=WKP[:, :, 1, a:a + 8], in_=WKP[:, :, 0, b_:b_ + 8]), [t_WKP], [t_WKP])
            sch.dma("pool", [(WUQ[:, r, :, 0, :], wuq_d[r * 128:(r + 1) * 128, :].rearrange("p (h e) -> p h e", h=8)) for r in range(2)], writes=[t_WUQ])
            op("pool", lambda e: e.tensor_copy(out=WUQ[:, :, :, 1, 0:64], in_=WUQ[:, :, :, 0, 0:64]), [t_WUQ], [t_WUQ])
            for a, b_ in ((64, 72), (72, 64), (80, 88), (88, 80)):
                op("pool", lambda e, a=a, b_=b_: e.tensor_copy(out=WUQ[:, :, :, 1, a:a + 8], in_=WUQ[:, :, :, 0, b_:b_ + 8]), [t_WUQ], [t_WUQ])
            sch.dma("pool", [(WUKV[:], wukv_d[:, :].rearrange("(r p) (h e) -> p r h e", p=128, h=8))], writes=[t_WUKV])
            sch.dma("sp", [(MTab[64:96, :, :], mt_d[:, :, :])], writes=[t_MTab])
            sch.dma("sp", [(GQK[:, 0:2], gq_d.ap().rearrange("o (r p) -> p (o r)", p=128)),
                           (GQK[:, 2:4], gkv_d.ap().rearrange("o (r p) -> p (o r)", p=128))], writes=[t_GQK], allow_slow_non_contiguous=True)
            for i in range(2):
                op("pool", lambda e, i=i: e.memset(VA[i][:, :, (64 - 64 * i):(128 - 64 * i)], 1.0), [], [t_VA[i]])

            if stop == "C0":
                sch.flush()
                return nc, dbg_out
            def c1_a(t):
                b = t % 3
                i4 = t % 4
                ts_ = slice(t * 128, (t + 1) * 128)
                pb_ = PS[:, b * 512:(b + 1) * 512]
                for k in range(8):
                    op("pe", lambda e, k=k: e.matmul(pb_, hT[:, k, ts_], WM[:, k, :], start=(k == 0), stop=(k == 7)), [t_hT, t_WM], [t_ps[b]])
                for g in range(2):
                    op("act", lambda e, g=g: e.activation(out=junk2[:], in_=pb_[:, g * 256:(g + 1) * 256], func=AF.Square, accum_out=ss2[:, t, g:g + 1]),
                       [t_ps[b], t_ss2all], [t_junk2, t_ss2[t]])
                op("act", lambda e: e.activation(out=ss2[:, t, :], in_=ss2[:, t, :], func=AF.Sqrt, scale=1.0 / 256, bias=EPS), [t_ss2[t]], [t_ss2[t]])
                op("dve", lambda e: e.reciprocal(out=ss2[:, t, :], in_=ss2[:, t, :]), [t_ss2[t]], [t_ss2[t]])
                op("dve", lambda e: e.tensor_tensor(out=CN[i4][:].rearrange("p (g n) -> p g n", g=2), in0=pb_.rearrange("p (g n) -> p g n", g=2),
                                                    in1=ss2[:, t, :].unsqueeze(2).to_broadcast([128, 2, 256]), op=ALU.mult),
                   [t_ps[b], t_ss2[t]], [t_CN[i4]])

            def c1_b(t):
                i4 = t % 4
                ts_ = slice(t * 128, (t + 1) * 128)
                pt_ = 3 + (t % 2)
                pvb = PS[:, pt_ * 512:pt_ * 512 + 256].bitcast(BF)
                for j in range(4):
                    op("pe", lambda e, j=j: e.transpose(pvb[:, j * 128:(j + 1) * 128], CN[i4][:, j * 128:(j + 1) * 128], ident[:]),
                       [t_CN[i4], t_ident], [t_ps[pt_]])
                pv3 = pvb.rearrange("p (j t) -> p j t", j=4)
                if t < NT:
                    op("dve", lambda e: e.tensor_tensor(out=CQT[:, :, ts_], in0=pv3[:, 0:2, :], in1=GQK[:, 0:2].unsqueeze(2).to_broadcast([128, 2, 128]), op=ALU.mult),
                       [t_ps[pt_], t_GQK], [t_CQT])
                op("dve", lambda e: e.tensor_tensor(out=CKT[:, :, ts_], in0=pv3[:, 2:4, :], in1=GQK[:, 2:4].unsqueeze(2).to_broadcast([128, 2, 128]), op=ALU.mult),
                   [t_ps[pt_], t_GQK], [t_CKT])

            for t in range(NTC + 2):
                if t < NTC:
                    c1_a(t)
                if t >= 2:
                    c1_b(t - 2)

            if stop == "C1":
                sch.flush()
                return nc, dbg_out
            def rope_rows(psA, psB, tA, tB, cols, dst, t_dst, n):
                i2 = n % 2
                op("dve", lambda e: e.tensor_tensor(out=T1[i2][64:96, :], in0=psA[64:96, :], in1=MTab[64:96, 0, cols], op=ALU.mult), [tA, t_MTab], [t_T1[i2]])
                op("dve", lambda e: e.tensor_tensor(out=T2[i2][64:96, :], in0=psB[64:96, :], in1=MTab[64:96, 1, cols], op=ALU.mult), [tB, t_MTab], [t_T2[i2]])
                op("pool", lambda e: e.tensor_tensor(out=dst, in0=T1[i2][64:96, :], in1=T2[i2][64:96, :], op=ALU.add), [t_T1[i2], t_T2[i2]], [t_dst])

            for blk in range(5):
                cols = slice(blk * 512, blk * 512 + (512 if blk < 4 else 256))
                ncol = 512 if blk < 4 else 256
                ba = 5 + (blk % 2) * 0
                pA = PS[0:96, 5 * 512:5 * 512 + ncol]; pB = PS[0:96, 6 * 512:6 * 512 + ncol]
                for k in range(8):
                    op("pe", lambda e, k=k, pA=pA, cols=cols: e.matmul(pA, WKP[:, k, 0, :], hT[:, k, cols], start=(k == 0), stop=(k == 7)), [t_WKP, t_hT], [t_ps[5]])
                if blk < 4:
                    for k in range(8):
                        op("pe", lambda e, k=k, pB=pB, cols=cols: e.matmul(pB, WKP[:, k, 1, :], hT[:, k, cols], start=(k == 0), stop=(k == 7)), [t_WKP, t_hT], [t_ps[6]])
                    rope_rows(pA, pB, t_ps[5], t_ps[6], cols, KPT[64:96, cols], t_KPT, blk)
                else:
                    op("act", lambda e, pA=pA, cols=cols: e.activation(out=KPT[64:96, cols], in_=pA[64:96, :], func=AF.Copy), [t_ps[5]], [t_KPT])
            if stop == "C1k":
                sch.flush()
                return nc, dbg_out
            NPRE = 9
            for t in range(NPRE):
                sch.dma("sp", [(XN[:, t, :], x_d[t * 128:(t + 1) * 128, :])], writes=[t_hT, t_XN[t]])
            for t in range(NTC):
                b = t % 3
                ts_ = slice(t * 128, (t + 1) * 128)
                pb_ = PS[:, b * 512:(b + 1) * 512]
                for r in range(2):
                    op("pe", lambda e, r=r, pb_=pb_, ts_=ts_: e.matmul(pb_.rearrange("p (h e) -> p h e", h=8), CKT[:, r, ts_], WUKV[:, r, :, 64:128], start=(r == 0), stop=(r == 1)),
                       [t_CKT, t_WUKV], [t_ps[b]])
                op("act", lambda e, pb_=pb_, t=t: e.activation(out=VM[:, t, :, :], in_=pb_.rearrange("p (h e) -> p h e", h=8), func=AF.Copy), [t_ps[b]], [t_VM])
            dbg("CQT", CQT[:], [t_CQT], [128, 2, S], BF)
            dbg("CKT", CKT[:], [t_CKT], [128, 2, SC], BF)
            dbg("KPT", KPT[64:96, :], [t_KPT], [32, SC], BF)
            dbg("VM", VM, [t_VM], [128, NTC, 8, 64], BF)

            if stop == "C1v":
                sch.flush()
                return nc, dbg_out
            def proj_units(h):
                i2 = h % 2
                units = []

                def q_unit(blk):
                    cols = slice(blk * 512, (blk + 1) * 512)
                    pA = PS[0:96, 5 * 512:6 * 512]; pB = PS[0:96, 6 * 512:7 * 512]
                    for v, (pp, tp) in enumerate(((pA, t_ps[5]), (pB, t_ps[6]))):
                        for r in range(2):
                            op("pe", lambda e, v=v, r=r, pp=pp: e.matmul(pp, WUQ[:, r, h, v, :], CQT[:, r, cols], start=(r == 0), stop=(r == 1)),
                               [t_WUQ, t_CQT], [tp])
                    op("dve", lambda e: e.tensor_copy(out=QTm[i2][0:64, cols], in_=pA[0:64, :]), [t_ps[5]], [t_QTm[i2]])
                    rope_rows(pA, pB, t_ps[5], t_ps[6], cols, QTm[i2][64:96, cols], t_QTm[i2], blk)

                def k_unit(blk):
                    ncol = 512 if blk < 4 else 256
                    cols = slice(blk * 512, blk * 512 + ncol)
                    pk = PS[0:64, 6 * 512:6 * 512 + ncol]
                    for r in range(2):
                        op("pe", lambda e, r=r: e.matmul(pk, WUKV[:, r, h, 0:64], CKT[:, r, cols], start=(r == 0), stop=(r == 1)), [t_WUKV, t_CKT], [t_ps[6]])
                    op("dve", lambda e: e.tensor_copy(out=KTm[i2][0:64, cols], in_=pk), [t_ps[6]], [t_KTm[i2]])

                def v_unit():
                    op("pool", lambda e: e.tensor_copy(out=KTm[i2][64:96, :], in_=KPT[64:96, :]), [t_KPT], [t_KTm[i2]])
                    op("pool", lambda e: e.tensor_copy(out=VA[i2][:, :, 64 * i2:64 * i2 + 64], in_=VM[:, :, h, :]), [t_VM], [t_VA[i2]])

                units.append(v_unit)
                for blk in range(5):
                    units.append(lambda blk=blk: k_unit(blk))
                for blk in range(4):
                    units.append(lambda blk=blk: q_unit(blk))
                return units

            def proj(h):
                for u in proj_units(h):
                    u()

            items = [(h, qb, kt) for h in range(8) for qb in range(4) for kt in range(NTC)]
            LAG = 3
            NPT = 5
            SBANK = (0, 1, 2, 7)

            def emit_S(n):
                h, qb, kt = items[n]
                i2 = h % 2
                b = SBANK[n % 4]
                pS_ = PS[:, b * 512:(b + 1) * 512]
                op("pe", lambda e: e.matmul(pS_, KTm[i2][0:96, kt * 128:(kt + 1) * 128], QTm[i2][0:96, qb * 512:(qb + 1) * 512], start=True, stop=True),
                   [t_KTm[i2], t_QTm[i2]], [t_ps[b]])
                op("act", lambda e: e.activation(out=PTb[n % NPT][:], in_=pS_, func=AF.Exp, scale=MLA_SCALE), [t_ps[b]], [t_PTb[n % NPT]])

            def emit_PV(n):
                h, qb, kt = items[n]
                i2 = h % 2
                po = 3 + (qb % 2)
                pO = PS[:, po * 512:(po + 1) * 512]
                qs = slice(qb * 512, (qb + 1) * 512)
                op("pe", lambda e: e.matmul(pO, VA[i2][:, kt, :], PTb[n % NPT][:], start=(kt == 0), stop=(kt == NTC - 1)), [t_VA[i2], t_PTb[n % NPT]], [t_ps[po]])
                if kt != NTC - 1:
                    return
                r2 = qb % 2
                ro = (h % 2) * 64
                dn = 64 - ro
                op("dve", lambda e: e.tensor_copy(out=RDn[r2][dn:dn + 64, :], in_=pO[dn:dn + 64, :]), [t_ps[po]], [t_RDn[r2]])
                op("dve", lambda e: e.reciprocal(out=RDn[r2][dn:dn + 64, :], in_=RDn[r2][dn:dn + 64, :]), [t_RDn[r2]], [t_RDn[r2]])
                if ro == 64:
                    op("dve", lambda e: e.tensor_copy(out=RDn[r2][64:128, :], in_=RDn[r2][0:64, :]), [t_RDn[r2]], [t_RDn[r2]])
                    dn = 64
                op("dve", lambda e: e.tensor_tensor(out=OT[ro:ro + 64, h // 2, qs], in0=pO[ro:ro + 64, :], in1=RDn[r2][dn:dn + 64, :], op=ALU.mult),
                   [t_ps[po], t_RDn[r2]], [t_OT[h // 2]])

            proj(0)
            if "QTm0" in debug:
                dbg("QTm0", QTm[0][:], [t_QTm[0]], [96, S], BF)
                dbg("KTm0", KTm[0][:], [t_KTm[0]], [96, SC], BF)
            if stop == "C2":
                sch.flush()
                return nc, dbg_out
            pending = []
            for n in range(len(items) + LAG):
                if n < len(items):
                    emit_S(n)
                if n - LAG >= 0:
                    emit_PV(n - LAG)
                    h, qb, kt = items[n - LAG]
                    if qb == 0 and kt == 0 and h + 1 < 8:
                        assert not pending
                        pending = proj_units(h + 1)
                    if pending and (n % 6 == 0):
                        pending.pop(0)()
            dbg("OMT", OT[:, 0:4, :], t_OT[0:4], [128, 4, S], BF)
            sch.flush()
            if stop == "C":
                return nc, dbg_out

        with ExitStack() as ph:
            PS = ph.enter_context(nc.psum_tensor("psD", [128, 4096], F32))
            t_ps = [Tile("psD%d" % b, True) for b in range(8)]
            WO = sb("WO", [128, 8, D], BF, ph); t_WO = Tile("WO")
            GT1 = sb("GT1", [128, D], F32, ph); t_GT1 = Tile("GT1")
            XT = [sb("XTd%d" % i, [128, D], F32, ph) for i in range(3)]
            t_XT = [Tile("XTd%d" % i) for i in range(3)]
            TMP = [sb("TMPd%d" % i, [128, D], F32, ph) for i in range(2)]
            t_TMP = [Tile("TMPd%d" % i) for i in range(2)]
            t_WOh = [Tile("WO0"), Tile("WO1")]
            for hf in range(2):
                sch.dma("pool", [(WO[:, :, hf * 512:(hf + 1) * 512], wout_d[:, hf * 512:(hf + 1) * 512].rearrange("(j p) n -> p j n", p=128))], writes=[t_WOh[hf]])
            sch.dma("sp", [(GT1[:], bc(modp_d, 0, D))], reads=[t_modp], writes=[t_GT1])
            for t in range(NT):
                ts_ = slice(t * 128, (t + 1) * 128)
                i3 = t % 3; i2 = t % 2
                pre = t < 9
                if not pre:
                    sch.dma("sp", [(XT[i3][:], x_d[ts_, :])], writes=[t_XT[i3]])
                b0 = (t % 4) * 2
                for hf in range(2):
                    for j in range(8):
                        op("pe", lambda e, hf=hf, j=j, b0=b0, ts_=ts_: e.matmul(PS[:, (b0 + hf) * 512:(b0 + hf + 1) * 512], OT[:, j, ts_], WO[:, j, hf * 512:(hf + 1) * 512],
                                                                                start=(j == 0), stop=(j == 7)), [t_OT[j], t_WOh[hf]], [t_ps[b0 + hf]])
                op("dve", lambda e, b0=b0, i2=i2: e.tensor_tensor(out=TMP[i2][:], in0=PS[:, b0 * 512:(b0 + 2) * 512], in1=GT1[:], op=ALU.mult),
                   [t_ps[b0], t_ps[b0 + 1], t_GT1], [t_TMP[i2]])
                if pre:
                    op("pool", lambda e, t=t, i2=i2: e.tensor_tensor(out=XN[:, t, :], in0=TMP[i2][:], in1=XN[:, t, :], op=ALU.add),
                       [t_TMP[i2], t_XN[t]], [t_XN[t]])
                else:
                    op("pool", lambda e, t=t, i2=i2, i3=i3: e.tensor_tensor(out=XN[:, t, :], in0=TMP[i2][:], in1=XT[i3][:], op=ALU.add),
                       [t_TMP[i2], t_XT[i3]] + list(t_OT) + [t_hT], [t_XN[t]])
            dbg("XN", XN, t_XN, [128, NT, D], F32)
            sch.flush()
            if stop == "D":
                return nc, dbg_out

        H2T = OT
        t_H2T = Tile("H2T")
        with ExitStack() as ph:
            PS = ph.enter_context(nc.psum_tensor("psE", [128, 4096], F32))
            t_ps = [Tile("psE%d" % b, True) for b in range(8)]
            CW = sb("CW", [128, 44, 3], F32, ph); CB = sb("CB", [128, 44], F32, ph)
            t_CW = Tile("CW")
            GT2 = sb("GT2", [128, D], F32, ph); t_GT2 = Tile("GT2")
            GFB = sb("GFB", [128, D], F32, ph); t_GFB = Tile("GFB")
            def late_loads():
                sch.dma("sp", [(GT2[:], bc(modp_d, 3 * D, D))], reads=[t_modp], writes=[t_GT2])
                sch.dma("sp", [(GFB[:], bc(gf_d, 0, D))], writes=[t_GFB])
                sch.dma("sp", [(CW[:, :, t], cw_d.ap()[t:t + 1, :].rearrange("o (c p) -> p (o c)", p=128)) for t in range(3)] +
                        [(CB[:], cb_d.ap().rearrange("o (c p) -> p (o c)", p=128))], writes=[t_CW], allow_slow_non_contiguous=True)
            ss = sb("ssE", [128, 2 * NT], F32, ph)
            rs = sb("rsE", [128, 2 * NT], F32, ph)
            t_ss = [Tile("ssE%d" % t) for t in range(2 * NT)]
            t_rs = [Tile("rsE%d" % t) for t in range(2 * NT)]
            t_ssall = Tile("ssEall")
            op("dve", lambda e: e.memset(ss[:], 0.0), [], [t_ssall])
            t_junk = Tile("junkE")
            TMP = [sb("TMPe%d" % i, [128, D], F32, ph) for i in range(2)]
            t_TMP = [Tile("TMPe%d" % i) for i in range(2)]
            WU = [sb("WU%d" % i, [128, 8, 2, 128], BF, ph) for i in range(3)]
            t_WU = [Tile("WU%d" % i) for i in range(3)]
            WD = [sb("WD%d" % i, [128, GMAX, D], BF, ph) for i in range(1)]
            t_WD = [Tile("WD%d" % i) for i in range(1)]

            def load_wu(j):
                w = j % 3
                sch.dma("pool", [(WU[w][:, :, 0, :], wup_d[:, j * 128:(j + 1) * 128].rearrange("(k p) n -> p k n", p=128)),
                                 (WU[w][:, :, 1, :], wup_d[:, DFF + j * 128:DFF + (j + 1) * 128].rearrange("(k p) n -> p k n", p=128))], writes=[t_WU[w]])

            def load_wd(gi):
                j0, j1 = GROUPS[gi]
                sch.dma("pool", [(WD[0][:, 0:j1 - j0, :], wdn_d[j0 * 128:j1 * 128, :].rearrange("(j p) n -> p j n", p=128))], writes=[t_WD[0]])

            load_wu(0); load_wu(1); load_wd(0)
            with ExitStack() as ph0:
                junk = sb("junkE", [128, D], BF, ph0)
                S2 = sb("S2", [128, D], F32, ph0); SH2 = sb("SH2", [128, D], F32, ph0); G2B = sb("G2B", [128, D], F32, ph0)
                HB = [sb("HBe%d" % i, [128, D], BF, ph0) for i in range(2)]
                t_HB = [Tile("HBe%d" % i) for i in range(2)]
                t_S2 = Tile("S2"); t_SH2 = Tile("SH2"); t_G2B = Tile("G2B")
                sch.dma("sp", [(SH2[:], bc(modp_d, D, D))], reads=[t_modp], writes=[t_SH2])
                sch.dma("sp", [(S2[:], bc(modp_d, 2 * D, D))], reads=[t_modp], writes=[t_S2])
                sch.dma("sp", [(G2B[:], bc(g2_d, 0, D))], writes=[t_G2B])
                late_loads()
                op("dve", lambda e: e.scalar_tensor_tensor(out=S2[:], in0=S2[:], scalar=1.0, in1=G2B[:], op0=ALU.add, op1=ALU.mult), [t_S2, t_G2B], [t_S2])
                def e0_a(t):
                    i2 = t % 2
                    src = XN[:, t, :]
                    op("act", lambda e: e.activation(out=junk[:], in_=src, func=AF.Square, accum_out=ss[:, t:t + 1]), [t_XN[t], t_ssall], [t_junk, t_ss[t]])
                    op("act", lambda e: e.activation(out=rs[:, t:t + 1], in_=ss[:, t:t + 1], func=AF.Sqrt, scale=1.0 / D, bias=EPS), [t_ss[t]], [t_rs[t]])
                    op("dve", lambda e: e.reciprocal(out=rs[:, t:t + 1], in_=rs[:, t:t + 1]), [t_rs[t]], [t_rs[t]])
                    op("dve", lambda e: e.scalar_tensor_tensor(out=TMP[i2][:], in0=src, scalar=rs[:, t:t + 1], in1=S2[:], op0=ALU.mult, op1=ALU.mult),
                       [t_XN[t], t_rs[t], t_S2], [t_TMP[i2]])
                    op("dve", lambda e: e.tensor_tensor(out=HB[i2][:], in0=TMP[i2][:], in1=SH2[:], op=ALU.add), [t_TMP[i2], t_SH2], [t_HB[i2]])

                def e0_b(t):
                    i2 = t % 2
                    pbk = t % 2
                    pv = PS[:, pbk * 512:(pbk + 1) * 512].bitcast(BF)
                    for k in range(8):
                        op("pe", lambda e, k=k: e.transpose(pv[:, k * 128:(k + 1) * 128], HB[i2][:, k * 128:(k + 1) * 128], ident[:]), [t_HB[i2], t_ident], [t_ps[pbk]])
                    op("act", lambda e: e.activation(out=H2T[:, :, t * 128:(t + 1) * 128], in_=pv.rearrange("p (k t) -> p k t", k=8), func=AF.Copy),
                       [t_ps[pbk]], [t_H2T])

                for t in range(NT + 1):
                    if t < NT:
                        e0_a(t)
                    if t >= 1:
                        e0_b(t - 1)
                dbg("H2T", H2T[:], [t_H2T], [128, 8, S], BF)
                sch.flush()
            with ExitStack() as ph1:
                ACTT = sb("ACTT", [128, GMAX, S], BF, ph1)
                t_ACTT = [Tile("ACTT%d" % i) for i in range(GMAX)]
                AaL = [sb("Aa%d" % i, [128, S], F32, ph1) for i in range(2)]
                GgL = [sb("Gg%d" % i, [128, S], F32, ph1) for i in range(2)]
                t_AaL = [(Tile("Aa%da" % i), Tile("Aa%db" % i)) for i in range(2)]; t_GgL = [(Tile("Gg%da" % i), Tile("Gg%db" % i)) for i in range(2)]
                t_out = Tile("out")
                npair = [0]

                for gi, (j0, j1) in enumerate(GROUPS):
                    ng = j1 - j0
                    wd = 0
                    if gi > 0:
                        load_wd(gi)
                    for j in range(j0, j1):
                        jj = j - j0
                        w = j % 3
                        if j + 2 < NCH:
                            load_wu(j + 2)
                        Aa = AaL[j % 2]; Gg = GgL[j % 2]; t_Aa = t_AaL[j % 2]; t_Gg = t_GgL[j % 2]
                        for half, (ACC, t_ACC, pb0) in enumerate(((Aa, t_Aa, 0), (Gg, t_Gg, 4))):
                            c = half * NCH + j
                            pall = PS[:, pb0 * 512:(pb0 + 4) * 512]
                            for blk in range(4):
                                for k in range(8):
                                    op("pe", lambda e, k=k, blk=blk, half=half, pb0=pb0, w=w: e.matmul(PS[:, (pb0 + blk) * 512:(pb0 + blk + 1) * 512], WU[w][:, k, half, :],
                                                                                                      H2T[:, k, blk * 512:(blk + 1) * 512], start=(k == 0), stop=(k == 7)),
                                       [t_WU[w], t_H2T], [t_ps[pb0 + blk]])
                            H = S // 2
                            tA = t_ps[pb0:pb0 + 2]; tB = t_ps[pb0 + 2:pb0 + 4]
                            tacc = t_ACC
                            op("act", lambda e, ACC=ACC, pall=pall, c=c: e.activation(out=ACC[:, 0:H], in_=pall[:, 0:H], func=AF.Identity, scale=CW[:, c, 1:2], bias=CB[:, c:c + 1]),
                               tA + [t_CW], [tacc[0]])
                            op("act", lambda e, ACC=ACC, pall=pall, c=c: e.activation(out=ACC[:, H:S], in_=pall[:, H:S], func=AF.Identity, scale=CW[:, c, 1:2], bias=CB[:, c:c + 1]),
                               tB + [t_CW], [tacc[1]])
                            op("dve", lambda e, ACC=ACC, pall=pall, c=c: e.scalar_tensor_tensor(out=ACC[:, 1:H], in0=pall[:, 0:H - 1], scalar=CW[:, c, 0:1], in1=ACC[:, 1:H],
                                                                                                op0=ALU.mult, op1=ALU.add), tA + [t_CW, tacc[0]], [tacc[0]])
                            op("dve", lambda e, ACC=ACC, pall=pall, c=c: e.scalar_tensor_tensor(out=ACC[:, H:S], in0=pall[:, H - 1:S - 1], scalar=CW[:, c, 0:1], in1=ACC[:, H:S],
                                                                                                op0=ALU.mult, op1=ALU.add), tA[1:2] + tB + [t_CW, tacc[1]], [tacc[1]])
                            op("dve", lambda e, ACC=ACC, pall=pall, c=c: e.scalar_tensor_tensor(out=ACC[:, 0:H], in0=pall[:, 1:H + 1], scalar=CW[:, c, 2:3], in1=ACC[:, 0:H],
                                                                                                op0=ALU.mult, op1=ALU.add), tA + tB[0:1] + [t_CW, tacc[0]], [tacc[0]])
                            op("dve", lambda e, ACC=ACC, pall=pall, c=c: e.scalar_tensor_tensor(out=ACC[:, H:S - 1], in0=pall[:, H + 1:S], scalar=CW[:, c, 2:3], in1=ACC[:, H:S - 1],
                                                                                                op0=ALU.mult, op1=ALU.add), tB + [t_CW, tacc[1]], [tacc[1]])
                        op("act", lambda e: e.activation(out=Gg[:], in_=Gg[:], func=AF.Silu), list(t_Gg), list(t_Gg))
                        op("pool", lambda e, jj=jj: e.tensor_tensor(out=ACTT[:, jj, :], in0=Aa[:], in1=Gg[:], op=ALU.mult), list(t_Aa) + list(t_Gg), [t_ACTT[jj]])
                    last = gi == len(GROUPS) - 1
                    for t in range(NT):
                        ts_ = slice(t * 128, (t + 1) * 128)
                        b0 = (t % 4) * 2
                        i2 = t % 2
                        for hf in range(2):
                            for jj in range(ng):
                                op("pe", lambda e, hf=hf, jj=jj, b0=b0, ts_=ts_, wd=wd, ng=ng: e.matmul(PS[:, (b0 + hf) * 512:(b0 + hf + 1) * 512], ACTT[:, jj, ts_],
                                                                                                        WD[wd][:, jj, hf * 512:(hf + 1) * 512], start=(jj == 0), stop=(jj == ng - 1)),
                                   [t_ACTT[jj], t_WD[wd]], [t_ps[b0 + hf]])
                        op("dve", lambda e, b0=b0, i2=i2: e.tensor_tensor(out=TMP[i2][:], in0=PS[:, b0 * 512:(b0 + 2) * 512], in1=GT2[:], op=ALU.mult),
                           [t_ps[b0], t_ps[b0 + 1], t_GT2], [t_TMP[i2]])
                        op("pool", lambda e, t=t, i2=i2: e.tensor_tensor(out=XN[:, t, :], in0=XN[:, t, :], in1=TMP[i2][:], op=ALU.add), [t_TMP[i2], t_XN[t]], [t_XN[t]])
                        if last:
                            u = NT + t
                            src = XN[:, t, :]
                            jv = TMP[i2][:, 0:D // 2].bitcast(BF)
                            op("act", lambda e, src=src, u=u, jv=jv: e.activation(out=jv, in_=src, func=AF.Square, accum_out=ss[:, u:u + 1]), [t_XN[t], t_ssall], [t_TMP[i2], t_ss[u]])
                            op("act", lambda e, u=u: e.activation(out=rs[:, u:u + 1], in_=ss[:, u:u + 1], func=AF.Sqrt, scale=1.0 / D, bias=EPS), [t_ss[u]], [t_rs[u]])
                            op("dve", lambda e, u=u: e.reciprocal(out=rs[:, u:u + 1], in_=rs[:, u:u + 1]), [t_rs[u]], [t_rs[u]])
                            op("dve", lambda e, src=src, u=u: e.scalar_tensor_tensor(out=src, in0=src, scalar=rs[:, u:u + 1], in1=GFB[:], op0=ALU.mult, op1=ALU.mult),
                               [t_XN[t], t_rs[u], t_GFB], [t_XN[t]])
                            sch.dma("sp", [(out_d[ts_, :], src)], reads=[t_XN[t]], writes=[t_out])
                sch.wait_tiles("sp", [t_out])
                sch.flush()
    return nc, dbg_out


_CACHE = {}


def _prep_inputs(inputs):
    c = _consts()
    f = lambda a: np.ascontiguousarray(np.asarray(a, dtype=np.float32))
    shared = {
        "c_ctx": f(inputs["c_ctx"]).reshape(1, D),
        "w_ada": f(inputs["w_ada"])[0],
        "b_ada": f(inputs["b_ada"]).reshape(1, 6 * D),
        "g_norm1": f(inputs["g_norm1"]).reshape(1, D),
        "w_in": f(inputs["w_in"])[0],
        "g_q": f(inputs["g_q"]).reshape(1, 256),
        "w_uq": f(inputs["w_uq"])[0],
        "g_kv": f(inputs["g_kv"]).reshape(1, 256),
        "w_ukv": f(inputs["w_ukv"])[0],
        "ret_decay": f(inputs["ret_decay"]).reshape(1, 8),
        "g_ret": f(inputs["g_ret"]).reshape(1, 512),
        "w_out": f(inputs["w_out"])[0],
        "g_norm2": f(inputs["g_norm2"]).reshape(1, D),
        "w_up": f(inputs["w_up"])[0],
        "conv_w": f(inputs["conv_w"])[0],
        "conv_b": f(inputs["conv_b"]).reshape(1, 2 * DFF),
        "w_down": f(inputs["w_down"])[0],
        "g_final": f(inputs["g_final"]).reshape(1, D),
        "k_ident": c["ident"], "k_rt": c["rt"], "k_mt": c["mt"], "k_cst": c["cst"],
    }
    x = f(inputs["x"]); cc = f(inputs["c"]); ctx = f(inputs["ctx"])
    maps = []
    for b in range(8):
        m = dict(shared)
        m["x"] = x[b]
        m["c"] = cc[b].reshape(1, D)
        m["ctx"] = ctx[b]
        maps.append(m)
    return maps


def kernel(**inputs):
    if "nc" not in _CACHE:
        _CACHE["nc"] = build()[0]
    nc = _CACHE["nc"]
    maps = _prep_inputs(inputs)
    res = run_bass_kernel_spmd(nc, maps, core_ids=list(range(8)))
    out = np.stack([np.asarray(r["out"], dtype=np.float32) for r in res.results], axis=0)
    return out
```

```python
import math
import os
import numpy as np
import ml_dtypes
import concourse.bass as bass
import concourse.mybir as mybir
from concourse.bass_utils import run_bass_kernel_spmd

F32 = mybir.dt.float32
BF = mybir.dt.bfloat16
AF = mybir.ActivationFunctionType
ALU = mybir.AluOpType

S = 2048
D = 1024
L = 256
NT = 16
NTC = 18
SC = S + L
DFF = 2816
NCH = 22
EPS = 1e-6
MLA_SCALE = 96 ** -0.5
LN_S = math.log(128 ** -0.5)
GROUPS = [(0, 8), (8, 15), (15, 22)]
GMAX = 8


class Tile:
    __slots__ = ("name", "w", "r", "excl")

    def __init__(self, name, excl=False):
        self.name = name
        self.w = None
        self.r = {}
        self.excl = excl


class _Rec:
    def __getattr__(self, name):
        def call(*a, **k):
            self.__dict__["call"] = (name, a, k)
            return self
        return call


class Sched:
    ENG = ("pe", "act", "dve", "pool", "sp")

    def __init__(self, nc, ndma=48):
        self.nc = nc
        self.ops = {e: [] for e in self.ENG}
        self.cnt = {e: 0 for e in self.ENG}
        self.seen = {e: {} for e in self.ENG}
        self.sems = {}
        self.ndma = ndma
        self.dcnt = [0] * ndma
        self.rr = {"sp": 0, "pool": 0}
        self.half = ndma // 2
        self.stack = None

    def open(self, stack):
        for e in self.ENG:
            self.sems[e] = stack.enter_context(self.nc.semaphore("s_" + e))
        for j in range(self.ndma):
            self.sems[("d", j)] = stack.enter_context(self.nc.semaphore("s_d%d" % j))

    def _deps(self, eng, reads, writes):
        deps = []
        writes = list(writes) + [t for t in reads if t.excl and t not in writes]
        for t in reads:
            if t.w is not None:
                deps.append(t.w)
        for t in writes:
            if t.w is not None:
                deps.append(t.w)
            deps.extend(t.r.items())
        waits = []
        seen = self.seen[eng]
        for k, v in deps:
            if eng == "pe" and k == "pe":
                continue
            if seen.get(k, 0) >= v:
                continue
            seen[k] = v
            waits.append((k, v))
        m = {}
        for k, v in waits:
            m[k] = max(m.get(k, 0), v)
        return list(m.items())

    def _mark(self, ev, reads, writes):
        k, v = ev
        writes = list(writes) + [t for t in reads if t.excl and t not in writes]
        for t in reads:
            if t.r.get(k, 0) < v:
                t.r[k] = v
        for t in writes:
            t.w = ev
            t.r = {}

    def op(self, eng, fn, reads=(), writes=()):
        rec = _Rec()
        fn(rec)
        name, a, k = rec.call
        fn = lambda e, name=name, a=a, k=k: getattr(e, name)(*a, **k)
        waits = self._deps(eng, reads, writes)
        self.cnt[eng] += 1
        ev = (eng, self.cnt[eng])
        self.ops[eng].append((waits, fn, (eng, 1)))
        self._mark(ev, reads, writes)

    def dma(self, q, pairs, reads=(), writes=(), **kw):
        j = self.rr[q] + (0 if q == "sp" else self.half)
        self.rr[q] = (self.rr[q] + 1) % self.half
        key = ("d", j)
        waits = self._deps(q, reads, writes)
        seen = self.seen[q]
        if self.dcnt[j] > 0 and seen.get(key, 0) < self.dcnt[j]:
            seen[key] = self.dcnt[j]
            waits.append((key, self.dcnt[j]))
        for i, (o, i_) in enumerate(pairs):
            def fn(e, o=o, i_=i_):
                return e.dma_start(out=o, in_=i_, **kw)
            self.ops[q].append((waits if i == 0 else [], fn, (key, 16)))
            self.dcnt[j] += 16
        ev = (key, self.dcnt[j])
        self._mark(ev, reads, writes)

    def wait_tiles(self, eng, tiles):
        waits = self._deps(eng, tiles, tiles)
        if waits:
            self.cnt[eng] += 1
            self.ops[eng].append((waits, lambda e: e.nop(), (eng, 1)))

    def drain_dmas(self):
        waits = []
        seen = self.seen["sp"]
        for j in range(self.ndma):
            key = ("d", j)
            if self.dcnt[j] > 0 and seen.get(key, 0) < self.dcnt[j]:
                seen[key] = self.dcnt[j]
                waits.append((key, self.dcnt[j]))
        if waits:
            self.cnt["sp"] += 1
            self.ops["sp"].append((waits, lambda e: e.nop(), ("sp", 1)))

    def flush(self):
        self.drain_dmas()
        nc = self.nc
        if os.environ.get("KSBUF"):
            print("SBUF remaining at flush:", nc.sbuf_bytes_remaining)
        sems = self.sems
        ops = self.ops

        def replay(name):
            def run(e):
                for waits, fn, inc in ops[name]:
                    for k, v in waits[:-1]:
                        e.wait_ge(sems[k], v)
                    ins = fn(e)
                    if waits:
                        ins._wait_ge(sems[waits[-1][0]], waits[-1][1])
                    ins.then_inc(sems[inc[0]], inc[1])
            return run

        with nc.Block() as block:
            block.tensor(replay("pe"))
            block.scalar(replay("act"))
            block.vector(replay("dve"))
            block.gpsimd(replay("pool"))
            block.sync(replay("sp"))
        self.ops = {e: [] for e in self.ENG}


def _consts():
    c = {}
    c["ident"] = np.eye(128, dtype=np.float32).astype(ml_dtypes.bfloat16)
    pos = np.arange(S, dtype=np.float64)
    inv = 10000.0 ** (-np.arange(0, 128, 2, dtype=np.float64) / 128.0)
    ang = pos[:, None] * inv[None, :]
    rt = np.stack([np.cos(ang), np.sin(ang)], axis=1)
    c["rt"] = np.ascontiguousarray(rt.reshape(NT, 128, 2, 64).transpose(1, 0, 2, 3)).reshape(128, NT * 128).astype(np.float32)
    inv8 = 10000.0 ** (-np.arange(0, 16, 2, dtype=np.float64) / 16.0)
    prow = (np.arange(S) // 64).astype(np.float64)
    pcol = (np.arange(S) % 64).astype(np.float64)
    ct = np.zeros((32, S)); st = np.zeros((32, S))
    for r in range(32):
        p = prow if r < 16 else pcol
        a = p * inv8[r % 8]
        ct[r] = np.cos(a)
        st[r] = -np.sin(a) if (r % 16) < 8 else np.sin(a)
    c["mt"] = np.stack([ct, st], axis=1).astype(np.float32)
    i = np.arange(128, dtype=np.float64)
    e1 = np.maximum(i[None, :] - i[:, None], 0.0)
    e2 = np.maximum(i[:, None] - i[None, :], 0.0)
    c3 = np.tile((i + 1.0)[None, :], (128, 1))
    c4 = np.tile((128.0 - i)[None, :], (128, 1))
    cv = np.zeros((128, 8))
    cv[:, 0] = 127.0 - i
    cv[:, 1] = i
    cv[:, 2] = 255.0 - i
    cv[:, 3] = 127.0 - i
    cv[:, 4] = i
    cv[:, 5] = 128.0 + i
    c["cst"] = np.concatenate([e1, e2, c3, c4, cv], axis=1).astype(np.float32)
    return c


def build(debug=(), stop=None):
    from contextlib import ExitStack
    nc = bass.Bass("TRN2", target_bir_lowering=False)
    dbg_out = {}

    def dram_in(name, shape, dt=F32):
        return nc.dram_tensor(name, list(shape), dt, kind="ExternalInput")

    x_d = dram_in("x", [S, D]).ap()
    c_d = dram_in("c", [1, D])
    ctx_d = dram_in("ctx", [L, D]).ap()
    cctx_d = dram_in("c_ctx", [1, D])
    wada_d = dram_in("w_ada", [D, 6 * D]).ap()
    bada_d = dram_in("b_ada", [1, 6 * D])
    g1_d = dram_in("g_norm1", [1, D])
    win_d = dram_in("w_in", [D, 2592]).ap()
    gq_d = dram_in("g_q", [1, 256])
    wuq_d = dram_in("w_uq", [256, 768]).ap()
    gkv_d = dram_in("g_kv", [1, 256])
    wukv_d = dram_in("w_ukv", [256, 1024]).ap()
    rdec_d = dram_in("ret_decay", [1, 8])
    gret_d = dram_in("g_ret", [1, 512])
    wout_d = dram_in("w_out", [D, D]).ap()
    g2_d = dram_in("g_norm2", [1, D])
    wup_d = dram_in("w_up", [D, 2 * DFF]).ap()
    cw_d = dram_in("conv_w", [3, 2 * DFF])
    cb_d = dram_in("conv_b", [1, 2 * DFF])
    wdn_d = dram_in("w_down", [DFF, D]).ap()
    gf_d = dram_in("g_final", [1, D])
    ident_d = dram_in("k_ident", [128, 128], BF).ap()
    rt_d = dram_in("k_rt", [128, NT * 128]).ap()
    mt_d = dram_in("k_mt", [32, 2, S]).ap()
    cst_d = dram_in("k_cst", [128, 520]).ap()
    out_d = nc.dram_tensor("out", [S, D], F32, kind="ExternalOutput").ap()
    modp_d = nc.dram_tensor("modp_scratch", [1, 4 * D], F32, kind="Internal")

    def bc(t, off, n, parts=128):
        return bass.AP(t, off, [[0, parts], [1, n]])

    es = ExitStack()
    with es:
        sch = Sched(nc)
        sch.open(es)
        op = sch.op

        def sb(name, shape, dt, stack=es):
            return stack.enter_context(nc.sbuf_tensor(name, list(shape), dt))

        def dbg(name, ap, tiles, shape, dt=F32):
            if name not in debug:
                return
            dd = nc.dram_tensor("dbg_" + name, list(shape), dt, kind="ExternalOutput").ap()
            dbg_out[name] = dd
            t = Tile("dbg_" + name)
            sch.dma("sp", [(dd, ap)], reads=tiles, writes=[t])
            sch.wait_tiles("sp", [t])

        ident = sb("ident", [128, 128], BF)
        cst = sb("cst", [128, 520], F32)
        OT = sb("OT", [128, 8, S], BF)
        ARENA = sb("ARENA", [128, 16384], F32)
        t_ident = Tile("ident"); t_cst = Tile("cst")
        sch.dma("sp", [(ident[:], ident_d[:, :])], writes=[t_ident])
        sch.dma("sp", [(cst[:], cst_d[:, :])], writes=[t_cst])
        E1 = cst[:, 0:128]; E2 = cst[:, 128:256]; C3 = cst[:, 256:384]; C4 = cst[:, 384:512]
        CV = cst[:, 512:520]
        hT = ARENA[:, 0:9216].bitcast(BF).rearrange("p (k t) -> p k t", k=8)
        SPARE = ARENA[:, 9216:16384]
        XN = ARENA[:].rearrange("p (t d) -> p t d", t=NT)
        t_hT = Tile("hT")
        t_XN = [Tile("XN%d" % t) for t in range(NT)]
        t_OT = [Tile("OT%d" % j) for j in range(8)]
        t_modp = Tile("modp")

        with ExitStack() as ph:
            PS = ph.enter_context(nc.psum_tensor("psA", [128, 4096], F32))
            t_ps = [Tile("psA%d" % b, True) for b in range(8)]
            cc = sb("cc", [128, 8, 2], F32, ph)
            scs = sb("scs", [128, 8, 2], F32, ph)
            CR = sb("CR", [128, 16, 128], BF, ph)
            WA = [sb("WA%d" % i, [128, 8, 512], BF, ph) for i in range(4)]
            t_WA = [Tile("WA%d" % i) for i in range(4)]
            BB = [sb("BB%d" % i, [128, 512], F32, ph) for i in range(2)]
            t_BB = [Tile("BB%d" % i) for i in range(2)]
            MOD1 = sb("MOD1", [128, 2048], F32, ph)
            MODC = sb("MODC", [128, 2048], F32, ph)
            MT = [sb("MT%d" % i, [128, 512], F32, ph) for i in range(4)]
            t_MT = [Tile("MT%d" % i) for i in range(4)]
            G1B = sb("G1B", [128, 1024], F32, ph)
            S1 = sb("S1", [128, 1024], F32, ph)
            S1C = sb("S1C", [128, 1024], F32, ph)
            XT = [sb("XT%d" % i, [128, 1024], F32, ph) for i in range(4)]
            t_XT = [Tile("XT%d" % i) for i in range(4)]
            TMP = [sb("TMP%d" % i, [128, 1024], F32, ph) for i in range(2)]
            t_TMP = [Tile("TMP%d" % i) for i in range(2)]
            HB = [sb("HB%d" % i, [128, 1024], BF, ph) for i in range(2)]
            t_HB = [Tile("HB%d" % i) for i in range(2)]
            junk = sb("junk", [128, 1024], BF, ph)
            t_junk = Tile("junk")
            ss = sb("ss", [128, NTC], F32, ph)
            rs = sb("rs", [128, NTC], F32, ph)
            t_cc = Tile("cc"); t_scs = Tile("scs"); t_CR = Tile("CR")
            t_MOD1 = Tile("MOD1"); t_MODC = Tile("MODC"); t_G1B = Tile("G1B")
            t_S1 = Tile("S1"); t_S1C = Tile("S1C")
            t_ss = [Tile("ss%d" % t) for t in range(NTC)]
            t_rs = [Tile("rs%d" % t) for t in range(NTC)]
            t_ssall = Tile("ssall")

            sch.dma("sp", [(cc[:, :, 0], c_d.ap().rearrange("o (k p) -> p (o k)", p=128)),
                           (cc[:, :, 1], cctx_d.ap().rearrange("o (k p) -> p (o k)", p=128))],
                    writes=[t_cc], allow_slow_non_contiguous=True)
            sch.dma("sp", [(G1B[:], bc(g1_d, 0, 1024))], writes=[t_G1B])
            op("act", lambda e: e.activation(out=scs[:], in_=cc[:], func=AF.Silu), [t_cc], [t_scs])
            op("dve", lambda e: e.tensor_copy(out=CR[:], in_=scs[:].rearrange("p k v -> p (k v)").unsqueeze(2).to_broadcast([128, 16, 128])),
               [t_scs], [t_CR])
            op("dve", lambda e: e.memset(ss[:], 0.0), [], [t_ssall])
            CRv = CR[:].rearrange("p (k v) r -> p k v r", v=2)
            pbc = [0]
            modp_pend = []

            def ada_block(j):
                w = j % 4
                sch.dma("pool", [(WA[w][:], wada_d[:, j * 512:(j + 1) * 512].rearrange("(k p) n -> p k n", p=128))],
                        writes=[t_WA[w]])
                sch.dma("pool", [(BB[j % 2][:], bc(bada_d, j * 512, 512))], writes=[t_BB[j % 2]])
                while len(modp_pend) > 1:
                    modp_pend.pop(0)()
                for v in ((0, 1) if j < 4 else (0,)):
                    b = 2 + (pbc[0] % 6); pbc[0] += 1
                    for k in range(8):
                        op("pe", lambda e, k=k, v=v, b=b, w=w: e.matmul(PS[:, b * 512:(b + 1) * 512], CRv[:, k, v, :], WA[w][:, k, :],
                                                                        start=(k == 0), stop=(k == 7)),
                           [t_CR, t_WA[w]], [t_ps[b]])
                    if j < 4:
                        dst = (MOD1 if v == 0 else MODC)[:, j * 512:(j + 1) * 512]
                        td = t_MOD1 if v == 0 else t_MODC
                        op("dve", lambda e, b=b, dst=dst, j=j: e.tensor_tensor(out=dst, in0=PS[:, b * 512:(b + 1) * 512], in1=BB[j % 2][:], op=ALU.add),
                           [t_ps[b], t_BB[j % 2]], [td])
                    else:
                        m = j % 4
                        op("dve", lambda e, b=b, m=m, j=j: e.tensor_tensor(out=MT[m][:], in0=PS[:, b * 512:(b + 1) * 512], in1=BB[j % 2][:], op=ALU.add),
                           [t_ps[b], t_BB[j % 2]], [t_MT[m]])
                        modp_pend.append(lambda j=j, m=m: sch.dma("pool", [(modp_d.ap()[0:1, (j - 4) * 512:(j - 3) * 512], MT[m][0:1, :])],
                                                                  reads=[t_MT[m]], writes=[t_modp]))

            for j in range(4):
                ada_block(j)
            ada_rest = list(range(4, 12))
            op("dve", lambda e: e.scalar_tensor_tensor(out=S1[:], in0=MOD1[:, 1024:2048], scalar=1.0, in1=G1B[:], op0=ALU.add, op1=ALU.mult),
               [t_MOD1, t_G1B], [t_S1])
            op("dve", lambda e: e.scalar_tensor_tensor(out=S1C[:], in0=MODC[:, 1024:2048], scalar=1.0, in1=G1B[:], op0=ALU.add, op1=ALU.mult),
               [t_MODC, t_G1B], [t_S1C])

            def norm_a(t, src_ap, src_tiles, scale_ap, t_scale, shift_ap, t_shift):
                i2 = t % 2
                op("act", lambda e: e.activation(out=junk[:], in_=src_ap, func=AF.Square, accum_out=ss[:, t:t + 1]),
                   src_tiles + [t_ssall], [t_junk, t_ss[t]])
                op("act", lambda e: e.activation(out=rs[:, t:t + 1], in_=ss[:, t:t + 1], func=AF.Sqrt, scale=1.0 / D, bias=EPS),
                   [t_ss[t]], [t_rs[t]])
                op("dve", lambda e: e.reciprocal(out=rs[:, t:t + 1], in_=rs[:, t:t + 1]), [t_rs[t]], [t_rs[t]])
                op("dve", lambda e: e.scalar_tensor_tensor(out=TMP[i2][:], in0=src_ap, scalar=rs[:, t:t + 1], in1=scale_ap,
                                                           op0=ALU.mult, op1=ALU.mult),
                   src_tiles + [t_rs[t], t_scale], [t_TMP[i2]])
                op("dve", lambda e: e.tensor_tensor(out=HB[i2][:], in0=TMP[i2][:], in1=shift_ap, op=ALU.add),
                   [t_TMP[i2], t_shift], [t_HB[i2]])

            def norm_b(t, dstT, t_dst, pbank):
                i2 = t % 2
                pv = PS[:, pbank * 512:(pbank + 1) * 512].bitcast(BF)
                for k in range(8):
                    op("pe", lambda e, k=k: e.transpose(pv[:, k * 128:(k + 1) * 128], HB[i2][:, k * 128:(k + 1) * 128], ident[:]),
                       [t_HB[i2], t_ident], [t_ps[pbank]])
                op("act", lambda e: e.activation(out=dstT[:, :, t * 128:(t + 1) * 128], in_=pv.rearrange("p (k t) -> p k t", k=8), func=AF.Copy),
                   [t_ps[pbank]], [t_dst])

            for t in range(NTC + 1):
                if t < NTC:
                    i3 = t % 4
                    src = x_d[t * 128:(t + 1) * 128, :] if t < NT else ctx_d[(t - NT) * 128:(t - NT + 1) * 128, :]
                    sch.dma("sp", [(XT[i3][:], src)], writes=[t_XT[i3]])
                    if t < NT:
                        norm_a(t, XT[i3][:], [t_XT[i3]], S1[:], t_S1, MOD1[:, 0:1024], t_MOD1)
                    else:
                        norm_a(t, XT[i3][:], [t_XT[i3]], S1C[:], t_S1C, MODC[:, 0:1024], t_MODC)
                if t >= 1:
                    norm_b(t - 1, hT, t_hT, (t - 1) % 2)
                if ada_rest and t % 2 == 1:
                    ada_block(ada_rest.pop(0))
            while ada_rest:
                ada_block(ada_rest.pop(0))
            while modp_pend:
                modp_pend.pop(0)()
            dbg("hT", hT, [t_hT], [128, 8, SC], BF)
            sch.flush()
            if stop == "A":
                return nc, dbg_out

        with ExitStack() as ph:
            PS = ph.enter_context(nc.psum_tensor("psB", [128, 4096], F32))
            t_ps = [Tile("psB%d" % b, True) for b in range(8)]
            RT = SPARE[:, 0:2048].rearrange("p (t c f) -> p t c f", t=NT, c=2)
            VR = SPARE[:, 2048:2048 + 4608].bitcast(BF).rearrange("p (t n) -> p t n", t=NTC)
            t_RT = Tile("RT"); t_VR = Tile("VR")
            WB = [sb("WB%d" % i, [128, 8, 512], BF, ph) for i in range(2)]
            t_WB = [Tile("WB%d" % i) for i in range(2)]
            QT = sb("QT", [128, 4, S], BF, ph); t_QT = Tile("QT")
            KR = sb("KR", [128, NT, 512], BF, ph); t_KR = Tile("KR")
            KC = sb("KC", [128, 2, 2, 512], BF, ph); t_KC = Tile("KC")
            SBall = OT[:, 0:4, :].rearrange("p j (c n) -> p (j c) n", n=512)
            t_SBall = [Tile("SBall%d" % c) for c in range(NT)]
            RD = sb("RD", [128, 8], F32, ph); LG = sb("LG", [128, 8], F32, ph); GC = sb("GC", [128, 8], F32, ph)
            GCF = sb("GCF", [128, 2, 512], F32, ph)
            DTm = sb("DTm", [128, 512], BF, ph)
            XFB = sb("XFB", [128, 2, 512], BF, ph)
            ZZ = sb("ZZ", [128, 2, 4], F32, ph)
            ZZF = sb("ZZF", [128, 2, 512], F32, ph)
            WC = sb("WC", [128, 2, 2, 4], F32, ph)
            tmpd = sb("tmpd", [128, 128], F32, ph)
            GRB = sb("GRB", [128, 512], F32, ph)
            t_dec = Tile("dec"); t_tmpd = Tile("tmpd"); t_GRB = Tile("GRB")
            ROT = [sb("ROT%d" % i, [128, 2, 256], F32, ph) for i in range(2)]
            t_ROT = [Tile("ROT%d" % i) for i in range(2)]
            QR = [sb("QR%d" % i, [128, 512], BF, ph) for i in range(3)]
            t_QR = [Tile("QR%d" % i) for i in range(3)]
            SX = [sb("SX%d" % i, [128, 512], F32, ph) for i in range(3)]
            t_SX = [Tile("SX%d" % i) for i in range(3)]
            SXb = [0, 1]
            SXf = [2, 0]
            SFb = [sb("SFb%d" % i, [128, 512], BF, ph) for i in range(2)]
            t_SFb = [Tile("SFb%d" % i) for i in range(2)]
            NB = 3
            AD = [sb("AD%d" % i, [128, 512], BF, ph) for i in range(NB)]
            QF = [sb("QF%d" % i, [128, 512], BF, ph) for i in range(NB)]
            QB = [sb("QB%d" % i, [128, 512], BF, ph) for i in range(NB)]
            KFc = [sb("KFc%d" % i, [128, 512], BF, ph) for i in range(NB)]
            GS = [sb("GS%d" % i, [128, 512], BF, ph) for i in range(NB)]
            YN = [sb("YN%d" % i, [128, 512], F32, ph) for i in range(NB)]
            KTc = [sb("KTc%d" % i, [128, 512], BF, ph) for i in range(NB)]
            ORk = [sb("ORk%d" % i, [128, 512], BF, ph) for i in range(NB)]
            BST = [sb("BST%d" % i, [128, 4, 6], F32, ph) for i in range(NB)]
            MV = [sb("MV%d" % i, [128, 4, 2], F32, ph) for i in range(NB)]
            SD = [sb("SD%d" % i, [128, 4], F32, ph) for i in range(NB)]
            def tl(n):
                return [Tile("%s%d" % (n, i)) for i in range(NB)]
            t_AD, t_QF, t_QB, t_KFc, t_KBc, t_GS, t_YN, t_KTc, t_ORk, t_BST, t_MV, t_SD = [tl(n) for n in
                ("AD", "QF", "QB", "KFc", "KBc", "GS", "YN", "KTc", "ORk", "BST", "MV", "SD")]
            KBc = KFc; t_KBc = t_KFc

            sch.dma("sp", [(RT, rt_d[:, :].rearrange("p (t c f) -> p t c f", t=NT, c=2))], writes=[t_RT])
            sch.dma("sp", [(RD[:], bc(rdec_d, 0, 8))], writes=[t_dec])
            sch.dma("sp", [(GRB[:], bc(gret_d, 0, 512))], writes=[t_GRB])
            def decay_tables():
                op("act", lambda e: e.activation(out=LG[:], in_=RD[:], func=AF.Exp), [t_dec], [t_dec])
                op("dve", lambda e: e.tensor_scalar(out=LG[:], in0=LG[:], scalar1=-1.0, scalar2=None, op0=ALU.mult), [t_dec], [t_dec])
                op("act", lambda e: e.activation(out=GC[:], in_=LG[:], func=AF.Exp, scale=128.0), [t_dec], [t_dec])
                op("dve", lambda e: e.tensor_copy(out=GCF[:].rearrange("p d (h n) -> p (d h) n", h=4),
                                                  in_=GC[:].unsqueeze(2).to_broadcast([128, 8, 128])), [t_dec], [t_dec])
                for h in range(4):
                    op("dve", lambda e, h=h: e.tensor_scalar(out=tmpd[:], in0=E1, scalar1=LG[:, h:h + 1], scalar2=None, op0=ALU.mult),
                       [t_cst, t_dec], [t_tmpd])
                    op("dve", lambda e, h=h: e.scalar_tensor_tensor(out=tmpd[:], in0=E2, scalar=LG[:, 4 + h:5 + h], in1=tmpd[:], op0=ALU.mult, op1=ALU.add),
                       [t_cst, t_dec, t_tmpd], [t_tmpd])
                    op("act", lambda e, h=h: e.activation(out=DTm[:, h * 128:(h + 1) * 128], in_=tmpd[:], func=AF.Exp, bias=LN_S),
                       [t_tmpd], [t_dec])
                    op("act", lambda e, h=h: e.activation(out=XFB[:, 0, h * 128:(h + 1) * 128], in_=C3, func=AF.Exp, scale=LG[:, h:h + 1]),
                       [t_cst, t_dec], [t_dec])
                    op("act", lambda e, h=h: e.activation(out=XFB[:, 1, h * 128:(h + 1) * 128], in_=C4, func=AF.Exp, scale=LG[:, 4 + h:5 + h]),
                       [t_cst, t_dec], [t_dec])
                    for d in range(2):
                        op("act", lambda e, h=h, d=d: e.activation(out=ZZ[:, d, h:h + 1], in_=CV[:, d:d + 1], func=AF.Exp,
                                                                   scale=LG[:, 4 * d + h:4 * d + h + 1], bias=LN_S), [t_cst, t_dec], [t_dec])
                        for t in range(2):
                            op("act", lambda e, h=h, d=d, t=t: e.activation(out=WC[:, d, t, h:h + 1], in_=CV[:, 2 + 2 * d + t:3 + 2 * d + t], func=AF.Exp,
                                                                            scale=LG[:, 4 * d + h:4 * d + h + 1], bias=LN_S), [t_cst, t_dec], [t_dec])
                op("dve", lambda e: e.tensor_copy(out=ZZF[:].rearrange("p d (h n) -> p (d h) n", h=4),
                                                  in_=ZZ[:].rearrange("p d h -> p (d h)").unsqueeze(2).to_broadcast([128, 8, 128])), [t_dec], [t_dec])


            def load_wb(cols, wbi):
                sch.dma("pool", [(WB[wbi][:], win_d[:, cols:cols + 512].rearrange("(k p) n -> p k n", p=128))], writes=[t_WB[wbi]])

            def inproj(cols, wbi, tiles, post, lag=2, extra=None, prefetch=None):
                if prefetch is not None:
                    load_wb(*prefetch)
                pend = []
                for n, t in enumerate(tiles):
                    b = n % 4
                    for k in range(8):
                        op("pe", lambda e, k=k, b=b, t=t: e.matmul(PS[:, b * 512:(b + 1) * 512], hT[:, k, t * 128:(t + 1) * 128], WB[wbi][:, k, :],
                                                                   start=(k == 0), stop=(k == 7)),
                           [t_hT, t_WB[wbi]], [t_ps[b]])
                    pend.append(post(t, b, n))
                    if extra is not None:
                        extra(n)
                    if len(pend) > lag:
                        f = pend.pop(0)
                        if f is not None:
                            f()
                for f in pend:
                    if f is not None:
                        f()

            load_wb(1568, 0)
            load_wb(1056, 1)
            decay_tables()
            inproj(1568, 0, range(NTC),
                   lambda t, b, n: op("act", lambda e: e.activation(out=VR[:, t, :], in_=PS[:, b * 512:(b + 1) * 512], func=AF.Copy),
                                      [t_ps[b]], [t_VR]))
            if stop == "B1":
                sch.flush()
                return nc, dbg_out

            def rope_post(dst, t_dst, dstT, t_dstT):
                def post(t, b, n):
                    do_T = dstT is not None
                    i2 = n % 2
                    i4 = n % 3
                    pv = PS[:, b * 512:(b + 1) * 512].rearrange("p (h c f) -> p h c f", h=4, c=2)
                    x1 = pv[:, :, 0, :]; x2 = pv[:, :, 1, :]
                    cs = RT[:, t, 0, :].unsqueeze(1).to_broadcast([128, 4, 64])
                    sn = RT[:, t, 1, :].unsqueeze(1).to_broadcast([128, 4, 64])
                    ra = ROT[i2][:, 0, :].rearrange("p (h f) -> p h f", h=4)
                    rb = ROT[i2][:, 1, :].rearrange("p (h f) -> p h f", h=4)
                    o = dst(t, i4).rearrange("p (h c f) -> p h c f", h=4, c=2)
                    tds = t_dst(t, i4)
                    op("dve", lambda e: e.tensor_tensor(out=ra, in0=x1, in1=cs, op=ALU.mult), [t_ps[b], t_RT], [t_ROT[i2]])
                    op("dve", lambda e: e.tensor_tensor(out=rb, in0=x2, in1=sn, op=ALU.mult), [t_ps[b], t_RT], [t_ROT[i2]])
                    op("pool", lambda e: e.tensor_tensor(out=o[:, :, 0, :], in0=ra, in1=rb, op=ALU.subtract), [t_ROT[i2]], [tds])
                    i2b = i2
                    op("dve", lambda e: e.tensor_tensor(out=ra, in0=x1, in1=sn, op=ALU.mult), [t_ps[b], t_RT], [t_ROT[i2]])
                    op("dve", lambda e: e.tensor_tensor(out=rb, in0=x2, in1=cs, op=ALU.mult), [t_ps[b], t_RT], [t_ROT[i2]])
                    op("pool", lambda e: e.tensor_tensor(out=o[:, :, 1, :], in0=ra, in1=rb, op=ALU.add), [t_ROT[i2]], [tds])
                    if not do_T:
                        return None

                    def part2():
                        pb = 4 + (n % 2)
                        pvb = PS[:, pb * 512:pb * 512 + 256].bitcast(BF)
                        src = dst(t, i4)
                        for h in range(4):
                            op("pe", lambda e, h=h: e.transpose(pvb[:, h * 128:(h + 1) * 128], src[:, h * 128:(h + 1) * 128], ident[:]),
                               [tds, t_ident], [t_ps[pb]])
                        op("act", lambda e: e.activation(out=dstT[:, :, t * 128:(t + 1) * 128], in_=pvb.rearrange("p (h t) -> p h t", h=4), func=AF.Copy),
                           [t_ps[pb]], [t_dstT])
                    return part2
                return post

            if stop == "B3":
                sch.flush()
                return nc, dbg_out

            kpost = rope_post(lambda t, i2: KR[:, t, :], lambda t, i2: t_KR, None, None)

            def kpost_all(t, b, n):
                if t < NT:
                    return kpost(t, b, n)
                else:
                    tc_ = t - NT
                    for d in range(2):
                        op("dve", lambda e, d=d: e.tensor_tensor(out=KC[:, d, tc_, :].rearrange("p (h n) -> p h n", h=4),
                                                                 in0=PS[:, b * 512:(b + 1) * 512].rearrange("p (h n) -> p h n", h=4),
                                                                 in1=WC[:, d, tc_, :].unsqueeze(2).to_broadcast([128, 4, 128]), op=ALU.mult),
                           [t_ps[b], t_dec], [t_KC])
            inproj(1056, 1, range(NTC), kpost_all, prefetch=(544, 0))
            dbg("VR", VR, [t_VR], [128, NTC, 512], BF)
            dbg("QT", QT[:], [t_QT], [128, 4, S], BF)
            dbg("KR", KR[:], [t_KR], [128, NT, 512], BF)
            dbg("DTm", DTm[:], [t_dec], [128, 512], BF)
            if stop == "B4":
                sch.flush()
                return nc, dbg_out


            def hs(h):
                return slice(h * 128, (h + 1) * 128)

            import os
            KV = os.environ.get("KVAR", "")
            for d, (S32, t_S32) in enumerate(((SX[SXf[0]], t_SX[SXf[0]]), (SX[SXb[0]], t_SX[SXb[0]]))):
                if "noinit" in KV:
                    break
                if "init1" in KV and d == 1:
                    break
                for h in range(4):
                    for t in range(2):
                        op("pe", lambda e, d=d, h=h, t=t: e.matmul(PS[:, 6 * 512 + h * 128:6 * 512 + (h + 1) * 128], KC[:, d, t, hs(h)], VR[:, NT + t, hs(h)],
                                                                   start=(t == 0), stop=(t == 1)), [t_KC, t_VR], [t_ps[6]])
                if "nocopy" in KV:
                    continue
                op("dve", lambda e, S32=S32: e.tensor_copy(out=S32[:], in_=PS[:, 6 * 512:7 * 512]), [t_ps[6]], [t_S32])
                if "noact" in KV:
                    continue
                if d == 0:
                    op("act", lambda e: e.activation(out=SFb[0][:], in_=PS[:, 6 * 512:7 * 512], func=AF.Copy), [t_ps[6]], [t_SFb[0]])
                else:
                    op("act", lambda e: e.activation(out=SBall[:, NT - 1, :], in_=PS[:, 6 * 512:7 * 512], func=AF.Copy), [t_ps[6]], [t_SBall[NT - 1]])
            if stop == "B5a":
                sch.flush()
                return nc, dbg_out
            def bwd_step(c):
                i2 = c % NB
                op("pool", lambda e: e.tensor_tensor(out=KBc[i2][:], in0=KR[:, c, :], in1=ZZF[:, 1, :], op=ALU.mult),
                   [t_KR, t_dec], [t_KBc[i2]])
                pbk = 6 + (c % 2)
                for h in range(4):
                    op("pe", lambda e, h=h: e.matmul(PS[:, pbk * 512 + h * 128:pbk * 512 + (h + 1) * 128], KBc[i2][:, hs(h)], VR[:, c, hs(h)],
                                                     start=True, stop=True), [t_KBc[i2], t_VR], [t_ps[pbk]])
                k_ = NT - 1 - c
                src_ = SXb[k_ % 2]; dst_ = SXb[(k_ + 1) % 2]
                op("dve", lambda e: e.tensor_tensor(out=SX[dst_][:], in0=SX[src_][:], in1=GCF[:, 1, :], op=ALU.mult), [t_SX[src_], t_dec], [t_SX[dst_]])
                op("dve", lambda e: e.tensor_tensor(out=SX[dst_][:], in0=SX[dst_][:], in1=PS[:, pbk * 512:(pbk + 1) * 512], op=ALU.add),
                   [t_SX[dst_], t_ps[pbk]], [t_SX[dst_]])
                op("act", lambda e: e.activation(out=SBall[:, c - 1, :], in_=SX[dst_][:], func=AF.Copy), [t_SX[dst_]], [t_SBall[c - 1]])

            bsteps = list(range(NT - 1, 0, -1))
            inproj(544, 0, range(NT), rope_post(lambda t, i2: QR[i2][:], lambda t, i2: t_QR[i2], QT, t_QT), prefetch=(2080, 1))
            while bsteps:
                bwd_step(bsteps.pop(0))
            dbg("SBall", SBall[:], t_SBall, [128, NT, 512], BF)
            if stop == "B5b":
                sch.flush()
                return nc, dbg_out

            def stage1(c):
                i2 = c % NB
                cs_ = slice(c * 128, (c + 1) * 128)
                pa = c % 2
                pk_ = 6 + (c % 2)
                pkb = PS[:, pk_ * 512:pk_ * 512 + 256].bitcast(BF)
                for h in range(4):
                    op("pe", lambda e, h=h: e.transpose(pkb[:, h * 128:(h + 1) * 128], KR[:, c, h * 128:(h + 1) * 128], ident[:]), [t_KR, t_ident], [t_ps[pk_]])
                op("act", lambda e: e.activation(out=KTc[i2][:], in_=pkb, func=AF.Copy), [t_ps[pk_]], [t_KTc[i2]])
                for h in range(4):
                    op("pe", lambda e, h=h: e.matmul(PS[:, pa * 512 + h * 128:pa * 512 + (h + 1) * 128], KTc[i2][:, h * 128:(h + 1) * 128], QT[:, h, cs_], start=True, stop=True),
                       [t_KTc[i2], t_QT], [t_ps[pa]])
                op("dve", lambda e: e.tensor_tensor(out=AD[i2][:], in0=PS[:, pa * 512:(pa + 1) * 512], in1=DTm[:], op=ALU.mult),
                   [t_ps[pa], t_dec], [t_AD[i2]])
                op("pool", lambda e: e.tensor_tensor(out=QF[i2][:].rearrange("p (h n) -> p h n", h=4), in0=QT[:, :, cs_],
                                                     in1=XFB[:, 0, :].rearrange("p (h n) -> p h n", h=4), op=ALU.mult), [t_QT, t_dec], [t_QF[i2]])
                op("pool", lambda e: e.tensor_tensor(out=QB[i2][:].rearrange("p (h n) -> p h n", h=4), in0=QT[:, :, cs_],
                                                     in1=XFB[:, 1, :].rearrange("p (h n) -> p h n", h=4), op=ALU.mult), [t_QT, t_dec], [t_QB[i2]])
                op("pool", lambda e: e.tensor_tensor(out=KFc[i2][:], in0=KR[:, c, :], in1=ZZF[:, 0, :], op=ALU.mult),
                   [t_KR, t_dec], [t_KFc[i2]])
                pg = 2 + (c % 2)
                for k in range(8):
                    op("pe", lambda e, k=k: e.matmul(PS[:, pg * 512:(pg + 1) * 512], hT[:, k, cs_], WB[1][:, k, :], start=(k == 0), stop=(k == 7)),
                       [t_hT, t_WB[1]], [t_ps[pg]])
                op("act", lambda e: e.activation(out=GS[i2][:], in_=PS[:, pg * 512:(pg + 1) * 512], func=AF.Silu), [t_ps[pg]], [t_GS[i2]])

            def stage2(c):
                i2 = c % NB
                cs_ = slice(c * 128, (c + 1) * 128)
                py = 4 + (c % 2)
                for h in range(4):
                    o = PS[:, py * 512 + h * 128:py * 512 + (h + 1) * 128]
                    op("pe", lambda e, h=h, o=o: e.matmul(o, AD[i2][:, hs(h)], VR[:, c, hs(h)], start=True, stop=False), [t_AD[i2], t_VR], [t_ps[py]])
                    op("pe", lambda e, h=h, o=o: e.matmul(o, QF[i2][:, hs(h)], SFb[c % 2][:, hs(h)], start=False, stop=False), [t_QF[i2], t_SFb[c % 2]], [t_ps[py]])
                    op("pe", lambda e, h=h, o=o: e.matmul(o, QB[i2][:, hs(h)], SBall[:, c, hs(h)], start=False, stop=True), [t_QB[i2], t_SBall[c]], [t_ps[py]])
                if c < NT - 1:
                    pu = 6 + (c % 2)
                    for h in range(4):
                        op("pe", lambda e, h=h: e.matmul(PS[:, pu * 512 + h * 128:pu * 512 + (h + 1) * 128], KFc[i2][:, hs(h)], VR[:, c, hs(h)], start=True, stop=True),
                           [t_KFc[i2], t_VR], [t_ps[pu]])
                    src_ = SXf[c % 2]; dst_ = SXf[(c + 1) % 2]
                    op("dve", lambda e: e.tensor_tensor(out=SX[dst_][:], in0=SX[src_][:], in1=GCF[:, 0, :], op=ALU.mult), [t_SX[src_], t_dec], [t_SX[dst_]])
                    op("dve", lambda e: e.tensor_tensor(out=SX[dst_][:], in0=SX[dst_][:], in1=PS[:, pu * 512:(pu + 1) * 512], op=ALU.add), [t_SX[dst_], t_ps[pu]], [t_SX[dst_]])
                    op("act", lambda e: e.activation(out=SFb[(c + 1) % 2][:], in_=SX[dst_][:], func=AF.Copy), [t_SX[dst_]], [t_SFb[(c + 1) % 2]])
                for h in range(4):
                    op("dve", lambda e, h=h: e.bn_stats(out=BST[i2][:, h, :], in_=PS[:, py * 512 + h * 128:py * 512 + (h + 1) * 128]), [t_ps[py]], [t_BST[i2]])
                for h in range(4):
                    op("dve", lambda e, h=h: e.bn_aggr(out=MV[i2][:, h, :], in_=BST[i2][:, h, :]), [t_BST[i2]], [t_MV[i2]])
                op("act", lambda e: e.activation(out=SD[i2][:], in_=MV[i2][:, :, 1], func=AF.Sqrt, bias=EPS), [t_MV[i2]], [t_SD[i2]])
                op("dve", lambda e: e.reciprocal(out=SD[i2][:], in_=SD[i2][:]), [t_SD[i2]], [t_SD[i2]])
                pv = PS[:, py * 512:(py + 1) * 512].rearrange("p (h n) -> p h n", h=4)
                op("dve", lambda e: e.tensor_tensor(out=YN[i2][:].rearrange("p (h n) -> p h n", h=4), in0=pv,
                                                    in1=MV[i2][:, :, 0:1].to_broadcast([128, 4, 128]), op=ALU.subtract), [t_ps[py], t_MV[i2]], [t_YN[i2]])
                op("dve", lambda e: e.tensor_tensor(out=YN[i2][:].rearrange("p (h n) -> p h n", h=4), in0=YN[i2][:].rearrange("p (h n) -> p h n", h=4),
                                                    in1=SD[i2][:].unsqueeze(2).to_broadcast([128, 4, 128]), op=ALU.mult), [t_YN[i2], t_SD[i2]], [t_YN[i2]])

            def stage3(c):
                i2 = c % NB
                cs_ = slice(c * 128, (c + 1) * 128)
                op("pool", lambda e: e.tensor_tensor(out=YN[i2][:], in0=YN[i2][:], in1=GRB[:], op=ALU.mult), [t_YN[i2], t_GRB], [t_YN[i2]])
                op("pool", lambda e: e.tensor_tensor(out=ORk[i2][:], in0=YN[i2][:], in1=GS[i2][:], op=ALU.mult), [t_YN[i2], t_GS[i2]], [t_ORk[i2]])
                pt_ = 2 + (c % 2)
                pvb = PS[:, pt_ * 512:pt_ * 512 + 256].bitcast(BF)
                for h in range(4):
                    op("pe", lambda e, h=h: e.transpose(pvb[:, h * 128:(h + 1) * 128], ORk[i2][:, hs(h)], ident[:]), [t_ORk[i2], t_ident], [t_ps[pt_]])
                op("act", lambda e: e.activation(out=OT[:, 4:8, cs_], in_=pvb.rearrange("p (h t) -> p h t", h=4), func=AF.Copy),
                   [t_ps[pt_]], t_OT[4:8])

            for c in range(NT + 2):
                if c < NT:
                    stage1(c)
                if 1 <= c <= NT:
                    stage2(c - 1)
                if c >= 2:
                    stage3(c - 2)
            dbg("ORT", OT[:, 4:8, :], t_OT[4:8], [128, 4, S], BF)
            sch.flush()
            if stop == "B":
                return nc, dbg_out

        with ExitStack() as ph:
            PS = ph.enter_context(nc.psum_tensor("psC", [128, 4096], F32))
            t_ps = [Tile("psC%d" % b, True) for b in range(8)]
            VM = SPARE[:, 0:4608].bitcast(BF).rearrange("p (t h e) -> p t h e", t=NTC, h=8)
            t_VM = Tile("VM")
            WM = sb("WM", [128, 8, 512], BF, ph); t_WM = Tile("WM")
            WKP = sb("WKP", [128, 8, 2, 96], BF, ph); t_WKP = Tile("WKP")
            CQT = sb("CQT", [128, 2, S], BF, ph); t_CQT = Tile("CQT")
            CKT = sb("CKT", [128, 2, SC], BF, ph); t_CKT = Tile("CKT")
            WUQ = sb("WUQ", [128, 2, 8, 2, 96], BF, ph); t_WUQ = Tile("WUQ")
            WUKV = sb("WUKV", [128, 2, 8, 128], BF, ph); t_WUKV = Tile("WUKV")
            KPT = sb("KPT", [96, SC], BF, ph); t_KPT = Tile("KPT")
            MTab = sb("MTab", [96, 2, S], F32, ph); t_MTab = Tile("MTab")
            GQK = sb("GQK", [128, 4], F32, ph); t_GQK = Tile("GQK")
            QTm = [sb("QTm%d" % i, [96, S], BF, ph) for i in range(2)]
            KTm = [sb("KTm%d" % i, [96, SC], BF, ph) for i in range(2)]
            VA = [sb("VA%d" % i, [128, NTC, 128], BF, ph) for i in range(2)]
            t_QTm = [Tile("QTm%d" % i) for i in range(2)]
            t_KTm = [Tile("KTm%d" % i) for i in range(2)]
            t_VA = [Tile("VA%d" % i) for i in range(2)]
            PTb = [sb("PTb%d" % i, [128, 512], BF, ph) for i in range(6)]
            t_PTb = [Tile("PTb%d" % i) for i in range(6)]
            RDn = [sb("RDn%d" % i, [128, 512], F32, ph) for i in range(2)]
            t_RDn = [Tile("RDn%d" % i) for i in range(2)]
            T1 = [sb("T1_%d" % i, [96, 512], F32, ph) for i in range(2)]
            T2 = [sb("T2_%d" % i, [96, 512], F32, ph) for i in range(2)]
            t_T1 = [Tile("T1_%d" % i) for i in range(2)]
            t_T2 = [Tile("T2_%d" % i) for i in range(2)]
            CN = [sb("CN%d" % i, [128, 512], BF, ph) for i in range(4)]
            t_CN = [Tile("CN%d" % i) for i in range(4)]
            junk2 = sb("junk2", [128, 256], BF, ph); t_junk2 = Tile("junk2")
            ss2 = sb("ss2", [128, NTC, 2], F32, ph)
            t_ss2 = [Tile("ss2_%d" % t) for t in range(NTC)]
            t_ss2all = Tile("ss2all")

            sch.dma("pool", [(WM[:], win_d[:, 0:512].rearrange("(k p) n -> p k n", p=128))], writes=[t_WM])
            op("dve", lambda e: e.memset(WKP[:], 0.0), [], [t_WKP])
            op("dve", lambda e: e.memset(ss2[:], 0.0), [], [t_ss2all])
            sch.dma("pool", [(WKP[:, :, 0, 64:96], win_d[:, 512:544].rearrange("(k p) n -> p k n", p=128))], writes=[t_WKP])
            for a, b_ in ((64, 72), (72, 64), (80, 88), (88, 80)):
                op("dve", lambda e, a=a, b_=b_: e.tensor_copy(out=WKP[:, :, 1, a:a + 8], in_=WKP[:, :, 0, b_:b_ + 8]), [t_WKP], [t_WKP])
            sch.dma("pool", [(WUQ[:, r, :, 0, :], wuq_d[r * 128:(r + 1) * 128, :].rearrange("p (h e) -> p h e", h=8)) for r in range(2)], writes=[t_WUQ])
            op("pool", lambda e: e.tensor_copy(out=WUQ[:, :, :, 1, 0:64], in_=WUQ[:, :, :, 0, 0:64]), [t_WUQ], [t_WUQ])
            for a, b_ in ((64, 72), (72, 64), (80, 88), (88, 80)):
                op("pool", lambda e, a=a, b_=b_: e.tensor_copy(out=WUQ[:, :, :, 1, a:a + 8], in_=WUQ[:, :, :, 0, b_:b_ + 8]), [t_WUQ], [t_WUQ])
            sch.dma("pool", [(WUKV[:], wukv_d[:, :].rearrange("(r p) (h e) -> p r h e", p=128, h=8))], writes=[t_WUKV])
            sch.dma("sp", [(MTab[64:96, :, :], mt_d[:, :, :])], writes=[t_MTab])
            sch.dma("sp", [(GQK[:, 0:2], gq_d.ap().rearrange("o (r p) -> p (o r)", p=128)),
                           (GQK[:, 2:4], gkv_d.ap().rearrange("o (r p) -> p (o r)", p=128))], writes=[t_GQK], allow_slow_non_contiguous=True)
            for i in range(2):
                op("pool", lambda e, i=i: e.memset(VA[i][:, :, (64 - 64 * i):(128 - 64 * i)], 1.0), [], [t_VA[i]])

            if stop == "C0":
                sch.flush()
                return nc, dbg_out
            def c1_a(t):
                b = t % 3
                i4 = t % 4
                ts_ = slice(t * 128, (t + 1) * 128)
                pb_ = PS[:, b * 512:(b + 1) * 512]
                for k in range(8):
                    op("pe", lambda e, k=k: e.matmul(pb_, hT[:, k, ts_], WM[:, k, :], start=(k == 0), stop=(k == 7)), [t_hT, t_WM], [t_ps[b]])
                for g in range(2):
                    op("act", lambda e, g=g: e.activation(out=junk2[:], in_=pb_[:, g * 256:(g + 1) * 256], func=AF.Square, accum_out=ss2[:, t, g:g + 1]),
                       [t_ps[b], t_ss2all], [t_junk2, t_ss2[t]])
                op("act", lambda e: e.activation(out=ss2[:, t, :], in_=ss2[:, t, :], func=AF.Sqrt, scale=1.0 / 256, bias=EPS), [t_ss2[t]], [t_ss2[t]])
                op("dve", lambda e: e.reciprocal(out=ss2[:, t, :], in_=ss2[:, t, :]), [t_ss2[t]], [t_ss2[t]])
                op("dve", lambda e: e.tensor_tensor(out=CN[i4][:].rearrange("p (g n) -> p g n", g=2), in0=pb_.rearrange("p (g n) -> p g n", g=2),
                                                    in1=ss2[:, t, :].unsqueeze(2).to_broadcast([128, 2, 256]), op=ALU.mult),
                   [t_ps[b], t_ss2[t]], [t_CN[i4]])

            def c1_b(t):
                i4 = t % 4
                ts_ = slice(t * 128, (t + 1) * 128)
                pt_ = 3 + (t % 2)
                pvb = PS[:, pt_ * 512:pt_ * 512 + 256].bitcast(BF)
                for j in range(4):
                    op("pe", lambda e, j=j: e.transpose(pvb[:, j * 128:(j + 1) * 128], CN[i4][:, j * 128:(j + 1) * 128], ident[:]),
                       [t_CN[i4], t_ident], [t_ps[pt_]])
                pv3 = pvb.rearrange("p (j t) -> p j t", j=4)
                if t < NT:
                    op("dve", lambda e: e.tensor_tensor(out=CQT[:, :, ts_], in0=pv3[:, 0:2, :], in1=GQK[:, 0:2].unsqueeze(2).to_broadcast([128, 2, 128]), op=ALU.mult),
                       [t_ps[pt_], t_GQK], [t_CQT])
                op("dve", lambda e: e.tensor_tensor(out=CKT[:, :, ts_], in0=pv3[:, 2:4, :], in1=GQK[:, 2:4].unsqueeze(2).to_broadcast([128, 2, 128]), op=ALU.mult),
                   [t_ps[pt_], t_GQK], [t_CKT])

            for t in range(NTC + 2):
                if t < NTC:
                    c1_a(t)
                if t >= 2:
                    c1_b(t - 2)

            if stop == "C1":
                sch.flush()
                return nc, dbg_out
            def rope_rows(psA, psB, tA, tB, cols, dst, t_dst, n):
                i2 = n % 2
                op("dve", lambda e: e.tensor_tensor(out=T1[i2][64:96, :], in0=psA[64:96, :], in1=MTab[64:96, 0, cols], op=ALU.mult), [tA, t_MTab], [t_T1[i2]])
                op("dve", lambda e: e.tensor_tensor(out=T2[i2][64:96, :], in0=psB[64:96, :], in1=MTab[64:96, 1, cols], op=ALU.mult), [tB, t_MTab], [t_T2[i2]])
                op("pool", lambda e: e.tensor_tensor(out=dst, in0=T1[i2][64:96, :], in1=T2[i2][64:96, :], op=ALU.add), [t_T1[i2], t_T2[i2]], [t_dst])

            for blk in range(5):
                cols = slice(blk * 512, blk * 512 + (512 if blk < 4 else 256))
                ncol = 512 if blk < 4 else 256
                ba = 5 + (blk % 2) * 0
                pA = PS[0:96, 5 * 512:5 * 512 + ncol]; pB = PS[0:96, 6 * 512:6 * 512 + ncol]
                for k in range(8):
                    op("pe", lambda e, k=k, pA=pA, cols=cols: e.matmul(pA, WKP[:, k, 0, :], hT[:, k, cols], start=(k == 0), stop=(k == 7)), [t_WKP, t_hT], [t_ps[5]])
                if blk < 4:
                    for k in range(8):
                        op("pe", lambda e, k=k, pB=pB, cols=cols: e.matmul(pB, WKP[:, k, 1, :], hT[:, k, cols], start=(k == 0), stop=(k == 7)), [t_WKP, t_hT], [t_ps[6]])
                    rope_rows(pA, pB, t_ps[5], t_ps[6], cols, KPT[64:96, cols], t_KPT, blk)
                else:
                    op("act", lambda e, pA=pA, cols=cols: e.activation(out=KPT[64:96, cols], in_=pA[64:96, :], func=AF.Copy), [t_ps[5]], [t_KPT])
            if stop == "C1k":
                sch.flush()
                return nc, dbg_out
            NPRE = 9
            for t in range(NPRE):
                sch.dma("sp", [(XN[:, t, :], x_d[t * 128:(t + 1) * 128, :])], writes=[t_hT, t_XN[t]])
            for t in range(NTC):
                b = t % 3
                ts_ = slice(t * 128, (t + 1) * 128)
                pb_ = PS[:, b * 512:(b + 1) * 512]
                for r in range(2):
                    op("pe", lambda e, r=r, pb_=pb_, ts_=ts_: e.matmul(pb_.rearrange("p (h e) -> p h e", h=8), CKT[:, r, ts_], WUKV[:, r, :, 64:128], start=(r == 0), stop=(r == 1)),
                       [t_CKT, t_WUKV], [t_ps[b]])
                op("act", lambda e, pb_=pb_, t=t: e.activation(out=VM[:, t, :, :], in_=pb_.rearrange("p (h e) -> p h e", h=8), func=AF.Copy), [t_ps[b]], [t_VM])
            dbg("CQT", CQT[:], [t_CQT], [128, 2, S], BF)
            dbg("CKT", CKT[:], [t_CKT], [128, 2, SC], BF)
            dbg("KPT", KPT[64:96, :], [t_KPT], [32, SC], BF)
            dbg("VM", VM, [t_VM], [128, NTC, 8, 64], BF)

            if stop == "C1v":
                sch.flush()
                return nc, dbg_out
            def proj_units(h):
                i2 = h % 2
                units = []

                def q_unit(blk):
                    cols = slice(blk * 512, (blk + 1) * 512)
                    pA = PS[0:96, 5 * 512:6 * 512]; pB = PS[0:96, 6 * 512:7 * 512]
                    for v, (pp, tp) in enumerate(((pA, t_ps[5]), (pB, t_ps[6]))):
                        for r in range(2):
                            op("pe", lambda e, v=v, r=r, pp=pp: e.matmul(pp, WUQ[:, r, h, v, :], CQT[:, r, cols], start=(r == 0), stop=(r == 1)),
                               [t_WUQ, t_CQT], [tp])
                    op("dve", lambda e: e.tensor_copy(out=QTm[i2][0:64, cols], in_=pA[0:64, :]), [t_ps[5]], [t_QTm[i2]])
                    rope_rows(pA, pB, t_ps[5], t_ps[6], cols, QTm[i2][64:96, cols], t_QTm[i2], blk)

                def k_unit(blk):
                    ncol = 512 if blk < 4 else 256
                    cols = slice(blk * 512, blk * 512 + ncol)
                    pk = PS[0:64, 6 * 512:6 * 512 + ncol]
                    for r in range(2):
                        op("pe", lambda e, r=r: e.matmul(pk, WUKV[:, r, h, 0:64], CKT[:, r, cols], start=(r == 0), stop=(r == 1)), [t_WUKV, t_CKT], [t_ps[6]])
                    op("dve", lambda e: e.tensor_copy(out=KTm[i2][0:64, cols], in_=pk), [t_ps[6]], [t_KTm[i2]])

                def v_unit():
                    op("pool", lambda e: e.tensor_copy(out=KTm[i2][64:96, :], in_=KPT[64:96, :]), [t_KPT], [t_KTm[i2]])
                    op("pool", lambda e: e.tensor_copy(out=VA[i2][:, :, 64 * i2:64 * i2 + 64], in_=VM[:, :, h, :]), [t_VM], [t_VA[i2]])

                units.append(v_unit)
                for blk in range(5):
                    units.append(lambda blk=blk: k_unit(blk))
                for blk in range(4):
                    units.append(lambda blk=blk: q_unit(blk))
                return units

            def proj(h):
                for u in proj_units(h):
                    u()

            items = [(h, qb, kt) for h in range(8) for qb in range(4) for kt in range(NTC)]
            LAG = 3
            NPT = 6
            SBANK = (0, 1, 2, 7)

            def emit_S(n):
                h, qb, kt = items[n]
                i2 = h % 2
                b = SBANK[n % 4]
                pS_ = PS[:, b * 512:(b + 1) * 512]
                op("pe", lambda e: e.matmul(pS_, KTm[i2][0:96, kt * 128:(kt + 1) * 128], QTm[i2][0:96, qb * 512:(qb + 1) * 512], start=True, stop=True),
                   [t_KTm[i2], t_QTm[i2]], [t_ps[b]])
                op("act", lambda e: e.activation(out=PTb[n % NPT][:], in_=pS_, func=AF.Exp, scale=MLA_SCALE), [t_ps[b]], [t_PTb[n % NPT]])

            def emit_PV(n):
                h, qb, kt = items[n]
                i2 = h % 2
                po = 3 + (qb % 2)
                pO = PS[:, po * 512:(po + 1) * 512]
                qs = slice(qb * 512, (qb + 1) * 512)
                op("pe", lambda e: e.matmul(pO, VA[i2][:, kt, :], PTb[n % NPT][:], start=(kt == 0), stop=(kt == NTC - 1)), [t_VA[i2], t_PTb[n % NPT]], [t_ps[po]])
                if kt != NTC - 1:
                    return
                r2 = qb % 2
                ro = (h % 2) * 64
                dn = 64 - ro
                op("dve", lambda e: e.tensor_copy(out=RDn[r2][dn:dn + 64, :], in_=pO[dn:dn + 64, :]), [t_ps[po]], [t_RDn[r2]])
                op("dve", lambda e: e.reciprocal(out=RDn[r2][dn:dn + 64, :], in_=RDn[r2][dn:dn + 64, :]), [t_RDn[r2]], [t_RDn[r2]])
                if ro == 64:
                    op("dve", lambda e: e.tensor_copy(out=RDn[r2][64:128, :], in_=RDn[r2][0:64, :]), [t_RDn[r2]], [t_RDn[r2]])
                    dn = 64
                op("dve", lambda e: e.tensor_tensor(out=OT[ro:ro + 64, h // 2, qs], in0=pO[ro:ro + 64, :], in1=RDn[r2][dn:dn + 64, :], op=ALU.mult),
                   [t_ps[po], t_RDn[r2]], [t_OT[h // 2]])

            proj(0)
            if "QTm0" in debug:
                dbg("QTm0", QTm[0][:], [t_QTm[0]], [96, S], BF)
                dbg("KTm0", KTm[0][:], [t_KTm[0]], [96, SC], BF)
            if stop == "C2":
                sch.flush()
                return nc, dbg_out
            pending = []
            for n in range(len(items) + LAG):
                if n < len(items):
                    emit_S(n)
                if n - LAG >= 0:
                    emit_PV(n - LAG)
                    h, qb, kt = items[n - LAG]
                    if qb == 0 and kt == 0 and h + 1 < 8:
                        assert not pending
                        pending = proj_units(h + 1)
                    if pending and (n % 6 == 0):
                        pending.pop(0)()
            dbg("OMT", OT[:, 0:4, :], t_OT[0:4], [128, 4, S], BF)
            sch.flush()
            if stop == "C":
                return nc, dbg_out

        with ExitStack() as ph:
            PS = ph.enter_context(nc.psum_tensor("psD", [128, 4096], F32))
            t_ps = [Tile("psD%d" % b, True) for b in range(8)]
            WO = sb("WO", [128, 8, D], BF, ph); t_WO = Tile("WO")
            GT1 = sb("GT1", [128, D], F32, ph); t_GT1 = Tile("GT1")
            XT = [sb("XTd%d" % i, [128, D], F32, ph) for i in range(3)]
            t_XT = [Tile("XTd%d" % i) for i in range(3)]
            TMP = [sb("TMPd%d" % i, [128, D], F32, ph) for i in range(2)]
            t_TMP = [Tile("TMPd%d" % i) for i in range(2)]
            t_WOh = [Tile("WO0"), Tile("WO1")]
            for hf in range(2):
                sch.dma("pool", [(WO[:, :, hf * 512:(hf + 1) * 512], wout_d[:, hf * 512:(hf + 1) * 512].rearrange("(j p) n -> p j n", p=128))], writes=[t_WOh[hf]])
            sch.dma("sp", [(GT1[:], bc(modp_d, 0, D))], reads=[t_modp], writes=[t_GT1])
            for t in range(NT):
                ts_ = slice(t * 128, (t + 1) * 128)
                i3 = t % 3; i2 = t % 2
                pre = t < 9
                if not pre:
                    sch.dma("sp", [(XT[i3][:], x_d[ts_, :])], writes=[t_XT[i3]])
                b0 = (t % 4) * 2
                for hf in range(2):
                    for j in range(8):
                        op("pe", lambda e, hf=hf, j=j, b0=b0, ts_=ts_: e.matmul(PS[:, (b0 + hf) * 512:(b0 + hf + 1) * 512], OT[:, j, ts_], WO[:, j, hf * 512:(hf + 1) * 512],
                                                                                start=(j == 0), stop=(j == 7)), [t_OT[j], t_WOh[hf]], [t_ps[b0 + hf]])
                op("dve", lambda e, b0=b0, i2=i2: e.tensor_tensor(out=TMP[i2][:], in0=PS[:, b0 * 512:(b0 + 2) * 512], in1=GT1[:], op=ALU.mult),
                   [t_ps[b0], t_ps[b0 + 1], t_GT1], [t_TMP[i2]])
                if pre:
                    op("pool", lambda e, t=t, i2=i2: e.tensor_tensor(out=XN[:, t, :], in0=TMP[i2][:], in1=XN[:, t, :], op=ALU.add),
                       [t_TMP[i2], t_XN[t]], [t_XN[t]])
                else:
                    op("pool", lambda e, t=t, i2=i2, i3=i3: e.tensor_tensor(out=XN[:, t, :], in0=TMP[i2][:], in1=XT[i3][:], op=ALU.add),
                       [t_TMP[i2], t_XT[i3]] + list(t_OT) + [t_hT], [t_XN[t]])
            dbg("XN", XN, t_XN, [128, NT, D], F32)
            sch.flush()
            if stop == "D":
                return nc, dbg_out

        H2T = OT
        t_H2T = Tile("H2T")
        with ExitStack() as ph:
            PS = ph.enter_context(nc.psum_tensor("psE", [128, 4096], F32))
            t_ps = [Tile("psE%d" % b, True) for b in range(8)]
            CW = sb("CW", [128, 44, 3], F32, ph); CB = sb("CB", [128, 44], F32, ph)
            t_CW = Tile("CW")
            GT2 = sb("GT2", [128, D], F32, ph); t_GT2 = Tile("GT2")
            GFB = sb("GFB", [128, D], F32, ph); t_GFB = Tile("GFB")
            def late_loads():
                sch.dma("sp", [(GT2[:], bc(modp_d, 3 * D, D))], reads=[t_modp], writes=[t_GT2])
                sch.dma("sp", [(GFB[:], bc(gf_d, 0, D))], writes=[t_GFB])
                sch.dma("sp", [(CW[:, :, t], cw_d.ap()[t:t + 1, :].rearrange("o (c p) -> p (o c)", p=128)) for t in range(3)] +
                        [(CB[:], cb_d.ap().rearrange("o (c p) -> p (o c)", p=128))], writes=[t_CW], allow_slow_non_contiguous=True)
            ss = sb("ssE", [128, 2 * NT], F32, ph)
            rs = sb("rsE", [128, 2 * NT], F32, ph)
            t_ss = [Tile("ssE%d" % t) for t in range(2 * NT)]
            t_rs = [Tile("rsE%d" % t) for t in range(2 * NT)]
            t_ssall = Tile("ssEall")
            op("dve", lambda e: e.memset(ss[:], 0.0), [], [t_ssall])
            t_junk = Tile("junkE")
            TMP = [sb("TMPe%d" % i, [128, D], F32, ph) for i in range(2)]
            t_TMP = [Tile("TMPe%d" % i) for i in range(2)]
            WU = [sb("WU%d" % i, [128, 8, 2, 128], BF, ph) for i in range(3)]
            t_WU = [Tile("WU%d" % i) for i in range(3)]
            WD = [sb("WD%d" % i, [128, GMAX, D], BF, ph) for i in range(1)]
            t_WD = [Tile("WD%d" % i) for i in range(1)]

            def load_wu(j):
                w = j % 3
                sch.dma("pool", [(WU[w][:, :, 0, :], wup_d[:, j * 128:(j + 1) * 128].rearrange("(k p) n -> p k n", p=128)),
                                 (WU[w][:, :, 1, :], wup_d[:, DFF + j * 128:DFF + (j + 1) * 128].rearrange("(k p) n -> p k n", p=128))], writes=[t_WU[w]])

            def load_wd(gi):
                j0, j1 = GROUPS[gi]
                sch.dma("pool", [(WD[0][:, 0:j1 - j0, :], wdn_d[j0 * 128:j1 * 128, :].rearrange("(j p) n -> p j n", p=128))], writes=[t_WD[0]])

            load_wu(0); load_wu(1); load_wd(0)
            with ExitStack() as ph0:
                junk = sb("junkE", [128, D], BF, ph0)
                S2 = sb("S2", [128, D], F32, ph0); SH2 = sb("SH2", [128, D], F32, ph0); G2B = sb("G2B", [128, D], F32, ph0)
                HB = [sb("HBe%d" % i, [128, D], BF, ph0) for i in range(2)]
                t_HB = [Tile("HBe%d" % i) for i in range(2)]
                t_S2 = Tile("S2"); t_SH2 = Tile("SH2"); t_G2B = Tile("G2B")
                sch.dma("sp", [(SH2[:], bc(modp_d, D, D))], reads=[t_modp], writes=[t_SH2])
                sch.dma("sp", [(S2[:], bc(modp_d, 2 * D, D))], reads=[t_modp], writes=[t_S2])
                sch.dma("sp", [(G2B[:], bc(g2_d, 0, D))], writes=[t_G2B])
                late_loads()
                op("dve", lambda e: e.scalar_tensor_tensor(out=S2[:], in0=S2[:], scalar=1.0, in1=G2B[:], op0=ALU.add, op1=ALU.mult), [t_S2, t_G2B], [t_S2])
                def e0_a(t):
                    i2 = t % 2
                    src = XN[:, t, :]
                    op("act", lambda e: e.activation(out=junk[:], in_=src, func=AF.Square, accum_out=ss[:, t:t + 1]), [t_XN[t], t_ssall], [t_junk, t_ss[t]])
                    op("act", lambda e: e.activation(out=rs[:, t:t + 1], in_=ss[:, t:t + 1], func=AF.Sqrt, scale=1.0 / D, bias=EPS), [t_ss[t]], [t_rs[t]])
                    op("dve", lambda e: e.reciprocal(out=rs[:, t:t + 1], in_=rs[:, t:t + 1]), [t_rs[t]], [t_rs[t]])
                    op("dve", lambda e: e.scalar_tensor_tensor(out=TMP[i2][:], in0=src, scalar=rs[:, t:t + 1], in1=S2[:], op0=ALU.mult, op1=ALU.mult),
                       [t_XN[t], t_rs[t], t_S2], [t_TMP[i2]])
                    op("dve", lambda e: e.tensor_tensor(out=HB[i2][:], in0=TMP[i2][:], in1=SH2[:], op=ALU.add), [t_TMP[i2], t_SH2], [t_HB[i2]])

                def e0_b(t):
                    i2 = t % 2
                    pbk = t % 2
                    pv = PS[:, pbk * 512:(pbk + 1) * 512].bitcast(BF)
                    for k in range(8):
                        op("pe", lambda e, k=k: e.transpose(pv[:, k * 128:(k + 1) * 128], HB[i2][:, k * 128:(k + 1) * 128], ident[:]), [t_HB[i2], t_ident], [t_ps[pbk]])
                    op("act", lambda e: e.activation(out=H2T[:, :, t * 128:(t + 1) * 128], in_=pv.rearrange("p (k t) -> p k t", k=8), func=AF.Copy),
                       [t_ps[pbk]], [t_H2T])

                for t in range(NT + 1):
                    if t < NT:
                        e0_a(t)
                    if t >= 1:
                        e0_b(t - 1)
                dbg("H2T", H2T[:], [t_H2T], [128, 8, S], BF)
                sch.flush()
            with ExitStack() as ph1:
                ACTT = sb("ACTT", [128, GMAX, S], BF, ph1)
                t_ACTT = [Tile("ACTT%d" % i) for i in range(GMAX)]
                AaL = [sb("Aa%d" % i, [128, S], F32, ph1) for i in range(2)]
                GgL = [sb("Gg%d" % i, [128, S], F32, ph1) for i in range(2)]
                t_AaL = [(Tile("Aa%da" % i), Tile("Aa%db" % i)) for i in range(2)]; t_GgL = [(Tile("Gg%da" % i), Tile("Gg%db" % i)) for i in range(2)]
                t_out = Tile("out")
                npair = [0]

                for gi, (j0, j1) in enumerate(GROUPS):
                    ng = j1 - j0
                    wd = 0
                    if gi > 0:
                        load_wd(gi)
                    for j in range(j0, j1):
                        jj = j - j0
                        w = j % 3
                        if j + 2 < NCH:
                            load_wu(j + 2)
                        Aa = AaL[j % 2]; Gg = GgL[j % 2]; t_Aa = t_AaL[j % 2]; t_Gg = t_GgL[j % 2]
                        for half, (ACC, t_ACC, pb0) in enumerate(((Aa, t_Aa, 0), (Gg, t_Gg, 4))):
                            c = half * NCH + j
                            pall = PS[:, pb0 * 512:(pb0 + 4) * 512]
                            for blk in range(4):
                                for k in range(8):
                                    op("pe", lambda e, k=k, blk=blk, half=half, pb0=pb0, w=w: e.matmul(PS[:, (pb0 + blk) * 512:(pb0 + blk + 1) * 512], WU[w][:, k, half, :],
                                                                                                      H2T[:, k, blk * 512:(blk + 1) * 512], start=(k == 0), stop=(k == 7)),
                                       [t_WU[w], t_H2T], [t_ps[pb0 + blk]])
                            H = S // 2
                            tA = t_ps[pb0:pb0 + 2]; tB = t_ps[pb0 + 2:pb0 + 4]
                            tacc = t_ACC
                            op("act", lambda e, ACC=ACC, pall=pall, c=c: e.activation(out=ACC[:, 0:H], in_=pall[:, 0:H], func=AF.Identity, scale=CW[:, c, 1:2], bias=CB[:, c:c + 1]),
                               tA + [t_CW], [tacc[0]])
                            op("act", lambda e, ACC=ACC, pall=pall, c=c: e.activation(out=ACC[:, H:S], in_=pall[:, H:S], func=AF.Identity, scale=CW[:, c, 1:2], bias=CB[:, c:c + 1]),
                               tB + [t_CW], [tacc[1]])
                            op("dve", lambda e, ACC=ACC, pall=pall, c=c: e.scalar_tensor_tensor(out=ACC[:, 1:H], in0=pall[:, 0:H - 1], scalar=CW[:, c, 0:1], in1=ACC[:, 1:H],
                                                                                                op0=ALU.mult, op1=ALU.add), tA + [t_CW, tacc[0]], [tacc[0]])
                            op("dve", lambda e, ACC=ACC, pall=pall, c=c: e.scalar_tensor_tensor(out=ACC[:, H:S], in0=pall[:, H - 1:S - 1], scalar=CW[:, c, 0:1], in1=ACC[:, H:S],
                                                                                                op0=ALU.mult, op1=ALU.add), tA[1:2] + tB + [t_CW, tacc[1]], [tacc[1]])
                            op("dve", lambda e, ACC=ACC, pall=pall, c=c: e.scalar_tensor_tensor(out=ACC[:, 0:H], in0=pall[:, 1:H + 1], scalar=CW[:, c, 2:3], in1=ACC[:, 0:H],
                                                                                                op0=ALU.mult, op1=ALU.add), tA + tB[0:1] + [t_CW, tacc[0]], [tacc[0]])
                            op("dve", lambda e, ACC=ACC, pall=pall, c=c: e.scalar_tensor_tensor(out=ACC[:, H:S - 1], in0=pall[:, H + 1:S], scalar=CW[:, c, 2:3], in1=ACC[:, H:S - 1],
                                                                                                op0=ALU.mult, op1=ALU.add), tB + [t_CW, tacc[1]], [tacc[1]])
                        op("act", lambda e: e.activation(out=Gg[:], in_=Gg[:], func=AF.Silu), list(t_Gg), list(t_Gg))
                        op("pool", lambda e, jj=jj: e.tensor_tensor(out=ACTT[:, jj, :], in0=Aa[:], in1=Gg[:], op=ALU.mult), list(t_Aa) + list(t_Gg), [t_ACTT[jj]])
                    last = gi == len(GROUPS) - 1
                    for t in range(NT):
                        ts_ = slice(t * 128, (t + 1) * 128)
                        b0 = (t % 4) * 2
                        i2 = t % 2
                        for hf in range(2):
                            for jj in range(ng):
                                op("pe", lambda e, hf=hf, jj=jj, b0=b0, ts_=ts_, wd=wd, ng=ng: e.matmul(PS[:, (b0 + hf) * 512:(b0 + hf + 1) * 512], ACTT[:, jj, ts_],
                                                                                                        WD[wd][:, jj, hf * 512:(hf + 1) * 512], start=(jj == 0), stop=(jj == ng - 1)),
                                   [t_ACTT[jj], t_WD[wd]], [t_ps[b0 + hf]])
                        op("dve", lambda e, b0=b0, i2=i2: e.tensor_tensor(out=TMP[i2][:], in0=PS[:, b0 * 512:(b0 + 2) * 512], in1=GT2[:], op=ALU.mult),
                           [t_ps[b0], t_ps[b0 + 1], t_GT2], [t_TMP[i2]])
                        op("pool", lambda e, t=t, i2=i2: e.tensor_tensor(out=XN[:, t, :], in0=XN[:, t, :], in1=TMP[i2][:], op=ALU.add), [t_TMP[i2], t_XN[t]], [t_XN[t]])
                        if last:
                            u = NT + t
                            src = XN[:, t, :]
                            jv = TMP[i2][:, 0:D // 2].bitcast(BF)
                            op("act", lambda e, src=src, u=u, jv=jv: e.activation(out=jv, in_=src, func=AF.Square, accum_out=ss[:, u:u + 1]), [t_XN[t], t_ssall], [t_TMP[i2], t_ss[u]])
                            op("act", lambda e, u=u: e.activation(out=rs[:, u:u + 1], in_=ss[:, u:u + 1], func=AF.Sqrt, scale=1.0 / D, bias=EPS), [t_ss[u]], [t_rs[u]])
                            op("dve", lambda e, u=u: e.reciprocal(out=rs[:, u:u + 1], in_=rs[:, u:u + 1]), [t_rs[u]], [t_rs[u]])
                            op("dve", lambda e, src=src, u=u: e.scalar_tensor_tensor(out=src, in0=src, scalar=rs[:, u:u + 1], in1=GFB[:], op0=ALU.mult, op1=ALU.mult),
                               [t_XN[t], t_rs[u], t_GFB], [t_XN[t]])
                            sch.dma("sp", [(out_d[ts_, :], src)], reads=[t_XN[t]], writes=[t_out])
                sch.wait_tiles("sp", [t_out])
                sch.flush()
    return nc, dbg_out


_CACHE = {}


def _prep_inputs(inputs):
    c = _consts()
    f = lambda a: np.ascontiguousarray(np.asarray(a, dtype=np.float32))
    shared = {
        "c_ctx": f(inputs["c_ctx"]).reshape(1, D),
        "w_ada": f(inputs["w_ada"])[0],
        "b_ada": f(inputs["b_ada"]).reshape(1, 6 * D),
        "g_norm1": f(inputs["g_norm1"]).reshape(1, D),
        "w_in": f(inputs["w_in"])[0],
        "g_q": f(inputs["g_q"]).reshape(1, 256),
        "w_uq": f(inputs["w_uq"])[0],
        "g_kv": f(inputs["g_kv"]).reshape(1, 256),
        "w_ukv": f(inputs["w_ukv"])[0],
        "ret_decay": f(inputs["ret_decay"]).reshape(1, 8),
        "g_ret": f(inputs["g_ret"]).reshape(1, 512),
        "w_out": f(inputs["w_out"])[0],
        "g_norm2": f(inputs["g_norm2"]).reshape(1, D),
        "w_up": f(inputs["w_up"])[0],
        "conv_w": f(inputs["conv_w"])[0],
        "conv_b": f(inputs["conv_b"]).reshape(1, 2 * DFF),
        "w_down": f(inputs["w_down"])[0],
        "g_final": f(inputs["g_final"]).reshape(1, D),
        "k_ident": c["ident"], "k_rt": c["rt"], "k_mt": c["mt"], "k_cst": c["cst"],
    }
    x = f(inputs["x"]); cc = f(inputs["c"]); ctx = f(inputs["ctx"])
    maps = []
    for b in range(8):
        m = dict(shared)
        m["x"] = x[b]
        m["c"] = cc[b].reshape(1, D)
        m["ctx"] = ctx[b]
        maps.append(m)
    return maps


def kernel(**inputs):
    if "nc" not in _CACHE:
        _CACHE["nc"] = build()[0]
    nc = _CACHE["nc"]
    maps = _prep_inputs(inputs)
    res = run_bass_kernel_spmd(nc, maps, core_ids=list(range(8)))
    out = np.stack([np.asarray(r["out"], dtype=np.float32) for r in res.results], axis=0)
    return out
```
